# Optimizing a Trainium2 kernel written in Bass

```python
import math
import jax
import jax.numpy as jnp
from jax import lax
import numpy as np

D_MODEL = 1024
BATCH = 4
SEQ = 8192
DEPTH = 4

CTX_LEN = 256
GRID_W = 64
D_MIX = D_MODEL
D_HYENA = D_MIX // 2
S5_WIDTH = D_MIX - D_HYENA
S5_GROUP = 16
S5_GROUPS = S5_WIDTH // S5_GROUP
S5_STATE = 64
D_PROJ = 3 * D_HYENA + S5_WIDTH
N_DIR = 2
FILTER_EMB = 33
FILTER_HIDDEN = 64
FILTER_INNER = 2
DECAY_TARGET = 1e-2
FAST_DECAY_PCT = 0.3
SLOW_DECAY_PCT = 1.5
D_FF = 2816
RMS_EPS = 1e-6
N_MOD = 6

kernel_name = 'hybrid_hyena_s5_prefix_block'


def rms_norm(x, g):
    x32 = x.astype(jnp.float32)
    y = x32 * lax.rsqrt(jnp.mean(x32 * x32, axis=-1, keepdims=True) + RMS_EPS)
    return (y * g.astype(jnp.float32)).astype(x.dtype)


def modulated_norm(x, g, shift, scale):
    return rms_norm(x, g) * (1 + scale) + shift


def dwconv3_seq(u, w, b):
    up = jnp.pad(u, ((0, 0), (1, 1), (0, 0)))
    return up[:, :-2] * w[0] + up[:, 1:-1] * w[1] + up[:, 2:] * w[2] + b


def hyena_filter(L, w_in, b_in, w_hid, b_hid, freq, w_out):
    f32 = jnp.float32
    t = jnp.linspace(0.0, 1.0, L, dtype=f32)[:, None]
    bands = (FILTER_EMB - 1) // 2
    w = (2.0 * math.pi / L) * jnp.arange(L, dtype=f32)[:, None]
    f = jnp.linspace(1e-4, bands - 1, bands, dtype=f32)[None, :]
    z = jnp.concatenate([t, jnp.cos(f * w), -jnp.sin(f * w)], axis=-1)
    fr = freq.astype(f32)
    h = jnp.sin(fr * (z @ w_in.astype(f32) + b_in.astype(f32)))
    for i in range(FILTER_INNER):
        h = jnp.sin(fr * (h @ w_hid[i].astype(f32) + b_hid[i].astype(f32)))
    h = (h @ w_out.astype(f32)).reshape(L, N_DIR, D_HYENA)
    deltas = jnp.abs(jnp.linspace(math.log(DECAY_TARGET) / FAST_DECAY_PCT,
                                  math.log(DECAY_TARGET) / SLOW_DECAY_PCT, D_HYENA, dtype=f32))
    h = h * jnp.exp(-t * deltas)[:, None, :]
    return jnp.concatenate([h[:, 0], jnp.zeros((1, D_HYENA), f32), h[:0:-1, 1]], axis=0)


def fft_long_conv(u, k, bias):
    L = u.shape[1]
    n = 2 * L
    u_f = jnp.fft.rfft(u, n=n, axis=1)
    k_f = jnp.fft.rfft(k, n=n, axis=0)
    y = jnp.fft.irfft(u_f * k_f[None], n=n, axis=1)[:, :L]
    return y + u * bias


def hyena_mixer(p, short_w, short_b, f_w_in, f_b_in, f_w_hid, f_b_hid, f_freq, f_w_out, bias):
    L = p.shape[1]
    p = dwconv3_seq(p, short_w, short_b)
    x0, x1, v = jnp.split(p, 3, axis=-1)
    k = hyena_filter(L, f_w_in, f_b_in, f_w_hid, f_b_hid, f_freq, f_w_out)
    y = fft_long_conv((x1 * v).astype(jnp.float32), k, bias.astype(jnp.float32))
    return x0 * y.astype(p.dtype)


def s5_discretize(lam_re, lam_im, log_step, b_re, b_im):
    f32 = jnp.float32
    lr, li = lam_re.astype(f32), lam_im.astype(f32)
    dt = jnp.exp(log_step.astype(f32))[:, None]
    mag = jnp.exp(lr * dt)
    a_re, a_im = mag * jnp.cos(li * dt), mag * jnp.sin(li * dt)
    den = lr * lr + li * li
    q_re = ((a_re - 1.0) * lr + a_im * li) / den
    q_im = (a_im * lr - (a_re - 1.0) * li) / den
    br, bi = b_re.astype(f32), b_im.astype(f32)
    bb_re = q_re[..., None] * br - q_im[..., None] * bi
    bb_im = q_re[..., None] * bi + q_im[..., None] * br
    return a_re, a_im, bb_re, bb_im


def complex_affine_combine(e1, e2):
    a1r, a1i, b1r, b1i = e1
    a2r, a2i, b2r, b2i = e2
    return (a2r * a1r - a2i * a1i, a2r * a1i + a2i * a1r,
            a2r * b1r - a2i * b1i + b2r, a2r * b1i + a2i * b1r + b2i)


def s5_scan(ug, a_re, a_im, bb_re, bb_im, s0, reverse):
    L = ug.shape[1]
    bu_re = jnp.einsum('blgh,gph->blgp', ug, bb_re)
    bu_im = jnp.einsum('blgh,gph->blgp', ug, bb_im)
    if s0 is not None:
        s0_re, s0_im = s0
        edge = L - 1 if reverse else 0
        bu_re = bu_re.at[:, edge].add(a_re * s0_re - a_im * s0_im)
        bu_im = bu_im.at[:, edge].add(a_re * s0_im + a_im * s0_re)
    shape = (1, L) + a_re.shape
    elems = (jnp.broadcast_to(a_re, shape), jnp.broadcast_to(a_im, shape), bu_re, bu_im)
    _, _, s_re, s_im = lax.associative_scan(complex_affine_combine, elems, reverse=reverse, axis=1)
    return s_re, s_im


def s5_mixer(u, lam_re, lam_im, log_step, b_re, b_im, c_re, c_im, d, w_glu, b_glu, init, with_output):
    bsz, L, _ = u.shape
    f32 = jnp.float32
    ug = u.astype(f32).reshape(bsz, L, S5_GROUPS, S5_GROUP)
    y = None
    finals = []
    for dr in range(N_DIR):
        rev = dr == 1
        a_re, a_im, bb_re, bb_im = s5_discretize(lam_re[dr], lam_im[dr], log_step[dr], b_re[dr], b_im[dr])
        s0 = None if init is None else init[dr]
        s_re, s_im = s5_scan(ug, a_re, a_im, bb_re, bb_im, s0, rev)
        edge = 0 if rev else L - 1
        finals.append((s_re[:, edge], s_im[:, edge]))
        if with_output:
            y_dir = (jnp.einsum('blgp,ghp->blgh', s_re, c_re[dr].astype(f32))
                     - jnp.einsum('blgp,ghp->blgh', s_im, c_im[dr].astype(f32)))
            y = y_dir if y is None else y + y_dir
    if not with_output:
        return None, finals
    y = y + ug * d.astype(f32).reshape(S5_GROUPS, S5_GROUP)
    y = jax.nn.gelu(y.reshape(bsz, L, S5_WIDTH)).astype(u.dtype)
    return y * jax.nn.sigmoid(y @ w_glu + b_glu), finals


def conv_glu_ffn(h, rows, w_up, conv_w, conv_b, w_down):
    bsz, L, _ = h.shape
    up = h @ w_up
    g, v = up[..., :D_FF], up[..., D_FF:]
    g = lax.conv_general_dilated(g.reshape(bsz, rows, L // rows, D_FF),
                                 conv_w[:, :, None, :].astype(g.dtype), (1, 1), 'SAME',
                                 dimension_numbers=('NHWC', 'HWIO', 'NHWC'),
                                 feature_group_count=D_FF)
    g = g.reshape(bsz, L, D_FF) + conv_b
    return (jax.nn.gelu(g) * v) @ w_down


def setup_inputs(seed: int = 0) -> dict:
    key = jax.random.key(seed)
    ks = iter(jax.random.split(key, 40))
    f32 = jnp.float32

    def nrm(shape, s):
        return s * jax.random.normal(next(ks), shape, f32)

    P, G, H = S5_STATE, S5_GROUPS, S5_GROUP
    return {
        'x': nrm((BATCH, SEQ, D_MODEL), 1.0),
        'c': nrm((BATCH, D_MODEL), 1.0),
        'ctx': nrm((BATCH, CTX_LEN, D_MODEL), 1.0),
        'c_ctx': nrm((D_MODEL,), 1.0),
        'w_ada': nrm((DEPTH, D_MODEL, N_MOD * D_MODEL), D_MODEL ** -0.5),
        'b_ada': nrm((DEPTH, N_MOD * D_MODEL), 0.01),
        'norm_g': 1.0 + nrm((DEPTH, 4, D_MODEL), 0.05),
        'w_in': nrm((DEPTH, D_MODEL, D_PROJ), D_MODEL ** -0.5),
        'hy_short_w': nrm((DEPTH, 3, 3 * D_HYENA), 3 ** -0.5),
        'hy_short_b': nrm((DEPTH, 3 * D_HYENA), 0.01),
        'filt_w_in': nrm((DEPTH, FILTER_EMB, FILTER_HIDDEN), FILTER_EMB ** -0.5),
        'filt_b_in': nrm((DEPTH, FILTER_HIDDEN), 0.01),
        'filt_w_hid': nrm((DEPTH, FILTER_INNER, FILTER_HIDDEN, FILTER_HIDDEN), FILTER_HIDDEN ** -0.5),
        'filt_b_hid': nrm((DEPTH, FILTER_INNER, FILTER_HIDDEN), 0.01),
        'filt_freq': 1.0 + nrm((DEPTH, FILTER_HIDDEN), 0.05),
        'filt_w_out': nrm((DEPTH, FILTER_HIDDEN, N_DIR * D_HYENA), FILTER_HIDDEN ** -0.5),
        'hy_bias': nrm((DEPTH, D_HYENA), 1.0),
        's5_lam_re': -0.5 + nrm((DEPTH, N_DIR, G, P), 0.01),
        's5_lam_im': math.pi * jnp.arange(P, dtype=f32) + nrm((DEPTH, N_DIR, G, P), 0.01),
        's5_log_step': jax.random.uniform(next(ks), (DEPTH, N_DIR, G), f32, math.log(1e-3), math.log(1e-1)),
        's5_b_re': nrm((DEPTH, N_DIR, G, P, H), (2 * H) ** -0.5),
        's5_b_im': nrm((DEPTH, N_DIR, G, P, H), (2 * H) ** -0.5),
        's5_c_re': nrm((DEPTH, N_DIR, G, H, P), 0.5),
        's5_c_im': nrm((DEPTH, N_DIR, G, H, P), 0.5),
        's5_d': nrm((DEPTH, S5_WIDTH), 1.0),
        's5_w_glu': nrm((DEPTH, S5_WIDTH, S5_WIDTH), S5_WIDTH ** -0.5),
        's5_b_glu': nrm((DEPTH, S5_WIDTH), 0.01),
        'w_out': nrm((DEPTH, D_MIX, D_MODEL), D_MIX ** -0.5),
        'ffn_w_up': nrm((DEPTH, D_MODEL, 2 * D_FF), D_MODEL ** -0.5),
        'ffn_conv_w': nrm((DEPTH, 3, 3, D_FF), 1.0 / 3.0),
        'ffn_conv_b': nrm((DEPTH, D_FF), 0.01),
        'ffn_w_down': nrm((DEPTH, D_FF, D_MODEL), D_FF ** -0.5),
    }


def reference(x, c, ctx, c_ctx, w_ada, b_ada, norm_g, w_in, hy_short_w, hy_short_b,
              filt_w_in, filt_b_in, filt_w_hid, filt_b_hid, filt_freq, filt_w_out, hy_bias,
              s5_lam_re, s5_lam_im, s5_log_step, s5_b_re, s5_b_im, s5_c_re, s5_c_im, s5_d,
              s5_w_glu, s5_b_glu, w_out, ffn_w_up, ffn_conv_w, ffn_conv_b, ffn_w_down):
    seq_len = x.shape[1]
    rows = seq_len // GRID_W
    cond_x = jax.nn.silu(c)
    cond_c = jax.nn.silu(c_ctx)
    h_split = 3 * D_HYENA
    for l in range(DEPTH):
        last = l == DEPTH - 1
        mod_x = jnp.split((cond_x @ w_ada[l] + b_ada[l])[:, None, :], N_MOD, axis=-1)
        mod_c = jnp.split(cond_c @ w_ada[l] + b_ada[l], N_MOD, axis=-1)
        hyena_args = (hy_short_w[l], hy_short_b[l], filt_w_in[l], filt_b_in[l], filt_w_hid[l],
                      filt_b_hid[l], filt_freq[l], filt_w_out[l], hy_bias[l])
        s5_args = (s5_lam_re[l], s5_lam_im[l], s5_log_step[l], s5_b_re[l], s5_b_im[l],
                   s5_c_re[l], s5_c_im[l], s5_d[l], s5_w_glu[l], s5_b_glu[l])
        ffn_args = (ffn_w_up[l], ffn_conv_w[l], ffn_conv_b[l], ffn_w_down[l])

        pc = modulated_norm(ctx, norm_g[l, 0], mod_c[0], mod_c[1]) @ w_in[l]
        yc_s5, ctx_state = s5_mixer(pc[..., h_split:], *s5_args, None, not last)

        px = modulated_norm(x, norm_g[l, 0], mod_x[0], mod_x[1]) @ w_in[l]
        yx_s5, _ = s5_mixer(px[..., h_split:], *s5_args, ctx_state, True)
        yx_hy = hyena_mixer(px[..., :h_split], *hyena_args)
        yx = jnp.concatenate([yx_hy, yx_s5], axis=-1) @ w_out[l]
        x = x + mod_x[2] * rms_norm(yx, norm_g[l, 1])
        hx = modulated_norm(x, norm_g[l, 2], mod_x[3], mod_x[4])
        x = x + mod_x[5] * rms_norm(conv_glu_ffn(hx, rows, *ffn_args), norm_g[l, 3])

        if not last:
            yc_hy = hyena_mixer(pc[..., :h_split], *hyena_args)
            yc = jnp.concatenate([yc_hy, yc_s5], axis=-1) @ w_out[l]
            ctx = ctx + mod_c[2] * rms_norm(yc, norm_g[l, 1])
            hc = modulated_norm(ctx, norm_g[l, 2], mod_c[3], mod_c[4])
            ctx = ctx + mod_c[5] * rms_norm(conv_glu_ffn(hc, 1, *ffn_args), norm_g[l, 3])
    return x
```

```python
import math
import numpy as np
from contextlib import ExitStack
import concourse.bass as bass
import concourse.mybir as mybir
from concourse.bass_utils import run_bass_kernel_spmd

F32 = mybir.dt.float32
BF16 = mybir.dt.bfloat16
AF = mybir.ActivationFunctionType
ALU = mybir.AluOpType

D = 1024
SEQ = 8192
CTX = 256
DEPTH = 4
DFF = 2816
NFF = DFF // 128
PAD = 8
EPS = 1e-6
MAGIC = 12582912.0
TWO_PI = 2.0 * math.pi


class _Op:
    __slots__ = ("eng", "fn", "waits", "sem", "val", "dma")


class Prog:
    COMPUTE = ("pe", "act", "dve", "pool")
    NDMA = 8

    def __init__(self, nc, stack):
        self.nc = nc
        self.h = {"pe": nc.tensor, "act": nc.scalar, "dve": nc.vector, "pool": nc.gpsimd, "sp": nc.sync}
        self.streams = {e: [] for e in self.h}
        self.esem = {e: stack.enter_context(nc.semaphore("s_" + e)) for e in self.COMPUTE}
        self.ecnt = {e: 0 for e in self.COMPUTE}
        self.dsem, self.dcnt, self.drr = {}, {}, {}
        for q in ("sp", "pool", "act"):
            self.dsem[q] = [stack.enter_context(nc.semaphore("d_%s%d" % (q, i))) for i in range(self.NDMA)]
            self.dcnt[q] = [0] * self.NDMA
            self.drr[q] = 0
        self.lastw = {}
        self.readers = {}
        self.waited = {e: {} for e in self.h}
        self.nops = 0

    def _dep(self, eng, op, waits):
        if op is None:
            return
        if (not op.dma) and op.eng == eng and eng == "pe":
            return
        key = id(op.sem)
        if self.waited[eng].get(key, 0) >= op.val:
            return
        cur = waits.get(key)
        if cur is None or cur[1] < op.val:
            waits[key] = (op.sem, op.val)

    def op(self, eng, fn, reads=(), writes=(), dma=False):
        o = _Op()
        o.eng, o.fn, o.dma = eng, fn, dma
        waits = {}
        for r in reads:
            self._dep(eng, self.lastw.get(r), waits)
        for wk in writes:
            self._dep(eng, self.lastw.get(wk), waits)
            for rd in self.readers.get(wk, ()):
                self._dep(eng, rd, waits)
        if dma:
            i = self.drr[eng]
            self.drr[eng] = (i + 1) % self.NDMA
            sem = self.dsem[eng][i]
            prev = self.dcnt[eng][i]
            if prev > 0 and self.waited[eng].get(id(sem), 0) < prev:
                cur = waits.get(id(sem))
                if cur is None or cur[1] < prev:
                    waits[id(sem)] = (sem, prev)
            self.dcnt[eng][i] = prev + 16
            o.sem, o.val = sem, prev + 16
        else:
            self.ecnt[eng] += 1
            o.sem, o.val = self.esem[eng], self.ecnt[eng]
        for key, (s, v) in waits.items():
            self.waited[eng][key] = v
        o.waits = list(waits.values())
        self.streams[eng].append(o)
        for r in reads:
            self.readers.setdefault(r, []).append(o)
        for wk in writes:
            self.lastw[wk] = o
            self.readers[wk] = []
        self.nops += 1
        return o

    def barrier(self, engines=None):
        tot = {}
        for e in self.COMPUTE:
            if self.ecnt[e] > 0:
                tot[id(self.esem[e])] = (self.esem[e], self.ecnt[e])
        for q in self.dsem:
            for i in range(self.NDMA):
                if self.dcnt[q][i] > 0:
                    tot[id(self.dsem[q][i])] = (self.dsem[q][i], self.dcnt[q][i])
        for eng in (engines or list(self.h)):
            waits = []
            for key, (s, v) in tot.items():
                if self.waited[eng].get(key, 0) < v:
                    if eng in self.COMPUTE and s is self.esem[eng]:
                        continue
                    waits.append((s, v))
                    self.waited[eng][key] = v
            if waits:
                o = _Op()
                o.eng, o.fn, o.dma, o.sem, o.val = eng, None, False, None, 0
                o.waits = waits
                self.streams[eng].append(o)
        self.lastw = {}
        self.readers = {}

    def emit(self, block):
        def mk(ename):
            def body(e):
                for o in self.streams[ename]:
                    for (s, v) in o.waits:
                        e.wait_ge(s, v)
                    if o.fn is None:
                        continue
                    o.fn(e).then_inc(o.sem, 16 if o.dma else 1)
            return body
        block.sync(mk("sp"))
        block.scalar(mk("act"))
        block.vector(mk("dve"))
        block.gpsimd(mk("pool"))
        block.tensor(mk("pe"))


def _f32(a):
    return np.ascontiguousarray(np.asarray(a, dtype=np.float32))


MON = {"x": (128, 64, 65, SEQ), "c": (4, 2, 3, CTX)}


def host_constants():
    c = {}
    c["ident"] = _f32(np.eye(128))
    k2 = np.arange(128)[:, None]
    n2 = np.arange(128)[None, :]
    G = np.exp(2j * np.pi * k2 * n2 / 128.0)
    gt = np.zeros((128, 2, 2, 128))
    for hf in range(2):
        sl = slice(hf * 64, hf * 64 + 64)
        gt[:, hf, 0, 0:64] = G.real[:, sl]
        gt[:, hf, 0, 64:128] = G.imag[:, sl]
        gt[:, hf, 1, 0:64] = -G.imag[:, sl]
        gt[:, hf, 1, 64:128] = G.real[:, sl]
    c["gtab"] = _f32(gt)
    deltas = np.abs(np.linspace(math.log(1e-2) / 0.3, math.log(1e-2) / 1.5, 512))
    for s, (N1, NZ, NK, L) in MON.items():
        N = 128 * N1
        n1 = np.arange(NZ)[:, None]
        k1 = np.arange(NK)[None, :]
        ang = 2 * np.pi * n1 * k1 / N1
        c["f1tab_" + s] = _f32(np.concatenate([np.cos(ang), -np.sin(ang)], axis=1))
        f2 = np.zeros((NK, 128, 4, 128))
        nn = np.arange(128)[:, None]
        kk = np.arange(128)[None, :]
        for a in range(NK):
            M = np.exp(-2j * np.pi * nn * (a + N1 * kk) / N)
            f2[a, :, 0] = M.real
            f2[a, :, 1] = M.imag
            f2[a, :, 2] = -M.imag
            f2[a, :, 3] = -M.real
        c["f2tab_" + s] = _f32(f2.reshape(NK, 128, 512))
        ck = np.full(NK, 2.0)
        ck[0] = 1.0
        ck[NK - 1] = 1.0
        i2 = np.zeros((NK, 128, 2, NZ))
        kq = np.arange(NK)[:, None, None]
        nq = np.arange(128)[None, :, None]
        mq = np.arange(NZ)[None, None, :]
        R = (ck[:, None, None] / N) * np.exp(2j * np.pi * kq * (128 * mq + nq) / N)
        i2[:, :, 0, :] = R.real
        i2[:, :, 1, :] = -R.imag
        c["i2tab_" + s] = _f32(i2)
        t = np.arange(L) / (L - 1.0)
        wv = (2.0 * np.pi / L) * np.arange(L)
        f = np.linspace(1e-4, 15.0, 16)
        z = np.concatenate([t[:, None], np.cos(f[None, :] * wv[:, None]), -np.sin(f[None, :] * wv[:, None])], axis=1)
        c["zT_" + s] = _f32(z.T)
        c["d1_" + s] = _f32(np.exp(-(128.0 * np.arange(NZ)[:, None] / (L - 1.0)) * deltas[None, :]))
        c["d2_" + s] = _f32(np.exp(-(np.arange(128)[:, None] / (L - 1.0)) * deltas[None, :]))
    sel = np.zeros((128, 64, 128))
    for gl in range(8):
        for jj in range(8):
            for hi in range(16):
                sel[gl * 16 + hi, gl * 8 + jj, jj * 16 + hi] = 1.0
    c["sel"] = _f32(sel)
    c["selT"] = _f32(sel.transpose(2, 1, 0))
    J = np.zeros((128, 128))
    for p in range(64):
        J[p, 64 + p] = 1.0
        J[64 + p, p] = 1.0
    c["jswap"] = _f32(J)
    msk = np.zeros((128, 2, 2, 256))
    idp = np.zeros((128, 2, 256))
    for jh in range(2):
        for jj in range(8):
            j = jh * 8 + jj
            for hi in range(16):
                for t in range(16):
                    if t >= j:
                        msk[jj * 16 + hi, 0, jh, t * 16:(t + 1) * 16] = 1.0
                    if t <= j:
                        msk[jj * 16 + hi, 1, jh, t * 16:(t + 1) * 16] = 1.0
                idp[jj * 16 + hi, jh, j * 16 + hi] = 1.0
    c["s5msk"] = _f32(msk)
    c["s5idp"] = _f32(idp)
    sg = np.zeros((128, 4))
    sg[:64, 0], sg[64:, 0] = -1.0, 1.0
    sg[:64, 1], sg[64:, 1] = 1.0, -1.0
    sg[:64, 2], sg[64:, 2] = math.pi / 2, 0.0
    sg[:64, 3], sg[64:, 3] = 0.0, math.pi / 2
    c["s5sg"] = _f32(sg)
    return c


def layout_weights(inp):
    w = {}
    g = lambda k: np.asarray(inp[k], dtype=np.float32)
    w["w_ada"] = _f32(g("w_ada"))
    w["b_adaT"] = _f32(g("b_ada").reshape(DEPTH, 48, 128).transpose(0, 2, 1))
    w["ngT"] = _f32(g("norm_g").reshape(DEPTH, 4, 8, 128).transpose(0, 3, 1, 2))
    w["w_in"] = _f32(g("w_in"))
    w["w_out"] = _f32(g("w_out"))
    w["w_up"] = _f32(g("ffn_w_up"))
    w["w_down"] = _f32(g("ffn_w_down"))
    w["fcwT"] = _f32(g("ffn_conv_w").reshape(DEPTH, 9, NFF, 128).transpose(0, 3, 1, 2))
    w["fcbT"] = _f32(g("ffn_conv_b").reshape(DEPTH, NFF, 128).transpose(0, 2, 1))
    w["hswT"] = _f32(g("hy_short_w").reshape(DEPTH, 3, 12, 128).transpose(0, 3, 1, 2))
    w["hsbT"] = _f32(g("hy_short_b").reshape(DEPTH, 12, 128).transpose(0, 2, 1))
    w["hbiasT"] = _f32(g("hy_bias").reshape(DEPTH, 4, 128).transpose(0, 2, 1))
    w["f_win"] = _f32(g("filt_w_in"))
    w["f_whid"] = _f32(g("filt_w_hid").transpose(0, 2, 1, 3))
    w["f_b"] = _f32(np.concatenate([g("filt_b_in")[:, :, None], g("filt_b_hid").transpose(0, 2, 1)], axis=2))
    w["f_freq"] = _f32(g("filt_freq")[:, :, None])
    w["f_wout"] = _f32(g("filt_w_out"))
    dup = lambda a: np.concatenate([a, a], axis=1)
    lam = np.stack([g("s5_lam_re").reshape(DEPTH, 64, 64).transpose(0, 2, 1),
                    g("s5_lam_im").reshape(DEPTH, 64, 64).transpose(0, 2, 1)], axis=2)
    w["s5lam"] = _f32(dup(lam))
    w["s5ls"] = _f32(np.broadcast_to(g("s5_log_step").reshape(DEPTH, 1, 64), (DEPTH, 128, 64)))
    bb = np.stack([g("s5_b_re").reshape(DEPTH, 64, 64, 16).transpose(0, 2, 1, 3),
                   g("s5_b_im").reshape(DEPTH, 64, 64, 16).transpose(0, 2, 1, 3)], axis=2)
    w["s5b"] = _f32(dup(bb))
    cc = np.stack([g("s5_c_re").reshape(DEPTH, 64, 16, 64).transpose(0, 3, 1, 2),
                   g("s5_c_im").reshape(DEPTH, 64, 16, 64).transpose(0, 3, 1, 2)], axis=2)
    w["s5c"] = _f32(dup(cc))
    dd = g("s5_d").reshape(DEPTH, 32, 16).transpose(0, 2, 1)
    w["s5dT"] = _f32(np.tile(dd, (1, 8, 1)))
    w["s5bglu"] = _f32(g("s5_b_glu").reshape(DEPTH, 4, 128).transpose(0, 2, 1))
    w["w_glu"] = _f32(g("s5_w_glu"))
    return w


class Seq:
    pass


def build_program(depth_run=DEPTH, feed_y=False, dbg=False):
    nc = bass.Bass("TRN2", target_bir_lowering=False)
    dram_in = {}

    def din(name, shape, dt=F32):
        dram_in[name] = nc.dram_tensor(name, list(shape), dt, kind="ExternalInput").ap()
        return dram_in[name]

    def dscr(name, shape, dt):
        if dbg and name in dbg:
            return nc.dram_tensor(name, list(shape), dt, kind="ExternalOutput").ap()
        return nc.dram_tensor(name, list(shape), dt, kind="Internal").ap()

    x_d = din("x", [SEQ, D])
    ctx_d = din("ctx", [CTX, D])
    cT_d = din("cT", [128, 8, 2])
    w_ada_d = din("w_ada", [DEPTH, D, 6 * D])
    b_adaT_d = din("b_adaT", [DEPTH, 128, 48])
    ngT_d = din("ngT", [DEPTH, 128, 4, 8])
    w_in_d = din("w_in", [DEPTH, D, 2 * D])
    w_out_d = din("w_out", [DEPTH, D, D])
    w_up_d = din("w_up", [DEPTH, D, 2 * DFF])
    w_down_d = din("w_down", [DEPTH, DFF, D])
    fcwT_d = din("fcwT", [DEPTH, 128, 9, NFF])
    fcbT_d = din("fcbT", [DEPTH, 128, NFF])
    ident_d = din("ident", [128, 128])
    hswT_d = din("hswT", [DEPTH, 128, 3, 12])
    hsbT_d = din("hsbT", [DEPTH, 128, 12])
    hbiasT_d = din("hbiasT", [DEPTH, 128, 4])
    f_win_d = din("f_win", [DEPTH, 33, 64])
    f_whid_d = din("f_whid", [DEPTH, 64, 2, 64])
    f_b_d = din("f_b", [DEPTH, 64, 3])
    f_freq_d = din("f_freq", [DEPTH, 64, 1])
    f_wout_d = din("f_wout", [DEPTH, 64, 1024])
    gtab_d = din("gtab", [128, 2, 2, 128])
    s5lam_d = din("s5lam", [DEPTH, 128, 2, 64])
    s5ls_d = din("s5ls", [DEPTH, 128, 64])
    s5b_d = din("s5b", [DEPTH, 128, 2, 64, 16])
    s5c_d = din("s5c", [DEPTH, 128, 2, 64, 16])
    s5dT_d = din("s5dT", [DEPTH, 128, 32])
    s5bglu_d = din("s5bglu", [DEPTH, 128, 4])
    w_glu_d = din("w_glu", [DEPTH, 512, 512])
    sel_d = din("sel", [128, 64, 128])
    selT_d = din("selT", [128, 64, 128])
    jswap_d = din("jswap", [128, 128])
    s5msk_d = din("s5msk", [128, 2, 2, 256])
    s5idp_d = din("s5idp", [128, 2, 256])
    s5sg_d = din("s5sg", [128, 4])
    mon_d = {}
    for s_, (N1_, NZ_, NK_, L_) in MON.items():
        mon_d[s_] = dict(f1=din("f1tab_" + s_, [NZ_, 2 * NK_]), f2=din("f2tab_" + s_, [NK_, 128, 512]),
                         i2=din("i2tab_" + s_, [NK_, 128, 2, NZ_]), z=din("zT_" + s_, [33, L_]),
                         d1=din("d1_" + s_, [NZ_, 512]), d2=din("d2_" + s_, [128, 512]))
    if feed_y:
        fy_x = din("fy_x", [D, SEQ])
        fy_c = din("fy_c", [D, CTX])

    y_d = nc.dram_tensor("y", [SEQ, D], F32, kind="ExternalOutput").ap()
    dbg_out = {}

    cres_d = dscr("cres", [CTX, D], F32)
    xmix_d = dscr("xmix", [SEQ, D], F32)
    cmix_d = dscr("cmix", [CTX, D], F32)
    gv_d = dscr("gvec", [2, 2, D], F32)
    wupb_d = dscr("wupb", [D, 2 * DFF], BF16)
    pT = {"x": dscr("pT_x", [2 * D, SEQ + 2 * PAD], BF16), "c": dscr("pT_c", [2 * D, CTX + 2 * PAD], BF16)}
    yT = {"x": dscr("yT_x", [D, SEQ], BF16), "c": dscr("yT_c", [D, CTX], BF16)}
    s5tab_d = dscr("s5tab", [32, 128, 1536], BF16)
    uT = {"x": dscr("uT_x", [512, SEQ], BF16), "c": dscr("uT_c", [512, CTX], BF16)}
    x0T = {"x": dscr("x0T_x", [512, SEQ], BF16), "c": dscr("x0T_c", [512, CTX], BF16)}
    yhT = {"x": dscr("yhT_x", [512, SEQ], BF16), "c": dscr("yhT_c", [512, CTX], BF16)}

    with ExitStack() as st:
        P = Prog(nc, st)

        uniq = [0]

        def sb(stack, name, shape, dt):
            uniq[0] += 1
            return stack.enter_context(nc.sbuf_tensor("%s_%d" % (name, uniq[0]), list(shape), dt))

        ps = [st.enter_context(nc.psum_tensor("ps%d" % i, [128, 512], F32)) for i in range(8)]
        psc = [0]

        def bank():
            i = psc[0]
            psc[0] = (i + 1) % 8
            return ps[i], ("ps", i)

        def dma(q, out, in_, reads=(), writes=(), slow=False):
            if slow:
                P.op(q, lambda e: e.dma_start(out=out, in_=in_, allow_slow_non_contiguous=True), reads, writes, dma=True)
            else:
                P.op(q, lambda e: e.dma_start(out=out, in_=in_), reads, writes, dma=True)

        def mm(out, lhsT, rhs, start, stop, reads, writes):
            P.op("pe", lambda e: e.matmul(out, lhsT=lhsT, rhs=rhs, start=start, stop=stop), reads, writes)

        def act(out, in_, func, reads, writes, scale=None, bias=None, accum=None):
            kw = {}
            if scale is not None:
                kw["scale"] = scale
            if bias is not None:
                kw["bias"] = bias
            if accum is not None:
                kw["accum_out"] = accum
            P.op("act", lambda e: e.activation(out=out, in_=in_, func=func, **kw), reads, writes)

        def tt(eng, out, in0, in1, op, reads, writes):
            P.op(eng, lambda e: e.tensor_tensor(out=out, in0=in0, in1=in1, op=op), reads, writes)

        def ts(eng, out, in0, s1, s2, op0, op1, reads, writes):
            if op1 is None:
                P.op(eng, lambda e: e.tensor_scalar(out=out, in0=in0, scalar1=s1, scalar2=None, op0=op0), reads, writes)
            else:
                P.op(eng, lambda e: e.tensor_scalar(out=out, in0=in0, scalar1=s1, scalar2=s2, op0=op0, op1=op1), reads, writes)

        def stt(out, in0, scalar, in1, op0, op1, reads, writes):
            P.op("dve", lambda e: e.scalar_tensor_tensor(out=out, in0=in0, scalar=scalar, in1=in1, op0=op0, op1=op1), reads, writes)

        def cp(eng, out, in_, reads, writes):
            if eng == "act":
                act(out, in_, AF.Copy, reads, writes)
            else:
                P.op(eng, lambda e: e.tensor_copy(out=out, in_=in_), reads, writes)

        def memset(eng, ap, val, writes):
            P.op(eng, lambda e: e.memset(ap, val), (), writes)

        identf = sb(st, "identf", [128, 128], F32)
        identb = sb(st, "identb", [128, 128], BF16)
        cond = sb(st, "cond", [128, 8, 2], F32)
        modT = sb(st, "modT", [128, 48, 2], F32)
        ngt = sb(st, "ngt", [128, 4, 8], F32)
        vec = {}
        for s in ("x", "c"):
            for nm in ("gs1", "sh1", "gs3", "sh3", "ga2", "ga4"):
                vec[(s, nm)] = sb(st, "v_%s_%s" % (s, nm), [128, 8], F32)
        G2 = {s: sb(st, "G2" + s, [128, D], F32) for s in ("x", "c")}
        G4 = {s: sb(st, "G4" + s, [128, D], F32) for s in ("x", "c")}
        SI = {"x": 0, "c": 1}
        epsb = sb(st, "epsb", [128, 1], F32)

        dma("sp", identf[:], ident_d[:, :], writes=["identf"])
        cp("dve", identb[:], identf[:], ["identf"], ["identb"])
        memset("dve", epsb[:], EPS, ["epsb"])
        dma("sp", cond[:], cT_d[:, :, :], writes=["cond"])
        act(cond[:], cond[:], AF.Silu, ["cond"], ["cond"])
        with ExitStack() as ph:
            zt = sb(ph, "zt", [128, 16, PAD], BF16)
            memset("dve", zt[:], 0.0, ["zt"])
            for s, L in (("x", SEQ), ("c", CTX)):
                v = pT[s].rearrange("(m p) t -> p m t", p=128)
                dma("sp", v[:, :, 0:PAD], zt[:], reads=["zt"])
                dma("sp", v[:, :, PAD + L:PAD + L + PAD], zt[:], reads=["zt"])
            P.barrier()

        seqs = {}
        for s, L, src in (("c", CTX, ctx_d), ("x", SEQ, x_d)):
            q = Seq()
            q.name, q.L, q.src = s, L, src
            q.res = y_d if s == "x" else cres_d
            q.mix = xmix_d if s == "x" else cmix_d
            seqs[s] = q

        def phase_mod(l):
            with ExitStack() as ph:
                wa = [sb(ph, "wa%d" % i, [128, 8, 512], F32) for i in range(2)]
                bad = sb(ph, "bad", [128, 48], F32)
                tmp = sb(ph, "modtmp", [128, 8], F32)
                dma("sp", bad[:], b_adaT_d[l], writes=["bad"])
                dma("sp", ngt[:], ngT_d[l], writes=["ngt"])
                wv = w_ada_d[l].rearrange("(k p) n -> p k n", p=128)
                bk, bkey = bank()
                for cb in range(12):
                    slot = cb % 2
                    dma("sp" if cb % 2 == 0 else "pool", wa[slot][:], wv[:, :, cb * 512:(cb + 1) * 512], writes=[("wa", slot)])
                    for mi in range(4):
                        m = cb * 4 + mi
                        for k in range(8):
                            mm(bk[:, 2 * m:2 * m + 2], wa[slot][:, k, mi * 128:(mi + 1) * 128], cond[:, k, :],
                               k == 0, k == 7, [("wa", slot), "cond"], [bkey])
                tt("dve", modT[:], bk[:, 0:96].rearrange("p (m s) -> p m s", s=2),
                   bad[:].unsqueeze(2).to_broadcast([128, 48, 2]), ALU.add, [bkey, "bad"], ["modT"])
                for s in ("x", "c"):
                    si = SI[s]
                    md = lambda i: modT[:, i * 8:(i + 1) * 8, si]
                    stt(vec[(s, "gs1")][:], md(1), 1.0, ngt[:, 0, :], ALU.add, ALU.mult, ["modT", "ngt"], [("v", s, "gs1")])
                    cp("dve", vec[(s, "sh1")][:], md(0), ["modT"], [("v", s, "sh1")])
                    stt(vec[(s, "gs3")][:], md(4), 1.0, ngt[:, 2, :], ALU.add, ALU.mult, ["modT", "ngt"], [("v", s, "gs3")])
                    cp("dve", vec[(s, "sh3")][:], md(3), ["modT"], [("v", s, "sh3")])
                    tt("dve", vec[(s, "ga2")][:], md(2), ngt[:, 1, :], ALU.mult, ["modT", "ngt"], [("v", s, "ga2")])
                    tt("dve", vec[(s, "ga4")][:], md(5), ngt[:, 3, :], ALU.mult, ["modT", "ngt"], [("v", s, "ga4")])
                    for j, nm in enumerate(("ga2", "ga4")):
                        dma("sp", gv_d[si, j].rearrange("(k p) -> p k", p=128), vec[(s, nm)][:],
                            reads=[("v", s, nm)], writes=[("gv", si, j)], slow=True)
                    dma("sp", G2[s][:], gv_d[si, 0:1, :].to_broadcast([128, D]), reads=[("gv", si, 0)], writes=[("G2", s)])
                    dma("sp", G4[s][:], gv_d[si, 1:2, :].to_broadcast([128, D]), reads=[("gv", si, 1)], writes=[("G4", s)])
                P.barrier()

        def norm_rows(xt_ap, na, ss, rs, junk, keys_r, key_ss, key_rs):
            for a in range(na):
                act(junk[:], xt_ap[:, a, :], AF.Square, keys_r, [key_ss], accum=ss[:, a:a + 1])
            act(rs[:, 0:na], ss[:, 0:na], AF.Sqrt, [key_ss], [key_rs], scale=1.0 / D, bias=epsb[:])
            P.op("dve", lambda e: e.reciprocal(out=rs[:, 0:na], in_=rs[:, 0:na]), [key_rs], [key_rs])

        def phase_in(l):
            with ExitStack() as ph:
                win = sb(ph, "win", [128, 8, 2 * D], BF16)
                wv = w_in_d[l].rearrange("(k p) n -> p k n", p=128)
                for k in range(8):
                    dma("pool", win[:, k, :], wv[:, k, :], writes=["win"])
                xts = [sb(ph, "xt%d" % i, [128, 4, D], F32) for i in range(2)]
                xss = [sb(ph, "xs%d" % i, [128, 4, D], BF16) for i in range(2)]
                xnT = [sb(ph, "xnT%d" % i, [128, 8, 512], BF16) for i in range(2)]
                pout = [sb(ph, "pout%d" % i, [128, 16, 512], BF16) for i in range(2)]
                junk = sb(ph, "junk", [128, D], BF16)
                ssq = [sb(ph, "ssq%d" % i, [128, 4], F32) for i in range(2)]
                rsd = [sb(ph, "rsd%d" % i, [128, 4], F32) for i in range(2)]
                cnt = 0
                for s in ("c", "x"):
                    q = seqs[s]
                    src = q.src if l == 0 else q.res
                    TB = min(512, q.L)
                    for t0 in range(0, q.L, TB):
                        nt = TB
                        na = nt // 128
                        sl = cnt % 2
                        cnt += 1
                        xt, xs = xts[sl], xss[sl]
                        dma("sp", xt[:, 0:na, :], src[t0:t0 + nt, :].rearrange("(a p) f -> p a f", p=128), writes=[("xt", sl)])
                        norm_rows(xt, na, ssq[sl], rsd[sl], junk, [("xt", sl)], ("ss", sl), ("rs", sl))
                        for a in range(na):
                            ts("dve" if a % 2 == 0 else "pool", xs[:, a, :], xt[:, a, :], rsd[sl][:, a:a + 1], None, ALU.mult, None,
                               [("xt", sl), ("rs", sl)], [("xs", sl, a)])
                        for k in range(8):
                            bk, bkey = bank()
                            for a in range(na):
                                mm(bk[:, a * 128:(a + 1) * 128], xs[:, a, k * 128:(k + 1) * 128], identb[:], True, True,
                                   [("xs", sl, a), "identb"], [bkey])
                            act(xnT[sl][:, k, 0:nt], bk[:, 0:nt], AF.Identity, [bkey, ("v", s, "gs1"), ("v", s, "sh1")], [("xnT", sl, k)],
                                scale=vec[(s, "gs1")][:, k:k + 1], bias=vec[(s, "sh1")][:, k:k + 1])
                        for m in range(16):
                            bk, bkey = bank()
                            for k in range(8):
                                mm(bk[:, 0:nt], win[:, k, m * 128:(m + 1) * 128], xnT[sl][:, k, 0:nt], k == 0, k == 7,
                                   ["win", ("xnT", sl, k)], [bkey])
                            cp("act" if m % 2 == 0 else "dve", pout[sl][:, m, 0:nt], bk[:, 0:nt], [bkey], [("pout", sl)])
                        dma("sp", pT[s].rearrange("(m p) t -> p m t", p=128)[:, :, PAD + t0:PAD + t0 + nt], pout[sl][:, :, 0:nt],
                            reads=[("pout", sl)])
                P.barrier()

        def resid_epilogue(bk0, bk1, k0, k1, s, Gt, Gkey, xres_ap, xres_key, out_ap, out_key, scr):
            ss2, rs1, junk2, tmp = scr
            act(junk2[:, 0:512], bk0[:], AF.Square, [k0], ["ss2"], accum=ss2[:, 0:1])
            act(junk2[:, 512:1024], bk1[:], AF.Square, [k1], ["ss2"], accum=ss2[:, 1:2])
            tt("dve", rs1[:], ss2[:, 0:1], ss2[:, 1:2], ALU.add, ["ss2"], ["rs1"])
            act(rs1[:], rs1[:], AF.Sqrt, ["rs1"], ["rs1"], scale=1.0 / D, bias=epsb[:])
            P.op("dve", lambda e: e.reciprocal(out=rs1[:], in_=rs1[:]), ["rs1"], ["rs1"])
            stt(tmp[:, 0:512], bk0[:], rs1[:, 0:1], Gt[:, 0:512], ALU.mult, ALU.mult, [k0, "rs1", Gkey], ["etmp0"])
            stt(tmp[:, 512:1024], bk1[:], rs1[:, 0:1], Gt[:, 512:1024], ALU.mult, ALU.mult, [k1, "rs1", Gkey], ["etmp1"])
            tt("pool", out_ap, tmp[:], xres_ap, ALU.add, ["etmp0", "etmp1", xres_key], [out_key])

        def phase_out(l):
            with ExitStack() as ph:
                wo = sb(ph, "wo", [128, 8, D], BF16)
                wv = w_out_d[l].rearrange("(k p) n -> p k n", p=128)
                for k in range(8):
                    dma("pool", wo[:, k, :], wv[:, k, :], writes=["wo"])
                yts = [sb(ph, "yt%d" % i, [128, 8, 512], BF16) for i in range(2)]
                xts = [sb(ph, "xt%d" % i, [128, 4, D], F32) for i in range(2)]
                xos = [sb(ph, "xo%d" % i, [128, 4, D], F32) for i in range(2)]
                scr = (sb(ph, "ss2", [128, 2], F32), sb(ph, "rs1", [128, 1], F32), sb(ph, "junk2", [128, D], BF16),
                       sb(ph, "etmp", [128, D], F32))
                cnt = 0
                for s in ("c", "x"):
                    q = seqs[s]
                    src = q.src if l == 0 else q.res
                    TB = min(512, q.L)
                    for t0 in range(0, q.L, TB):
                        nt = TB
                        na = nt // 128
                        sl = cnt % 2
                        cnt += 1
                        dma("sp", yts[sl][:, :, 0:nt], yT[s].rearrange("(k p) t -> p k t", p=128)[:, :, t0:t0 + nt], writes=[("yt", sl)])
                        dma("sp", xts[sl][:, 0:na, :], src[t0:t0 + nt, :].rearrange("(a p) f -> p a f", p=128), writes=[("xt", sl)])
                        for a in range(na):
                            b0, k0 = bank()
                            b1, k1 = bank()
                            for k in range(8):
                                mm(b0[:], yts[sl][:, k, a * 128:(a + 1) * 128], wo[:, k, 0:512], k == 0, k == 7, [("yt", sl), "wo"], [k0])
                                mm(b1[:], yts[sl][:, k, a * 128:(a + 1) * 128], wo[:, k, 512:1024], k == 0, k == 7, [("yt", sl), "wo"], [k1])
                            resid_epilogue(b0, b1, k0, k1, s, G2[s], ("G2", s), xts[sl][:, a, :], ("xt", sl), xos[sl][:, a, :], ("xo", sl, a), scr)
                        dma("sp", q.mix[t0:t0 + nt, :].rearrange("(a p) f -> p a f", p=128), xos[sl][:, 0:na, :],
                            reads=[("xo", sl, a) for a in range(na)])
                P.barrier()

        def phase_ffn(l):
            with ExitStack() as ph:
                wd = sb(ph, "wd", [128, NFF, D], BF16)
                wv = w_down_d[l].rearrange("(k p) n -> p k n", p=128)
                for k in range(NFF):
                    dma("pool", wd[:, k, :], wv[:, k, :], writes=["wd"])
                stg = [sb(ph, "stg%d" % i, [128, 2 * DFF], BF16) for i in range(2)]
                for k in range(8):
                    dma("pool", stg[k % 2][:], w_up_d[l, k * 128:(k + 1) * 128, :], writes=[("stg", k % 2)])
                    dma("sp", wupb_d[k * 128:(k + 1) * 128, :], stg[k % 2][:], reads=[("stg", k % 2)])
                P.barrier()
                wupv = wupb_d.rearrange("(k p) n -> p k n", p=128)
                cw = sb(ph, "cw", [128, 9, NFF], F32)
                cb = sb(ph, "cb", [128, NFF], F32)
                dma("sp", cw[:], fcwT_d[l], writes=["cw"])
                dma("sp", cb[:], fcbT_d[l], writes=["cb"])
                XE = 1152
                xnT = sb(ph, "fxnT", [128, 8, XE], BF16)
                hT = sb(ph, "hT", [128, NFF, 1024], BF16)
                xt1 = [sb(ph, "fxt%d" % i, [128, D], F32) for i in range(2)]
                xs1 = [sb(ph, "fxs%d" % i, [128, D], BF16) for i in range(2)]
                ss1 = [sb(ph, "fss%d" % i, [128, 1], F32) for i in range(2)]
                rs1b = [sb(ph, "frs%d" % i, [128, 1], F32) for i in range(2)]
                junk = sb(ph, "fjunk", [128, D], BF16)
                wg = [sb(ph, "wg%d" % i, [128, 8, 128], BF16) for i in range(2)]
                wvv = [sb(ph, "wv%d" % i, [128, 8, 128], BF16) for i in range(2)]
                dg = [sb(ph, "dg%d" % i, [128, 9, 128], BF16) for i in range(2)]
                gbuf = [sb(ph, "gbuf%d" % i, [128, 18, 64], BF16) for i in range(2)]
                gel = [sb(ph, "gel%d" % i, [128, 512], F32) for i in range(2)]
                xo = [sb(ph, "fxo%d" % i, [128, D], F32) for i in range(2)]
                scr = (sb(ph, "ss2", [128, 2], F32), sb(ph, "rs1", [128, 1], F32), sb(ph, "junk2", [128, D], BF16),
                       sb(ph, "etmp", [128, D], F32))
                tcnt = 0
                mcnt = 0
                for s in ("c", "x"):
                    q = seqs[s]
                    L = q.L
                    if s == "x":
                        ncols, BR = 64, 16
                    else:
                        ncols, BR = 256, 1
                    NT = BR * ncols
                    vert = s == "x"
                    for t0 in range(0, L, NT):
                        top = vert and t0 > 0
                        bot = vert and t0 + NT < L
                        e0 = t0 - (64 if top else 0)
                        e1 = t0 + NT + (64 if bot else 0)
                        tiles = []
                        tt0 = e0
                        while tt0 < e1:
                            n = min(128, e1 - tt0)
                            tiles.append((tt0, n))
                            tt0 += n
                        for (ta, n) in tiles:
                            sl = tcnt % 2
                            tcnt += 1
                            dma("sp", xt1[sl][0:n, :], q.mix[ta:ta + n, :], writes=[("fxt", sl)])
                            act(junk[0:n, :], xt1[sl][0:n, :], AF.Square, [("fxt", sl)], [("fss", sl)], accum=ss1[sl][0:n, :])
                            act(rs1b[sl][0:n, :], ss1[sl][0:n, :], AF.Sqrt, [("fss", sl)], [("frs", sl)], scale=1.0 / D, bias=epsb[0:n, :])
                            P.op("dve", (lambda r, n: lambda e: e.reciprocal(out=r[0:n, :], in_=r[0:n, :]))(rs1b[sl], n), [("frs", sl)], [("frs", sl)])
                            ts("dve", xs1[sl][0:n, :], xt1[sl][0:n, :], rs1b[sl][0:n, 0:1], None, ALU.mult, None, [("fxt", sl), ("frs", sl)], [("fxs", sl)])
                            c0 = ta - e0
                            for kk in range(2):
                                bk, bkey = bank()
                                for k4 in range(4):
                                    k = kk * 4 + k4
                                    mm(bk[:, k4 * 128:k4 * 128 + n], xs1[sl][0:n, k * 128:(k + 1) * 128], identb[0:n, 0:n], True, True,
                                       [("fxs", sl), "identb"], [bkey])
                                for k4 in range(4):
                                    k = kk * 4 + k4
                                    act(xnT[:, k, c0:c0 + n], bk[:, k4 * 128:k4 * 128 + n], AF.Identity,
                                        [bkey, ("v", s, "gs3"), ("v", s, "sh3")], [("fxnT", k)],
                                        scale=vec[(s, "gs3")][:, k:k + 1], bias=vec[(s, "sh3")][:, k:k + 1])
                        ne = e1 - e0
                        goff = 0 if top else 1
                        nrows_e = ne // ncols if vert else 1
                        for m in range(NFF):
                            sl = mcnt % 2
                            mcnt += 1
                            dma("sp", wg[sl][:], wupv[:, :, m * 128:(m + 1) * 128], writes=[("wg", sl)])
                            dma("sp", wvv[sl][:], wupv[:, :, DFF + m * 128:DFF + (m + 1) * 128], writes=[("wv", sl)])
                            for tap in range(9):
                                if not vert and tap // 3 != 1:
                                    continue
                                ts("pool", dg[sl][:, tap, :], identf[:], cw[:, tap, m:m + 1], None, ALU.mult, None,
                                   ["identf", "cw"], [("dg", sl)])
                            gb = gbuf[sl]
                            gview = gb[:].rearrange("p r c -> p (r c)")
                            if vert:
                                if not top:
                                    memset("pool", gb[:, 0, :], 0.0, [("gbuf", sl)])
                                if not bot:
                                    memset("pool", gb[:, 17, :], 0.0, [("gbuf", sl)])
                            o = 0
                            while o < ne:
                                n = min(512, ne - o)
                                bk, bkey = bank()
                                for k in range(8):
                                    mm(bk[:, 0:n], wg[sl][:, k, :], xnT[:, k, o:o + n], k == 0, k == 7, [("wg", sl), ("fxnT", k)], [bkey])
                                gdst = gview[:, goff * 64 + o:goff * 64 + o + n] if vert else gview[:, o:o + n]
                                cp("act", gdst, bk[:, 0:n], [bkey], [("gbuf", sl)])
                                o += n
                            cen0 = t0 - e0
                            for sbk in range(0, NT, 512):
                                nsub = min(512, NT - sbk)
                                bv, bvkey = bank()
                                for k in range(8):
                                    mm(bv[:, 0:nsub], wvv[sl][:, k, :], xnT[:, k, cen0 + sbk:cen0 + sbk + nsub], k == 0, k == 7,
                                       [("wv", sl), ("fxnT", k)], [bvkey])
                                bc, bckey = bank()
                                if vert:
                                    r0 = 1 + sbk // 64
                                    nr = nsub // 64
                                    bc3 = bc[:, 0:nsub].rearrange("p (r c) -> p r c", c=64)
                                    first = True
                                    order = [4, 1, 7, 3, 5, 0, 2, 6, 8]
                                    for ti, tap in enumerate(order):
                                        dy, dx = tap // 3 - 1, tap % 3 - 1
                                        if dx == 0:
                                            rhs = gb[:, r0 + dy:r0 + dy + nr, :]
                                            out = bc3
                                        elif dx == -1:
                                            rhs = gb[:, r0 + dy:r0 + dy + nr, 0:63]
                                            out = bc3[:, :, 1:64]
                                        else:
                                            rhs = gb[:, r0 + dy:r0 + dy + nr, 1:64]
                                            out = bc3[:, :, 0:63]
                                        mm(out, dg[sl][:, tap, :], rhs, first, ti == 8, [("dg", sl), ("gbuf", sl)], [bckey])
                                        first = False
                                else:
                                    for ti, tap in enumerate([4, 3, 5]):
                                        dx = tap % 3 - 1
                                        if dx == 0:
                                            rhs, out = gview[:, 0:nsub], bc[:, 0:nsub]
                                        elif dx == -1:
                                            rhs, out = gview[:, 0:nsub - 1], bc[:, 1:nsub]
                                        else:
                                            rhs, out = gview[:, 1:nsub], bc[:, 0:nsub - 1]
                                        mm(out, dg[sl][:, tap, :], rhs, ti == 0, ti == 2, [("dg", sl), ("gbuf", sl)], [bckey])
                                gsl = (mcnt + sbk // 512) % 2
                                act(gel[gsl][:, 0:nsub], bc[:, 0:nsub], AF.Gelu_apprx_tanh, [bckey, "cb"], [("gel", gsl)], bias=cb[:, m:m + 1])
                                tt("dve", hT[:, m, sbk:sbk + nsub], bv[:, 0:nsub], gel[gsl][:, 0:nsub], ALU.mult, [bvkey, ("gel", gsl)], [("hT", m)])
                        for a in range(NT // 128):
                            sl = a % 2
                            b0, k0 = bank()
                            b1, k1 = bank()
                            for k in range(NFF):
                                mm(b0[:], hT[:, k, a * 128:(a + 1) * 128], wd[:, k, 0:512], k == 0, k == NFF - 1, [("hT", k), "wd"], [k0])
                                mm(b1[:], hT[:, k, a * 128:(a + 1) * 128], wd[:, k, 512:1024], k == 0, k == NFF - 1, [("hT", k), "wd"], [k1])
                            ta = t0 + a * 128
                            dma("sp", xt1[sl][:], q.mix[ta:ta + 128, :], writes=[("fxt", sl)])
                            resid_epilogue(b0, b1, k0, k1, s, G4[s], ("G4", s), xt1[sl][:], ("fxt", sl), xo[sl][:], ("fxo", sl), scr)
                            dma("sp", q.res[ta:ta + 128, :], xo[sl][:], reads=[("fxo", sl)])
                P.barrier()


        def sin_rr(out_ap, arg, tmpk, n_part, keys_arg, key_out, key_tmp):
            ts("dve", tmpk, arg, 1.0 / TWO_PI, MAGIC, ALU.mult, ALU.add, keys_arg, [key_tmp])
            ts("dve", tmpk, tmpk, MAGIC, -TWO_PI, ALU.subtract, ALU.mult, [key_tmp], [key_tmp])
            tt("dve", tmpk, arg, tmpk, ALU.add, keys_arg + [key_tmp], [key_tmp])
            ts("dve", tmpk, tmpk, 3.1415925, -3.1415925, ALU.min, ALU.max, [key_tmp], [key_tmp])
            act(out_ap, tmpk, AF.Sin, [key_tmp], [key_out])

        def phase_hyprep(l):
            with ExitStack() as ph:
                hw = sb(ph, "hw", [128, 3, 12], F32)
                hb = sb(ph, "hb", [128, 12], F32)
                dma("sp", hw[:], hswT_d[l], writes=["hw"])
                dma("sp", hb[:], hsbT_d[l], writes=["hb"])
                dgs = sb(ph, "hdg", [128, 36, 128], BF16)
                for ch in range(12):
                    for d in range(3):
                        ts("pool", dgs[:, ch * 3 + d, :], identf[:], hw[:, d, ch:ch + 1], None, ALU.mult, None, ["identf", "hw"], ["hdg"])
                pin = [sb(ph, "pin%d" % i, [128, 3, 514], BF16) for i in range(2)]
                vB = [sb(ph, "vB%d" % i, [128, 512], F32) for i in range(2)]
                uo = [sb(ph, "uo%d" % i, [128, 512], BF16) for i in range(2)]
                xo = [sb(ph, "x0o%d" % i, [128, 512], BF16) for i in range(2)]
                cnt = 0
                for s in ("c", "x"):
                    L = seqs[s].L
                    TB = min(512, L)
                    for cc in range(4):
                        for t0 in range(0, L, TB):
                            sl = cnt % 2
                            cnt += 1
                            for j in range(3):
                                r0 = j * 512 + cc * 128
                                dma("sp" if j != 1 else "pool", pin[sl][:, j, 0:TB + 2], pT[s][r0:r0 + 128, PAD + t0 - 1:PAD + t0 + TB + 1], writes=[("pin", sl, j)])
                            bks = []
                            for j in range(3):
                                bk, bkey = bank()
                                ch = j * 4 + cc
                                for d in range(3):
                                    mm(bk[:, 0:TB], dgs[:, ch * 3 + d, :], pin[sl][:, j, d:d + TB], d == 0, d == 2, ["hdg", ("pin", sl, j)], [bkey])
                                bks.append((bk, bkey))
                            act(vB[sl][:, 0:TB], bks[2][0][:, 0:TB], AF.Identity, [bks[2][1], "hb"], [("vB", sl)], bias=hb[:, 8 + cc:9 + cc])
                            act(xo[sl][:, 0:TB], bks[0][0][:, 0:TB], AF.Identity, [bks[0][1], "hb"], [("x0o", sl)], bias=hb[:, cc:cc + 1])
                            stt(uo[sl][:, 0:TB], bks[1][0][:, 0:TB], hb[:, 4 + cc:5 + cc], vB[sl][:, 0:TB], ALU.add, ALU.mult,
                                [bks[1][1], "hb", ("vB", sl)], [("uo", sl)])
                            dma("sp", uT[s][cc * 128:(cc + 1) * 128, t0:t0 + TB], uo[sl][:, 0:TB], reads=[("uo", sl)])
                            dma("sp", x0T[s][cc * 128:(cc + 1) * 128, t0:t0 + TB], xo[sl][:, 0:TB], reads=[("x0o", sl)])
                P.barrier()

        def phase_hyena(l, s):
            N1, NZ, NK, L = MON[s]
            md = mon_d[s]
            NK2 = 2 * NK
            with ExitStack() as ph:
                hidT = sb(ph, "hidT", [64, L], BF16)
                woutb = sb(ph, "woutb", [64, 1024], BF16)
                d1 = sb(ph, "d1", [NZ, 512], F32)
                d2 = sb(ph, "d2", [128, 512], F32)
                f1t = sb(ph, "f1t", [NZ, NK2], BF16)
                gtb = sb(ph, "gtb", [128, 2, 2, 128], BF16)
                hbias = sb(ph, "hbias", [128, 4], F32)
                dma("pool", woutb[:], f_wout_d[l], writes=["woutb"])
                dma("sp", d1[:], md["d1"][:, :], writes=["d1"])
                dma("sp", d2[:], md["d2"][:, :], writes=["d2"])
                dma("pool", f1t[:], md["f1"][:, :], writes=["f1t"])
                dma("pool", gtb[:], gtab_d[:, :, :, :], writes=["gtb"])
                with ExitStack() as p2:
                    zt = sb(p2, "zt", [33, L], F32)
                    fwin = sb(p2, "fwin", [33, 64], F32)
                    fwh = sb(p2, "fwh", [64, 2, 64], F32)
                    fb = sb(p2, "fb", [64, 3], F32)
                    ffr = sb(p2, "ffr", [64, 1], F32)
                    frb = sb(p2, "frb", [64, 3], F32)
                    dma("sp", zt[:], md["z"][:, :], writes=["zt"])
                    dma("sp", fwin[:], f_win_d[l], writes=["fwin"])
                    dma("sp", fwh[:], f_whid_d[l], writes=["fwh"])
                    dma("sp", fb[:], f_b_d[l], writes=["fb"])
                    dma("sp", ffr[:], f_freq_d[l], writes=["ffr"])
                    ts("dve", frb[:], fb[:], ffr[:, 0:1], None, ALU.mult, None, ["fb", "ffr"], ["frb"])
                    args = [sb(p2, "farg%d" % i, [64, 512], F32) for i in range(2)]
                    tmpk = [sb(p2, "ftmp%d" % i, [64, 512], F32) for i in range(2)]
                    hts = [sb(p2, "fh%d" % i, [64, 512], F32) for i in range(4)]
                    TB = min(512, L)
                    cnt = 0
                    for t0 in range(0, L, TB):
                        prev = None
                        for li in range(3):
                            sl = cnt % 2
                            cnt += 1
                            bk, bkey = bank()
                            if li == 0:
                                mm(bk[0:64, 0:TB], fwin[:], zt[:, t0:t0 + TB], True, True, ["fwin", "zt"], [bkey])
                            else:
                                mm(bk[0:64, 0:TB], fwh[:, li - 1, :], prev[0][:, 0:TB], True, True, ["fwh", prev[1]], [bkey])
                            act(args[sl][:, 0:TB], bk[0:64, 0:TB], AF.Identity, [bkey, "ffr", "frb"], [("farg", sl)],
                                scale=ffr[:, 0:1], bias=frb[:, li:li + 1])
                            if li < 2:
                                hsl = (t0 // TB * 2 + li) % 4
                                sin_rr(hts[hsl][:, 0:TB], args[sl][:, 0:TB], tmpk[sl][:, 0:TB], 64, [("farg", sl)], ("fh", hsl), ("ftmp", sl))
                                prev = (hts[hsl], ("fh", hsl))
                            else:
                                sin_rr(hidT[:, t0:t0 + TB], args[sl][:, 0:TB], tmpk[sl][:, 0:TB], 64, [("farg", sl)], "hidT", ("ftmp", sl))
                    P.barrier()
                dma("sp", hbias[:], hbiasT_d[l], writes=["hbias"])
                X = sb(ph, "monX", [NZ, 128, 128], BF16)
                A = sb(ph, "monA", [128, 2, NK, 128], BF16)
                Kf = sb(ph, "monK", [128, 2, NK, 128], BF16)
                BB = sb(ph, "monB", [128, 2 * 65 * 128], BF16)
                Ab = BB[:, 0:2 * NK * 128].rearrange("p (r k c) -> p r k c", r=2, k=NK, c=128)
                B0 = BB[0:NK, 0:2 * 64 * 128].rearrange("p (r n c) -> p r n c", r=2, n=64, c=128)
                f2r = [sb(ph, "f2r%d" % i, [128, 4, 128], BF16) for i in range(8)]
                i2r = [sb(ph, "i2r%d" % i, [NK, 4, 2, NZ], BF16) for i in range(4)]
                tm = [sb(ph, "fmt%d" % i, [128, 512], F32) for i in range(4)]
                KB = 4
                nkb = (NK + KB - 1) // KB
                Akeys = [("A", kb) for kb in range(nkb)]
                f2c = [0]
                i2c = [0]
                nb1 = max(1, min(128, 512 // NK2))

                def f1(dst, dstkeys, scale_d2, cg):
                    c = 0
                    i = 0
                    while c < 128:
                        nb = min(nb1, 128 - c)
                        bk, bkey = bank()
                        for q in range(nb):
                            mm(bk[:, q * NK2:(q + 1) * NK2], X[0:NZ, c + q, :], f1t[0:NZ, :], True, True, ["X", "f1t"], [bkey])
                        for ri in range(2):
                            src = bk[:, 0:nb * NK2].rearrange("p (i r k) -> p i r k", r=2, k=NK)[:, :, ri, :]
                            out = dst[:, ri, :, c:c + nb].rearrange("p k i -> p i k")
                            if scale_d2:
                                tt("dve", out, src, d2[:, cg * 128 + c:cg * 128 + c + nb].unsqueeze(2).to_broadcast([128, nb, NK]), ALU.mult,
                                   [bkey, "d2"], dstkeys)
                            else:
                                cp("act" if (i + ri) % 2 == 0 else "dve", out, src, [bkey], dstkeys)
                        c += nb
                        i += 1

                def load_f2(k1):
                    sl = f2c[0] % 8
                    f2c[0] += 1
                    dma("pool", f2r[sl][:].rearrange("p v k -> p (v k)"), md["f2"][k1], writes=[("f2r", sl)])
                    return f2r[sl], ("f2r", sl)

                for cg in range(4):
                    for dr in range(2):
                        for n20 in range(0, 128, 4):
                            bk, bkey = bank()
                            for q in range(4):
                                n2 = n20 + q
                                mm(bk[0:NZ, q * 128:(q + 1) * 128], hidT[:, n2:L:128], woutb[:, dr * 512 + cg * 128:dr * 512 + (cg + 1) * 128],
                                   True, True, ["hidT", "woutb"], [bkey])
                            tt("dve", X[0:NZ, :, n20:n20 + 4].rearrange("p c q -> p q c"),
                               bk[0:NZ, 0:512].rearrange("p (q c) -> p q c", c=128),
                               d1[0:NZ, cg * 128:(cg + 1) * 128].unsqueeze(1).to_broadcast([NZ, 4, 128]), ALU.mult, [bkey, "d1"], ["X"])
                        if dr == 1:
                            memset("dve", X[0:1, :, 0:1], 0.0, ["X"])
                        if dr == 0:
                            f1(A, Akeys, True, cg)
                        else:
                            f1(Ab, ["B0"], True, cg)
                    for kb in range(nkb):
                        k1s = list(range(kb * KB, min(NK, (kb + 1) * KB)))
                        br, brk = bank()
                        bi, bik = bank()
                        for q, k1 in enumerate(k1s):
                            f2, f2k = load_f2(k1)
                            cs = slice(q * 128, (q + 1) * 128)
                            mm(br[:, cs], f2[:, 0, :], A[:, 0, k1, :], True, False, [f2k, ("A", kb)], [brk])
                            mm(br[:, cs], f2[:, 2, :], A[:, 1, k1, :], False, False, [f2k, ("A", kb)], [brk])
                            mm(br[:, cs], f2[:, 0, :], Ab[:, 0, k1, :], False, False, [f2k, "B0"], [brk])
                            mm(br[:, cs], f2[:, 2, :], Ab[:, 1, k1, :], False, True, [f2k, "B0"], [brk])
                            mm(bi[:, cs], f2[:, 1, :], A[:, 0, k1, :], True, False, [f2k, ("A", kb)], [bik])
                            mm(bi[:, cs], f2[:, 0, :], A[:, 1, k1, :], False, False, [f2k, ("A", kb)], [bik])
                            mm(bi[:, cs], f2[:, 2, :], Ab[:, 0, k1, :], False, False, [f2k, "B0"], [bik])
                            mm(bi[:, cs], f2[:, 3, :], Ab[:, 1, k1, :], False, True, [f2k, "B0"], [bik])
                        nn = len(k1s) * 128
                        cp("act", Kf[:, 0, k1s[0]:k1s[-1] + 1, :].rearrange("p k c -> p (k c)"), br[:, 0:nn], [brk], [("Kf", kb)])
                        cp("act", Kf[:, 1, k1s[0]:k1s[-1] + 1, :].rearrange("p k c -> p (k c)"), bi[:, 0:nn], [bik], [("Kf", kb)])
                    dma("sp", X[0:NZ, :, :], uT[s][cg * 128:(cg + 1) * 128, :].rearrange("c (a b) -> a c b", b=128), writes=["X"])
                    f1(A, Akeys, False, cg)
                    for kb in range(nkb):
                        k1s = list(range(kb * KB, min(NK, (kb + 1) * KB)))
                        br, brk = bank()
                        bi, bik = bank()
                        for q, k1 in enumerate(k1s):
                            f2, f2k = load_f2(k1)
                            cs = slice(q * 128, (q + 1) * 128)
                            mm(br[:, cs], f2[:, 0, :], A[:, 0, k1, :], True, False, [f2k, ("A", kb)], [brk])
                            mm(br[:, cs], f2[:, 2, :], A[:, 1, k1, :], False, True, [f2k, ("A", kb)], [brk])
                            mm(bi[:, cs], f2[:, 1, :], A[:, 0, k1, :], True, False, [f2k, ("A", kb)], [bik])
                            mm(bi[:, cs], f2[:, 0, :], A[:, 1, k1, :], False, True, [f2k, ("A", kb)], [bik])
                        nn = len(k1s) * 128
                        kr = Kf[:, 0, k1s[0]:k1s[-1] + 1, :].rearrange("p k c -> p (k c)")
                        ki = Kf[:, 1, k1s[0]:k1s[-1] + 1, :].rearrange("p k c -> p (k c)")
                        tt("dve", tm[0][:, 0:nn], br[:, 0:nn], kr, ALU.mult, [brk, ("Kf", kb)], [("tm", 0)])
                        tt("dve", tm[1][:, 0:nn], bi[:, 0:nn], ki, ALU.mult, [bik, ("Kf", kb)], [("tm", 1)])
                        tt("dve", tm[2][:, 0:nn], br[:, 0:nn], ki, ALU.mult, [brk, ("Kf", kb)], [("tm", 2)])
                        tt("dve", tm[3][:, 0:nn], bi[:, 0:nn], kr, ALU.mult, [bik, ("Kf", kb)], [("tm", 3)])
                        tt("pool", A[:, 0, k1s[0]:k1s[-1] + 1, :].rearrange("p k c -> p (k c)"), tm[0][:, 0:nn], tm[1][:, 0:nn], ALU.subtract,
                           [("tm", 0), ("tm", 1)], [("A", kb)])
                        tt("pool", A[:, 1, k1s[0]:k1s[-1] + 1, :].rearrange("p k c -> p (k c)"), tm[2][:, 0:nn], tm[3][:, 0:nn], ALU.add,
                           [("tm", 2), ("tm", 3)], [("A", kb)])
                    for hf in range(2):
                        for c0 in range(0, 128, 4):
                            bk, bkey = bank()
                            for q in range(4):
                                cs = slice(q * 128, (q + 1) * 128)
                                mm(bk[0:NK, cs], A[:, 0, 0:NK, c0 + q], gtb[:, hf, 0, :], True, False, Akeys + ["gtb"], [bkey])
                                mm(bk[0:NK, cs], A[:, 1, 0:NK, c0 + q], gtb[:, hf, 1, :], False, True, Akeys + ["gtb"], [bkey])
                            for ri in range(2):
                                src = bk[0:NK, 0:512].rearrange("p (i r n) -> p i r n", r=2, n=64)[:, :, ri, :]
                                out = B0[:, ri, :, c0:c0 + 4].rearrange("p n i -> p i n")
                                cp("act" if ri == 0 else "dve", out, src, [bkey], ["B0"])
                        for n20 in range(0, 64, 4):
                            sl = i2c[0] % 4
                            i2c[0] += 1
                            ng = hf * 64 + n20
                            dma("pool", i2r[sl][:], md["i2"][:, ng:ng + 4, :, :], writes=[("i2r", sl)])
                            bk, bkey = bank()
                            for q in range(4):
                                cs = slice(q * 128, (q + 1) * 128)
                                mm(bk[0:NZ, cs], i2r[sl][:, q, 0, :], B0[:, 0, n20 + q, :], True, False, [("i2r", sl), "B0"], [bkey])
                                mm(bk[0:NZ, cs], i2r[sl][:, q, 1, :], B0[:, 1, n20 + q, :], False, True, [("i2r", sl), "B0"], [bkey])
                            cp("act" if (n20 // 4) % 2 == 0 else "dve", X[0:NZ, :, ng:ng + 4].rearrange("p c q -> p q c"),
                               bk[0:NZ, 0:512].rearrange("p (q c) -> p q c", c=128), [bkey], ["X"])
                    dma("sp", yhT[s][cg * 128:(cg + 1) * 128, :].rearrange("c (a b) -> a c b", b=128), X[0:NZ, :, :], reads=["X"])
                P.barrier()
            with ExitStack() as ph:
                hbias = sb(ph, "hbias2", [128, 4], F32)
                dma("sp", hbias[:], hbiasT_d[l], writes=["hbias"])
                TB = min(2048, L)
                ty = [sb(ph, "ey%d" % i, [128, TB], BF16) for i in range(2)]
                tu = [sb(ph, "eu%d" % i, [128, TB], BF16) for i in range(2)]
                tx = [sb(ph, "ex%d" % i, [128, TB], BF16) for i in range(2)]
                t1 = [sb(ph, "et%d" % i, [128, TB], F32) for i in range(2)]
                to = [sb(ph, "eo%d" % i, [128, TB], BF16) for i in range(2)]
                cnt = 0
                for cc in range(4):
                    for t0 in range(0, L, TB):
                        sl = cnt % 2
                        cnt += 1
                        rs_ = slice(cc * 128, (cc + 1) * 128)
                        dma("sp", ty[sl][:], yhT[s][rs_, t0:t0 + TB], writes=[("ey", sl)])
                        dma("sp", tu[sl][:], uT[s][rs_, t0:t0 + TB], writes=[("eu", sl)])
                        dma("pool", tx[sl][:], x0T[s][rs_, t0:t0 + TB], writes=[("ex", sl)])
                        stt(t1[sl][:], tu[sl][:], hbias[:, cc:cc + 1], ty[sl][:], ALU.mult, ALU.add, [("eu", sl), ("ey", sl), "hbias"], [("et", sl)])
                        tt("pool", to[sl][:], t1[sl][:], tx[sl][:], ALU.mult, [("et", sl), ("ex", sl)], [("eo", sl)])
                        dma("sp", yT[s][rs_, t0:t0 + TB], to[sl][:], reads=[("eo", sl)])
                P.barrier()

        lamP = sb(st, "lamP", [128, 9, 2, 64], F32)
        st0 = sb(st, "st0", [128, 64], BF16)
        jswapf = sb(st, "jswapf", [128, 128], F32)
        sg = sb(st, "sg", [128, 4], F32)
        dma("sp", jswapf[:], jswap_d[:, :], writes=["jswapf"])
        dma("sp", sg[:], s5sg_d[:, :], writes=["sg"])
        SLOT_M = {0: [15 - i for i in range(16)] + [-i for i in range(16)] + [i for i in range(16)] + [i + 1 for i in range(16)],
                  1: [i for i in range(16)] + [-i for i in range(16)] + [16 - i for i in range(16)] + [0] * 16}

        def phase_s5tab(l):
            with ExitStack() as ph:
                lam = sb(ph, "lam", [128, 2, 64], F32)
                ls = sb(ph, "ls", [128, 64], F32)
                Bd = sb(ph, "Bd", [128, 2, 64, 16], F32)
                Cd = sb(ph, "Cd", [128, 2, 64, 16], F32)
                dT = sb(ph, "dT", [128, 32], F32)
                msk = sb(ph, "msk", [128, 2, 2, 256], F32)
                idp = sb(ph, "idp", [128, 2, 256], F32)
                dma("sp", lam[:], s5lam_d[l], writes=["lam"])
                dma("sp", ls[:], s5ls_d[l], writes=["ls"])
                dma("sp", Bd[:], s5b_d[l], writes=["Bd"])
                dma("sp", Cd[:], s5c_d[l], writes=["Cd"])
                dma("sp", dT[:], s5dT_d[l], writes=["dT"])
                dma("sp", msk[:], s5msk_d[:, :, :, :], writes=["msk"])
                dma("sp", idp[:], s5idp_d[:, :, :], writes=["idp"])
                xr = sb(ph, "xr", [128, 64], F32)
                xi = sb(ph, "xi", [128, 64], F32)
                act(ls[:], ls[:], AF.Exp, ["ls"], ["ls"])
                tt("dve", xr[:], lam[:, 0, :], ls[:], ALU.mult, ["lam", "ls"], ["xr"])
                tt("dve", xi[:], lam[:, 1, :], ls[:], ALU.mult, ["lam", "ls"], ["xi"])
                E1 = sb(ph, "E1", [128, 2, 64, 32], F32)
                E2 = sb(ph, "E2", [128, 2, 64, 32], F32)
                p2 = ExitStack()
                MAG = sb(p2, "MAG", [128, 2, 64, 32], F32)
                TK = sb(p2, "TK", [128, 2, 64, 32], F32)
                for d in range(2):
                    for sl_, m in enumerate(SLOT_M[d]):
                        us = slice(d * 32, d * 32 + 32)
                        act(MAG[:, d, sl_, :], xr[:, us], AF.Exp, ["xr"], ["MAG"], scale=float(m))
                        ts("dve", E1[:, d, sl_, :], xi[:, us], float(m), sg[:, 2:3], ALU.mult, ALU.add, ["xi", "sg"], ["E1"])
                        ts("pool", E2[:, d, sl_, :], xi[:, us], float(m), sg[:, 3:4], ALU.mult, ALU.add, ["xi", "sg"], ["E2"])
                fl = lambda t: t[:].rearrange("p d s u -> p (d s u)")
                sin_rr(fl(E1), fl(E1), fl(TK), 128, ["E1"], "E1", "TK")
                sin_rr(fl(E2), fl(E2), fl(TK), 128, ["E2"], "E2", "TK")
                tt("dve", fl(E1), fl(E1), fl(MAG), ALU.mult, ["E1", "MAG"], ["E1"])
                tt("pool", fl(E2), fl(E2), fl(MAG), ALU.mult, ["E2", "MAG"], ["E2"])
                P.barrier()
                p2.close()
                a1r = sb(ph, "a1r", [128, 64], F32)
                a1i = sb(ph, "a1i", [128, 64], F32)
                Lr = sb(ph, "Lr", [128, 64], F32)
                Li = sb(ph, "Li", [128, 64], F32)
                for d in range(2):
                    us = slice(d * 32, d * 32 + 32)
                    s1 = 33 if d == 0 else 1
                    s16 = 63 if d == 0 else 32
                    for (dst, slot) in ((a1r, s1), (Lr, s16)):
                        cp("dve", dst[0:64, us], E1[0:64, d, slot, :], ["E1"], [dst.name])
                        cp("dve", dst[64:128, us], E2[64:128, d, slot, :], ["E2"], [dst.name])
                    for (dst, slot) in ((a1i, s1), (Li, s16)):
                        cp("dve", dst[0:64, us], E2[0:64, d, slot, :], ["E2"], [dst.name])
                        cp("dve", dst[64:128, us], E1[64:128, d, slot, :], ["E1"], [dst.name])
                t1 = sb(ph, "sq1", [128, 64], F32)
                t2 = sb(ph, "sq2", [128, 64], F32)
                for k in range(9):
                    cp("dve", lamP[:, k, 0, :], Lr[:], [Lr.name], ["lamP"])
                    ts("dve", lamP[:, k, 1, :], Li[:], sg[:, 1:2], None, ALU.mult, None, [Li.name, "sg"], ["lamP"])
                    if k < 8:
                        tt("dve", t1[:], Lr[:], Lr[:], ALU.mult, [Lr.name], ["sq1"])
                        tt("dve", t2[:], Li[:], Li[:], ALU.mult, [Li.name], ["sq2"])
                        tt("dve", Li[:], Lr[:], Li[:], ALU.mult, [Lr.name, Li.name], [Li.name])
                        ts("dve", Li[:], Li[:], 2.0, None, ALU.mult, None, [Li.name], [Li.name])
                        tt("dve", Lr[:], t1[:], t2[:], ALU.subtract, ["sq1", "sq2"], [Lr.name])
                qr = sb(ph, "qr", [128, 64], F32)
                qi = sb(ph, "qi", [128, 64], F32)
                den = sb(ph, "den", [128, 64], F32)
                ts("dve", a1r[:], a1r[:], -1.0, None, ALU.add, None, [a1r.name], [a1r.name])
                tt("dve", t1[:], lam[:, 0, :], lam[:, 0, :], ALU.mult, ["lam"], ["sq1"])
                tt("dve", t2[:], lam[:, 1, :], lam[:, 1, :], ALU.mult, ["lam"], ["sq2"])
                tt("dve", den[:], t1[:], t2[:], ALU.add, ["sq1", "sq2"], ["den"])
                P.op("dve", lambda e: e.reciprocal(out=den[:], in_=den[:]), ["den"], ["den"])
                tt("dve", t1[:], a1r[:], lam[:, 0, :], ALU.mult, [a1r.name, "lam"], ["sq1"])
                tt("dve", t2[:], a1i[:], lam[:, 1, :], ALU.mult, [a1i.name, "lam"], ["sq2"])
                tt("dve", qr[:], t1[:], t2[:], ALU.add, ["sq1", "sq2"], ["qr"])
                tt("dve", qr[:], qr[:], den[:], ALU.mult, ["qr", "den"], ["qr"])
                tt("dve", t1[:], a1i[:], lam[:, 0, :], ALU.mult, [a1i.name, "lam"], ["sq1"])
                tt("dve", t2[:], a1r[:], lam[:, 1, :], ALU.mult, [a1r.name, "lam"], ["sq2"])
                tt("dve", qi[:], t1[:], t2[:], ALU.subtract, ["sq1", "sq2"], ["qi"])
                tt("dve", qi[:], qi[:], den[:], ALU.mult, ["qi", "den"], ["qi"])
                Za = sb(ph, "Za", [128, 64, 16], F32)
                Zb = sb(ph, "Zb", [128, 64, 16], F32)
                tb1 = sb(ph, "tb1", [128, 64, 16], F32)
                qrb = qr[:].unsqueeze(2).to_broadcast([128, 64, 16])
                qib = qi[:].unsqueeze(2).to_broadcast([128, 64, 16])
                tt("dve", Za[:], Bd[:, 0], qrb, ALU.mult, ["Bd", "qr"], ["Za"])
                tt("dve", tb1[:], Bd[:, 1], qib, ALU.mult, ["Bd", "qi"], ["tb1"])
                tt("dve", Za[:], Za[:], tb1[:], ALU.subtract, ["Za", "tb1"], ["Za"])
                tt("dve", Zb[:], Bd[:, 0], qib, ALU.mult, ["Bd", "qi"], ["Zb"])
                tt("dve", tb1[:], Bd[:, 1], qrb, ALU.mult, ["Bd", "qr"], ["tb1"])
                tt("dve", Zb[:], Zb[:], tb1[:], ALU.add, ["Zb", "tb1"], ["Zb"])
                ts("dve", Zb[:], Zb[:], sg[:, 0:1], None, ALU.mult, None, ["Zb", "sg"], ["Zb"])
                ts("dve", Cd[:, 0], Cd[:, 0], sg[:, 1:2], None, ALU.mult, None, ["Cd"], ["Cd"])
                ts("dve", Cd[:, 1], Cd[:, 1], -1.0, None, ALU.mult, None, ["Cd"], ["Cd"])
                prod = [sb(ph, "prod%d" % i, [128, 8, 256], BF16) for i in range(7)]
                pt = [sb(ph, "pt%d" % i, [128, 8, 16, 16], F32) for i in range(4)]
                stage = sb(ph, "stage", [128, 8, 1536], BF16)
                wt = [sb(ph, "wkt%d" % i, [128, 256], F32) for i in range(4)]
                pcnt = [0]

                def product(dst, d, blk, g0, Z1, Z2, zkey):
                    i = pcnt[0] % 2
                    pcnt[0] += 1
                    eng = "dve" if i == 0 else "pool"
                    e1 = E1[:, d, blk * 16:(blk + 1) * 16, g0:g0 + 8].rearrange("p m g -> p g m").unsqueeze(3).to_broadcast([128, 8, 16, 16])
                    e2 = E2[:, d, blk * 16:(blk + 1) * 16, g0:g0 + 8].rearrange("p m g -> p g m").unsqueeze(3).to_broadcast([128, 8, 16, 16])
                    u0 = d * 32 + g0
                    z1 = Z1[:, u0:u0 + 8, :].unsqueeze(2).to_broadcast([128, 8, 16, 16])
                    z2 = Z2[:, u0:u0 + 8, :].unsqueeze(2).to_broadcast([128, 8, 16, 16])
                    ta, tb = pt[2 * i], pt[2 * i + 1]
                    zk = ["Za", "Zb"] if zkey == "Za" else [zkey]
                    tt(eng, ta[:], e1, z1, ALU.mult, ["E1"] + zk, [("pt", 2 * i)])
                    tt(eng, tb[:], e2, z2, ALU.mult, ["E2"] + zk, [("pt", 2 * i + 1)])
                    tt(eng, dst[:].rearrange("p g (m h) -> p g m h", h=16), ta[:], tb[:], ALU.add, [("pt", 2 * i), ("pt", 2 * i + 1)], [dst.name])

                for gb in range(4):
                    g0 = gb * 8
                    XQ0, KL0, KR0, WS0, XQ1, KR1, WS1 = prod
                    product(XQ0, 0, 0, g0, Za, Zb, "Za")
                    product(KL0, 0, 1, g0, Za, Zb, "Za")
                    product(KR0, 0, 2, g0, Cd[:, 0], Cd[:, 1], "Cd")
                    product(WS0, 0, 3, g0, Cd[:, 0], Cd[:, 1], "Cd")
                    product(XQ1, 1, 0, g0, Za, Zb, "Za")
                    product(KR1, 1, 1, g0, Cd[:, 0], Cd[:, 1], "Cd")
                    product(WS1, 1, 2, g0, Cd[:, 0], Cd[:, 1], "Cd")
                    for gl in range(8):
                        g = g0 + gl
                        for d, XQ in ((0, XQ0), (1, XQ1)):
                            bk, bkey = bank()
                            for jh in range(2):
                                mm(bk[:, jh * 128:(jh + 1) * 128], XQ[:, gl, jh * 128:(jh + 1) * 128], identb[:], True, True, [XQ.name, "identb"], [bkey])
                            cp("act", stage[:, gl, d * 256:(d + 1) * 256], bk[:, 0:256], [bkey], ["stage"])
                        cp("pool", stage[:, gl, 512:768], WS0[:, gl, :], [WS0.name], ["stage"])
                        cp("pool", stage[:, gl, 768:1024], WS1[:, gl, :], [WS1.name], ["stage"])
                        for jh in range(2):
                            bf, bfk = bank()
                            bb_, bbk = bank()
                            mm(bf[:, 0:256], KL0[:, gl, jh * 128:(jh + 1) * 128], KR0[:, gl, :], True, True, [KL0.name, KR0.name], [bfk])
                            mm(bb_[:, 0:256], XQ1[:, gl, jh * 128:(jh + 1) * 128], KR1[:, gl, :], True, True, [XQ1.name, KR1.name], [bbk])
                            tt("dve", wt[0][:], bf[:, 0:256], msk[:, 0, jh, :], ALU.mult, [bfk, "msk"], ["wkt0"])
                            tt("dve", wt[1][:], bb_[:, 0:256], msk[:, 1, jh, :], ALU.mult, [bbk, "msk"], ["wkt1"])
                            stt(wt[2][:], idp[:, jh, :], dT[:, g:g + 1], wt[0][:], ALU.mult, ALU.add, ["idp", "dT", "wkt0"], ["wkt2"])
                            tt("pool", stage[:, gl, 1024 + jh * 256:1024 + (jh + 1) * 256], wt[2][:], wt[1][:], ALU.add, ["wkt2", "wkt1"], ["stage"])
                    dma("sp", s5tab_d[g0:g0 + 8].rearrange("g p n -> p g n"), stage[:], reads=["stage"])
                P.barrier()

        def phase_s5(l, s):
            L = seqs[s].L
            NCH = L // 16
            nsteps = int(round(math.log2(NCH)))
            with ExitStack() as ph:
                selb = sb(ph, "selb", [128, 64, 128], BF16)
                selTb = sb(ph, "selTb", [128, 64, 128], BF16)
                dma("pool", selb[:], sel_d[:, :, :], writes=["selb"])
                dma("pool", selTb[:], selT_d[:, :, :], writes=["selTb"])
                uTb = [sb(ph, "uTb%d" % i, [128, L], BF16) for i in range(2)]
                tabr = [sb(ph, "tabr%d" % i, [128, 1536], BF16) for i in range(2)]
                Ug = [sb(ph, "Ug%d" % i, [128, 2, NCH], BF16) for i in range(2)]
                Sb = [sb(ph, "Sb%d" % i, [128, NCH], BF16) for i in range(4)]
                Sext = [[sb(ph, "Sext%d_%d" % (d, i), [128, NCH + 2], BF16) for i in range(2)] for d in range(2)]
                LamT = [sb(ph, "LamT%d" % i, [128, 9, 128], BF16) for i in range(4)]
                ltmp = [sb(ph, "ltmp%d" % i, [128, 128], F32) for i in range(4)]
                Yg = sb(ph, "Yg", [128, 8, 2, NCH], BF16)
                ysT = sb(ph, "ysT", [128, 4, L], BF16)
                zcol = sb(ph, "zcol", [128, 1], BF16)
                memset("dve", zcol[:], 0.0, ["zcol"])
                gcnt = 0
                ucnt = 0
                for cbk in range(4):
                    ub = uTb[cbk % 2]
                    ubk = ("uTb", cbk % 2)
                    dma("sp", ub[:], pT[s][1536 + cbk * 128:1536 + (cbk + 1) * 128, PAD:PAD + L], writes=[ubk])
                    for gl in range(8):
                        g = cbk * 8 + gl
                        gs_ = gcnt % 2
                        gcnt += 1
                        tb = tabr[gs_]
                        tbk = ("tabr", gs_)
                        dma("sp", tb[:], s5tab_d[g], writes=[tbk])
                        ug = Ug[gs_]
                        ugk = ("Ug", gs_)
                        for jh in range(2):
                            bk, bkey = bank()
                            for jj in range(8):
                                j = jh * 8 + jj
                                mm(bk[:, 0:NCH], selb[:, gl * 8 + jj, :], ub[:, j:L:16], jj == 0, jj == 7, ["selb", ubk], [bkey])
                            cp("act" if jh == 0 else "dve", ug[:, jh, :], bk[:, 0:NCH], [bkey], [ugk])
                        sx = Sext[0][gs_], Sext[1][gs_]
                        sxk = ("Sext", 0, gs_), ("Sext", 1, gs_)
                        for d in range(2):
                            u = d * 32 + g
                            li = ucnt % 4
                            ucnt += 1
                            lt = LamT[li]
                            ltk = ("LamT", li)
                            for k in range(nsteps):
                                ts("pool", ltmp[0][:], jswapf[:], lamP[:, k, 1, u:u + 1], None, ALU.mult, None, ["jswapf", "lamP"], [("ltmp", 0)])
                                ts("pool", ltmp[1][:], identf[:], lamP[:, k, 0, u:u + 1], None, ALU.mult, None, ["identf", "lamP"], [("ltmp", 1)])
                                tt("pool", lt[:, k, :], ltmp[0][:], ltmp[1][:], ALU.add, [("ltmp", 0), ("ltmp", 1)], [ltk])
                            bk, bkey = bank()
                            init = s == "x"
                            mm(bk[:, 0:NCH], tb[:, d * 256:d * 256 + 128], ug[:, 0, :], True, False, [tbk, ugk], [bkey])
                            mm(bk[:, 0:NCH], tb[:, d * 256 + 128:d * 256 + 256], ug[:, 1, :], False, not init, [tbk, ugk], [bkey])
                            if init:
                                col = 0 if d == 0 else NCH - 1
                                mm(bk[:, col:col + 1], lt[:, 0, :], st0[:, u:u + 1], False, True, [ltk, "st0"], [bkey])
                            pp = (d * 2) % 4
                            cur, curk = Sb[pp], ("Sb", pp)
                            cp("act", cur[:], bk[:, 0:NCH], [bkey], [curk])
                            for k in range(nsteps):
                                sh = 1 << k
                                bk, bkey = bank()
                                mm(bk[:, 0:NCH], identb[:], cur[:, 0:NCH], True, False, ["identb", curk], [bkey])
                                if d == 0:
                                    mm(bk[:, sh:NCH], lt[:, k, :], cur[:, 0:NCH - sh], False, True, [ltk, curk], [bkey])
                                else:
                                    mm(bk[:, 0:NCH - sh], lt[:, k, :], cur[:, sh:NCH], False, True, [ltk, curk], [bkey])
                                if k == nsteps - 1:
                                    off = 1 if d == 0 else 0
                                    cp("act" if k % 2 == 0 else "dve", sx[d][:, off:off + NCH], bk[:, 0:NCH], [bkey], [sxk[d]])
                                else:
                                    pp2 = d * 2 + (1 - (pp % 2))
                                    nxt, nxtk = Sb[pp2], ("Sb", pp2)
                                    cp("act" if k % 2 == 0 else "dve", nxt[:], bk[:, 0:NCH], [bkey], [nxtk])
                                    cur, curk, pp = nxt, nxtk, pp2
                            ecol = 0 if d == 0 else NCH
                            if init:
                                cp("dve", sx[d][:, ecol:ecol + 1], st0[:, u:u + 1], ["st0"], [sxk[d]])
                            else:
                                cp("dve", sx[d][:, ecol:ecol + 1], zcol[:], ["zcol"], [sxk[d]])
                                fcol = NCH if d == 0 else 0
                                cp("dve", st0[:, u:u + 1], sx[d][:, fcol:fcol + 1], [sxk[d]], ["st0"])
                        for th in range(2):
                            bk, bkey = bank()
                            cs = slice(th * 128, (th + 1) * 128)
                            mm(bk[:, 0:NCH], tb[:, 1024:1280][:, cs], ug[:, 0, :], True, False, [tbk, ugk], [bkey])
                            mm(bk[:, 0:NCH], tb[:, 1280:1536][:, cs], ug[:, 1, :], False, False, [tbk, ugk], [bkey])
                            mm(bk[:, 0:NCH], tb[:, 512:768][:, cs], sx[0][:, 0:NCH], False, False, [tbk, sxk[0]], [bkey])
                            mm(bk[:, 0:NCH], tb[:, 768:1024][:, cs], sx[1][:, 1:NCH + 1], False, True, [tbk, sxk[1]], [bkey])
                            act(Yg[:, gl, th, :], bk[:, 0:NCH], AF.Gelu_apprx_tanh, [bkey], [("Yg", gl)])
                    for j in range(16):
                        jh, jj = divmod(j, 8)
                        bk, bkey = bank()
                        for gl in range(8):
                            mm(bk[:, 0:NCH], selTb[:, gl * 8 + jj, :], Yg[:, gl, jh, :], gl == 0, gl == 7, ["selTb", ("Yg", gl)], [bkey])
                        cp("act" if j % 2 == 0 else "dve", ysT[:, cbk, j:L:16], bk[:, 0:NCH], [bkey], [("ysT", cbk)])
                wgl = sb(ph, "wgl", [128, 4, 512], BF16)
                bgl = sb(ph, "bgl", [128, 4], F32)
                dma("pool", wgl[:], w_glu_d[l].rearrange("(k p) n -> p k n", p=128), writes=["wgl"])
                dma("sp", bgl[:], s5bglu_d[l], writes=["bgl"])
                TB = min(512, L)
                sig = [sb(ph, "sig%d" % i, [128, TB], F32) for i in range(2)]
                go = [sb(ph, "go%d" % i, [128, TB], BF16) for i in range(2)]
                cnt = 0
                for t0 in range(0, L, TB):
                    for mo in range(4):
                        sl = cnt % 2
                        cnt += 1
                        bk, bkey = bank()
                        for k in range(4):
                            mm(bk[:, 0:TB], wgl[:, k, mo * 128:(mo + 1) * 128], ysT[:, k, t0:t0 + TB], k == 0, k == 3, ["wgl", ("ysT", k)], [bkey])
                        act(sig[sl][:], bk[:, 0:TB], AF.Sigmoid, [bkey, "bgl"], [("sig", sl)], bias=bgl[:, mo:mo + 1])
                        tt("dve" if mo % 2 == 0 else "pool", go[sl][:], sig[sl][:], ysT[:, mo, t0:t0 + TB], ALU.mult, [("sig", sl), ("ysT", mo)], [("go", sl)])
                        dma("sp", yT[s][512 + mo * 128:512 + (mo + 1) * 128, t0:t0 + TB], go[sl][:], reads=[("go", sl)])
                P.barrier()

        def phase_feed_y():
            with ExitStack() as ph:
                k0, k1_ = feed_y
                nk = k1_ - k0
                t = sb(ph, "fyt", [128, nk, 2048], F32)
                tb = sb(ph, "fytb", [128, nk, 2048], BF16)
                for s, srcd in (("c", fy_c), ("x", fy_x)):
                    L = seqs[s].L
                    TB = min(2048, L)
                    for t0 in range(0, L, TB):
                        dma("sp", t[:, :, 0:TB], srcd.rearrange("(k p) t -> p k t", p=128)[:, k0:k1_, t0:t0 + TB], writes=["fyt"])
                        cp("dve", tb[:, :, 0:TB], t[:, :, 0:TB], ["fyt"], ["fytb"])
                        dma("sp", yT[s].rearrange("(k p) t -> p k t", p=128)[:, k0:k1_, t0:t0 + TB], tb[:, :, 0:TB], reads=["fytb"])
                P.barrier()

        for l in range(depth_run):
            phase_mod(l)
            phase_in(l)
            if not feed_y or feed_y[1] < 8:
                phase_s5tab(l)
                phase_s5(l, "c")
                phase_s5(l, "x")
            if not feed_y or feed_y[0] > 0:
                phase_hyprep(l)
                phase_hyena(l, "c")
                phase_hyena(l, "x")
            if feed_y:
                phase_feed_y()
            phase_out(l)
            phase_ffn(l)

        P.barrier()
        print("program ops:", P.nops, {e: len(v) for e, v in P.streams.items()})
        with nc.Block() as block:
            P.emit(block)
    return nc, dram_in


_CACHE = {}


def kernel(**inputs):
    x = np.asarray(inputs["x"], dtype=np.float32)
    ctx = np.asarray(inputs["ctx"], dtype=np.float32)
    c = np.asarray(inputs["c"], dtype=np.float32)
    c_ctx = np.asarray(inputs["c_ctx"], dtype=np.float32)
    B = x.shape[0]
    if "nc" not in _CACHE:
        _CACHE["nc"] = build_program()
    nc, _ = _CACHE["nc"]
    shared = layout_weights(inputs)
    shared.update(host_constants())
    in_maps = []
    for core in range(8):
        b = core % B
        m = dict(shared)
        m["x"] = _f32(x[b])
        m["ctx"] = _f32(ctx[b])
        cT = np.stack([c[b].reshape(8, 128).T, c_ctx.reshape(8, 128).T], axis=-1)
        m["cT"] = _f32(cT)
        in_maps.append(m)
    res = run_bass_kernel_spmd(nc, in_maps, core_ids=list(range(8)))
    out = np.stack([np.asarray(res.results[b]["y"], dtype=np.float32) for b in range(B)], axis=0)
    return out
```

```python
import math
import numpy as np
from contextlib import ExitStack
import concourse.bass as bass
import concourse.mybir as mybir
from concourse.bass_utils import run_bass_kernel_spmd

F32 = mybir.dt.float32
BF16 = mybir.dt.bfloat16
AF = mybir.ActivationFunctionType
ALU = mybir.AluOpType

D = 1024
SEQ = 8192
CTX = 256
DEPTH = 4
DFF = 2816
NFF = DFF // 128
PAD = 8
EPS = 1e-6
MAGIC = 12582912.0
TWO_PI = 2.0 * math.pi


class _Op:
    __slots__ = ("eng", "fn", "waits", "sem", "val", "dma")


class Prog:
    COMPUTE = ("pe", "act", "dve", "pool")
    NDMA = 8

    def __init__(self, nc, stack):
        self.nc = nc
        self.h = {"pe": nc.tensor, "act": nc.scalar, "dve": nc.vector, "pool": nc.gpsimd, "sp": nc.sync}
        self.streams = {e: [] for e in self.h}
        self.esem = {e: stack.enter_context(nc.semaphore("s_" + e)) for e in self.COMPUTE}
        self.ecnt = {e: 0 for e in self.COMPUTE}
        self.dsem, self.dcnt, self.drr = {}, {}, {}
        for q in ("sp", "pool", "act"):
            self.dsem[q] = [stack.enter_context(nc.semaphore("d_%s%d" % (q, i))) for i in range(self.NDMA)]
            self.dcnt[q] = [0] * self.NDMA
            self.drr[q] = 0
        self.lastw = {}
        self.readers = {}
        self.waited = {e: {} for e in self.h}
        self.nops = 0

    def _dep(self, eng, op, waits):
        if op is None:
            return
        if (not op.dma) and op.eng == eng and eng == "pe":
            return
        key = id(op.sem)
        if self.waited[eng].get(key, 0) >= op.val:
            return
        cur = waits.get(key)
        if cur is None or cur[1] < op.val:
            waits[key] = (op.sem, op.val)

    def op(self, eng, fn, reads=(), writes=(), dma=False):
        o = _Op()
        o.eng, o.fn, o.dma = eng, fn, dma
        waits = {}
        for r in reads:
            self._dep(eng, self.lastw.get(r), waits)
        for wk in writes:
            self._dep(eng, self.lastw.get(wk), waits)
            for rd in self.readers.get(wk, ()):
                self._dep(eng, rd, waits)
        if dma:
            i = self.drr[eng]
            self.drr[eng] = (i + 1) % self.NDMA
            sem = self.dsem[eng][i]
            prev = self.dcnt[eng][i]
            if prev > 0 and self.waited[eng].get(id(sem), 0) < prev:
                cur = waits.get(id(sem))
                if cur is None or cur[1] < prev:
                    waits[id(sem)] = (sem, prev)
            self.dcnt[eng][i] = prev + 16
            o.sem, o.val = sem, prev + 16
        else:
            self.ecnt[eng] += 1
            o.sem, o.val = self.esem[eng], self.ecnt[eng]
        for key, (s, v) in waits.items():
            self.waited[eng][key] = v
        o.waits = list(waits.values())
        self.streams[eng].append(o)
        for r in reads:
            self.readers.setdefault(r, []).append(o)
        for wk in writes:
            self.lastw[wk] = o
            self.readers[wk] = []
        self.nops += 1
        return o

    def barrier(self, engines=None):
        tot = {}
        for e in self.COMPUTE:
            if self.ecnt[e] > 0:
                tot[id(self.esem[e])] = (self.esem[e], self.ecnt[e])
        for q in self.dsem:
            for i in range(self.NDMA):
                if self.dcnt[q][i] > 0:
                    tot[id(self.dsem[q][i])] = (self.dsem[q][i], self.dcnt[q][i])
        for eng in (engines or list(self.h)):
            waits = []
            for key, (s, v) in tot.items():
                if self.waited[eng].get(key, 0) < v:
                    if eng in self.COMPUTE and s is self.esem[eng]:
                        continue
                    waits.append((s, v))
                    self.waited[eng][key] = v
            if waits:
                o = _Op()
                o.eng, o.fn, o.dma, o.sem, o.val = eng, None, False, None, 0
                o.waits = waits
                self.streams[eng].append(o)
        self.lastw = {}
        self.readers = {}

    def emit(self, block):
        def mk(ename):
            def body(e):
                for o in self.streams[ename]:
                    for (s, v) in o.waits:
                        e.wait_ge(s, v)
                    if o.fn is None:
                        continue
                    o.fn(e).then_inc(o.sem, 16 if o.dma else 1)
            return body
        block.sync(mk("sp"))
        block.scalar(mk("act"))
        block.vector(mk("dve"))
        block.gpsimd(mk("pool"))
        block.tensor(mk("pe"))


def _f32(a):
    return np.ascontiguousarray(np.asarray(a, dtype=np.float32))


MON = {"x": (128, 64, 65, SEQ), "c": (4, 2, 3, CTX)}


def host_constants():
    c = {}
    c["ident"] = _f32(np.eye(128))
    k2 = np.arange(128)[:, None]
    n2 = np.arange(128)[None, :]
    G = np.exp(2j * np.pi * k2 * n2 / 128.0)
    gt = np.zeros((128, 2, 2, 128))
    for hf in range(2):
        sl = slice(hf * 64, hf * 64 + 64)
        gt[:, hf, 0, 0:64] = G.real[:, sl]
        gt[:, hf, 0, 64:128] = G.imag[:, sl]
        gt[:, hf, 1, 0:64] = -G.imag[:, sl]
        gt[:, hf, 1, 64:128] = G.real[:, sl]
    c["gtab"] = _f32(gt)
    deltas = np.abs(np.linspace(math.log(1e-2) / 0.3, math.log(1e-2) / 1.5, 512))
    for s, (N1, NZ, NK, L) in MON.items():
        N = 128 * N1
        n1 = np.arange(NZ)[:, None]
        k1 = np.arange(NK)[None, :]
        ang = 2 * np.pi * n1 * k1 / N1
        c["f1tab_" + s] = _f32(np.concatenate([np.cos(ang), -np.sin(ang)], axis=1))
        f2 = np.zeros((NK, 128, 4, 128))
        nn = np.arange(128)[:, None]
        kk = np.arange(128)[None, :]
        for a in range(NK):
            M = np.exp(-2j * np.pi * nn * (a + N1 * kk) / N)
            f2[a, :, 0] = M.real
            f2[a, :, 1] = M.imag
            f2[a, :, 2] = -M.imag
            f2[a, :, 3] = -M.real
        c["f2tab_" + s] = _f32(f2.reshape(NK, 128, 512))
        ck = np.full(NK, 2.0)
        ck[0] = 1.0
        ck[NK - 1] = 1.0
        i2 = np.zeros((NK, 128, 2, NZ))
        kq = np.arange(NK)[:, None, None]
        nq = np.arange(128)[None, :, None]
        mq = np.arange(NZ)[None, None, :]
        R = (ck[:, None, None] / N) * np.exp(2j * np.pi * kq * (128 * mq + nq) / N)
        i2[:, :, 0, :] = R.real
        i2[:, :, 1, :] = -R.imag
        c["i2tab_" + s] = _f32(i2)
        t = np.arange(L) / (L - 1.0)
        wv = (2.0 * np.pi / L) * np.arange(L)
        f = np.linspace(1e-4, 15.0, 16)
        z = np.concatenate([t[:, None], np.cos(f[None, :] * wv[:, None]), -np.sin(f[None, :] * wv[:, None])], axis=1)
        c["zT_" + s] = _f32(z.T)
        c["d1_" + s] = _f32(np.exp(-(128.0 * np.arange(NZ)[:, None] / (L - 1.0)) * deltas[None, :]))
        c["d2_" + s] = _f32(np.exp(-(np.arange(128)[:, None] / (L - 1.0)) * deltas[None, :]))
    sel = np.zeros((128, 64, 128))
    for gl in range(8):
        for jj in range(8):
            for hi in range(16):
                sel[gl * 16 + hi, gl * 8 + jj, jj * 16 + hi] = 1.0
    c["sel"] = _f32(sel)
    c["selT"] = _f32(sel.transpose(2, 1, 0))
    J = np.zeros((128, 128))
    for p in range(64):
        J[p, 64 + p] = 1.0
        J[64 + p, p] = 1.0
    c["jswap"] = _f32(J)
    msk = np.zeros((128, 2, 2, 256))
    idp = np.zeros((128, 2, 256))
    for jh in range(2):
        for jj in range(8):
            j = jh * 8 + jj
            for hi in range(16):
                for t in range(16):
                    if t >= j:
                        msk[jj * 16 + hi, 0, jh, t * 16:(t + 1) * 16] = 1.0
                    if t <= j:
                        msk[jj * 16 + hi, 1, jh, t * 16:(t + 1) * 16] = 1.0
                idp[jj * 16 + hi, jh, j * 16 + hi] = 1.0
    c["s5msk"] = _f32(msk)
    c["s5idp"] = _f32(idp)
    sg = np.zeros((128, 4))
    sg[:64, 0], sg[64:, 0] = -1.0, 1.0
    sg[:64, 1], sg[64:, 1] = 1.0, -1.0
    sg[:64, 2], sg[64:, 2] = math.pi / 2, 0.0
    sg[:64, 3], sg[64:, 3] = 0.0, math.pi / 2
    c["s5sg"] = _f32(sg)
    return c


def layout_weights(inp):
    w = {}
    g = lambda k: np.asarray(inp[k], dtype=np.float32)
    w["w_ada"] = _f32(g("w_ada"))
    w["b_adaT"] = _f32(g("b_ada").reshape(DEPTH, 48, 128).transpose(0, 2, 1))
    w["ngT"] = _f32(g("norm_g").reshape(DEPTH, 4, 8, 128).transpose(0, 3, 1, 2))
    w["w_in"] = _f32(g("w_in"))
    w["w_out"] = _f32(g("w_out"))
    w["w_up"] = _f32(g("ffn_w_up"))
    w["w_down"] = _f32(g("ffn_w_down"))
    w["fcwT"] = _f32(g("ffn_conv_w").reshape(DEPTH, 9, NFF, 128).transpose(0, 3, 1, 2))
    w["fcbT"] = _f32(g("ffn_conv_b").reshape(DEPTH, NFF, 128).transpose(0, 2, 1))
    w["hswT"] = _f32(g("hy_short_w").reshape(DEPTH, 3, 12, 128).transpose(0, 3, 1, 2))
    w["hsbT"] = _f32(g("hy_short_b").reshape(DEPTH, 12, 128).transpose(0, 2, 1))
    w["hbiasT"] = _f32(g("hy_bias").reshape(DEPTH, 4, 128).transpose(0, 2, 1))
    w["f_win"] = _f32(g("filt_w_in"))
    w["f_whid"] = _f32(g("filt_w_hid").transpose(0, 2, 1, 3))
    w["f_b"] = _f32(np.concatenate([g("filt_b_in")[:, :, None], g("filt_b_hid").transpose(0, 2, 1)], axis=2))
    w["f_freq"] = _f32(g("filt_freq")[:, :, None])
    w["f_wout"] = _f32(g("filt_w_out"))
    dup = lambda a: np.concatenate([a, a], axis=1)
    lam = np.stack([g("s5_lam_re").reshape(DEPTH, 64, 64).transpose(0, 2, 1),
                    g("s5_lam_im").reshape(DEPTH, 64, 64).transpose(0, 2, 1)], axis=2)
    w["s5lam"] = _f32(dup(lam))
    w["s5ls"] = _f32(np.broadcast_to(g("s5_log_step").reshape(DEPTH, 1, 64), (DEPTH, 128, 64)))
    bb = np.stack([g("s5_b_re").reshape(DEPTH, 64, 64, 16).transpose(0, 2, 1, 3),
                   g("s5_b_im").reshape(DEPTH, 64, 64, 16).transpose(0, 2, 1, 3)], axis=2)
    w["s5b"] = _f32(dup(bb))
    cc = np.stack([g("s5_c_re").reshape(DEPTH, 64, 16, 64).transpose(0, 3, 1, 2),
                   g("s5_c_im").reshape(DEPTH, 64, 16, 64).transpose(0, 3, 1, 2)], axis=2)
    w["s5c"] = _f32(dup(cc))
    dd = g("s5_d").reshape(DEPTH, 32, 16).transpose(0, 2, 1)
    w["s5dT"] = _f32(np.tile(dd, (1, 8, 1)))
    w["s5bglu"] = _f32(g("s5_b_glu").reshape(DEPTH, 4, 128).transpose(0, 2, 1))
    w["w_glu"] = _f32(g("s5_w_glu"))
    return w


class Seq:
    pass


def build_program(depth_run=DEPTH, feed_y=False, dbg=False):
    nc = bass.Bass("TRN2", target_bir_lowering=False)
    dram_in = {}

    def din(name, shape, dt=F32):
        dram_in[name] = nc.dram_tensor(name, list(shape), dt, kind="ExternalInput").ap()
        return dram_in[name]

    def dscr(name, shape, dt):
        if dbg and name in dbg:
            return nc.dram_tensor(name, list(shape), dt, kind="ExternalOutput").ap()
        return nc.dram_tensor(name, list(shape), dt, kind="Internal").ap()

    x_d = din("x", [SEQ, D])
    ctx_d = din("ctx", [CTX, D])
    cT_d = din("cT", [128, 8, 2])
    w_ada_d = din("w_ada", [DEPTH, D, 6 * D])
    b_adaT_d = din("b_adaT", [DEPTH, 128, 48])
    ngT_d = din("ngT", [DEPTH, 128, 4, 8])
    w_in_d = din("w_in", [DEPTH, D, 2 * D])
    w_out_d = din("w_out", [DEPTH, D, D])
    w_up_d = din("w_up", [DEPTH, D, 2 * DFF])
    w_down_d = din("w_down", [DEPTH, DFF, D])
    fcwT_d = din("fcwT", [DEPTH, 128, 9, NFF])
    fcbT_d = din("fcbT", [DEPTH, 128, NFF])
    ident_d = din("ident", [128, 128])
    hswT_d = din("hswT", [DEPTH, 128, 3, 12])
    hsbT_d = din("hsbT", [DEPTH, 128, 12])
    hbiasT_d = din("hbiasT", [DEPTH, 128, 4])
    f_win_d = din("f_win", [DEPTH, 33, 64])
    f_whid_d = din("f_whid", [DEPTH, 64, 2, 64])
    f_b_d = din("f_b", [DEPTH, 64, 3])
    f_freq_d = din("f_freq", [DEPTH, 64, 1])
    f_wout_d = din("f_wout", [DEPTH, 64, 1024])
    gtab_d = din("gtab", [128, 2, 2, 128])
    s5lam_d = din("s5lam", [DEPTH, 128, 2, 64])
    s5ls_d = din("s5ls", [DEPTH, 128, 64])
    s5b_d = din("s5b", [DEPTH, 128, 2, 64, 16])
    s5c_d = din("s5c", [DEPTH, 128, 2, 64, 16])
    s5dT_d = din("s5dT", [DEPTH, 128, 32])
    s5bglu_d = din("s5bglu", [DEPTH, 128, 4])
    w_glu_d = din("w_glu", [DEPTH, 512, 512])
    sel_d = din("sel", [128, 64, 128])
    selT_d = din("selT", [128, 64, 128])
    jswap_d = din("jswap", [128, 128])
    s5msk_d = din("s5msk", [128, 2, 2, 256])
    s5idp_d = din("s5idp", [128, 2, 256])
    s5sg_d = din("s5sg", [128, 4])
    mon_d = {}
    for s_, (N1_, NZ_, NK_, L_) in MON.items():
        mon_d[s_] = dict(f1=din("f1tab_" + s_, [NZ_, 2 * NK_]), f2=din("f2tab_" + s_, [NK_, 128, 512]),
                         i2=din("i2tab_" + s_, [NK_, 128, 2, NZ_]), z=din("zT_" + s_, [33, L_]),
                         d1=din("d1_" + s_, [NZ_, 512]), d2=din("d2_" + s_, [128, 512]))
    if feed_y:
        fy_x = din("fy_x", [D, SEQ])
        fy_c = din("fy_c", [D, CTX])

    y_d = nc.dram_tensor("y", [SEQ, D], F32, kind="ExternalOutput").ap()
    dbg_out = {}

    cres_d = dscr("cres", [CTX, D], F32)
    xmix_d = dscr("xmix", [SEQ, D], F32)
    cmix_d = dscr("cmix", [CTX, D], F32)
    gv_d = dscr("gvec", [2, 2, D], F32)
    wupb_d = dscr("wupb", [D, 2 * DFF], BF16)
    pT = {"x": dscr("pT_x", [2 * D, SEQ + 2 * PAD], BF16), "c": dscr("pT_c", [2 * D, CTX + 2 * PAD], BF16)}
    yT = {"x": dscr("yT_x", [D, SEQ], BF16), "c": dscr("yT_c", [D, CTX], BF16)}
    s5tab_d = dscr("s5tab", [32, 128, 1536], BF16)
    lamt_d = dscr("lamt", [64, 128, 1152], BF16)
    dgff_d = dscr("dgff", [NFF, 128, 1152], BF16)
    f2b_d = {s_: dscr("f2b_" + s_, [MON[s_][2], 128, 512], BF16) for s_ in MON}
    i2b_d = {s_: dscr("i2b_" + s_, [MON[s_][2], 128, 2 * MON[s_][1]], BF16) for s_ in MON}
    uT = {"x": dscr("uT_x", [512, SEQ], BF16), "c": dscr("uT_c", [512, CTX], BF16)}
    x0T = {"x": dscr("x0T_x", [512, SEQ], BF16), "c": dscr("x0T_c", [512, CTX], BF16)}
    yhT = {"x": dscr("yhT_x", [512, SEQ], BF16), "c": dscr("yhT_c", [512, CTX], BF16)}

    with ExitStack() as st:
        P = Prog(nc, st)

        uniq = [0]

        def sb(stack, name, shape, dt):
            uniq[0] += 1
            return stack.enter_context(nc.sbuf_tensor("%s_%d" % (name, uniq[0]), list(shape), dt))

        ps = [st.enter_context(nc.psum_tensor("ps%d" % i, [128, 512], F32)) for i in range(8)]
        psc = [0]

        def bank():
            i = psc[0]
            psc[0] = (i + 1) % 8
            return ps[i], ("ps", i)

        def dma(q, out, in_, reads=(), writes=(), slow=False):
            if slow:
                P.op(q, lambda e: e.dma_start(out=out, in_=in_, allow_slow_non_contiguous=True), reads, writes, dma=True)
            else:
                P.op(q, lambda e: e.dma_start(out=out, in_=in_), reads, writes, dma=True)

        def mm(out, lhsT, rhs, start, stop, reads, writes):
            P.op("pe", lambda e: e.matmul(out, lhsT=lhsT, rhs=rhs, start=start, stop=stop), reads, writes)

        def act(out, in_, func, reads, writes, scale=None, bias=None, accum=None):
            kw = {}
            if scale is not None:
                kw["scale"] = scale
            if bias is not None:
                kw["bias"] = bias
            if accum is not None:
                kw["accum_out"] = accum
            P.op("act", lambda e: e.activation(out=out, in_=in_, func=func, **kw), reads, writes)

        def tt(eng, out, in0, in1, op, reads, writes):
            P.op(eng, lambda e: e.tensor_tensor(out=out, in0=in0, in1=in1, op=op), reads, writes)

        def ts(eng, out, in0, s1, s2, op0, op1, reads, writes):
            if op1 is None:
                P.op(eng, lambda e: e.tensor_scalar(out=out, in0=in0, scalar1=s1, scalar2=None, op0=op0), reads, writes)
            else:
                P.op(eng, lambda e: e.tensor_scalar(out=out, in0=in0, scalar1=s1, scalar2=s2, op0=op0, op1=op1), reads, writes)

        def stt(out, in0, scalar, in1, op0, op1, reads, writes):
            P.op("dve", lambda e: e.scalar_tensor_tensor(out=out, in0=in0, scalar=scalar, in1=in1, op0=op0, op1=op1), reads, writes)

        def cp(eng, out, in_, reads, writes):
            if eng == "act":
                act(out, in_, AF.Copy, reads, writes)
            else:
                P.op(eng, lambda e: e.tensor_copy(out=out, in_=in_), reads, writes)

        def memset(eng, ap, val, writes):
            P.op(eng, lambda e: e.memset(ap, val), (), writes)

        identf = sb(st, "identf", [128, 128], F32)
        identb = sb(st, "identb", [128, 128], BF16)
        cond = sb(st, "cond", [128, 8, 2], F32)
        modT = sb(st, "modT", [128, 48, 2], F32)
        ngt = sb(st, "ngt", [128, 4, 8], F32)
        vec = {}
        for s in ("x", "c"):
            for nm in ("gs1", "sh1", "gs3", "sh3", "ga2", "ga4"):
                vec[(s, nm)] = sb(st, "v_%s_%s" % (s, nm), [128, 8], F32)
        G2 = {s: sb(st, "G2" + s, [128, D], F32) for s in ("x", "c")}
        G4 = {s: sb(st, "G4" + s, [128, D], F32) for s in ("x", "c")}
        SI = {"x": 0, "c": 1}
        epsb = sb(st, "epsb", [128, 1], F32)

        dma("sp", identf[:], ident_d[:, :], writes=["identf"])
        cp("dve", identb[:], identf[:], ["identf"], ["identb"])
        memset("dve", epsb[:], EPS, ["epsb"])
        dma("sp", cond[:], cT_d[:, :, :], writes=["cond"])
        act(cond[:], cond[:], AF.Silu, ["cond"], ["cond"])
        with ExitStack() as ph:
            zt = sb(ph, "zt", [128, 16, PAD], BF16)
            memset("dve", zt[:], 0.0, ["zt"])
            for s, L in (("x", SEQ), ("c", CTX)):
                v = pT[s].rearrange("(m p) t -> p m t", p=128)
                dma("sp", v[:, :, 0:PAD], zt[:], reads=["zt"])
                dma("sp", v[:, :, PAD + L:PAD + L + PAD], zt[:], reads=["zt"])
            P.barrier()

        with ExitStack() as ph:
            stg = [sb(ph, "tcs%d" % i, [128, 8, 512], BF16) for i in range(2)]
            cnt = 0
            for s_ in MON:
                NK_ = MON[s_][2]
                NZ_ = MON[s_][1]
                for k0 in range(0, NK_, 8):
                    nk = min(8, NK_ - k0)
                    sl = cnt % 2
                    cnt += 1
                    dma("pool", stg[sl][:, 0:nk, :], mon_d[s_]["f2"][k0:k0 + nk].rearrange("k p n -> p k n"), writes=[("tcs", sl)])
                    dma("sp", f2b_d[s_][k0:k0 + nk].rearrange("k p n -> p k n"), stg[sl][:, 0:nk, :], reads=[("tcs", sl)])
                w_ = 2 * NZ_
                for n0 in range(0, 128, 32):
                    sl = cnt % 2
                    cnt += 1
                    v = stg[sl][0:NK_, :, :].rearrange("p a b -> p (a b)")[:, 0:32 * w_]
                    dma("pool", v, mon_d[s_]["i2"][:, n0:n0 + 32].rearrange("k n r z -> k (n r z)"), writes=[("tcs", sl)])
                    dma("sp", i2b_d[s_][:, n0:n0 + 32, :].rearrange("k n w -> k (n w)"), v, reads=[("tcs", sl)])
            P.barrier()

        seqs = {}
        for s, L, src in (("c", CTX, ctx_d), ("x", SEQ, x_d)):
            q = Seq()
            q.name, q.L, q.src = s, L, src
            q.res = y_d if s == "x" else cres_d
            q.mix = xmix_d if s == "x" else cmix_d
            seqs[s] = q

        def phase_mod(l):
            with ExitStack() as ph:
                wa = [sb(ph, "wa%d" % i, [128, 8, 512], F32) for i in range(2)]
                bad = sb(ph, "bad", [128, 48], F32)
                tmp = sb(ph, "modtmp", [128, 8], F32)
                dma("sp", bad[:], b_adaT_d[l], writes=["bad"])
                dma("sp", ngt[:], ngT_d[l], writes=["ngt"])
                wv = w_ada_d[l].rearrange("(k p) n -> p k n", p=128)
                bk, bkey = bank()
                for cb in range(12):
                    slot = cb % 2
                    dma("sp" if cb % 2 == 0 else "pool", wa[slot][:], wv[:, :, cb * 512:(cb + 1) * 512], writes=[("wa", slot)])
                    for mi in range(4):
                        m = cb * 4 + mi
                        for k in range(8):
                            mm(bk[:, 2 * m:2 * m + 2], wa[slot][:, k, mi * 128:(mi + 1) * 128], cond[:, k, :],
                               k == 0, k == 7, [("wa", slot), "cond"], [bkey])
                tt("dve", modT[:], bk[:, 0:96].rearrange("p (m s) -> p m s", s=2),
                   bad[:].unsqueeze(2).to_broadcast([128, 48, 2]), ALU.add, [bkey, "bad"], ["modT"])
                for s in ("x", "c"):
                    si = SI[s]
                    md = lambda i: modT[:, i * 8:(i + 1) * 8, si]
                    stt(vec[(s, "gs1")][:], md(1), 1.0, ngt[:, 0, :], ALU.add, ALU.mult, ["modT", "ngt"], [("v", s, "gs1")])
                    cp("dve", vec[(s, "sh1")][:], md(0), ["modT"], [("v", s, "sh1")])
                    stt(vec[(s, "gs3")][:], md(4), 1.0, ngt[:, 2, :], ALU.add, ALU.mult, ["modT", "ngt"], [("v", s, "gs3")])
                    cp("dve", vec[(s, "sh3")][:], md(3), ["modT"], [("v", s, "sh3")])
                    tt("dve", vec[(s, "ga2")][:], md(2), ngt[:, 1, :], ALU.mult, ["modT", "ngt"], [("v", s, "ga2")])
                    tt("dve", vec[(s, "ga4")][:], md(5), ngt[:, 3, :], ALU.mult, ["modT", "ngt"], [("v", s, "ga4")])
                    for j, nm in enumerate(("ga2", "ga4")):
                        dma("sp", gv_d[si, j].rearrange("(k p) -> p k", p=128), vec[(s, nm)][:],
                            reads=[("v", s, nm)], writes=[("gv", si, j)], slow=True)
                    dma("sp", G2[s][:], gv_d[si, 0:1, :].to_broadcast([128, D]), reads=[("gv", si, 0)], writes=[("G2", s)])
                    dma("sp", G4[s][:], gv_d[si, 1:2, :].to_broadcast([128, D]), reads=[("gv", si, 1)], writes=[("G4", s)])
                P.barrier()

        def norm_rows(xt_ap, na, ss, rs, junk, keys_r, key_ss, key_rs):
            for a in range(na):
                act(junk[:], xt_ap[:, a, :], AF.Square, keys_r, [key_ss], accum=ss[:, a:a + 1])
            act(rs[:, 0:na], ss[:, 0:na], AF.Sqrt, [key_ss], [key_rs], scale=1.0 / D, bias=epsb[:])
            P.op("dve", lambda e: e.reciprocal(out=rs[:, 0:na], in_=rs[:, 0:na]), [key_rs], [key_rs])

        def phase_in(l):
            with ExitStack() as ph:
                win = sb(ph, "win", [128, 8, 2 * D], BF16)
                wv = w_in_d[l].rearrange("(k p) n -> p k n", p=128)
                for k in range(8):
                    dma("pool", win[:, k, :], wv[:, k, :], writes=["win"])
                xts = [sb(ph, "xt%d" % i, [128, 4, D], F32) for i in range(2)]
                xss = [sb(ph, "xs%d" % i, [128, 4, D], BF16) for i in range(2)]
                xnT = [sb(ph, "xnT%d" % i, [128, 8, 512], BF16) for i in range(2)]
                pout = [sb(ph, "pout%d" % i, [128, 16, 512], BF16) for i in range(2)]
                junk = sb(ph, "junk", [128, D], BF16)
                ssq = [sb(ph, "ssq%d" % i, [128, 4], F32) for i in range(2)]
                rsd = [sb(ph, "rsd%d" % i, [128, 4], F32) for i in range(2)]
                cnt = 0
                for s in ("c", "x"):
                    q = seqs[s]
                    src = q.src if l == 0 else q.res
                    TB = min(512, q.L)
                    for t0 in range(0, q.L, TB):
                        nt = TB
                        na = nt // 128
                        sl = cnt % 2
                        cnt += 1
                        xt, xs = xts[sl], xss[sl]
                        dma("sp", xt[:, 0:na, :], src[t0:t0 + nt, :].rearrange("(a p) f -> p a f", p=128), writes=[("xt", sl)])
                        norm_rows(xt, na, ssq[sl], rsd[sl], junk, [("xt", sl)], ("ss", sl), ("rs", sl))
                        for a in range(na):
                            if a % 2 == 0:
                                ts("dve", xs[:, a, :], xt[:, a, :], rsd[sl][:, a:a + 1], None, ALU.mult, None,
                                   [("xt", sl), ("rs", sl)], [("xs", sl, a)])
                            else:
                                act(xs[:, a, :], xt[:, a, :], AF.Copy, [("xt", sl), ("rs", sl)], [("xs", sl, a)], scale=rsd[sl][:, a:a + 1])
                        for k in range(8):
                            bk, bkey = bank()
                            for a in range(na):
                                mm(bk[:, a * 128:(a + 1) * 128], xs[:, a, k * 128:(k + 1) * 128], identb[:], True, True,
                                   [("xs", sl, a), "identb"], [bkey])
                            act(xnT[sl][:, k, 0:nt], bk[:, 0:nt], AF.Identity, [bkey, ("v", s, "gs1"), ("v", s, "sh1")], [("xnT", sl, k)],
                                scale=vec[(s, "gs1")][:, k:k + 1], bias=vec[(s, "sh1")][:, k:k + 1])
                        for m in range(16):
                            bk, bkey = bank()
                            for k in range(8):
                                mm(bk[:, 0:nt], win[:, k, m * 128:(m + 1) * 128], xnT[sl][:, k, 0:nt], k == 0, k == 7,
                                   ["win", ("xnT", sl, k)], [bkey])
                            cp("act" if m % 2 == 0 else "dve", pout[sl][:, m, 0:nt], bk[:, 0:nt], [bkey], [("pout", sl)])
                        dma("sp", pT[s].rearrange("(m p) t -> p m t", p=128)[:, :, PAD + t0:PAD + t0 + nt], pout[sl][:, :, 0:nt],
                            reads=[("pout", sl)])
                P.barrier()

        def resid_epilogue(bk0, bk1, k0, k1, s, Gt, Gkey, xres_ap, xres_key, out_ap, out_key, scr):
            ss2, rs1, junk2, tmp = scr
            act(junk2[:, 0:512], bk0[:], AF.Square, [k0], ["ss2"], accum=ss2[:, 0:1])
            act(junk2[:, 512:1024], bk1[:], AF.Square, [k1], ["ss2"], accum=ss2[:, 1:2])
            tt("dve", rs1[:], ss2[:, 0:1], ss2[:, 1:2], ALU.add, ["ss2"], ["rs1"])
            act(rs1[:], rs1[:], AF.Sqrt, ["rs1"], ["rs1"], scale=1.0 / D, bias=epsb[:])
            P.op("dve", lambda e: e.reciprocal(out=rs1[:], in_=rs1[:]), ["rs1"], ["rs1"])
            stt(tmp[:, 0:512], bk0[:], rs1[:, 0:1], Gt[:, 0:512], ALU.mult, ALU.mult, [k0, "rs1", Gkey], ["etmp0"])
            stt(tmp[:, 512:1024], bk1[:], rs1[:, 0:1], Gt[:, 512:1024], ALU.mult, ALU.mult, [k1, "rs1", Gkey], ["etmp1"])
            tt("pool", out_ap, tmp[:], xres_ap, ALU.add, ["etmp0", "etmp1", xres_key], [out_key])

        def phase_out(l):
            with ExitStack() as ph:
                wo = sb(ph, "wo", [128, 8, D], BF16)
                wv = w_out_d[l].rearrange("(k p) n -> p k n", p=128)
                for k in range(8):
                    dma("pool", wo[:, k, :], wv[:, k, :], writes=["wo"])
                yts = [sb(ph, "yt%d" % i, [128, 8, 512], BF16) for i in range(2)]
                xts = [sb(ph, "xt%d" % i, [128, 4, D], F32) for i in range(2)]
                xos = [sb(ph, "xo%d" % i, [128, 4, D], F32) for i in range(2)]
                scr = (sb(ph, "ss2", [128, 2], F32), sb(ph, "rs1", [128, 1], F32), sb(ph, "junk2", [128, D], BF16),
                       sb(ph, "etmp", [128, D], F32))
                cnt = 0
                for s in ("c", "x"):
                    q = seqs[s]
                    src = q.src if l == 0 else q.res
                    TB = min(512, q.L)
                    for t0 in range(0, q.L, TB):
                        nt = TB
                        na = nt // 128
                        sl = cnt % 2
                        cnt += 1
                        dma("sp", yts[sl][:, :, 0:nt], yT[s].rearrange("(k p) t -> p k t", p=128)[:, :, t0:t0 + nt], writes=[("yt", sl)])
                        dma("sp", xts[sl][:, 0:na, :], src[t0:t0 + nt, :].rearrange("(a p) f -> p a f", p=128), writes=[("xt", sl)])
                        for a in range(na):
                            b0, k0 = bank()
                            b1, k1 = bank()
                            for k in range(8):
                                mm(b0[:], yts[sl][:, k, a * 128:(a + 1) * 128], wo[:, k, 0:512], k == 0, k == 7, [("yt", sl), "wo"], [k0])
                                mm(b1[:], yts[sl][:, k, a * 128:(a + 1) * 128], wo[:, k, 512:1024], k == 0, k == 7, [("yt", sl), "wo"], [k1])
                            resid_epilogue(b0, b1, k0, k1, s, G2[s], ("G2", s), xts[sl][:, a, :], ("xt", sl), xos[sl][:, a, :], ("xo", sl, a), scr)
                        dma("sp", q.mix[t0:t0 + nt, :].rearrange("(a p) f -> p a f", p=128), xos[sl][:, 0:na, :],
                            reads=[("xo", sl, a) for a in range(na)])
                P.barrier()

        def phase_ffn(l):
            with ExitStack() as ph:
                wd = sb(ph, "wd", [128, NFF, D], BF16)
                wv = w_down_d[l].rearrange("(k p) n -> p k n", p=128)
                for k in range(NFF):
                    dma("pool", wd[:, k, :], wv[:, k, :], writes=["wd"])
                stg = [sb(ph, "stg%d" % i, [128, 2 * DFF], BF16) for i in range(2)]
                for k in range(8):
                    dma("pool", stg[k % 2][:], w_up_d[l, k * 128:(k + 1) * 128, :], writes=[("stg", k % 2)])
                    dma("sp", wupb_d[k * 128:(k + 1) * 128, :], stg[k % 2][:], reads=[("stg", k % 2)])
                P.barrier()
                wupv = wupb_d.rearrange("(k p) n -> p k n", p=128)
                cw = sb(ph, "cw", [128, 9, NFF], F32)
                cb = sb(ph, "cb", [128, NFF], F32)
                dma("sp", cw[:], fcwT_d[l], writes=["cw"])
                dma("sp", cb[:], fcbT_d[l], writes=["cb"])
                with ExitStack() as p3:
                    dst_ = [sb(p3, "dgst%d" % i, [128, 2, 1152], BF16) for i in range(2)]
                    for m0 in range(0, NFF, 2):
                        sl = (m0 // 2) % 2
                        for mi in range(2):
                            for tap in range(9):
                                act(dst_[sl][:, mi, tap * 128:(tap + 1) * 128], identf[:], AF.Copy, ["identf", "cw"], [("dgst", sl)],
                                    scale=cw[:, tap, m0 + mi:m0 + mi + 1])
                        dma("sp", dgff_d[m0:m0 + 2].rearrange("m p n -> p m n"), dst_[sl][:], reads=[("dgst", sl)])
                    P.barrier()
                XE = 1152
                xnT = sb(ph, "fxnT", [128, 8, XE], BF16)
                hT = sb(ph, "hT", [128, NFF, 1024], BF16)
                xt1 = [sb(ph, "fxt%d" % i, [128, D], F32) for i in range(2)]
                xs1 = [sb(ph, "fxs%d" % i, [128, D], BF16) for i in range(2)]
                ss1 = [sb(ph, "fss%d" % i, [128, 1], F32) for i in range(2)]
                rs1b = [sb(ph, "frs%d" % i, [128, 1], F32) for i in range(2)]
                junk = sb(ph, "fjunk", [128, D], BF16)
                wg = [sb(ph, "wg%d" % i, [128, 8, 128], BF16) for i in range(2)]
                wvv = [sb(ph, "wv%d" % i, [128, 8, 128], BF16) for i in range(2)]
                dg = [sb(ph, "dg%d" % i, [128, 9, 128], BF16) for i in range(2)]
                gbuf = [sb(ph, "gbuf%d" % i, [128, 18, 64], BF16) for i in range(2)]
                gel = [sb(ph, "gel%d" % i, [128, 512], F32) for i in range(2)]
                xo = [sb(ph, "fxo%d" % i, [128, D], F32) for i in range(2)]
                scr = (sb(ph, "ss2", [128, 2], F32), sb(ph, "rs1", [128, 1], F32), sb(ph, "junk2", [128, D], BF16),
                       sb(ph, "etmp", [128, D], F32))
                tcnt = 0
                mcnt = 0
                for s in ("c", "x"):
                    q = seqs[s]
                    L = q.L
                    if s == "x":
                        ncols, BR = 64, 16
                    else:
                        ncols, BR = 256, 1
                    NT = BR * ncols
                    vert = s == "x"
                    for t0 in range(0, L, NT):
                        top = vert and t0 > 0
                        bot = vert and t0 + NT < L
                        e0 = t0 - (64 if top else 0)
                        e1 = t0 + NT + (64 if bot else 0)
                        tiles = []
                        tt0 = e0
                        while tt0 < e1:
                            n = min(128, e1 - tt0)
                            tiles.append((tt0, n))
                            tt0 += n
                        for (ta, n) in tiles:
                            sl = tcnt % 2
                            tcnt += 1
                            dma("sp", xt1[sl][0:n, :], q.mix[ta:ta + n, :], writes=[("fxt", sl)])
                            act(junk[0:n, :], xt1[sl][0:n, :], AF.Square, [("fxt", sl)], [("fss", sl)], accum=ss1[sl][0:n, :])
                            act(rs1b[sl][0:n, :], ss1[sl][0:n, :], AF.Sqrt, [("fss", sl)], [("frs", sl)], scale=1.0 / D, bias=epsb[0:n, :])
                            P.op("dve", (lambda r, n: lambda e: e.reciprocal(out=r[0:n, :], in_=r[0:n, :]))(rs1b[sl], n), [("frs", sl)], [("frs", sl)])
                            ts("dve", xs1[sl][0:n, :], xt1[sl][0:n, :], rs1b[sl][0:n, 0:1], None, ALU.mult, None, [("fxt", sl), ("frs", sl)], [("fxs", sl)])
                            c0 = ta - e0
                            for kk in range(2):
                                bk, bkey = bank()
                                for k4 in range(4):
                                    k = kk * 4 + k4
                                    mm(bk[:, k4 * 128:k4 * 128 + n], xs1[sl][0:n, k * 128:(k + 1) * 128], identb[0:n, 0:n], True, True,
                                       [("fxs", sl), "identb"], [bkey])
                                for k4 in range(4):
                                    k = kk * 4 + k4
                                    act(xnT[:, k, c0:c0 + n], bk[:, k4 * 128:k4 * 128 + n], AF.Identity,
                                        [bkey, ("v", s, "gs3"), ("v", s, "sh3")], [("fxnT", k)],
                                        scale=vec[(s, "gs3")][:, k:k + 1], bias=vec[(s, "sh3")][:, k:k + 1])
                        ne = e1 - e0
                        goff = 0 if top else 1
                        nrows_e = ne // ncols if vert else 1
                        for m in range(NFF):
                            sl = mcnt % 2
                            mcnt += 1
                            dma("sp", wg[sl][:], wupv[:, :, m * 128:(m + 1) * 128], writes=[("wg", sl)])
                            dma("sp", wvv[sl][:], wupv[:, :, DFF + m * 128:DFF + (m + 1) * 128], writes=[("wv", sl)])
                            dma("sp", dg[sl][:].rearrange("p t k -> p (t k)"), dgff_d[m], writes=[("dg", sl)])
                            gb = gbuf[sl]
                            gview = gb[:].rearrange("p r c -> p (r c)")
                            if vert:
                                if not top:
                                    memset("pool", gb[:, 0, :], 0.0, [("gbuf", sl)])
                                if not bot:
                                    memset("pool", gb[:, 17, :], 0.0, [("gbuf", sl)])
                            o = 0
                            while o < ne:
                                n = min(512, ne - o)
                                bk, bkey = bank()
                                for k in range(8):
                                    mm(bk[:, 0:n], wg[sl][:, k, :], xnT[:, k, o:o + n], k == 0, k == 7, [("wg", sl), ("fxnT", k)], [bkey])
                                gdst = gview[:, goff * 64 + o:goff * 64 + o + n] if vert else gview[:, o:o + n]
                                cp("act", gdst, bk[:, 0:n], [bkey], [("gbuf", sl)])
                                o += n
                            cen0 = t0 - e0
                            for sbk in range(0, NT, 512):
                                nsub = min(512, NT - sbk)
                                bv, bvkey = bank()
                                for k in range(8):
                                    mm(bv[:, 0:nsub], wvv[sl][:, k, :], xnT[:, k, cen0 + sbk:cen0 + sbk + nsub], k == 0, k == 7,
                                       [("wv", sl), ("fxnT", k)], [bvkey])
                                bc, bckey = bank()
                                if vert:
                                    r0 = 1 + sbk // 64
                                    nr = nsub // 64
                                    bc3 = bc[:, 0:nsub].rearrange("p (r c) -> p r c", c=64)
                                    first = True
                                    order = [4, 1, 7, 3, 5, 0, 2, 6, 8]
                                    for ti, tap in enumerate(order):
                                        dy, dx = tap // 3 - 1, tap % 3 - 1
                                        if dx == 0:
                                            rhs = gb[:, r0 + dy:r0 + dy + nr, :]
                                            out = bc3
                                        elif dx == -1:
                                            rhs = gb[:, r0 + dy:r0 + dy + nr, 0:63]
                                            out = bc3[:, :, 1:64]
                                        else:
                                            rhs = gb[:, r0 + dy:r0 + dy + nr, 1:64]
                                            out = bc3[:, :, 0:63]
                                        mm(out, dg[sl][:, tap, :], rhs, first, ti == 8, [("dg", sl), ("gbuf", sl)], [bckey])
                                        first = False
                                else:
                                    for ti, tap in enumerate([4, 3, 5]):
                                        dx = tap % 3 - 1
                                        if dx == 0:
                                            rhs, out = gview[:, 0:nsub], bc[:, 0:nsub]
                                        elif dx == -1:
                                            rhs, out = gview[:, 0:nsub - 1], bc[:, 1:nsub]
                                        else:
                                            rhs, out = gview[:, 1:nsub], bc[:, 0:nsub - 1]
                                        mm(out, dg[sl][:, tap, :], rhs, ti == 0, ti == 2, [("dg", sl), ("gbuf", sl)], [bckey])
                                gsl = (mcnt + sbk // 512) % 2
                                act(gel[gsl][:, 0:nsub], bc[:, 0:nsub], AF.Gelu_apprx_tanh, [bckey, "cb"], [("gel", gsl)], bias=cb[:, m:m + 1])
                                tt("dve", hT[:, m, sbk:sbk + nsub], bv[:, 0:nsub], gel[gsl][:, 0:nsub], ALU.mult, [bvkey, ("gel", gsl)], [("hT", m)])
                        for a in range(NT // 128):
                            sl = a % 2
                            b0, k0 = bank()
                            b1, k1 = bank()
                            for k in range(NFF):
                                mm(b0[:], hT[:, k, a * 128:(a + 1) * 128], wd[:, k, 0:512], k == 0, k == NFF - 1, [("hT", k), "wd"], [k0])
                                mm(b1[:], hT[:, k, a * 128:(a + 1) * 128], wd[:, k, 512:1024], k == 0, k == NFF - 1, [("hT", k), "wd"], [k1])
                            ta = t0 + a * 128
                            dma("sp", xt1[sl][:], q.mix[ta:ta + 128, :], writes=[("fxt", sl)])
                            resid_epilogue(b0, b1, k0, k1, s, G4[s], ("G4", s), xt1[sl][:], ("fxt", sl), xo[sl][:], ("fxo", sl), scr)
                            dma("sp", q.res[ta:ta + 128, :], xo[sl][:], reads=[("fxo", sl)])
                P.barrier()


        def sin_rr(out_ap, arg, tmpk, n_part, keys_arg, key_out, key_tmp):
            ts("dve", tmpk, arg, 1.0 / TWO_PI, MAGIC, ALU.mult, ALU.add, keys_arg, [key_tmp])
            ts("dve", tmpk, tmpk, MAGIC, -TWO_PI, ALU.subtract, ALU.mult, [key_tmp], [key_tmp])
            tt("dve", tmpk, arg, tmpk, ALU.add, keys_arg + [key_tmp], [key_tmp])
            ts("dve", tmpk, tmpk, 3.1415925, -3.1415925, ALU.min, ALU.max, [key_tmp], [key_tmp])
            act(out_ap, tmpk, AF.Sin, [key_tmp], [key_out])

        def phase_hyprep(l):
            with ExitStack() as ph:
                hw = sb(ph, "hw", [128, 3, 12], F32)
                hb = sb(ph, "hb", [128, 12], F32)
                dma("sp", hw[:], hswT_d[l], writes=["hw"])
                dma("sp", hb[:], hsbT_d[l], writes=["hb"])
                dgs = sb(ph, "hdg", [128, 36, 128], BF16)
                for ch in range(12):
                    for d in range(3):
                        act(dgs[:, ch * 3 + d, :], identf[:], AF.Copy, ["identf", "hw"], ["hdg"], scale=hw[:, d, ch:ch + 1])
                pin = [sb(ph, "pin%d" % i, [128, 3, 514], BF16) for i in range(2)]
                vB = [sb(ph, "vB%d" % i, [128, 512], F32) for i in range(2)]
                uo = [sb(ph, "uo%d" % i, [128, 512], BF16) for i in range(2)]
                xo = [sb(ph, "x0o%d" % i, [128, 512], BF16) for i in range(2)]
                cnt = 0
                for s in ("c", "x"):
                    L = seqs[s].L
                    TB = min(512, L)
                    for cc in range(4):
                        for t0 in range(0, L, TB):
                            sl = cnt % 2
                            cnt += 1
                            for j in range(3):
                                r0 = j * 512 + cc * 128
                                dma("sp" if j != 1 else "pool", pin[sl][:, j, 0:TB + 2], pT[s][r0:r0 + 128, PAD + t0 - 1:PAD + t0 + TB + 1], writes=[("pin", sl, j)])
                            bks = []
                            for j in range(3):
                                bk, bkey = bank()
                                ch = j * 4 + cc
                                for d in range(3):
                                    mm(bk[:, 0:TB], dgs[:, ch * 3 + d, :], pin[sl][:, j, d:d + TB], d == 0, d == 2, ["hdg", ("pin", sl, j)], [bkey])
                                bks.append((bk, bkey))
                            act(vB[sl][:, 0:TB], bks[2][0][:, 0:TB], AF.Identity, [bks[2][1], "hb"], [("vB", sl)], bias=hb[:, 8 + cc:9 + cc])
                            act(xo[sl][:, 0:TB], bks[0][0][:, 0:TB], AF.Identity, [bks[0][1], "hb"], [("x0o", sl)], bias=hb[:, cc:cc + 1])
                            stt(uo[sl][:, 0:TB], bks[1][0][:, 0:TB], hb[:, 4 + cc:5 + cc], vB[sl][:, 0:TB], ALU.add, ALU.mult,
                                [bks[1][1], "hb", ("vB", sl)], [("uo", sl)])
                            dma("sp", uT[s][cc * 128:(cc + 1) * 128, t0:t0 + TB], uo[sl][:, 0:TB], reads=[("uo", sl)])
                            dma("sp", x0T[s][cc * 128:(cc + 1) * 128, t0:t0 + TB], xo[sl][:, 0:TB], reads=[("x0o", sl)])
                P.barrier()

        def phase_hyena(l, s):
            N1, NZ, NK, L = MON[s]
            md = mon_d[s]
            NK2 = 2 * NK
            with ExitStack() as ph:
                hidT = sb(ph, "hidT", [64, L], BF16)
                woutb = sb(ph, "woutb", [64, 1024], BF16)
                d1 = sb(ph, "d1", [NZ, 512], F32)
                d2 = sb(ph, "d2", [128, 512], F32)
                f1t = sb(ph, "f1t", [NZ, NK2], BF16)
                gtb = sb(ph, "gtb", [128, 2, 2, 128], BF16)
                hbias = sb(ph, "hbias", [128, 4], F32)
                dma("pool", woutb[:], f_wout_d[l], writes=["woutb"])
                dma("sp", d1[:], md["d1"][:, :], writes=["d1"])
                dma("sp", d2[:], md["d2"][:, :], writes=["d2"])
                dma("pool", f1t[:], md["f1"][:, :], writes=["f1t"])
                dma("pool", gtb[:], gtab_d[:, :, :, :], writes=["gtb"])
                with ExitStack() as p2:
                    zt = sb(p2, "zt", [33, L], F32)
                    fwin = sb(p2, "fwin", [33, 64], F32)
                    fwh = sb(p2, "fwh", [64, 2, 64], F32)
                    fb = sb(p2, "fb", [64, 3], F32)
                    ffr = sb(p2, "ffr", [64, 1], F32)
                    frb = sb(p2, "frb", [64, 3], F32)
                    dma("sp", zt[:], md["z"][:, :], writes=["zt"])
                    dma("sp", fwin[:], f_win_d[l], writes=["fwin"])
                    dma("sp", fwh[:], f_whid_d[l], writes=["fwh"])
                    dma("sp", fb[:], f_b_d[l], writes=["fb"])
                    dma("sp", ffr[:], f_freq_d[l], writes=["ffr"])
                    ts("dve", frb[:], fb[:], ffr[:, 0:1], None, ALU.mult, None, ["fb", "ffr"], ["frb"])
                    args = [sb(p2, "farg%d" % i, [64, 512], F32) for i in range(2)]
                    tmpk = [sb(p2, "ftmp%d" % i, [64, 512], F32) for i in range(2)]
                    hts = [sb(p2, "fh%d" % i, [64, 512], F32) for i in range(4)]
                    TB = min(512, L)
                    cnt = 0
                    for t0 in range(0, L, TB):
                        prev = None
                        for li in range(3):
                            sl = cnt % 2
                            cnt += 1
                            bk, bkey = bank()
                            if li == 0:
                                mm(bk[0:64, 0:TB], fwin[:], zt[:, t0:t0 + TB], True, True, ["fwin", "zt"], [bkey])
                            else:
                                mm(bk[0:64, 0:TB], fwh[:, li - 1, :], prev[0][:, 0:TB], True, True, ["fwh", prev[1]], [bkey])
                            act(args[sl][:, 0:TB], bk[0:64, 0:TB], AF.Identity, [bkey, "ffr", "frb"], [("farg", sl)],
                                scale=ffr[:, 0:1], bias=frb[:, li:li + 1])
                            if li < 2:
                                hsl = (t0 // TB * 2 + li) % 4
                                sin_rr(hts[hsl][:, 0:TB], args[sl][:, 0:TB], tmpk[sl][:, 0:TB], 64, [("farg", sl)], ("fh", hsl), ("ftmp", sl))
                                prev = (hts[hsl], ("fh", hsl))
                            else:
                                sin_rr(hidT[:, t0:t0 + TB], args[sl][:, 0:TB], tmpk[sl][:, 0:TB], 64, [("farg", sl)], "hidT", ("ftmp", sl))
                    P.barrier()
                dma("sp", hbias[:], hbiasT_d[l], writes=["hbias"])
                X = sb(ph, "monX", [NZ, 128, 128], BF16)
                A = sb(ph, "monA", [128, 2, NK, 128], BF16)
                Kf = sb(ph, "monK", [128, 2, NK, 128], BF16)
                BB = sb(ph, "monB", [128, 2 * 65 * 128], BF16)
                Ab = BB[:, 0:2 * NK * 128].rearrange("p (r k c) -> p r k c", r=2, k=NK, c=128)
                B0 = BB[0:NK, 0:2 * 64 * 128].rearrange("p (r n c) -> p r n c", r=2, n=64, c=128)
                f2r = [sb(ph, "f2r%d" % i, [128, 4, 128], BF16) for i in range(8)]
                i2r = [sb(ph, "i2r%d" % i, [NK, 4, 2, NZ], BF16) for i in range(4)]
                tm = [sb(ph, "fmt%d" % i, [128, 512], F32) for i in range(4)]
                KB = 4
                nkb = (NK + KB - 1) // KB
                Akeys = [("A", kb) for kb in range(nkb)]
                f2c = [0]
                i2c = [0]
                nb1 = max(1, min(128, 512 // NK2))

                def f1(dst, dstkeys, scale_d2, cg):
                    c = 0
                    i = 0
                    while c < 128:
                        nb = min(nb1, 128 - c)
                        bk, bkey = bank()
                        for q in range(nb):
                            mm(bk[:, q * NK2:(q + 1) * NK2], X[0:NZ, c + q, :], f1t[0:NZ, :], True, True, ["X", "f1t"], [bkey])
                        for ri in range(2):
                            src = bk[:, 0:nb * NK2].rearrange("p (i r k) -> p i r k", r=2, k=NK)[:, :, ri, :]
                            out = dst[:, ri, :, c:c + nb].rearrange("p k i -> p i k")
                            if scale_d2:
                                tt("dve", out, src, d2[:, cg * 128 + c:cg * 128 + c + nb].unsqueeze(2).to_broadcast([128, nb, NK]), ALU.mult,
                                   [bkey, "d2"], dstkeys)
                            else:
                                cp("act" if (i + ri) % 2 == 0 else "dve", out, src, [bkey], dstkeys)
                        c += nb
                        i += 1

                def load_f2(k1):
                    sl = f2c[0] % 8
                    f2c[0] += 1
                    dma("sp", f2r[sl][:].rearrange("p v k -> p (v k)"), f2b_d[s][k1], writes=[("f2r", sl)])
                    return f2r[sl], ("f2r", sl)

                for cg in range(4):
                    for dr in range(2):
                        for n20 in range(0, 128, 4):
                            bk, bkey = bank()
                            for q in range(4):
                                n2 = n20 + q
                                mm(bk[0:NZ, q * 128:(q + 1) * 128], hidT[:, n2:L:128], woutb[:, dr * 512 + cg * 128:dr * 512 + (cg + 1) * 128],
                                   True, True, ["hidT", "woutb"], [bkey])
                            tt("dve", X[0:NZ, :, n20:n20 + 4].rearrange("p c q -> p q c"),
                               bk[0:NZ, 0:512].rearrange("p (q c) -> p q c", c=128),
                               d1[0:NZ, cg * 128:(cg + 1) * 128].unsqueeze(1).to_broadcast([NZ, 4, 128]), ALU.mult, [bkey, "d1"], ["X"])
                        if dr == 1:
                            memset("dve", X[0:1, :, 0:1], 0.0, ["X"])
                        if dr == 0:
                            f1(A, Akeys, True, cg)
                        else:
                            f1(Ab, ["B0"], True, cg)
                    for kb in range(nkb):
                        k1s = list(range(kb * KB, min(NK, (kb + 1) * KB)))
                        br, brk = bank()
                        bi, bik = bank()
                        for q, k1 in enumerate(k1s):
                            f2, f2k = load_f2(k1)
                            cs = slice(q * 128, (q + 1) * 128)
                            mm(br[:, cs], f2[:, 0, :], A[:, 0, k1, :], True, False, [f2k, ("A", kb)], [brk])
                            mm(br[:, cs], f2[:, 2, :], A[:, 1, k1, :], False, False, [f2k, ("A", kb)], [brk])
                            mm(br[:, cs], f2[:, 0, :], Ab[:, 0, k1, :], False, False, [f2k, "B0"], [brk])
                            mm(br[:, cs], f2[:, 2, :], Ab[:, 1, k1, :], False, True, [f2k, "B0"], [brk])
                            mm(bi[:, cs], f2[:, 1, :], A[:, 0, k1, :], True, False, [f2k, ("A", kb)], [bik])
                            mm(bi[:, cs], f2[:, 0, :], A[:, 1, k1, :], False, False, [f2k, ("A", kb)], [bik])
                            mm(bi[:, cs], f2[:, 2, :], Ab[:, 0, k1, :], False, False, [f2k, "B0"], [bik])
                            mm(bi[:, cs], f2[:, 3, :], Ab[:, 1, k1, :], False, True, [f2k, "B0"], [bik])
                        nn = len(k1s) * 128
                        cp("act", Kf[:, 0, k1s[0]:k1s[-1] + 1, :].rearrange("p k c -> p (k c)"), br[:, 0:nn], [brk], [("Kf", kb)])
                        cp("act", Kf[:, 1, k1s[0]:k1s[-1] + 1, :].rearrange("p k c -> p (k c)"), bi[:, 0:nn], [bik], [("Kf", kb)])
                    dma("sp", X[0:NZ, :, :], uT[s][cg * 128:(cg + 1) * 128, :].rearrange("c (a b) -> a c b", b=128), writes=["X"])
                    f1(A, Akeys, False, cg)
                    for kb in range(nkb):
                        k1s = list(range(kb * KB, min(NK, (kb + 1) * KB)))
                        br, brk = bank()
                        bi, bik = bank()
                        for q, k1 in enumerate(k1s):
                            f2, f2k = load_f2(k1)
                            cs = slice(q * 128, (q + 1) * 128)
                            mm(br[:, cs], f2[:, 0, :], A[:, 0, k1, :], True, False, [f2k, ("A", kb)], [brk])
                            mm(br[:, cs], f2[:, 2, :], A[:, 1, k1, :], False, True, [f2k, ("A", kb)], [brk])
                            mm(bi[:, cs], f2[:, 1, :], A[:, 0, k1, :], True, False, [f2k, ("A", kb)], [bik])
                            mm(bi[:, cs], f2[:, 0, :], A[:, 1, k1, :], False, True, [f2k, ("A", kb)], [bik])
                        nn = len(k1s) * 128
                        kr = Kf[:, 0, k1s[0]:k1s[-1] + 1, :].rearrange("p k c -> p (k c)")
                        ki = Kf[:, 1, k1s[0]:k1s[-1] + 1, :].rearrange("p k c -> p (k c)")
                        tt("dve", tm[0][:, 0:nn], br[:, 0:nn], kr, ALU.mult, [brk, ("Kf", kb)], [("tm", 0)])
                        tt("dve", tm[1][:, 0:nn], bi[:, 0:nn], ki, ALU.mult, [bik, ("Kf", kb)], [("tm", 1)])
                        tt("dve", tm[2][:, 0:nn], br[:, 0:nn], ki, ALU.mult, [brk, ("Kf", kb)], [("tm", 2)])
                        tt("dve", tm[3][:, 0:nn], bi[:, 0:nn], kr, ALU.mult, [bik, ("Kf", kb)], [("tm", 3)])
                        tt("pool", A[:, 0, k1s[0]:k1s[-1] + 1, :].rearrange("p k c -> p (k c)"), tm[0][:, 0:nn], tm[1][:, 0:nn], ALU.subtract,
                           [("tm", 0), ("tm", 1)], [("A", kb)])
                        tt("pool", A[:, 1, k1s[0]:k1s[-1] + 1, :].rearrange("p k c -> p (k c)"), tm[2][:, 0:nn], tm[3][:, 0:nn], ALU.add,
                           [("tm", 2), ("tm", 3)], [("A", kb)])
                    for hf in range(2):
                        for c0 in range(0, 128, 4):
                            bk, bkey = bank()
                            for q in range(4):
                                cs = slice(q * 128, (q + 1) * 128)
                                mm(bk[0:NK, cs], A[:, 0, 0:NK, c0 + q], gtb[:, hf, 0, :], True, False, Akeys + ["gtb"], [bkey])
                                mm(bk[0:NK, cs], A[:, 1, 0:NK, c0 + q], gtb[:, hf, 1, :], False, True, Akeys + ["gtb"], [bkey])
                            for ri in range(2):
                                src = bk[0:NK, 0:512].rearrange("p (i r n) -> p i r n", r=2, n=64)[:, :, ri, :]
                                out = B0[:, ri, :, c0:c0 + 4].rearrange("p n i -> p i n")
                                cp("act" if ri == 0 else "dve", out, src, [bkey], ["B0"])
                        for n20 in range(0, 64, 4):
                            sl = i2c[0] % 4
                            i2c[0] += 1
                            ng = hf * 64 + n20
                            dma("sp", i2r[sl][:].rearrange("k q r z -> k q (r z)"), i2b_d[s][:, ng:ng + 4, :], writes=[("i2r", sl)])
                            bk, bkey = bank()
                            for q in range(4):
                                cs = slice(q * 128, (q + 1) * 128)
                                mm(bk[0:NZ, cs], i2r[sl][:, q, 0, :], B0[:, 0, n20 + q, :], True, False, [("i2r", sl), "B0"], [bkey])
                                mm(bk[0:NZ, cs], i2r[sl][:, q, 1, :], B0[:, 1, n20 + q, :], False, True, [("i2r", sl), "B0"], [bkey])
                            cp("act" if (n20 // 4) % 2 == 0 else "dve", X[0:NZ, :, ng:ng + 4].rearrange("p c q -> p q c"),
                               bk[0:NZ, 0:512].rearrange("p (q c) -> p q c", c=128), [bkey], ["X"])
                    dma("sp", yhT[s][cg * 128:(cg + 1) * 128, :].rearrange("c (a b) -> a c b", b=128), X[0:NZ, :, :], reads=["X"])
                P.barrier()
            with ExitStack() as ph:
                hbias = sb(ph, "hbias2", [128, 4], F32)
                dma("sp", hbias[:], hbiasT_d[l], writes=["hbias"])
                TB = min(2048, L)
                ty = [sb(ph, "ey%d" % i, [128, TB], BF16) for i in range(2)]
                tu = [sb(ph, "eu%d" % i, [128, TB], BF16) for i in range(2)]
                tx = [sb(ph, "ex%d" % i, [128, TB], BF16) for i in range(2)]
                t1 = [sb(ph, "et%d" % i, [128, TB], F32) for i in range(2)]
                to = [sb(ph, "eo%d" % i, [128, TB], BF16) for i in range(2)]
                cnt = 0
                for cc in range(4):
                    for t0 in range(0, L, TB):
                        sl = cnt % 2
                        cnt += 1
                        rs_ = slice(cc * 128, (cc + 1) * 128)
                        dma("sp", ty[sl][:], yhT[s][rs_, t0:t0 + TB], writes=[("ey", sl)])
                        dma("sp", tu[sl][:], uT[s][rs_, t0:t0 + TB], writes=[("eu", sl)])
                        dma("pool", tx[sl][:], x0T[s][rs_, t0:t0 + TB], writes=[("ex", sl)])
                        stt(t1[sl][:], tu[sl][:], hbias[:, cc:cc + 1], ty[sl][:], ALU.mult, ALU.add, [("eu", sl), ("ey", sl), "hbias"], [("et", sl)])
                        tt("pool", to[sl][:], t1[sl][:], tx[sl][:], ALU.mult, [("et", sl), ("ex", sl)], [("eo", sl)])
                        dma("sp", yT[s][rs_, t0:t0 + TB], to[sl][:], reads=[("eo", sl)])
                P.barrier()

        lamP = sb(st, "lamP", [128, 9, 2, 64], F32)
        st0 = sb(st, "st0", [128, 64], BF16)
        jswapf = sb(st, "jswapf", [128, 128], F32)
        sg = sb(st, "sg", [128, 4], F32)
        dma("sp", jswapf[:], jswap_d[:, :], writes=["jswapf"])
        dma("sp", sg[:], s5sg_d[:, :], writes=["sg"])
        SLOT_M = {0: [15 - i for i in range(16)] + [-i for i in range(16)] + [i for i in range(16)] + [i + 1 for i in range(16)],
                  1: [i for i in range(16)] + [-i for i in range(16)] + [16 - i for i in range(16)] + [0] * 16}

        def phase_s5tab(l):
            with ExitStack() as ph:
                lam = sb(ph, "lam", [128, 2, 64], F32)
                ls = sb(ph, "ls", [128, 64], F32)
                Bd = sb(ph, "Bd", [128, 2, 64, 16], F32)
                Cd = sb(ph, "Cd", [128, 2, 64, 16], F32)
                dT = sb(ph, "dT", [128, 32], F32)
                msk = sb(ph, "msk", [128, 2, 2, 256], F32)
                idp = sb(ph, "idp", [128, 2, 256], F32)
                dma("sp", lam[:], s5lam_d[l], writes=["lam"])
                dma("sp", ls[:], s5ls_d[l], writes=["ls"])
                dma("sp", Bd[:], s5b_d[l], writes=["Bd"])
                dma("sp", Cd[:], s5c_d[l], writes=["Cd"])
                dma("sp", dT[:], s5dT_d[l], writes=["dT"])
                dma("sp", msk[:], s5msk_d[:, :, :, :], writes=["msk"])
                dma("sp", idp[:], s5idp_d[:, :, :], writes=["idp"])
                xr = sb(ph, "xr", [128, 64], F32)
                xi = sb(ph, "xi", [128, 64], F32)
                act(ls[:], ls[:], AF.Exp, ["ls"], ["ls"])
                tt("dve", xr[:], lam[:, 0, :], ls[:], ALU.mult, ["lam", "ls"], ["xr"])
                tt("dve", xi[:], lam[:, 1, :], ls[:], ALU.mult, ["lam", "ls"], ["xi"])
                E1 = sb(ph, "E1", [128, 2, 64, 32], F32)
                E2 = sb(ph, "E2", [128, 2, 64, 32], F32)
                p2 = ExitStack()
                MAG = sb(p2, "MAG", [128, 2, 64, 32], F32)
                TK = sb(p2, "TK", [128, 2, 64, 32], F32)
                for d in range(2):
                    for sl_, m in enumerate(SLOT_M[d]):
                        us = slice(d * 32, d * 32 + 32)
                        act(MAG[:, d, sl_, :], xr[:, us], AF.Exp, ["xr"], ["MAG"], scale=float(m))
                        ts("dve", E1[:, d, sl_, :], xi[:, us], float(m), sg[:, 2:3], ALU.mult, ALU.add, ["xi", "sg"], ["E1"])
                        ts("dve", E2[:, d, sl_, :], xi[:, us], float(m), sg[:, 3:4], ALU.mult, ALU.add, ["xi", "sg"], ["E2"])
                fl = lambda t: t[:].rearrange("p d s u -> p (d s u)")
                sin_rr(fl(E1), fl(E1), fl(TK), 128, ["E1"], "E1", "TK")
                sin_rr(fl(E2), fl(E2), fl(TK), 128, ["E2"], "E2", "TK")
                tt("dve", fl(E1), fl(E1), fl(MAG), ALU.mult, ["E1", "MAG"], ["E1"])
                tt("pool", fl(E2), fl(E2), fl(MAG), ALU.mult, ["E2", "MAG"], ["E2"])
                P.barrier()
                p2.close()
                a1r = sb(ph, "a1r", [128, 64], F32)
                a1i = sb(ph, "a1i", [128, 64], F32)
                Lr = sb(ph, "Lr", [128, 64], F32)
                Li = sb(ph, "Li", [128, 64], F32)
                for d in range(2):
                    us = slice(d * 32, d * 32 + 32)
                    s1 = 33 if d == 0 else 1
                    s16 = 63 if d == 0 else 32
                    for (dst, slot) in ((a1r, s1), (Lr, s16)):
                        cp("dve", dst[0:64, us], E1[0:64, d, slot, :], ["E1"], [dst.name])
                        cp("dve", dst[64:128, us], E2[64:128, d, slot, :], ["E2"], [dst.name])
                    for (dst, slot) in ((a1i, s1), (Li, s16)):
                        cp("dve", dst[0:64, us], E2[0:64, d, slot, :], ["E2"], [dst.name])
                        cp("dve", dst[64:128, us], E1[64:128, d, slot, :], ["E1"], [dst.name])
                t1 = sb(ph, "sq1", [128, 64], F32)
                t2 = sb(ph, "sq2", [128, 64], F32)
                for k in range(9):
                    cp("dve", lamP[:, k, 0, :], Lr[:], [Lr.name], ["lamP"])
                    ts("dve", lamP[:, k, 1, :], Li[:], sg[:, 1:2], None, ALU.mult, None, [Li.name, "sg"], ["lamP"])
                    if k < 8:
                        tt("dve", t1[:], Lr[:], Lr[:], ALU.mult, [Lr.name], ["sq1"])
                        tt("dve", t2[:], Li[:], Li[:], ALU.mult, [Li.name], ["sq2"])
                        tt("dve", Li[:], Lr[:], Li[:], ALU.mult, [Lr.name, Li.name], [Li.name])
                        ts("dve", Li[:], Li[:], 2.0, None, ALU.mult, None, [Li.name], [Li.name])
                        tt("dve", Lr[:], t1[:], t2[:], ALU.subtract, ["sq1", "sq2"], [Lr.name])
                with ExitStack() as p3:
                    lst = [sb(p3, "lst%d" % i, [128, 4, 1152], BF16) for i in range(2)]
                    lt1 = [sb(p3, "lt1_%d" % i, [128, 128], F32) for i in range(4)]
                    c4 = 0
                    for u0 in range(0, 64, 4):
                        sl = (u0 // 4) % 2
                        for ui in range(4):
                            u = u0 + ui
                            for k in range(9):
                                ti = c4 % 4
                                c4 += 1
                                act(lt1[ti][:], identf[:], AF.Copy, ["identf", "lamP"], [("lt1", ti)], scale=lamP[:, k, 0, u:u + 1])
                                stt(lst[sl][:, ui, k * 128:(k + 1) * 128], jswapf[:], lamP[:, k, 1, u:u + 1], lt1[ti][:], ALU.mult, ALU.add,
                                    ["jswapf", "lamP", ("lt1", ti)], [("lst", sl)])
                        dma("sp", lamt_d[u0:u0 + 4].rearrange("u p n -> p u n"), lst[sl][:], reads=[("lst", sl)])
                qr = sb(ph, "qr", [128, 64], F32)
                qi = sb(ph, "qi", [128, 64], F32)
                den = sb(ph, "den", [128, 64], F32)
                ts("dve", a1r[:], a1r[:], -1.0, None, ALU.add, None, [a1r.name], [a1r.name])
                tt("dve", t1[:], lam[:, 0, :], lam[:, 0, :], ALU.mult, ["lam"], ["sq1"])
                tt("dve", t2[:], lam[:, 1, :], lam[:, 1, :], ALU.mult, ["lam"], ["sq2"])
                tt("dve", den[:], t1[:], t2[:], ALU.add, ["sq1", "sq2"], ["den"])
                P.op("dve", lambda e: e.reciprocal(out=den[:], in_=den[:]), ["den"], ["den"])
                tt("dve", t1[:], a1r[:], lam[:, 0, :], ALU.mult, [a1r.name, "lam"], ["sq1"])
                tt("dve", t2[:], a1i[:], lam[:, 1, :], ALU.mult, [a1i.name, "lam"], ["sq2"])
                tt("dve", qr[:], t1[:], t2[:], ALU.add, ["sq1", "sq2"], ["qr"])
                tt("dve", qr[:], qr[:], den[:], ALU.mult, ["qr", "den"], ["qr"])
                tt("dve", t1[:], a1i[:], lam[:, 0, :], ALU.mult, [a1i.name, "lam"], ["sq1"])
                tt("dve", t2[:], a1r[:], lam[:, 1, :], ALU.mult, [a1r.name, "lam"], ["sq2"])
                tt("dve", qi[:], t1[:], t2[:], ALU.subtract, ["sq1", "sq2"], ["qi"])
                tt("dve", qi[:], qi[:], den[:], ALU.mult, ["qi", "den"], ["qi"])
                Za = sb(ph, "Za", [128, 64, 16], F32)
                Zb = sb(ph, "Zb", [128, 64, 16], F32)
                tb1 = sb(ph, "tb1", [128, 64, 16], F32)
                qrb = qr[:].unsqueeze(2).to_broadcast([128, 64, 16])
                qib = qi[:].unsqueeze(2).to_broadcast([128, 64, 16])
                tt("dve", Za[:], Bd[:, 0], qrb, ALU.mult, ["Bd", "qr"], ["Za"])
                tt("dve", tb1[:], Bd[:, 1], qib, ALU.mult, ["Bd", "qi"], ["tb1"])
                tt("dve", Za[:], Za[:], tb1[:], ALU.subtract, ["Za", "tb1"], ["Za"])
                tt("dve", Zb[:], Bd[:, 0], qib, ALU.mult, ["Bd", "qi"], ["Zb"])
                tt("dve", tb1[:], Bd[:, 1], qrb, ALU.mult, ["Bd", "qr"], ["tb1"])
                tt("dve", Zb[:], Zb[:], tb1[:], ALU.add, ["Zb", "tb1"], ["Zb"])
                ts("dve", Zb[:], Zb[:], sg[:, 0:1], None, ALU.mult, None, ["Zb", "sg"], ["Zb"])
                ts("dve", Cd[:, 0], Cd[:, 0], sg[:, 1:2], None, ALU.mult, None, ["Cd"], ["Cd"])
                ts("dve", Cd[:, 1], Cd[:, 1], -1.0, None, ALU.mult, None, ["Cd"], ["Cd"])
                prod = [sb(ph, "prod%d" % i, [128, 8, 256], BF16) for i in range(7)]
                pt = [sb(ph, "pt%d" % i, [128, 8, 16, 16], F32) for i in range(4)]
                stage = sb(ph, "stage", [128, 8, 1536], BF16)
                wt = [sb(ph, "wkt%d" % i, [128, 256], F32) for i in range(4)]
                pcnt = [0]

                def product(dst, d, blk, g0, Z1, Z2, zkey):
                    i = pcnt[0] % 2
                    pcnt[0] += 1
                    eng = "dve" if i == 0 else "pool"
                    e1 = E1[:, d, blk * 16:(blk + 1) * 16, g0:g0 + 8].rearrange("p m g -> p g m").unsqueeze(3).to_broadcast([128, 8, 16, 16])
                    e2 = E2[:, d, blk * 16:(blk + 1) * 16, g0:g0 + 8].rearrange("p m g -> p g m").unsqueeze(3).to_broadcast([128, 8, 16, 16])
                    u0 = d * 32 + g0
                    z1 = Z1[:, u0:u0 + 8, :].unsqueeze(2).to_broadcast([128, 8, 16, 16])
                    z2 = Z2[:, u0:u0 + 8, :].unsqueeze(2).to_broadcast([128, 8, 16, 16])
                    ta, tb = pt[2 * i], pt[2 * i + 1]
                    zk = ["Za", "Zb"] if zkey == "Za" else [zkey]
                    tt(eng, ta[:], e1, z1, ALU.mult, ["E1"] + zk, [("pt", 2 * i)])
                    tt(eng, tb[:], e2, z2, ALU.mult, ["E2"] + zk, [("pt", 2 * i + 1)])
                    tt(eng, dst[:].rearrange("p g (m h) -> p g m h", h=16), ta[:], tb[:], ALU.add, [("pt", 2 * i), ("pt", 2 * i + 1)], [dst.name])

                for gb in range(4):
                    g0 = gb * 8
                    XQ0, KL0, KR0, WS0, XQ1, KR1, WS1 = prod
                    product(XQ0, 0, 0, g0, Za, Zb, "Za")
                    product(KL0, 0, 1, g0, Za, Zb, "Za")
                    product(KR0, 0, 2, g0, Cd[:, 0], Cd[:, 1], "Cd")
                    product(WS0, 0, 3, g0, Cd[:, 0], Cd[:, 1], "Cd")
                    product(XQ1, 1, 0, g0, Za, Zb, "Za")
                    product(KR1, 1, 1, g0, Cd[:, 0], Cd[:, 1], "Cd")
                    product(WS1, 1, 2, g0, Cd[:, 0], Cd[:, 1], "Cd")
                    for gl in range(8):
                        g = g0 + gl
                        for d, XQ in ((0, XQ0), (1, XQ1)):
                            bk, bkey = bank()
                            for jh in range(2):
                                mm(bk[:, jh * 128:(jh + 1) * 128], XQ[:, gl, jh * 128:(jh + 1) * 128], identb[:], True, True, [XQ.name, "identb"], [bkey])
                            cp("act", stage[:, gl, d * 256:(d + 1) * 256], bk[:, 0:256], [bkey], ["stage"])
                        cp("pool", stage[:, gl, 512:768], WS0[:, gl, :], [WS0.name], ["stage"])
                        cp("pool", stage[:, gl, 768:1024], WS1[:, gl, :], [WS1.name], ["stage"])
                        for jh in range(2):
                            bf, bfk = bank()
                            bb_, bbk = bank()
                            mm(bf[:, 0:256], KL0[:, gl, jh * 128:(jh + 1) * 128], KR0[:, gl, :], True, True, [KL0.name, KR0.name], [bfk])
                            mm(bb_[:, 0:256], XQ1[:, gl, jh * 128:(jh + 1) * 128], KR1[:, gl, :], True, True, [XQ1.name, KR1.name], [bbk])
                            tt("dve", wt[0][:], bf[:, 0:256], msk[:, 0, jh, :], ALU.mult, [bfk, "msk"], ["wkt0"])
                            tt("dve", wt[1][:], bb_[:, 0:256], msk[:, 1, jh, :], ALU.mult, [bbk, "msk"], ["wkt1"])
                            stt(wt[2][:], idp[:, jh, :], dT[:, g:g + 1], wt[0][:], ALU.mult, ALU.add, ["idp", "dT", "wkt0"], ["wkt2"])
                            tt("pool", stage[:, gl, 1024 + jh * 256:1024 + (jh + 1) * 256], wt[2][:], wt[1][:], ALU.add, ["wkt2", "wkt1"], ["stage"])
                    dma("sp", s5tab_d[g0:g0 + 8].rearrange("g p n -> p g n"), stage[:], reads=["stage"])
                P.barrier()

        def phase_s5(l, s):
            L = seqs[s].L
            NCH = L // 16
            nsteps = int(round(math.log2(NCH)))
            with ExitStack() as ph:
                selb = sb(ph, "selb", [128, 64, 128], BF16)
                selTb = sb(ph, "selTb", [128, 64, 128], BF16)
                dma("pool", selb[:], sel_d[:, :, :], writes=["selb"])
                dma("pool", selTb[:], selT_d[:, :, :], writes=["selTb"])
                uTb = [sb(ph, "uTb%d" % i, [128, L], BF16) for i in range(2)]
                tabr = [sb(ph, "tabr%d" % i, [128, 1536], BF16) for i in range(2)]
                Ug = [sb(ph, "Ug%d" % i, [128, 2, NCH], BF16) for i in range(2)]
                Sb = [sb(ph, "Sb%d" % i, [128, NCH], BF16) for i in range(4)]
                Sext = [[sb(ph, "Sext%d_%d" % (d, i), [128, NCH + 2], BF16) for i in range(2)] for d in range(2)]
                LamT = [sb(ph, "LamT%d" % i, [128, 9, 128], BF16) for i in range(4)]
                ltmp = [sb(ph, "ltmp%d" % i, [128, 128], F32) for i in range(4)]
                Yg = sb(ph, "Yg", [128, 8, 2, NCH], BF16)
                ysT = sb(ph, "ysT", [128, 4, L], BF16)
                zcol = sb(ph, "zcol", [128, 1], BF16)
                memset("dve", zcol[:], 0.0, ["zcol"])
                gcnt = 0
                ucnt = 0
                for cbk in range(4):
                    ub = uTb[cbk % 2]
                    ubk = ("uTb", cbk % 2)
                    dma("sp", ub[:], pT[s][1536 + cbk * 128:1536 + (cbk + 1) * 128, PAD:PAD + L], writes=[ubk])
                    for gl in range(8):
                        g = cbk * 8 + gl
                        gs_ = gcnt % 2
                        gcnt += 1
                        tb = tabr[gs_]
                        tbk = ("tabr", gs_)
                        dma("sp", tb[:], s5tab_d[g], writes=[tbk])
                        ug = Ug[gs_]
                        ugk = ("Ug", gs_)
                        for jh in range(2):
                            bk, bkey = bank()
                            for jj in range(8):
                                j = jh * 8 + jj
                                mm(bk[:, 0:NCH], selb[:, gl * 8 + jj, :], ub[:, j:L:16], jj == 0, jj == 7, ["selb", ubk], [bkey])
                            cp("act" if jh == 0 else "dve", ug[:, jh, :], bk[:, 0:NCH], [bkey], [ugk])
                        sx = Sext[0][gs_], Sext[1][gs_]
                        sxk = ("Sext", 0, gs_), ("Sext", 1, gs_)
                        for d in range(2):
                            u = d * 32 + g
                            li = ucnt % 4
                            ucnt += 1
                            lt = LamT[li]
                            ltk = ("LamT", li)
                            dma("sp", lt[:].rearrange("p k q -> p (k q)"), lamt_d[u], writes=[ltk])
                            bk, bkey = bank()
                            init = s == "x"
                            mm(bk[:, 0:NCH], tb[:, d * 256:d * 256 + 128], ug[:, 0, :], True, False, [tbk, ugk], [bkey])
                            mm(bk[:, 0:NCH], tb[:, d * 256 + 128:d * 256 + 256], ug[:, 1, :], False, not init, [tbk, ugk], [bkey])
                            if init:
                                col = 0 if d == 0 else NCH - 1
                                mm(bk[:, col:col + 1], lt[:, 0, :], st0[:, u:u + 1], False, True, [ltk, "st0"], [bkey])
                            pp = (d * 2) % 4
                            cur, curk = Sb[pp], ("Sb", pp)
                            cp("act", cur[:], bk[:, 0:NCH], [bkey], [curk])
                            for k in range(nsteps):
                                sh = 1 << k
                                bk, bkey = bank()
                                mm(bk[:, 0:NCH], identb[:], cur[:, 0:NCH], True, False, ["identb", curk], [bkey])
                                if d == 0:
                                    mm(bk[:, sh:NCH], lt[:, k, :], cur[:, 0:NCH - sh], False, True, [ltk, curk], [bkey])
                                else:
                                    mm(bk[:, 0:NCH - sh], lt[:, k, :], cur[:, sh:NCH], False, True, [ltk, curk], [bkey])
                                if k == nsteps - 1:
                                    off = 1 if d == 0 else 0
                                    cp("act" if k % 2 == 0 else "dve", sx[d][:, off:off + NCH], bk[:, 0:NCH], [bkey], [sxk[d]])
                                else:
                                    pp2 = d * 2 + (1 - (pp % 2))
                                    nxt, nxtk = Sb[pp2], ("Sb", pp2)
                                    cp("act" if k % 2 == 0 else "dve", nxt[:], bk[:, 0:NCH], [bkey], [nxtk])
                                    cur, curk, pp = nxt, nxtk, pp2
                            ecol = 0 if d == 0 else NCH
                            if init:
                                cp("dve", sx[d][:, ecol:ecol + 1], st0[:, u:u + 1], ["st0"], [sxk[d]])
                            else:
                                cp("dve", sx[d][:, ecol:ecol + 1], zcol[:], ["zcol"], [sxk[d]])
                                fcol = NCH if d == 0 else 0
                                cp("dve", st0[:, u:u + 1], sx[d][:, fcol:fcol + 1], [sxk[d]], ["st0"])
                        for th in range(2):
                            bk, bkey = bank()
                            cs = slice(th * 128, (th + 1) * 128)
                            mm(bk[:, 0:NCH], tb[:, 1024:1280][:, cs], ug[:, 0, :], True, False, [tbk, ugk], [bkey])
                            mm(bk[:, 0:NCH], tb[:, 1280:1536][:, cs], ug[:, 1, :], False, False, [tbk, ugk], [bkey])
                            mm(bk[:, 0:NCH], tb[:, 512:768][:, cs], sx[0][:, 0:NCH], False, False, [tbk, sxk[0]], [bkey])
                            mm(bk[:, 0:NCH], tb[:, 768:1024][:, cs], sx[1][:, 1:NCH + 1], False, True, [tbk, sxk[1]], [bkey])
                            act(Yg[:, gl, th, :], bk[:, 0:NCH], AF.Gelu_apprx_tanh, [bkey], [("Yg", gl)])
                    for j in range(16):
                        jh, jj = divmod(j, 8)
                        bk, bkey = bank()
                        for gl in range(8):
                            mm(bk[:, 0:NCH], selTb[:, gl * 8 + jj, :], Yg[:, gl, jh, :], gl == 0, gl == 7, ["selTb", ("Yg", gl)], [bkey])
                        cp("act" if j % 2 == 0 else "dve", ysT[:, cbk, j:L:16], bk[:, 0:NCH], [bkey], [("ysT", cbk)])
                wgl = sb(ph, "wgl", [128, 4, 512], BF16)
                bgl = sb(ph, "bgl", [128, 4], F32)
                dma("pool", wgl[:], w_glu_d[l].rearrange("(k p) n -> p k n", p=128), writes=["wgl"])
                dma("sp", bgl[:], s5bglu_d[l], writes=["bgl"])
                TB = min(512, L)
                sig = [sb(ph, "sig%d" % i, [128, TB], F32) for i in range(2)]
                go = [sb(ph, "go%d" % i, [128, TB], BF16) for i in range(2)]
                cnt = 0
                for t0 in range(0, L, TB):
                    for mo in range(4):
                        sl = cnt % 2
                        cnt += 1
                        bk, bkey = bank()
                        for k in range(4):
                            mm(bk[:, 0:TB], wgl[:, k, mo * 128:(mo + 1) * 128], ysT[:, k, t0:t0 + TB], k == 0, k == 3, ["wgl", ("ysT", k)], [bkey])
                        act(sig[sl][:], bk[:, 0:TB], AF.Sigmoid, [bkey, "bgl"], [("sig", sl)], bias=bgl[:, mo:mo + 1])
                        tt("dve" if mo % 2 == 0 else "pool", go[sl][:], sig[sl][:], ysT[:, mo, t0:t0 + TB], ALU.mult, [("sig", sl), ("ysT", mo)], [("go", sl)])
                        dma("sp", yT[s][512 + mo * 128:512 + (mo + 1) * 128, t0:t0 + TB], go[sl][:], reads=[("go", sl)])
                P.barrier()

        def phase_feed_y():
            with ExitStack() as ph:
                k0, k1_ = feed_y
                nk = k1_ - k0
                t = sb(ph, "fyt", [128, nk, 2048], F32)
                tb = sb(ph, "fytb", [128, nk, 2048], BF16)
                for s, srcd in (("c", fy_c), ("x", fy_x)):
                    L = seqs[s].L
                    TB = min(2048, L)
                    for t0 in range(0, L, TB):
                        dma("sp", t[:, :, 0:TB], srcd.rearrange("(k p) t -> p k t", p=128)[:, k0:k1_, t0:t0 + TB], writes=["fyt"])
                        cp("dve", tb[:, :, 0:TB], t[:, :, 0:TB], ["fyt"], ["fytb"])
                        dma("sp", yT[s].rearrange("(k p) t -> p k t", p=128)[:, k0:k1_, t0:t0 + TB], tb[:, :, 0:TB], reads=["fytb"])
                P.barrier()

        for l in range(depth_run):
            phase_mod(l)
            phase_in(l)
            if not feed_y or feed_y[1] < 8:
                phase_s5tab(l)
                phase_s5(l, "c")
                phase_s5(l, "x")
            if not feed_y or feed_y[0] > 0:
                phase_hyprep(l)
                phase_hyena(l, "c")
                phase_hyena(l, "x")
            if feed_y:
                phase_feed_y()
            phase_out(l)
            phase_ffn(l)

        P.barrier()
        print("program ops:", P.nops, {e: len(v) for e, v in P.streams.items()})
        with nc.Block() as block:
            P.emit(block)
    return nc, dram_in


_CACHE = {}


def kernel(**inputs):
    x = np.asarray(inputs["x"], dtype=np.float32)
    ctx = np.asarray(inputs["ctx"], dtype=np.float32)
    c = np.asarray(inputs["c"], dtype=np.float32)
    c_ctx = np.asarray(inputs["c_ctx"], dtype=np.float32)
    B = x.shape[0]
    if "nc" not in _CACHE:
        _CACHE["nc"] = build_program()
    nc, _ = _CACHE["nc"]
    shared = layout_weights(inputs)
    shared.update(host_constants())
    in_maps = []
    for core in range(8):
        b = core % B
        m = dict(shared)
        m["x"] = _f32(x[b])
        m["ctx"] = _f32(ctx[b])
        cT = np.stack([c[b].reshape(8, 128).T, c_ctx.reshape(8, 128).T], axis=-1)
        m["cT"] = _f32(cT)
        in_maps.append(m)
    res = run_bass_kernel_spmd(nc, in_maps, core_ids=list(range(8)))
    out = np.stack([np.asarray(res.results[b]["y"], dtype=np.float32) for b in range(B)], axis=0)
    return out
```

```python
import math
import numpy as np
from contextlib import ExitStack
import concourse.bass as bass
import concourse.mybir as mybir
from concourse.bass_utils import run_bass_kernel_spmd

F32 = mybir.dt.float32
BF16 = mybir.dt.bfloat16
AF = mybir.ActivationFunctionType
ALU = mybir.AluOpType

D = 1024
SEQ = 8192
CTX = 256
DEPTH = 4
DFF = 2816
NFF = DFF // 128
PAD = 8
EPS = 1e-6
MAGIC = 12582912.0
TWO_PI = 2.0 * math.pi


class _Op:
    __slots__ = ("eng", "fn", "waits", "sem", "val", "dma")


class Prog:
    COMPUTE = ("pe", "act", "dve", "pool")
    NDMA = 8

    def __init__(self, nc, stack):
        self.nc = nc
        self.h = {"pe": nc.tensor, "act": nc.scalar, "dve": nc.vector, "pool": nc.gpsimd, "sp": nc.sync}
        self.streams = {e: [] for e in self.h}
        self.esem = {e: stack.enter_context(nc.semaphore("s_" + e)) for e in self.COMPUTE}
        self.ecnt = {e: 0 for e in self.COMPUTE}
        self.dsem, self.dcnt, self.drr = {}, {}, {}
        for q in ("sp", "pool", "act"):
            self.dsem[q] = [stack.enter_context(nc.semaphore("d_%s%d" % (q, i))) for i in range(self.NDMA)]
            self.dcnt[q] = [0] * self.NDMA
            self.drr[q] = 0
        self.lastw = {}
        self.readers = {}
        self.waited = {e: {} for e in self.h}
        self.nops = 0

    def _dep(self, eng, op, waits):
        if op is None:
            return
        if (not op.dma) and op.eng == eng and eng == "pe":
            return
        key = id(op.sem)
        if self.waited[eng].get(key, 0) >= op.val:
            return
        cur = waits.get(key)
        if cur is None or cur[1] < op.val:
            waits[key] = (op.sem, op.val)

    def op(self, eng, fn, reads=(), writes=(), dma=False):
        o = _Op()
        o.eng, o.fn, o.dma = eng, fn, dma
        waits = {}
        for r in reads:
            self._dep(eng, self.lastw.get(r), waits)
        for wk in writes:
            self._dep(eng, self.lastw.get(wk), waits)
            for rd in self.readers.get(wk, ()):
                self._dep(eng, rd, waits)
        if dma:
            i = self.drr[eng]
            self.drr[eng] = (i + 1) % self.NDMA
            sem = self.dsem[eng][i]
            prev = self.dcnt[eng][i]
            if prev > 0 and self.waited[eng].get(id(sem), 0) < prev:
                cur = waits.get(id(sem))
                if cur is None or cur[1] < prev:
                    waits[id(sem)] = (sem, prev)
            self.dcnt[eng][i] = prev + 16
            o.sem, o.val = sem, prev + 16
        else:
            self.ecnt[eng] += 1
            o.sem, o.val = self.esem[eng], self.ecnt[eng]
        for key, (s, v) in waits.items():
            self.waited[eng][key] = v
        o.waits = list(waits.values())
        self.streams[eng].append(o)
        for r in reads:
            self.readers.setdefault(r, []).append(o)
        for wk in writes:
            self.lastw[wk] = o
            self.readers[wk] = []
        self.nops += 1
        return o

    def barrier(self, engines=None):
        tot = {}
        for e in self.COMPUTE:
            if self.ecnt[e] > 0:
                tot[id(self.esem[e])] = (self.esem[e], self.ecnt[e])
        for q in self.dsem:
            for i in range(self.NDMA):
                if self.dcnt[q][i] > 0:
                    tot[id(self.dsem[q][i])] = (self.dsem[q][i], self.dcnt[q][i])
        for eng in (engines or list(self.h)):
            waits = []
            for key, (s, v) in tot.items():
                if self.waited[eng].get(key, 0) < v:
                    if eng in self.COMPUTE and s is self.esem[eng]:
                        continue
                    waits.append((s, v))
                    self.waited[eng][key] = v
            if waits:
                o = _Op()
                o.eng, o.fn, o.dma, o.sem, o.val = eng, None, False, None, 0
                o.waits = waits
                self.streams[eng].append(o)
        self.lastw = {}
        self.readers = {}

    def emit(self, block):
        def mk(ename):
            def body(e):
                for o in self.streams[ename]:
                    for (s, v) in o.waits:
                        e.wait_ge(s, v)
                    if o.fn is None:
                        continue
                    o.fn(e).then_inc(o.sem, 16 if o.dma else 1)
            return body
        block.sync(mk("sp"))
        block.scalar(mk("act"))
        block.vector(mk("dve"))
        block.gpsimd(mk("pool"))
        block.tensor(mk("pe"))


def _f32(a):
    return np.ascontiguousarray(np.asarray(a, dtype=np.float32))


MON = {"x": (128, 64, 65, SEQ), "c": (4, 2, 3, CTX)}


def host_constants():
    c = {}
    c["ident"] = _f32(np.eye(128))
    k2 = np.arange(128)[:, None]
    n2 = np.arange(128)[None, :]
    G = np.exp(2j * np.pi * k2 * n2 / 128.0)
    gt = np.zeros((128, 2, 2, 128))
    for hf in range(2):
        sl = slice(hf * 64, hf * 64 + 64)
        gt[:, hf, 0, 0:64] = G.real[:, sl]
        gt[:, hf, 0, 64:128] = G.imag[:, sl]
        gt[:, hf, 1, 0:64] = -G.imag[:, sl]
        gt[:, hf, 1, 64:128] = G.real[:, sl]
    c["gtab"] = _f32(gt)
    deltas = np.abs(np.linspace(math.log(1e-2) / 0.3, math.log(1e-2) / 1.5, 512))
    for s, (N1, NZ, NK, L) in MON.items():
        N = 128 * N1
        n1 = np.arange(NZ)[:, None]
        k1 = np.arange(NK)[None, :]
        ang = 2 * np.pi * n1 * k1 / N1
        c["f1tab_" + s] = _f32(np.concatenate([np.cos(ang), -np.sin(ang)], axis=1))
        f2 = np.zeros((NK, 128, 4, 128))
        nn = np.arange(128)[:, None]
        kk = np.arange(128)[None, :]
        for a in range(NK):
            M = np.exp(-2j * np.pi * nn * (a + N1 * kk) / N)
            f2[a, :, 0] = M.real
            f2[a, :, 1] = M.imag
            f2[a, :, 2] = -M.imag
            f2[a, :, 3] = -M.real
        c["f2tab_" + s] = _f32(f2.reshape(NK, 128, 512))
        ck = np.full(NK, 2.0)
        ck[0] = 1.0
        ck[NK - 1] = 1.0
        i2 = np.zeros((NK, 128, 2, NZ))
        kq = np.arange(NK)[:, None, None]
        nq = np.arange(128)[None, :, None]
        mq = np.arange(NZ)[None, None, :]
        R = (ck[:, None, None] / N) * np.exp(2j * np.pi * kq * (128 * mq + nq) / N)
        i2[:, :, 0, :] = R.real
        i2[:, :, 1, :] = -R.imag
        c["i2tab_" + s] = _f32(i2)
        t = np.arange(L) / (L - 1.0)
        wv = (2.0 * np.pi / L) * np.arange(L)
        f = np.linspace(1e-4, 15.0, 16)
        z = np.concatenate([t[:, None], np.cos(f[None, :] * wv[:, None]), -np.sin(f[None, :] * wv[:, None])], axis=1)
        c["zT_" + s] = _f32(z.T)
        c["d1_" + s] = _f32(np.exp(-(128.0 * np.arange(NZ)[:, None] / (L - 1.0)) * deltas[None, :]))
        c["d2_" + s] = _f32(np.exp(-(np.arange(128)[:, None] / (L - 1.0)) * deltas[None, :]))
    sel = np.zeros((128, 64, 128))
    for gl in range(8):
        for jj in range(8):
            for hi in range(16):
                sel[gl * 16 + hi, gl * 8 + jj, jj * 16 + hi] = 1.0
    c["sel"] = _f32(sel)
    c["selT"] = _f32(sel.transpose(2, 1, 0))
    J = np.zeros((128, 128))
    for p in range(64):
        J[p, 64 + p] = 1.0
        J[64 + p, p] = 1.0
    c["jswap"] = _f32(J)
    msk = np.zeros((128, 2, 2, 256))
    idp = np.zeros((128, 2, 256))
    for jh in range(2):
        for jj in range(8):
            j = jh * 8 + jj
            for hi in range(16):
                for t in range(16):
                    if t >= j:
                        msk[jj * 16 + hi, 0, jh, t * 16:(t + 1) * 16] = 1.0
                    if t <= j:
                        msk[jj * 16 + hi, 1, jh, t * 16:(t + 1) * 16] = 1.0
                idp[jj * 16 + hi, jh, j * 16 + hi] = 1.0
    c["s5msk"] = _f32(msk)
    c["s5idp"] = _f32(idp)
    sg = np.zeros((128, 4))
    sg[:64, 0], sg[64:, 0] = -1.0, 1.0
    sg[:64, 1], sg[64:, 1] = 1.0, -1.0
    sg[:64, 2], sg[64:, 2] = math.pi / 2, 0.0
    sg[:64, 3], sg[64:, 3] = 0.0, math.pi / 2
    c["s5sg"] = _f32(sg)
    return c


def layout_weights(inp):
    w = {}
    g = lambda k: np.asarray(inp[k], dtype=np.float32)
    w["w_ada"] = _f32(g("w_ada"))
    w["b_adaT"] = _f32(g("b_ada").reshape(DEPTH, 48, 128).transpose(0, 2, 1))
    w["ngT"] = _f32(g("norm_g").reshape(DEPTH, 4, 8, 128).transpose(0, 3, 1, 2))
    w["w_in"] = _f32(g("w_in"))
    w["w_out"] = _f32(g("w_out"))
    w["w_up"] = _f32(g("ffn_w_up"))
    w["w_down"] = _f32(g("ffn_w_down"))
    w["fcwT"] = _f32(g("ffn_conv_w").reshape(DEPTH, 9, NFF, 128).transpose(0, 3, 1, 2))
    w["fcbT"] = _f32(g("ffn_conv_b").reshape(DEPTH, NFF, 128).transpose(0, 2, 1))
    w["hswT"] = _f32(g("hy_short_w").reshape(DEPTH, 3, 12, 128).transpose(0, 3, 1, 2))
    w["hsbT"] = _f32(g("hy_short_b").reshape(DEPTH, 12, 128).transpose(0, 2, 1))
    w["hbiasT"] = _f32(g("hy_bias").reshape(DEPTH, 4, 128).transpose(0, 2, 1))
    w["f_win"] = _f32(g("filt_w_in"))
    w["f_whid"] = _f32(g("filt_w_hid").transpose(0, 2, 1, 3))
    w["f_b"] = _f32(np.concatenate([g("filt_b_in")[:, :, None], g("filt_b_hid").transpose(0, 2, 1)], axis=2))
    w["f_freq"] = _f32(g("filt_freq")[:, :, None])
    w["f_wout"] = _f32(g("filt_w_out"))
    dup = lambda a: np.concatenate([a, a], axis=1)
    lam = np.stack([g("s5_lam_re").reshape(DEPTH, 64, 64).transpose(0, 2, 1),
                    g("s5_lam_im").reshape(DEPTH, 64, 64).transpose(0, 2, 1)], axis=2)
    w["s5lam"] = _f32(dup(lam))
    w["s5ls"] = _f32(np.broadcast_to(g("s5_log_step").reshape(DEPTH, 1, 64), (DEPTH, 128, 64)))
    bb = np.stack([g("s5_b_re").reshape(DEPTH, 64, 64, 16).transpose(0, 2, 1, 3),
                   g("s5_b_im").reshape(DEPTH, 64, 64, 16).transpose(0, 2, 1, 3)], axis=2)
    w["s5b"] = _f32(dup(bb))
    cc = np.stack([g("s5_c_re").reshape(DEPTH, 64, 16, 64).transpose(0, 3, 1, 2),
                   g("s5_c_im").reshape(DEPTH, 64, 16, 64).transpose(0, 3, 1, 2)], axis=2)
    w["s5c"] = _f32(dup(cc))
    dd = g("s5_d").reshape(DEPTH, 32, 16).transpose(0, 2, 1)
    w["s5dT"] = _f32(np.tile(dd, (1, 8, 1)))
    w["s5bglu"] = _f32(g("s5_b_glu").reshape(DEPTH, 4, 128).transpose(0, 2, 1))
    w["w_glu"] = _f32(g("s5_w_glu"))
    return w


class Seq:
    pass


def build_program(depth_run=DEPTH, feed_y=False, dbg=False):
    nc = bass.Bass("TRN2", target_bir_lowering=False)
    dram_in = {}

    def din(name, shape, dt=F32):
        dram_in[name] = nc.dram_tensor(name, list(shape), dt, kind="ExternalInput").ap()
        return dram_in[name]

    def dscr(name, shape, dt):
        if dbg and name in dbg:
            return nc.dram_tensor(name, list(shape), dt, kind="ExternalOutput").ap()
        return nc.dram_tensor(name, list(shape), dt, kind="Internal").ap()

    x_d = din("x", [SEQ, D])
    ctx_d = din("ctx", [CTX, D])
    cT_d = din("cT", [128, 8, 2])
    w_ada_d = din("w_ada", [DEPTH, D, 6 * D])
    b_adaT_d = din("b_adaT", [DEPTH, 128, 48])
    ngT_d = din("ngT", [DEPTH, 128, 4, 8])
    w_in_d = din("w_in", [DEPTH, D, 2 * D])
    w_out_d = din("w_out", [DEPTH, D, D])
    w_up_d = din("w_up", [DEPTH, D, 2 * DFF])
    w_down_d = din("w_down", [DEPTH, DFF, D])
    fcwT_d = din("fcwT", [DEPTH, 128, 9, NFF])
    fcbT_d = din("fcbT", [DEPTH, 128, NFF])
    ident_d = din("ident", [128, 128])
    hswT_d = din("hswT", [DEPTH, 128, 3, 12])
    hsbT_d = din("hsbT", [DEPTH, 128, 12])
    hbiasT_d = din("hbiasT", [DEPTH, 128, 4])
    f_win_d = din("f_win", [DEPTH, 33, 64])
    f_whid_d = din("f_whid", [DEPTH, 64, 2, 64])
    f_b_d = din("f_b", [DEPTH, 64, 3])
    f_freq_d = din("f_freq", [DEPTH, 64, 1])
    f_wout_d = din("f_wout", [DEPTH, 64, 1024])
    gtab_d = din("gtab", [128, 2, 2, 128])
    s5lam_d = din("s5lam", [DEPTH, 128, 2, 64])
    s5ls_d = din("s5ls", [DEPTH, 128, 64])
    s5b_d = din("s5b", [DEPTH, 128, 2, 64, 16])
    s5c_d = din("s5c", [DEPTH, 128, 2, 64, 16])
    s5dT_d = din("s5dT", [DEPTH, 128, 32])
    s5bglu_d = din("s5bglu", [DEPTH, 128, 4])
    w_glu_d = din("w_glu", [DEPTH, 512, 512])
    sel_d = din("sel", [128, 64, 128])
    selT_d = din("selT", [128, 64, 128])
    jswap_d = din("jswap", [128, 128])
    s5msk_d = din("s5msk", [128, 2, 2, 256])
    s5idp_d = din("s5idp", [128, 2, 256])
    s5sg_d = din("s5sg", [128, 4])
    mon_d = {}
    for s_, (N1_, NZ_, NK_, L_) in MON.items():
        mon_d[s_] = dict(f1=din("f1tab_" + s_, [NZ_, 2 * NK_]), f2=din("f2tab_" + s_, [NK_, 128, 512]),
                         i2=din("i2tab_" + s_, [NK_, 128, 2, NZ_]), z=din("zT_" + s_, [33, L_]),
                         d1=din("d1_" + s_, [NZ_, 512]), d2=din("d2_" + s_, [128, 512]))
    if feed_y:
        fy_x = din("fy_x", [D, SEQ])
        fy_c = din("fy_c", [D, CTX])

    y_d = nc.dram_tensor("y", [SEQ, D], F32, kind="ExternalOutput").ap()
    dbg_out = {}

    cres_d = dscr("cres", [CTX, D], F32)
    xmix_d = dscr("xmix", [SEQ, D], F32)
    cmix_d = dscr("cmix", [CTX, D], F32)
    gv_d = dscr("gvec", [2, 2, D], F32)
    wupb_d = dscr("wupb", [D, 2 * DFF], BF16)
    pT = {"x": dscr("pT_x", [2 * D, SEQ + 2 * PAD], BF16), "c": dscr("pT_c", [2 * D, CTX + 2 * PAD], BF16)}
    yT = {"x": dscr("yT_x", [D, SEQ], BF16), "c": dscr("yT_c", [D, CTX], BF16)}
    s5tab_d = dscr("s5tab", [32, 128, 1536], BF16)
    lamt_d = dscr("lamt", [64, 128, 1152], BF16)
    dgff_d = dscr("dgff", [NFF, 128, 1152], BF16)
    f2b_d = {s_: dscr("f2b_" + s_, [MON[s_][2], 128, 512], BF16) for s_ in MON}
    i2b_d = {s_: dscr("i2b_" + s_, [MON[s_][2], 128, 2 * MON[s_][1]], BF16) for s_ in MON}
    uT = {"x": dscr("uT_x", [512, SEQ], BF16), "c": dscr("uT_c", [512, CTX], BF16)}
    x0T = {"x": dscr("x0T_x", [512, SEQ], BF16), "c": dscr("x0T_c", [512, CTX], BF16)}
    yhT = {"x": dscr("yhT_x", [512, SEQ], BF16), "c": dscr("yhT_c", [512, CTX], BF16)}

    with ExitStack() as st:
        P = Prog(nc, st)

        uniq = [0]

        def sb(stack, name, shape, dt):
            uniq[0] += 1
            return stack.enter_context(nc.sbuf_tensor("%s_%d" % (name, uniq[0]), list(shape), dt))

        ps = [st.enter_context(nc.psum_tensor("ps%d" % i, [128, 512], F32)) for i in range(8)]
        psc = [0]

        def bank():
            i = psc[0]
            psc[0] = (i + 1) % 8
            return ps[i], ("ps", i)

        def dma(q, out, in_, reads=(), writes=(), slow=False):
            if slow:
                P.op(q, lambda e: e.dma_start(out=out, in_=in_, allow_slow_non_contiguous=True), reads, writes, dma=True)
            else:
                P.op(q, lambda e: e.dma_start(out=out, in_=in_), reads, writes, dma=True)

        def mm(out, lhsT, rhs, start, stop, reads, writes):
            P.op("pe", lambda e: e.matmul(out, lhsT=lhsT, rhs=rhs, start=start, stop=stop), reads, writes)

        def act(out, in_, func, reads, writes, scale=None, bias=None, accum=None):
            kw = {}
            if scale is not None:
                kw["scale"] = scale
            if bias is not None:
                kw["bias"] = bias
            if accum is not None:
                kw["accum_out"] = accum
            P.op("act", lambda e: e.activation(out=out, in_=in_, func=func, **kw), reads, writes)

        def tt(eng, out, in0, in1, op, reads, writes):
            P.op(eng, lambda e: e.tensor_tensor(out=out, in0=in0, in1=in1, op=op), reads, writes)

        def ts(eng, out, in0, s1, s2, op0, op1, reads, writes):
            if op1 is None:
                P.op(eng, lambda e: e.tensor_scalar(out=out, in0=in0, scalar1=s1, scalar2=None, op0=op0), reads, writes)
            else:
                P.op(eng, lambda e: e.tensor_scalar(out=out, in0=in0, scalar1=s1, scalar2=s2, op0=op0, op1=op1), reads, writes)

        def stt(out, in0, scalar, in1, op0, op1, reads, writes):
            P.op("dve", lambda e: e.scalar_tensor_tensor(out=out, in0=in0, scalar=scalar, in1=in1, op0=op0, op1=op1), reads, writes)

        def cp(eng, out, in_, reads, writes):
            if eng == "act":
                act(out, in_, AF.Copy, reads, writes)
            else:
                P.op(eng, lambda e: e.tensor_copy(out=out, in_=in_), reads, writes)

        def memset(eng, ap, val, writes):
            P.op(eng, lambda e: e.memset(ap, val), (), writes)

        identf = sb(st, "identf", [128, 128], F32)
        identb = sb(st, "identb", [128, 128], BF16)
        cond = sb(st, "cond", [128, 8, 2], F32)
        modT = sb(st, "modT", [128, 48, 2], F32)
        ngt = sb(st, "ngt", [128, 4, 8], F32)
        vec = {}
        for s in ("x", "c"):
            for nm in ("gs1", "sh1", "gs3", "sh3", "ga2", "ga4"):
                vec[(s, nm)] = sb(st, "v_%s_%s" % (s, nm), [128, 8], F32)
        G2 = {s: sb(st, "G2" + s, [128, D], F32) for s in ("x", "c")}
        G4 = {s: sb(st, "G4" + s, [128, D], F32) for s in ("x", "c")}
        SI = {"x": 0, "c": 1}
        epsb = sb(st, "epsb", [128, 1], F32)

        dma("sp", identf[:], ident_d[:, :], writes=["identf"])
        cp("dve", identb[:], identf[:], ["identf"], ["identb"])
        memset("dve", epsb[:], EPS, ["epsb"])
        dma("sp", cond[:], cT_d[:, :, :], writes=["cond"])
        act(cond[:], cond[:], AF.Silu, ["cond"], ["cond"])
        with ExitStack() as ph:
            zt = sb(ph, "zt", [128, 16, PAD], BF16)
            memset("dve", zt[:], 0.0, ["zt"])
            for s, L in (("x", SEQ), ("c", CTX)):
                v = pT[s].rearrange("(m p) t -> p m t", p=128)
                dma("sp", v[:, :, 0:PAD], zt[:], reads=["zt"])
                dma("sp", v[:, :, PAD + L:PAD + L + PAD], zt[:], reads=["zt"])
            P.barrier()

        with ExitStack() as ph:
            stg = [sb(ph, "tcs%d" % i, [128, 8, 512], BF16) for i in range(2)]
            cnt = 0
            for s_ in MON:
                NK_ = MON[s_][2]
                NZ_ = MON[s_][1]
                for k0 in range(0, NK_, 8):
                    nk = min(8, NK_ - k0)
                    sl = cnt % 2
                    cnt += 1
                    dma("pool", stg[sl][:, 0:nk, :], mon_d[s_]["f2"][k0:k0 + nk].rearrange("k p n -> p k n"), writes=[("tcs", sl)])
                    dma("sp", f2b_d[s_][k0:k0 + nk].rearrange("k p n -> p k n"), stg[sl][:, 0:nk, :], reads=[("tcs", sl)])
                w_ = 2 * NZ_
                for n0 in range(0, 128, 32):
                    sl = cnt % 2
                    cnt += 1
                    v = stg[sl][0:NK_, :, :].rearrange("p a b -> p (a b)")[:, 0:32 * w_]
                    dma("pool", v, mon_d[s_]["i2"][:, n0:n0 + 32].rearrange("k n r z -> k (n r z)"), writes=[("tcs", sl)])
                    dma("sp", i2b_d[s_][:, n0:n0 + 32, :].rearrange("k n w -> k (n w)"), v, reads=[("tcs", sl)])
            P.barrier()

        seqs = {}
        for s, L, src in (("c", CTX, ctx_d), ("x", SEQ, x_d)):
            q = Seq()
            q.name, q.L, q.src = s, L, src
            q.res = y_d if s == "x" else cres_d
            q.mix = xmix_d if s == "x" else cmix_d
            seqs[s] = q

        def phase_mod(l):
            with ExitStack() as ph:
                wa = [sb(ph, "wa%d" % i, [128, 8, 512], F32) for i in range(6)]
                bad = sb(ph, "bad", [128, 48], F32)
                tmp = sb(ph, "modtmp", [128, 8], F32)
                dma("sp", bad[:], b_adaT_d[l], writes=["bad"])
                dma("sp", ngt[:], ngT_d[l], writes=["ngt"])
                wv = w_ada_d[l].rearrange("(k p) n -> p k n", p=128)
                bk, bkey = bank()
                for cb in range(12):
                    slot = cb % 6
                    dma(("sp", "pool", "act")[cb % 3], wa[slot][:], wv[:, :, cb * 512:(cb + 1) * 512], writes=[("wa", slot)])
                    for mi in range(4):
                        m = cb * 4 + mi
                        for k in range(8):
                            mm(bk[:, 2 * m:2 * m + 2], wa[slot][:, k, mi * 128:(mi + 1) * 128], cond[:, k, :],
                               k == 0, k == 7, [("wa", slot), "cond"], [bkey])
                tt("dve", modT[:], bk[:, 0:96].rearrange("p (m s) -> p m s", s=2),
                   bad[:].unsqueeze(2).to_broadcast([128, 48, 2]), ALU.add, [bkey, "bad"], ["modT"])
                for s in ("x", "c"):
                    si = SI[s]
                    md = lambda i: modT[:, i * 8:(i + 1) * 8, si]
                    stt(vec[(s, "gs1")][:], md(1), 1.0, ngt[:, 0, :], ALU.add, ALU.mult, ["modT", "ngt"], [("v", s, "gs1")])
                    cp("dve", vec[(s, "sh1")][:], md(0), ["modT"], [("v", s, "sh1")])
                    stt(vec[(s, "gs3")][:], md(4), 1.0, ngt[:, 2, :], ALU.add, ALU.mult, ["modT", "ngt"], [("v", s, "gs3")])
                    cp("dve", vec[(s, "sh3")][:], md(3), ["modT"], [("v", s, "sh3")])
                    tt("dve", vec[(s, "ga2")][:], md(2), ngt[:, 1, :], ALU.mult, ["modT", "ngt"], [("v", s, "ga2")])
                    tt("dve", vec[(s, "ga4")][:], md(5), ngt[:, 3, :], ALU.mult, ["modT", "ngt"], [("v", s, "ga4")])
                    for j, nm in enumerate(("ga2", "ga4")):
                        dma("sp", gv_d[si, j].rearrange("(k p) -> p k", p=128), vec[(s, nm)][:],
                            reads=[("v", s, nm)], writes=[("gv", si, j)], slow=True)
                    dma("sp", G2[s][:], gv_d[si, 0:1, :].to_broadcast([128, D]), reads=[("gv", si, 0)], writes=[("G2", s)])
                    dma("sp", G4[s][:], gv_d[si, 1:2, :].to_broadcast([128, D]), reads=[("gv", si, 1)], writes=[("G4", s)])
                P.barrier()

        def norm_rows(xt_ap, na, ss, rs, junk, keys_r, key_ss, key_rs):
            for a in range(na):
                act(junk[:], xt_ap[:, a, :], AF.Square, keys_r, [key_ss], accum=ss[:, a:a + 1])
            act(rs[:, 0:na], ss[:, 0:na], AF.Sqrt, [key_ss], [key_rs], scale=1.0 / D, bias=epsb[:])
            P.op("dve", lambda e: e.reciprocal(out=rs[:, 0:na], in_=rs[:, 0:na]), [key_rs], [key_rs])

        def phase_in(l):
            with ExitStack() as ph:
                win = sb(ph, "win", [128, 8, 2 * D], BF16)
                wv = w_in_d[l].rearrange("(k p) n -> p k n", p=128)
                for k in range(8):
                    dma("pool", win[:, k, :], wv[:, k, :], writes=["win"])
                xts = [sb(ph, "xt%d" % i, [128, 4, D], F32) for i in range(2)]
                xss = [sb(ph, "xs%d" % i, [128, 4, D], BF16) for i in range(2)]
                xnT = [sb(ph, "xnT%d" % i, [128, 8, 512], BF16) for i in range(2)]
                pout = [sb(ph, "pout%d" % i, [128, 16, 512], BF16) for i in range(2)]
                junk = sb(ph, "junk", [128, D], BF16)
                ssq = [sb(ph, "ssq%d" % i, [128, 4], F32) for i in range(2)]
                rsd = [sb(ph, "rsd%d" % i, [128, 4], F32) for i in range(2)]
                cnt = 0
                for s in ("c", "x"):
                    q = seqs[s]
                    src = q.src if l == 0 else q.res
                    TB = min(512, q.L)
                    for t0 in range(0, q.L, TB):
                        nt = TB
                        na = nt // 128
                        sl = cnt % 2
                        cnt += 1
                        xt, xs = xts[sl], xss[sl]
                        dma("sp", xt[:, 0:na, :], src[t0:t0 + nt, :].rearrange("(a p) f -> p a f", p=128), writes=[("xt", sl)])
                        norm_rows(xt, na, ssq[sl], rsd[sl], junk, [("xt", sl)], ("ss", sl), ("rs", sl))
                        for a in range(na):
                            if a % 2 == 0:
                                ts("dve", xs[:, a, :], xt[:, a, :], rsd[sl][:, a:a + 1], None, ALU.mult, None,
                                   [("xt", sl), ("rs", sl)], [("xs", sl, a)])
                            else:
                                act(xs[:, a, :], xt[:, a, :], AF.Copy, [("xt", sl), ("rs", sl)], [("xs", sl, a)], scale=rsd[sl][:, a:a + 1])
                        for k in range(8):
                            bk, bkey = bank()
                            for a in range(na):
                                mm(bk[:, a * 128:(a + 1) * 128], xs[:, a, k * 128:(k + 1) * 128], identb[:], True, True,
                                   [("xs", sl, a), "identb"], [bkey])
                            act(xnT[sl][:, k, 0:nt], bk[:, 0:nt], AF.Identity, [bkey, ("v", s, "gs1"), ("v", s, "sh1")], [("xnT", sl, k)],
                                scale=vec[(s, "gs1")][:, k:k + 1], bias=vec[(s, "sh1")][:, k:k + 1])
                        for m in range(16):
                            bk, bkey = bank()
                            for k in range(8):
                                mm(bk[:, 0:nt], win[:, k, m * 128:(m + 1) * 128], xnT[sl][:, k, 0:nt], k == 0, k == 7,
                                   ["win", ("xnT", sl, k)], [bkey])
                            cp("act" if m % 2 == 0 else "dve", pout[sl][:, m, 0:nt], bk[:, 0:nt], [bkey], [("pout", sl)])
                        dma("sp", pT[s].rearrange("(m p) t -> p m t", p=128)[:, :, PAD + t0:PAD + t0 + nt], pout[sl][:, :, 0:nt],
                            reads=[("pout", sl)])
                P.barrier()

        def resid_epilogue(bk0, bk1, k0, k1, s, Gt, Gkey, xres_ap, xres_key, out_ap, out_key, scr):
            ss2, rs1, junk2, tmp = scr
            act(junk2[:, 0:512], bk0[:], AF.Square, [k0], ["ss2"], accum=ss2[:, 0:1])
            act(junk2[:, 512:1024], bk1[:], AF.Square, [k1], ["ss2"], accum=ss2[:, 1:2])
            tt("dve", rs1[:], ss2[:, 0:1], ss2[:, 1:2], ALU.add, ["ss2"], ["rs1"])
            act(rs1[:], rs1[:], AF.Sqrt, ["rs1"], ["rs1"], scale=1.0 / D, bias=epsb[:])
            P.op("dve", lambda e: e.reciprocal(out=rs1[:], in_=rs1[:]), ["rs1"], ["rs1"])
            stt(tmp[:, 0:512], bk0[:], rs1[:, 0:1], Gt[:, 0:512], ALU.mult, ALU.mult, [k0, "rs1", Gkey], ["etmp0"])
            stt(tmp[:, 512:1024], bk1[:], rs1[:, 0:1], Gt[:, 512:1024], ALU.mult, ALU.mult, [k1, "rs1", Gkey], ["etmp1"])
            tt("pool", out_ap, tmp[:], xres_ap, ALU.add, ["etmp0", "etmp1", xres_key], [out_key])

        def phase_out(l):
            with ExitStack() as ph:
                wo = sb(ph, "wo", [128, 8, D], BF16)
                wv = w_out_d[l].rearrange("(k p) n -> p k n", p=128)
                for k in range(8):
                    dma("pool", wo[:, k, :], wv[:, k, :], writes=["wo"])
                yts = [sb(ph, "yt%d" % i, [128, 8, 512], BF16) for i in range(2)]
                xts = [sb(ph, "xt%d" % i, [128, 4, D], F32) for i in range(2)]
                xos = [sb(ph, "xo%d" % i, [128, 4, D], F32) for i in range(2)]
                scr = (sb(ph, "ss2", [128, 2], F32), sb(ph, "rs1", [128, 1], F32), sb(ph, "junk2", [128, D], BF16),
                       sb(ph, "etmp", [128, D], F32))
                cnt = 0
                for s in ("c", "x"):
                    q = seqs[s]
                    src = q.src if l == 0 else q.res
                    TB = min(512, q.L)
                    for t0 in range(0, q.L, TB):
                        nt = TB
                        na = nt // 128
                        sl = cnt % 2
                        cnt += 1
                        dma("sp", yts[sl][:, :, 0:nt], yT[s].rearrange("(k p) t -> p k t", p=128)[:, :, t0:t0 + nt], writes=[("yt", sl)])
                        dma("sp", xts[sl][:, 0:na, :], src[t0:t0 + nt, :].rearrange("(a p) f -> p a f", p=128), writes=[("xt", sl)])
                        for a in range(na):
                            b0, k0 = bank()
                            b1, k1 = bank()
                            for k in range(8):
                                mm(b0[:], yts[sl][:, k, a * 128:(a + 1) * 128], wo[:, k, 0:512], k == 0, k == 7, [("yt", sl), "wo"], [k0])
                                mm(b1[:], yts[sl][:, k, a * 128:(a + 1) * 128], wo[:, k, 512:1024], k == 0, k == 7, [("yt", sl), "wo"], [k1])
                            resid_epilogue(b0, b1, k0, k1, s, G2[s], ("G2", s), xts[sl][:, a, :], ("xt", sl), xos[sl][:, a, :], ("xo", sl, a), scr)
                        dma("sp", q.mix[t0:t0 + nt, :].rearrange("(a p) f -> p a f", p=128), xos[sl][:, 0:na, :],
                            reads=[("xo", sl, a) for a in range(na)])
                P.barrier()

        def phase_ffn(l):
            with ExitStack() as ph:
                wd = sb(ph, "wd", [128, NFF, D], BF16)
                wv = w_down_d[l].rearrange("(k p) n -> p k n", p=128)
                for k in range(NFF):
                    dma("pool", wd[:, k, :], wv[:, k, :], writes=["wd"])
                stg = [sb(ph, "stg%d" % i, [128, 2 * DFF], BF16) for i in range(2)]
                for k in range(8):
                    dma("pool", stg[k % 2][:], w_up_d[l, k * 128:(k + 1) * 128, :], writes=[("stg", k % 2)])
                    dma("sp", wupb_d[k * 128:(k + 1) * 128, :], stg[k % 2][:], reads=[("stg", k % 2)])
                P.barrier()
                wupv = wupb_d.rearrange("(k p) n -> p k n", p=128)
                cw = sb(ph, "cw", [128, 9, NFF], F32)
                cb = sb(ph, "cb", [128, NFF], F32)
                dma("sp", cw[:], fcwT_d[l], writes=["cw"])
                dma("sp", cb[:], fcbT_d[l], writes=["cb"])
                with ExitStack() as p3:
                    dst_ = [sb(p3, "dgst%d" % i, [128, 2, 1152], BF16) for i in range(2)]
                    for m0 in range(0, NFF, 2):
                        sl = (m0 // 2) % 2
                        for mi in range(2):
                            for tap in range(9):
                                act(dst_[sl][:, mi, tap * 128:(tap + 1) * 128], identf[:], AF.Copy, ["identf", "cw"], [("dgst", sl)],
                                    scale=cw[:, tap, m0 + mi:m0 + mi + 1])
                        dma("sp", dgff_d[m0:m0 + 2].rearrange("m p n -> p m n"), dst_[sl][:], reads=[("dgst", sl)])
                    P.barrier()
                XE = 1152
                xnT = sb(ph, "fxnT", [128, 8, XE], BF16)
                hT = sb(ph, "hT", [128, NFF, 1024], BF16)
                xt1 = [sb(ph, "fxt%d" % i, [128, D], F32) for i in range(2)]
                xs1 = [sb(ph, "fxs%d" % i, [128, D], BF16) for i in range(2)]
                ss1 = [sb(ph, "fss%d" % i, [128, 1], F32) for i in range(2)]
                rs1b = [sb(ph, "frs%d" % i, [128, 1], F32) for i in range(2)]
                junk = sb(ph, "fjunk", [128, D], BF16)
                wg = [sb(ph, "wg%d" % i, [128, 8, 128], BF16) for i in range(2)]
                wvv = [sb(ph, "wv%d" % i, [128, 8, 128], BF16) for i in range(2)]
                dg = [sb(ph, "dg%d" % i, [128, 9, 128], BF16) for i in range(2)]
                gbuf = [sb(ph, "gbuf%d" % i, [128, 18, 64], BF16) for i in range(2)]
                gel = [sb(ph, "gel%d" % i, [128, 512], F32) for i in range(2)]
                xo = [sb(ph, "fxo%d" % i, [128, D], F32) for i in range(2)]
                scr = (sb(ph, "ss2", [128, 2], F32), sb(ph, "rs1", [128, 1], F32), sb(ph, "junk2", [128, D], BF16),
                       sb(ph, "etmp", [128, D], F32))
                tcnt = 0
                mcnt = 0
                for s in ("c", "x"):
                    q = seqs[s]
                    L = q.L
                    if s == "x":
                        ncols, BR = 64, 16
                    else:
                        ncols, BR = 256, 1
                    NT = BR * ncols
                    vert = s == "x"
                    for t0 in range(0, L, NT):
                        top = vert and t0 > 0
                        bot = vert and t0 + NT < L
                        e0 = t0 - (64 if top else 0)
                        e1 = t0 + NT + (64 if bot else 0)
                        tiles = []
                        tt0 = e0
                        while tt0 < e1:
                            n = min(128, e1 - tt0)
                            tiles.append((tt0, n))
                            tt0 += n
                        for (ta, n) in tiles:
                            sl = tcnt % 2
                            tcnt += 1
                            dma("sp", xt1[sl][0:n, :], q.mix[ta:ta + n, :], writes=[("fxt", sl)])
                            act(junk[0:n, :], xt1[sl][0:n, :], AF.Square, [("fxt", sl)], [("fss", sl)], accum=ss1[sl][0:n, :])
                            act(rs1b[sl][0:n, :], ss1[sl][0:n, :], AF.Sqrt, [("fss", sl)], [("frs", sl)], scale=1.0 / D, bias=epsb[0:n, :])
                            P.op("dve", (lambda r, n: lambda e: e.reciprocal(out=r[0:n, :], in_=r[0:n, :]))(rs1b[sl], n), [("frs", sl)], [("frs", sl)])
                            ts("dve", xs1[sl][0:n, :], xt1[sl][0:n, :], rs1b[sl][0:n, 0:1], None, ALU.mult, None, [("fxt", sl), ("frs", sl)], [("fxs", sl)])
                            c0 = ta - e0
                            for kk in range(2):
                                bk, bkey = bank()
                                for k4 in range(4):
                                    k = kk * 4 + k4
                                    mm(bk[:, k4 * 128:k4 * 128 + n], xs1[sl][0:n, k * 128:(k + 1) * 128], identb[0:n, 0:n], True, True,
                                       [("fxs", sl), "identb"], [bkey])
                                for k4 in range(4):
                                    k = kk * 4 + k4
                                    act(xnT[:, k, c0:c0 + n], bk[:, k4 * 128:k4 * 128 + n], AF.Identity,
                                        [bkey, ("v", s, "gs3"), ("v", s, "sh3")], [("fxnT", k)],
                                        scale=vec[(s, "gs3")][:, k:k + 1], bias=vec[(s, "sh3")][:, k:k + 1])
                        ne = e1 - e0
                        goff = 0 if top else 1
                        nrows_e = ne // ncols if vert else 1
                        for m in range(NFF):
                            sl = mcnt % 2
                            mcnt += 1
                            dma("sp", wg[sl][:], wupv[:, :, m * 128:(m + 1) * 128], writes=[("wg", sl)])
                            dma("sp", wvv[sl][:], wupv[:, :, DFF + m * 128:DFF + (m + 1) * 128], writes=[("wv", sl)])
                            dma("sp", dg[sl][:].rearrange("p t k -> p (t k)"), dgff_d[m], writes=[("dg", sl)])
                            gb = gbuf[sl]
                            gview = gb[:].rearrange("p r c -> p (r c)")
                            if vert:
                                if not top:
                                    memset("pool", gb[:, 0, :], 0.0, [("gbuf", sl)])
                                if not bot:
                                    memset("pool", gb[:, 17, :], 0.0, [("gbuf", sl)])
                            o = 0
                            while o < ne:
                                n = min(512, ne - o)
                                bk, bkey = bank()
                                for k in range(8):
                                    mm(bk[:, 0:n], wg[sl][:, k, :], xnT[:, k, o:o + n], k == 0, k == 7, [("wg", sl), ("fxnT", k)], [bkey])
                                gdst = gview[:, goff * 64 + o:goff * 64 + o + n] if vert else gview[:, o:o + n]
                                cp("act", gdst, bk[:, 0:n], [bkey], [("gbuf", sl)])
                                o += n
                            cen0 = t0 - e0
                            for sbk in range(0, NT, 512):
                                nsub = min(512, NT - sbk)
                                bv, bvkey = bank()
                                for k in range(8):
                                    mm(bv[:, 0:nsub], wvv[sl][:, k, :], xnT[:, k, cen0 + sbk:cen0 + sbk + nsub], k == 0, k == 7,
                                       [("wv", sl), ("fxnT", k)], [bvkey])
                                bc, bckey = bank()
                                if vert:
                                    r0 = 1 + sbk // 64
                                    nr = nsub // 64
                                    bc3 = bc[:, 0:nsub].rearrange("p (r c) -> p r c", c=64)
                                    first = True
                                    order = [4, 1, 7, 3, 5, 0, 2, 6, 8]
                                    for ti, tap in enumerate(order):
                                        dy, dx = tap // 3 - 1, tap % 3 - 1
                                        if dx == 0:
                                            rhs = gb[:, r0 + dy:r0 + dy + nr, :]
                                            out = bc3
                                        elif dx == -1:
                                            rhs = gb[:, r0 + dy:r0 + dy + nr, 0:63]
                                            out = bc3[:, :, 1:64]
                                        else:
                                            rhs = gb[:, r0 + dy:r0 + dy + nr, 1:64]
                                            out = bc3[:, :, 0:63]
                                        mm(out, dg[sl][:, tap, :], rhs, first, ti == 8, [("dg", sl), ("gbuf", sl)], [bckey])
                                        first = False
                                else:
                                    for ti, tap in enumerate([4, 3, 5]):
                                        dx = tap % 3 - 1
                                        if dx == 0:
                                            rhs, out = gview[:, 0:nsub], bc[:, 0:nsub]
                                        elif dx == -1:
                                            rhs, out = gview[:, 0:nsub - 1], bc[:, 1:nsub]
                                        else:
                                            rhs, out = gview[:, 1:nsub], bc[:, 0:nsub - 1]
                                        mm(out, dg[sl][:, tap, :], rhs, ti == 0, ti == 2, [("dg", sl), ("gbuf", sl)], [bckey])
                                gsl = (mcnt + sbk // 512) % 2
                                act(gel[gsl][:, 0:nsub], bc[:, 0:nsub], AF.Gelu_apprx_tanh, [bckey, "cb"], [("gel", gsl)], bias=cb[:, m:m + 1])
                                tt("dve", hT[:, m, sbk:sbk + nsub], bv[:, 0:nsub], gel[gsl][:, 0:nsub], ALU.mult, [bvkey, ("gel", gsl)], [("hT", m)])
                        for a in range(NT // 128):
                            sl = a % 2
                            b0, k0 = bank()
                            b1, k1 = bank()
                            for k in range(NFF):
                                mm(b0[:], hT[:, k, a * 128:(a + 1) * 128], wd[:, k, 0:512], k == 0, k == NFF - 1, [("hT", k), "wd"], [k0])
                                mm(b1[:], hT[:, k, a * 128:(a + 1) * 128], wd[:, k, 512:1024], k == 0, k == NFF - 1, [("hT", k), "wd"], [k1])
                            ta = t0 + a * 128
                            dma("sp", xt1[sl][:], q.mix[ta:ta + 128, :], writes=[("fxt", sl)])
                            resid_epilogue(b0, b1, k0, k1, s, G4[s], ("G4", s), xt1[sl][:], ("fxt", sl), xo[sl][:], ("fxo", sl), scr)
                            dma("sp", q.res[ta:ta + 128, :], xo[sl][:], reads=[("fxo", sl)])
                P.barrier()


        def sin_rr(out_ap, arg, tmpk, n_part, keys_arg, key_out, key_tmp):
            ts("dve", tmpk, arg, 1.0 / TWO_PI, MAGIC, ALU.mult, ALU.add, keys_arg, [key_tmp])
            ts("dve", tmpk, tmpk, MAGIC, -TWO_PI, ALU.subtract, ALU.mult, [key_tmp], [key_tmp])
            tt("dve", tmpk, arg, tmpk, ALU.add, keys_arg + [key_tmp], [key_tmp])
            ts("dve", tmpk, tmpk, 3.1415925, -3.1415925, ALU.min, ALU.max, [key_tmp], [key_tmp])
            act(out_ap, tmpk, AF.Sin, [key_tmp], [key_out])

        def phase_hyprep(l):
            with ExitStack() as ph:
                hw = sb(ph, "hw", [128, 3, 12], F32)
                hb = sb(ph, "hb", [128, 12], F32)
                dma("sp", hw[:], hswT_d[l], writes=["hw"])
                dma("sp", hb[:], hsbT_d[l], writes=["hb"])
                dgs = sb(ph, "hdg", [128, 36, 128], BF16)
                for ch in range(12):
                    for d in range(3):
                        act(dgs[:, ch * 3 + d, :], identf[:], AF.Copy, ["identf", "hw"], ["hdg"], scale=hw[:, d, ch:ch + 1])
                pin = [sb(ph, "pin%d" % i, [128, 3, 514], BF16) for i in range(2)]
                vB = [sb(ph, "vB%d" % i, [128, 512], F32) for i in range(2)]
                uo = [sb(ph, "uo%d" % i, [128, 512], BF16) for i in range(2)]
                xo = [sb(ph, "x0o%d" % i, [128, 512], BF16) for i in range(2)]
                cnt = 0
                for s in ("c", "x"):
                    L = seqs[s].L
                    TB = min(512, L)
                    for cc in range(4):
                        for t0 in range(0, L, TB):
                            sl = cnt % 2
                            cnt += 1
                            for j in range(3):
                                r0 = j * 512 + cc * 128
                                dma("sp" if j != 1 else "pool", pin[sl][:, j, 0:TB + 2], pT[s][r0:r0 + 128, PAD + t0 - 1:PAD + t0 + TB + 1], writes=[("pin", sl, j)])
                            bks = []
                            for j in range(3):
                                bk, bkey = bank()
                                ch = j * 4 + cc
                                for d in range(3):
                                    mm(bk[:, 0:TB], dgs[:, ch * 3 + d, :], pin[sl][:, j, d:d + TB], d == 0, d == 2, ["hdg", ("pin", sl, j)], [bkey])
                                bks.append((bk, bkey))
                            act(vB[sl][:, 0:TB], bks[2][0][:, 0:TB], AF.Identity, [bks[2][1], "hb"], [("vB", sl)], bias=hb[:, 8 + cc:9 + cc])
                            act(xo[sl][:, 0:TB], bks[0][0][:, 0:TB], AF.Identity, [bks[0][1], "hb"], [("x0o", sl)], bias=hb[:, cc:cc + 1])
                            stt(uo[sl][:, 0:TB], bks[1][0][:, 0:TB], hb[:, 4 + cc:5 + cc], vB[sl][:, 0:TB], ALU.add, ALU.mult,
                                [bks[1][1], "hb", ("vB", sl)], [("uo", sl)])
                            dma("sp", uT[s][cc * 128:(cc + 1) * 128, t0:t0 + TB], uo[sl][:, 0:TB], reads=[("uo", sl)])
                            dma("sp", x0T[s][cc * 128:(cc + 1) * 128, t0:t0 + TB], xo[sl][:, 0:TB], reads=[("x0o", sl)])
                P.barrier()

        def phase_hyena(l, s):
            N1, NZ, NK, L = MON[s]
            md = mon_d[s]
            NK2 = 2 * NK
            with ExitStack() as ph:
                hidT = sb(ph, "hidT", [64, L], BF16)
                woutb = sb(ph, "woutb", [64, 1024], BF16)
                d1 = sb(ph, "d1", [NZ, 512], F32)
                d2 = sb(ph, "d2", [128, 512], F32)
                f1t = sb(ph, "f1t", [NZ, NK2], BF16)
                gtb = sb(ph, "gtb", [128, 2, 2, 128], BF16)
                hbias = sb(ph, "hbias", [128, 4], F32)
                dma("pool", woutb[:], f_wout_d[l], writes=["woutb"])
                dma("sp", d1[:], md["d1"][:, :], writes=["d1"])
                dma("sp", d2[:], md["d2"][:, :], writes=["d2"])
                dma("pool", f1t[:], md["f1"][:, :], writes=["f1t"])
                dma("pool", gtb[:], gtab_d[:, :, :, :], writes=["gtb"])
                with ExitStack() as p2:
                    zt = sb(p2, "zt", [33, L], F32)
                    fwin = sb(p2, "fwin", [33, 64], F32)
                    fwh = sb(p2, "fwh", [64, 2, 64], F32)
                    fb = sb(p2, "fb", [64, 3], F32)
                    ffr = sb(p2, "ffr", [64, 1], F32)
                    frb = sb(p2, "frb", [64, 3], F32)
                    dma("sp", zt[:], md["z"][:, :], writes=["zt"])
                    dma("sp", fwin[:], f_win_d[l], writes=["fwin"])
                    dma("sp", fwh[:], f_whid_d[l], writes=["fwh"])
                    dma("sp", fb[:], f_b_d[l], writes=["fb"])
                    dma("sp", ffr[:], f_freq_d[l], writes=["ffr"])
                    ts("dve", frb[:], fb[:], ffr[:, 0:1], None, ALU.mult, None, ["fb", "ffr"], ["frb"])
                    args = [sb(p2, "farg%d" % i, [64, 512], F32) for i in range(2)]
                    tmpk = [sb(p2, "ftmp%d" % i, [64, 512], F32) for i in range(2)]
                    hts = [sb(p2, "fh%d" % i, [64, 512], F32) for i in range(4)]
                    TB = min(512, L)
                    cnt = 0
                    for t0 in range(0, L, TB):
                        prev = None
                        for li in range(3):
                            sl = cnt % 2
                            cnt += 1
                            bk, bkey = bank()
                            if li == 0:
                                mm(bk[0:64, 0:TB], fwin[:], zt[:, t0:t0 + TB], True, True, ["fwin", "zt"], [bkey])
                            else:
                                mm(bk[0:64, 0:TB], fwh[:, li - 1, :], prev[0][:, 0:TB], True, True, ["fwh", prev[1]], [bkey])
                            act(args[sl][:, 0:TB], bk[0:64, 0:TB], AF.Identity, [bkey, "ffr", "frb"], [("farg", sl)],
                                scale=ffr[:, 0:1], bias=frb[:, li:li + 1])
                            if li < 2:
                                hsl = (t0 // TB * 2 + li) % 4
                                sin_rr(hts[hsl][:, 0:TB], args[sl][:, 0:TB], tmpk[sl][:, 0:TB], 64, [("farg", sl)], ("fh", hsl), ("ftmp", sl))
                                prev = (hts[hsl], ("fh", hsl))
                            else:
                                sin_rr(hidT[:, t0:t0 + TB], args[sl][:, 0:TB], tmpk[sl][:, 0:TB], 64, [("farg", sl)], "hidT", ("ftmp", sl))
                    P.barrier()
                dma("sp", hbias[:], hbiasT_d[l], writes=["hbias"])
                Xraw = sb(ph, "monX", [NZ, 128 * 128], BF16)
                X = Xraw[:, :].rearrange("p (c n) -> p c n", n=128)
                Xf = Xraw[:, :].rearrange("p (n c) -> p n c", c=128)
                A = sb(ph, "monA", [128, 128, 2, NK], BF16)
                Kf = sb(ph, "monK", [128, 2, NK, 128], BF16)
                BB = sb(ph, "monB", [128, 2 * 65 * 128], BF16)
                Ab = BB[:, 0:2 * NK * 128].rearrange("p (c r k) -> p c r k", r=2, k=NK, c=128)
                B0 = BB[0:NK, 0:2 * 64 * 128].rearrange("p (c r n) -> p c r n", r=2, n=64, c=128)
                f2r = [sb(ph, "f2r%d" % i, [128, 4, 128], BF16) for i in range(8)]
                i2r = [sb(ph, "i2r%d" % i, [NK, 4, 2, NZ], BF16) for i in range(4)]
                tm = [sb(ph, "fmt%d" % i, [128, 512], F32) for i in range(4)]
                KB = 4
                nkb = (NK + KB - 1) // KB
                Akeys = [("A", kb) for kb in range(nkb)]
                f2c = [0]
                i2c = [0]
                nb1 = max(1, min(128, 512 // NK2))

                def f1(dst, dstkeys, scale_d2, cg):
                    c = 0
                    i = 0
                    while c < 128:
                        nb = min(nb1, 128 - c)
                        bk, bkey = bank()
                        for q in range(nb):
                            lhs = Xf[0:NZ, :, c + q] if scale_d2 else X[0:NZ, c + q, :]
                            mm(bk[:, q * NK2:(q + 1) * NK2], lhs, f1t[0:NZ, :], True, True, ["X", "f1t"], [bkey])
                        out = dst[:, c:c + nb, :, :].rearrange("p c r k -> p c (r k)")
                        src = bk[:, 0:nb * NK2].rearrange("p (i x) -> p i x", x=NK2)
                        if scale_d2:
                            tt("dve", out, src, d2[:, cg * 128 + c:cg * 128 + c + nb].unsqueeze(2).to_broadcast([128, nb, NK2]), ALU.mult,
                               [bkey, "d2"], dstkeys)
                        else:
                            cp("act" if i % 2 == 0 else "dve", out, src, [bkey], dstkeys)
                        c += nb
                        i += 1

                def load_f2(k1):
                    sl = f2c[0] % 8
                    f2c[0] += 1
                    dma("sp", f2r[sl][:].rearrange("p v k -> p (v k)"), f2b_d[s][k1], writes=[("f2r", sl)])
                    return f2r[sl], ("f2r", sl)

                for cg in range(4):
                    for dr in range(2):
                        for n20 in range(0, 128, 4):
                            bk, bkey = bank()
                            for q in range(4):
                                n2 = n20 + q
                                mm(bk[0:NZ, q * 128:(q + 1) * 128], hidT[:, n2:L:128], woutb[:, dr * 512 + cg * 128:dr * 512 + (cg + 1) * 128],
                                   True, True, ["hidT", "woutb"], [bkey])
                            tt("dve", Xf[0:NZ, n20:n20 + 4, :],
                               bk[0:NZ, 0:512].rearrange("p (q c) -> p q c", c=128),
                               d1[0:NZ, cg * 128:(cg + 1) * 128].unsqueeze(1).to_broadcast([NZ, 4, 128]), ALU.mult, [bkey, "d1"], ["X"])
                        if dr == 1:
                            memset("dve", Xf[0:1, 0:1, :], 0.0, ["X"])
                        if dr == 0:
                            f1(A, Akeys, True, cg)
                        else:
                            f1(Ab, ["B0"], True, cg)
                    for kb in range(nkb):
                        k1s = list(range(kb * KB, min(NK, (kb + 1) * KB)))
                        br, brk = bank()
                        bi, bik = bank()
                        for q, k1 in enumerate(k1s):
                            f2, f2k = load_f2(k1)
                            cs = slice(q * 128, (q + 1) * 128)
                            mm(br[:, cs], f2[:, 0, :], A[:, :, 0, k1], True, False, [f2k, ("A", kb)], [brk])
                            mm(br[:, cs], f2[:, 2, :], A[:, :, 1, k1], False, False, [f2k, ("A", kb)], [brk])
                            mm(br[:, cs], f2[:, 0, :], Ab[:, :, 0, k1], False, False, [f2k, "B0"], [brk])
                            mm(br[:, cs], f2[:, 2, :], Ab[:, :, 1, k1], False, True, [f2k, "B0"], [brk])
                            mm(bi[:, cs], f2[:, 1, :], A[:, :, 0, k1], True, False, [f2k, ("A", kb)], [bik])
                            mm(bi[:, cs], f2[:, 0, :], A[:, :, 1, k1], False, False, [f2k, ("A", kb)], [bik])
                            mm(bi[:, cs], f2[:, 2, :], Ab[:, :, 0, k1], False, False, [f2k, "B0"], [bik])
                            mm(bi[:, cs], f2[:, 3, :], Ab[:, :, 1, k1], False, True, [f2k, "B0"], [bik])
                        nn = len(k1s) * 128
                        cp("act", Kf[:, 0, k1s[0]:k1s[-1] + 1, :].rearrange("p k c -> p (k c)"), br[:, 0:nn], [brk], [("Kf", kb)])
                        cp("act", Kf[:, 1, k1s[0]:k1s[-1] + 1, :].rearrange("p k c -> p (k c)"), bi[:, 0:nn], [bik], [("Kf", kb)])
                    dma("sp", X[0:NZ, :, :], uT[s][cg * 128:(cg + 1) * 128, :].rearrange("c (a b) -> a c b", b=128), writes=["X"])
                    f1(A, Akeys, False, cg)
                    for kb in range(nkb):
                        k1s = list(range(kb * KB, min(NK, (kb + 1) * KB)))
                        br, brk = bank()
                        bi, bik = bank()
                        for q, k1 in enumerate(k1s):
                            f2, f2k = load_f2(k1)
                            cs = slice(q * 128, (q + 1) * 128)
                            mm(br[:, cs], f2[:, 0, :], A[:, :, 0, k1], True, False, [f2k, ("A", kb)], [brk])
                            mm(br[:, cs], f2[:, 2, :], A[:, :, 1, k1], False, True, [f2k, ("A", kb)], [brk])
                            mm(bi[:, cs], f2[:, 1, :], A[:, :, 0, k1], True, False, [f2k, ("A", kb)], [bik])
                            mm(bi[:, cs], f2[:, 0, :], A[:, :, 1, k1], False, True, [f2k, ("A", kb)], [bik])
                        nn = len(k1s) * 128
                        kr = Kf[:, 0, k1s[0]:k1s[-1] + 1, :].rearrange("p k c -> p (k c)")
                        ki = Kf[:, 1, k1s[0]:k1s[-1] + 1, :].rearrange("p k c -> p (k c)")
                        tt("dve", tm[0][:, 0:nn], br[:, 0:nn], kr, ALU.mult, [brk, ("Kf", kb)], [("tm", 0)])
                        tt("dve", tm[1][:, 0:nn], bi[:, 0:nn], ki, ALU.mult, [bik, ("Kf", kb)], [("tm", 1)])
                        tt("dve", tm[2][:, 0:nn], br[:, 0:nn], ki, ALU.mult, [brk, ("Kf", kb)], [("tm", 2)])
                        tt("dve", tm[3][:, 0:nn], bi[:, 0:nn], kr, ALU.mult, [bik, ("Kf", kb)], [("tm", 3)])
                        v3 = lambda t: t[:, 0:nn].rearrange("p (k c) -> p k c", c=128)
                        tt("pool", A[:, :, 0, k1s[0]:k1s[-1] + 1].rearrange("p c k -> p k c"), v3(tm[0]), v3(tm[1]), ALU.subtract,
                           [("tm", 0), ("tm", 1)], [("A", kb)])
                        tt("pool", A[:, :, 1, k1s[0]:k1s[-1] + 1].rearrange("p c k -> p k c"), v3(tm[2]), v3(tm[3]), ALU.add,
                           [("tm", 2), ("tm", 3)], [("A", kb)])
                    for hf in range(2):
                        for c0 in range(0, 128, 4):
                            bk, bkey = bank()
                            for q in range(4):
                                cs = slice(q * 128, (q + 1) * 128)
                                mm(bk[0:NK, cs], A[:, c0 + q, 0, 0:NK], gtb[:, hf, 0, :], True, False, Akeys + ["gtb"], [bkey])
                                mm(bk[0:NK, cs], A[:, c0 + q, 1, 0:NK], gtb[:, hf, 1, :], False, True, Akeys + ["gtb"], [bkey])
                            cp("act" if (c0 // 4) % 2 == 0 else "dve", B0[:, c0:c0 + 4, :, :].rearrange("p c r n -> p (c r n)"), bk[0:NK, 0:512], [bkey], ["B0"])
                        for n20 in range(0, 64, 4):
                            sl = i2c[0] % 4
                            i2c[0] += 1
                            ng = hf * 64 + n20
                            dma("sp", i2r[sl][:].rearrange("k q r z -> k q (r z)"), i2b_d[s][:, ng:ng + 4, :], writes=[("i2r", sl)])
                            bk, bkey = bank()
                            for q in range(4):
                                cs = slice(q * 128, (q + 1) * 128)
                                mm(bk[0:NZ, cs], i2r[sl][:, q, 0, :], B0[:, :, 0, n20 + q], True, False, [("i2r", sl), "B0"], [bkey])
                                mm(bk[0:NZ, cs], i2r[sl][:, q, 1, :], B0[:, :, 1, n20 + q], False, True, [("i2r", sl), "B0"], [bkey])
                            cp("act" if (n20 // 4) % 2 == 0 else "dve", X[0:NZ, :, ng:ng + 4].rearrange("p c q -> p q c"),
                               bk[0:NZ, 0:512].rearrange("p (q c) -> p q c", c=128), [bkey], ["X"])
                    dma("sp", yhT[s][cg * 128:(cg + 1) * 128, :].rearrange("c (a b) -> a c b", b=128), X[0:NZ, :, :], reads=["X"])
                P.barrier()
            with ExitStack() as ph:
                hbias = sb(ph, "hbias2", [128, 4], F32)
                dma("sp", hbias[:], hbiasT_d[l], writes=["hbias"])
                TB = min(2048, L)
                ty = [sb(ph, "ey%d" % i, [128, TB], BF16) for i in range(2)]
                tu = [sb(ph, "eu%d" % i, [128, TB], BF16) for i in range(2)]
                tx = [sb(ph, "ex%d" % i, [128, TB], BF16) for i in range(2)]
                t1 = [sb(ph, "et%d" % i, [128, TB], F32) for i in range(2)]
                to = [sb(ph, "eo%d" % i, [128, TB], BF16) for i in range(2)]
                cnt = 0
                for cc in range(4):
                    for t0 in range(0, L, TB):
                        sl = cnt % 2
                        cnt += 1
                        rs_ = slice(cc * 128, (cc + 1) * 128)
                        dma("sp", ty[sl][:], yhT[s][rs_, t0:t0 + TB], writes=[("ey", sl)])
                        dma("sp", tu[sl][:], uT[s][rs_, t0:t0 + TB], writes=[("eu", sl)])
                        dma("pool", tx[sl][:], x0T[s][rs_, t0:t0 + TB], writes=[("ex", sl)])
                        stt(t1[sl][:], tu[sl][:], hbias[:, cc:cc + 1], ty[sl][:], ALU.mult, ALU.add, [("eu", sl), ("ey", sl), "hbias"], [("et", sl)])
                        tt("pool", to[sl][:], t1[sl][:], tx[sl][:], ALU.mult, [("et", sl), ("ex", sl)], [("eo", sl)])
                        dma("sp", yT[s][rs_, t0:t0 + TB], to[sl][:], reads=[("eo", sl)])
                P.barrier()

        lamP = sb(st, "lamP", [128, 9, 2, 64], F32)
        st0 = sb(st, "st0", [128, 64], BF16)
        jswapf = sb(st, "jswapf", [128, 128], F32)
        sg = sb(st, "sg", [128, 4], F32)
        dma("sp", jswapf[:], jswap_d[:, :], writes=["jswapf"])
        dma("sp", sg[:], s5sg_d[:, :], writes=["sg"])
        SLOT_M = {0: [15 - i for i in range(16)] + [-i for i in range(16)] + [i for i in range(16)] + [i + 1 for i in range(16)],
                  1: [i for i in range(16)] + [-i for i in range(16)] + [16 - i for i in range(16)] + [0] * 16}

        def phase_s5tab(l):
            with ExitStack() as ph:
                lam = sb(ph, "lam", [128, 2, 64], F32)
                ls = sb(ph, "ls", [128, 64], F32)
                Bd = sb(ph, "Bd", [128, 2, 64, 16], F32)
                Cd = sb(ph, "Cd", [128, 2, 64, 16], F32)
                dT = sb(ph, "dT", [128, 32], F32)
                msk = sb(ph, "msk", [128, 2, 2, 256], F32)
                idp = sb(ph, "idp", [128, 2, 256], F32)
                dma("sp", lam[:], s5lam_d[l], writes=["lam"])
                dma("sp", ls[:], s5ls_d[l], writes=["ls"])
                dma("sp", Bd[:], s5b_d[l], writes=["Bd"])
                dma("sp", Cd[:], s5c_d[l], writes=["Cd"])
                dma("sp", dT[:], s5dT_d[l], writes=["dT"])
                dma("sp", msk[:], s5msk_d[:, :, :, :], writes=["msk"])
                dma("sp", idp[:], s5idp_d[:, :, :], writes=["idp"])
                xr = sb(ph, "xr", [128, 64], F32)
                xi = sb(ph, "xi", [128, 64], F32)
                act(ls[:], ls[:], AF.Exp, ["ls"], ["ls"])
                tt("dve", xr[:], lam[:, 0, :], ls[:], ALU.mult, ["lam", "ls"], ["xr"])
                tt("dve", xi[:], lam[:, 1, :], ls[:], ALU.mult, ["lam", "ls"], ["xi"])
                E1 = sb(ph, "E1", [128, 2, 64, 32], F32)
                E2 = sb(ph, "E2", [128, 2, 64, 32], F32)
                p2 = ExitStack()
                MAG = sb(p2, "MAG", [128, 2, 64, 32], F32)
                TK = sb(p2, "TK", [128, 2, 64, 32], F32)
                for d in range(2):
                    for sl_, m in enumerate(SLOT_M[d]):
                        us = slice(d * 32, d * 32 + 32)
                        act(MAG[:, d, sl_, :], xr[:, us], AF.Exp, ["xr"], ["MAG"], scale=float(m))
                        ts("dve", E1[:, d, sl_, :], xi[:, us], float(m), sg[:, 2:3], ALU.mult, ALU.add, ["xi", "sg"], ["E1"])
                        ts("dve", E2[:, d, sl_, :], xi[:, us], float(m), sg[:, 3:4], ALU.mult, ALU.add, ["xi", "sg"], ["E2"])
                fl = lambda t: t[:].rearrange("p d s u -> p (d s u)")
                sin_rr(fl(E1), fl(E1), fl(TK), 128, ["E1"], "E1", "TK")
                sin_rr(fl(E2), fl(E2), fl(TK), 128, ["E2"], "E2", "TK")
                tt("dve", fl(E1), fl(E1), fl(MAG), ALU.mult, ["E1", "MAG"], ["E1"])
                tt("pool", fl(E2), fl(E2), fl(MAG), ALU.mult, ["E2", "MAG"], ["E2"])
                P.barrier()
                p2.close()
                a1r = sb(ph, "a1r", [128, 64], F32)
                a1i = sb(ph, "a1i", [128, 64], F32)
                Lr = sb(ph, "Lr", [128, 64], F32)
                Li = sb(ph, "Li", [128, 64], F32)
                for d in range(2):
                    us = slice(d * 32, d * 32 + 32)
                    s1 = 33 if d == 0 else 1
                    s16 = 63 if d == 0 else 32
                    for (dst, slot) in ((a1r, s1), (Lr, s16)):
                        cp("dve", dst[0:64, us], E1[0:64, d, slot, :], ["E1"], [dst.name])
                        cp("dve", dst[64:128, us], E2[64:128, d, slot, :], ["E2"], [dst.name])
                    for (dst, slot) in ((a1i, s1), (Li, s16)):
                        cp("dve", dst[0:64, us], E2[0:64, d, slot, :], ["E2"], [dst.name])
                        cp("dve", dst[64:128, us], E1[64:128, d, slot, :], ["E1"], [dst.name])
                t1 = sb(ph, "sq1", [128, 64], F32)
                t2 = sb(ph, "sq2", [128, 64], F32)
                for k in range(9):
                    cp("dve", lamP[:, k, 0, :], Lr[:], [Lr.name], ["lamP"])
                    ts("dve", lamP[:, k, 1, :], Li[:], sg[:, 1:2], None, ALU.mult, None, [Li.name, "sg"], ["lamP"])
                    if k < 8:
                        tt("dve", t1[:], Lr[:], Lr[:], ALU.mult, [Lr.name], ["sq1"])
                        tt("dve", t2[:], Li[:], Li[:], ALU.mult, [Li.name], ["sq2"])
                        tt("dve", Li[:], Lr[:], Li[:], ALU.mult, [Lr.name, Li.name], [Li.name])
                        ts("dve", Li[:], Li[:], 2.0, None, ALU.mult, None, [Li.name], [Li.name])
                        tt("dve", Lr[:], t1[:], t2[:], ALU.subtract, ["sq1", "sq2"], [Lr.name])
                with ExitStack() as p3:
                    lst = [sb(p3, "lst%d" % i, [128, 4, 1152], BF16) for i in range(2)]
                    lt1 = [sb(p3, "lt1_%d" % i, [128, 128], F32) for i in range(4)]
                    c4 = 0
                    for u0 in range(0, 64, 4):
                        sl = (u0 // 4) % 2
                        for ui in range(4):
                            u = u0 + ui
                            for k in range(9):
                                ti = c4 % 4
                                c4 += 1
                                act(lt1[ti][:], identf[:], AF.Copy, ["identf", "lamP"], [("lt1", ti)], scale=lamP[:, k, 0, u:u + 1])
                                stt(lst[sl][:, ui, k * 128:(k + 1) * 128], jswapf[:], lamP[:, k, 1, u:u + 1], lt1[ti][:], ALU.mult, ALU.add,
                                    ["jswapf", "lamP", ("lt1", ti)], [("lst", sl)])
                        dma("sp", lamt_d[u0:u0 + 4].rearrange("u p n -> p u n"), lst[sl][:], reads=[("lst", sl)])
                qr = sb(ph, "qr", [128, 64], F32)
                qi = sb(ph, "qi", [128, 64], F32)
                den = sb(ph, "den", [128, 64], F32)
                ts("dve", a1r[:], a1r[:], -1.0, None, ALU.add, None, [a1r.name], [a1r.name])
                tt("dve", t1[:], lam[:, 0, :], lam[:, 0, :], ALU.mult, ["lam"], ["sq1"])
                tt("dve", t2[:], lam[:, 1, :], lam[:, 1, :], ALU.mult, ["lam"], ["sq2"])
                tt("dve", den[:], t1[:], t2[:], ALU.add, ["sq1", "sq2"], ["den"])
                P.op("dve", lambda e: e.reciprocal(out=den[:], in_=den[:]), ["den"], ["den"])
                tt("dve", t1[:], a1r[:], lam[:, 0, :], ALU.mult, [a1r.name, "lam"], ["sq1"])
                tt("dve", t2[:], a1i[:], lam[:, 1, :], ALU.mult, [a1i.name, "lam"], ["sq2"])
                tt("dve", qr[:], t1[:], t2[:], ALU.add, ["sq1", "sq2"], ["qr"])
                tt("dve", qr[:], qr[:], den[:], ALU.mult, ["qr", "den"], ["qr"])
                tt("dve", t1[:], a1i[:], lam[:, 0, :], ALU.mult, [a1i.name, "lam"], ["sq1"])
                tt("dve", t2[:], a1r[:], lam[:, 1, :], ALU.mult, [a1r.name, "lam"], ["sq2"])
                tt("dve", qi[:], t1[:], t2[:], ALU.subtract, ["sq1", "sq2"], ["qi"])
                tt("dve", qi[:], qi[:], den[:], ALU.mult, ["qi", "den"], ["qi"])
                Za = sb(ph, "Za", [128, 64, 16], F32)
                Zb = sb(ph, "Zb", [128, 64, 16], F32)
                tb1 = sb(ph, "tb1", [128, 64, 16], F32)
                qrb = qr[:].unsqueeze(2).to_broadcast([128, 64, 16])
                qib = qi[:].unsqueeze(2).to_broadcast([128, 64, 16])
                tt("dve", Za[:], Bd[:, 0], qrb, ALU.mult, ["Bd", "qr"], ["Za"])
                tt("dve", tb1[:], Bd[:, 1], qib, ALU.mult, ["Bd", "qi"], ["tb1"])
                tt("dve", Za[:], Za[:], tb1[:], ALU.subtract, ["Za", "tb1"], ["Za"])
                tt("dve", Zb[:], Bd[:, 0], qib, ALU.mult, ["Bd", "qi"], ["Zb"])
                tt("dve", tb1[:], Bd[:, 1], qrb, ALU.mult, ["Bd", "qr"], ["tb1"])
                tt("dve", Zb[:], Zb[:], tb1[:], ALU.add, ["Zb", "tb1"], ["Zb"])
                ts("dve", Zb[:], Zb[:], sg[:, 0:1], None, ALU.mult, None, ["Zb", "sg"], ["Zb"])
                ts("dve", Cd[:, 0], Cd[:, 0], sg[:, 1:2], None, ALU.mult, None, ["Cd"], ["Cd"])
                ts("dve", Cd[:, 1], Cd[:, 1], -1.0, None, ALU.mult, None, ["Cd"], ["Cd"])
                prod = [sb(ph, "prod%d" % i, [128, 8, 256], BF16) for i in range(7)]
                pt = [sb(ph, "pt%d" % i, [128, 8, 16, 16], F32) for i in range(4)]
                stage = sb(ph, "stage", [128, 8, 1536], BF16)
                wt = [sb(ph, "wkt%d" % i, [128, 256], F32) for i in range(4)]
                pcnt = [0]

                def product(dst, d, blk, g0, Z1, Z2, zkey):
                    i = pcnt[0] % 2
                    pcnt[0] += 1
                    eng = "dve" if i == 0 else "pool"
                    e1 = E1[:, d, blk * 16:(blk + 1) * 16, g0:g0 + 8].rearrange("p m g -> p g m").unsqueeze(3).to_broadcast([128, 8, 16, 16])
                    e2 = E2[:, d, blk * 16:(blk + 1) * 16, g0:g0 + 8].rearrange("p m g -> p g m").unsqueeze(3).to_broadcast([128, 8, 16, 16])
                    u0 = d * 32 + g0
                    z1 = Z1[:, u0:u0 + 8, :].unsqueeze(2).to_broadcast([128, 8, 16, 16])
                    z2 = Z2[:, u0:u0 + 8, :].unsqueeze(2).to_broadcast([128, 8, 16, 16])
                    ta, tb = pt[2 * i], pt[2 * i + 1]
                    zk = ["Za", "Zb"] if zkey == "Za" else [zkey]
                    tt(eng, ta[:], e1, z1, ALU.mult, ["E1"] + zk, [("pt", 2 * i)])
                    tt(eng, tb[:], e2, z2, ALU.mult, ["E2"] + zk, [("pt", 2 * i + 1)])
                    tt(eng, dst[:].rearrange("p g (m h) -> p g m h", h=16), ta[:], tb[:], ALU.add, [("pt", 2 * i), ("pt", 2 * i + 1)], [dst.name])

                for gb in range(4):
                    g0 = gb * 8
                    XQ0, KL0, KR0, WS0, XQ1, KR1, WS1 = prod
                    product(XQ0, 0, 0, g0, Za, Zb, "Za")
                    product(KL0, 0, 1, g0, Za, Zb, "Za")
                    product(KR0, 0, 2, g0, Cd[:, 0], Cd[:, 1], "Cd")
                    product(WS0, 0, 3, g0, Cd[:, 0], Cd[:, 1], "Cd")
                    product(XQ1, 1, 0, g0, Za, Zb, "Za")
                    product(KR1, 1, 1, g0, Cd[:, 0], Cd[:, 1], "Cd")
                    product(WS1, 1, 2, g0, Cd[:, 0], Cd[:, 1], "Cd")
                    for gl in range(8):
                        g = g0 + gl
                        for d, XQ in ((0, XQ0), (1, XQ1)):
                            bk, bkey = bank()
                            for jh in range(2):
                                mm(bk[:, jh * 128:(jh + 1) * 128], XQ[:, gl, jh * 128:(jh + 1) * 128], identb[:], True, True, [XQ.name, "identb"], [bkey])
                            cp("act", stage[:, gl, d * 256:(d + 1) * 256], bk[:, 0:256], [bkey], ["stage"])
                        cp("pool", stage[:, gl, 512:768], WS0[:, gl, :], [WS0.name], ["stage"])
                        cp("pool", stage[:, gl, 768:1024], WS1[:, gl, :], [WS1.name], ["stage"])
                        for jh in range(2):
                            bf, bfk = bank()
                            bb_, bbk = bank()
                            mm(bf[:, 0:256], KL0[:, gl, jh * 128:(jh + 1) * 128], KR0[:, gl, :], True, True, [KL0.name, KR0.name], [bfk])
                            mm(bb_[:, 0:256], XQ1[:, gl, jh * 128:(jh + 1) * 128], KR1[:, gl, :], True, True, [XQ1.name, KR1.name], [bbk])
                            tt("dve", wt[0][:], bf[:, 0:256], msk[:, 0, jh, :], ALU.mult, [bfk, "msk"], ["wkt0"])
                            tt("dve", wt[1][:], bb_[:, 0:256], msk[:, 1, jh, :], ALU.mult, [bbk, "msk"], ["wkt1"])
                            stt(wt[2][:], idp[:, jh, :], dT[:, g:g + 1], wt[0][:], ALU.mult, ALU.add, ["idp", "dT", "wkt0"], ["wkt2"])
                            tt("pool", stage[:, gl, 1024 + jh * 256:1024 + (jh + 1) * 256], wt[2][:], wt[1][:], ALU.add, ["wkt2", "wkt1"], ["stage"])
                    dma("sp", s5tab_d[g0:g0 + 8].rearrange("g p n -> p g n"), stage[:], reads=["stage"])
                P.barrier()

        def phase_s5(l, s):
            L = seqs[s].L
            NCH = L // 16
            nsteps = int(round(math.log2(NCH)))
            with ExitStack() as ph:
                selb = sb(ph, "selb", [128, 64, 128], BF16)
                selTb = sb(ph, "selTb", [128, 64, 128], BF16)
                dma("pool", selb[:], sel_d[:, :, :], writes=["selb"])
                dma("pool", selTb[:], selT_d[:, :, :], writes=["selTb"])
                uTb = [sb(ph, "uTb%d" % i, [128, L], BF16) for i in range(2)]
                tabr = [sb(ph, "tabr%d" % i, [128, 1536], BF16) for i in range(2)]
                Ug = [sb(ph, "Ug%d" % i, [128, 2, NCH], BF16) for i in range(2)]
                Sb = [sb(ph, "Sb%d" % i, [128, NCH], BF16) for i in range(8)]
                Sext = [[sb(ph, "Sext%d_%d" % (d, i), [128, NCH + 2], BF16) for i in range(2)] for d in range(2)]
                LamT = [sb(ph, "LamT%d" % i, [128, 9, 128], BF16) for i in range(4)]
                Yg = sb(ph, "Yg", [128, 8, 2, NCH], BF16)
                ysT = sb(ph, "ysT", [128, 4, L], BF16)
                zcol = sb(ph, "zcol", [128, 1], BF16)
                memset("dve", zcol[:], 0.0, ["zcol"])
                gcnt = 0
                ucnt = 0
                for cbk in range(4):
                    ub = uTb[cbk % 2]
                    ubk = ("uTb", cbk % 2)
                    dma("sp", ub[:], pT[s][1536 + cbk * 128:1536 + (cbk + 1) * 128, PAD:PAD + L], writes=[ubk])
                    for gp in range(0, 8, 2):
                        chains = []
                        ginfo = []
                        for gl in (gp, gp + 1):
                            g = cbk * 8 + gl
                            gs_ = gl % 2
                            tb = tabr[gs_]
                            tbk = ("tabr", gs_)
                            dma("sp", tb[:], s5tab_d[g], writes=[tbk])
                            ug = Ug[gs_]
                            ugk = ("Ug", gs_)
                            for jh in range(2):
                                bk, bkey = bank()
                                for jj in range(8):
                                    j = jh * 8 + jj
                                    mm(bk[:, 0:NCH], selb[:, gl * 8 + jj, :], ub[:, j:L:16], jj == 0, jj == 7, ["selb", ubk], [bkey])
                                cp("act" if jh == 0 else "dve", ug[:, jh, :], bk[:, 0:NCH], [bkey], [ugk])
                            ginfo.append((gl, g, gs_, tb, tbk, ug, ugk))
                            for d in range(2):
                                u = d * 32 + g
                                ci = gs_ * 2 + d
                                lt = LamT[ci]
                                ltk = ("LamT", ci)
                                dma("sp", lt[:].rearrange("p k q -> p (k q)"), lamt_d[u], writes=[ltk])
                                bk, bkey = bank()
                                init = s == "x"
                                mm(bk[:, 0:NCH], tb[:, d * 256:d * 256 + 128], ug[:, 0, :], True, False, [tbk, ugk], [bkey])
                                mm(bk[:, 0:NCH], tb[:, d * 256 + 128:d * 256 + 256], ug[:, 1, :], False, not init, [tbk, ugk], [bkey])
                                if init:
                                    col = 0 if d == 0 else NCH - 1
                                    mm(bk[:, col:col + 1], lt[:, 0, :], st0[:, u:u + 1], False, True, [ltk, "st0"], [bkey])
                                cur, curk = Sb[ci * 2], ("Sb", ci * 2)
                                cp("act" if d == 0 else "dve", cur[:], bk[:, 0:NCH], [bkey], [curk])
                                chains.append(dict(d=d, u=u, ci=ci, lt=lt, ltk=ltk, cur=cur, curk=curk, par=0, gs=gs_, init=init))
                        for k in range(nsteps):
                            sh = 1 << k
                            for ch in chains:
                                d, cur, curk, lt, ltk = ch["d"], ch["cur"], ch["curk"], ch["lt"], ch["ltk"]
                                bk, bkey = bank()
                                mm(bk[:, 0:NCH], identb[:], cur[:, 0:NCH], True, False, ["identb", curk], [bkey])
                                if d == 0:
                                    mm(bk[:, sh:NCH], lt[:, k, :], cur[:, 0:NCH - sh], False, True, [ltk, curk], [bkey])
                                else:
                                    mm(bk[:, 0:NCH - sh], lt[:, k, :], cur[:, sh:NCH], False, True, [ltk, curk], [bkey])
                                eng = "act" if (k + ch["ci"]) % 2 == 0 else "dve"
                                if k == nsteps - 1:
                                    off = 1 if d == 0 else 0
                                    sxt, sxtk = Sext[d][ch["gs"]], ("Sext", d, ch["gs"])
                                    cp(eng, sxt[:, off:off + NCH], bk[:, 0:NCH], [bkey], [sxtk])
                                else:
                                    ch["par"] = 1 - ch["par"]
                                    ni = ch["ci"] * 2 + ch["par"]
                                    nxt, nxtk = Sb[ni], ("Sb", ni)
                                    cp(eng, nxt[:], bk[:, 0:NCH], [bkey], [nxtk])
                                    ch["cur"], ch["curk"] = nxt, nxtk
                        for ch in chains:
                            d, u = ch["d"], ch["u"]
                            sxt, sxtk = Sext[d][ch["gs"]], ("Sext", d, ch["gs"])
                            ecol = 0 if d == 0 else NCH
                            if ch["init"]:
                                cp("dve", sxt[:, ecol:ecol + 1], st0[:, u:u + 1], ["st0"], [sxtk])
                            else:
                                cp("dve", sxt[:, ecol:ecol + 1], zcol[:], ["zcol"], [sxtk])
                                fcol = NCH if d == 0 else 0
                                cp("dve", st0[:, u:u + 1], sxt[:, fcol:fcol + 1], [sxtk], ["st0"])
                        for (gl, g, gs_, tb, tbk, ug, ugk) in ginfo:
                            sx = Sext[0][gs_], Sext[1][gs_]
                            sxk = ("Sext", 0, gs_), ("Sext", 1, gs_)
                            for th in range(2):
                                bk, bkey = bank()
                                cs = slice(th * 128, (th + 1) * 128)
                                mm(bk[:, 0:NCH], tb[:, 1024:1280][:, cs], ug[:, 0, :], True, False, [tbk, ugk], [bkey])
                                mm(bk[:, 0:NCH], tb[:, 1280:1536][:, cs], ug[:, 1, :], False, False, [tbk, ugk], [bkey])
                                mm(bk[:, 0:NCH], tb[:, 512:768][:, cs], sx[0][:, 0:NCH], False, False, [tbk, sxk[0]], [bkey])
                                mm(bk[:, 0:NCH], tb[:, 768:1024][:, cs], sx[1][:, 1:NCH + 1], False, True, [tbk, sxk[1]], [bkey])
                                act(Yg[:, gl, th, :], bk[:, 0:NCH], AF.Gelu_apprx_tanh, [bkey], [("Yg", gl)])
                    for j in range(16):
                        jh, jj = divmod(j, 8)
                        bk, bkey = bank()
                        for gl in range(8):
                            mm(bk[:, 0:NCH], selTb[:, gl * 8 + jj, :], Yg[:, gl, jh, :], gl == 0, gl == 7, ["selTb", ("Yg", gl)], [bkey])
                        cp("act" if j % 2 == 0 else "dve", ysT[:, cbk, j:L:16], bk[:, 0:NCH], [bkey], [("ysT", cbk)])
                wgl = sb(ph, "wgl", [128, 4, 512], BF16)
                bgl = sb(ph, "bgl", [128, 4], F32)
                dma("pool", wgl[:], w_glu_d[l].rearrange("(k p) n -> p k n", p=128), writes=["wgl"])
                dma("sp", bgl[:], s5bglu_d[l], writes=["bgl"])
                TB = min(512, L)
                sig = [sb(ph, "sig%d" % i, [128, TB], BF16) for i in range(2)]
                go = [sb(ph, "go%d" % i, [128, TB], BF16) for i in range(2)]
                cnt = 0
                for t0 in range(0, L, TB):
                    for mo in range(4):
                        sl = cnt % 2
                        cnt += 1
                        bk, bkey = bank()
                        for k in range(4):
                            mm(bk[:, 0:TB], wgl[:, k, mo * 128:(mo + 1) * 128], ysT[:, k, t0:t0 + TB], k == 0, k == 3, ["wgl", ("ysT", k)], [bkey])
                        act(sig[sl][:], bk[:, 0:TB], AF.Sigmoid, [bkey, "bgl"], [("sig", sl)], bias=bgl[:, mo:mo + 1])
                        tt("dve" if mo % 2 == 0 else "pool", go[sl][:], sig[sl][:], ysT[:, mo, t0:t0 + TB], ALU.mult, [("sig", sl), ("ysT", mo)], [("go", sl)])
                        dma("sp", yT[s][512 + mo * 128:512 + (mo + 1) * 128, t0:t0 + TB], go[sl][:], reads=[("go", sl)])
                P.barrier()

        def phase_feed_y():
            with ExitStack() as ph:
                k0, k1_ = feed_y
                nk = k1_ - k0
                t = sb(ph, "fyt", [128, nk, 2048], F32)
                tb = sb(ph, "fytb", [128, nk, 2048], BF16)
                for s, srcd in (("c", fy_c), ("x", fy_x)):
                    L = seqs[s].L
                    TB = min(2048, L)
                    for t0 in range(0, L, TB):
                        dma("sp", t[:, :, 0:TB], srcd.rearrange("(k p) t -> p k t", p=128)[:, k0:k1_, t0:t0 + TB], writes=["fyt"])
                        cp("dve", tb[:, :, 0:TB], t[:, :, 0:TB], ["fyt"], ["fytb"])
                        dma("sp", yT[s].rearrange("(k p) t -> p k t", p=128)[:, k0:k1_, t0:t0 + TB], tb[:, :, 0:TB], reads=["fytb"])
                P.barrier()

        for l in range(depth_run):
            phase_mod(l)
            phase_in(l)
            if not feed_y or feed_y[1] < 8:
                phase_s5tab(l)
                phase_s5(l, "c")
                phase_s5(l, "x")
            if not feed_y or feed_y[0] > 0:
                phase_hyprep(l)
                phase_hyena(l, "c")
                phase_hyena(l, "x")
            if feed_y:
                phase_feed_y()
            phase_out(l)
            phase_ffn(l)

        P.barrier()
        print("program ops:", P.nops, {e: len(v) for e, v in P.streams.items()})
        with nc.Block() as block:
            P.emit(block)
    return nc, dram_in


_CACHE = {}


def kernel(**inputs):
    x = np.asarray(inputs["x"], dtype=np.float32)
    ctx = np.asarray(inputs["ctx"], dtype=np.float32)
    c = np.asarray(inputs["c"], dtype=np.float32)
    c_ctx = np.asarray(inputs["c_ctx"], dtype=np.float32)
    B = x.shape[0]
    if "nc" not in _CACHE:
        _CACHE["nc"] = build_program()
    nc, _ = _CACHE["nc"]
    shared = layout_weights(inputs)
    shared.update(host_constants())
    in_maps = []
    for core in range(8):
        b = core % B
        m = dict(shared)
        m["x"] = _f32(x[b])
        m["ctx"] = _f32(ctx[b])
        cT = np.stack([c[b].reshape(8, 128).T, c_ctx.reshape(8, 128).T], axis=-1)
        m["cT"] = _f32(cT)
        in_maps.append(m)
    res = run_bass_kernel_spmd(nc, in_maps, core_ids=list(range(8)))
    out = np.stack([np.asarray(res.results[b]["y"], dtype=np.float32) for b in range(B)], axis=0)
    return out
```

```python
import math
import numpy as np
from contextlib import ExitStack
import concourse.bass as bass
import concourse.mybir as mybir
from concourse.bass_utils import run_bass_kernel_spmd

F32 = mybir.dt.float32
BF16 = mybir.dt.bfloat16
AF = mybir.ActivationFunctionType
ALU = mybir.AluOpType

D = 1024
SEQ = 8192
CTX = 256
DEPTH = 4
DFF = 2816
NFF = DFF // 128
PAD = 8
EPS = 1e-6
MAGIC = 12582912.0
TWO_PI = 2.0 * math.pi


class _Op:
    __slots__ = ("eng", "fn", "waits", "sem", "val", "dma")


class Prog:
    COMPUTE = ("pe", "act", "dve", "pool")
    NDMA = 8

    def __init__(self, nc, stack):
        self.nc = nc
        self.h = {"pe": nc.tensor, "act": nc.scalar, "dve": nc.vector, "pool": nc.gpsimd, "sp": nc.sync}
        self.streams = {e: [] for e in self.h}
        self.esem = {e: stack.enter_context(nc.semaphore("s_" + e)) for e in self.COMPUTE}
        self.ecnt = {e: 0 for e in self.COMPUTE}
        self.dsem, self.dcnt, self.drr = {}, {}, {}
        for q in ("sp", "pool", "act"):
            self.dsem[q] = [stack.enter_context(nc.semaphore("d_%s%d" % (q, i))) for i in range(self.NDMA)]
            self.dcnt[q] = [0] * self.NDMA
            self.drr[q] = 0
        self.lastw = {}
        self.readers = {}
        self.waited = {e: {} for e in self.h}
        self.nops = 0

    def _dep(self, eng, op, waits):
        if op is None:
            return
        if (not op.dma) and op.eng == eng and eng == "pe":
            return
        key = id(op.sem)
        if self.waited[eng].get(key, 0) >= op.val:
            return
        cur = waits.get(key)
        if cur is None or cur[1] < op.val:
            waits[key] = (op.sem, op.val)

    def op(self, eng, fn, reads=(), writes=(), dma=False):
        o = _Op()
        o.eng, o.fn, o.dma = eng, fn, dma
        waits = {}
        for r in reads:
            self._dep(eng, self.lastw.get(r), waits)
        for wk in writes:
            self._dep(eng, self.lastw.get(wk), waits)
            for rd in self.readers.get(wk, ()):
                self._dep(eng, rd, waits)
        if dma:
            i = self.drr[eng]
            self.drr[eng] = (i + 1) % self.NDMA
            sem = self.dsem[eng][i]
            prev = self.dcnt[eng][i]
            if prev > 0 and self.waited[eng].get(id(sem), 0) < prev:
                cur = waits.get(id(sem))
                if cur is None or cur[1] < prev:
                    waits[id(sem)] = (sem, prev)
            self.dcnt[eng][i] = prev + 16
            o.sem, o.val = sem, prev + 16
        else:
            self.ecnt[eng] += 1
            o.sem, o.val = self.esem[eng], self.ecnt[eng]
        for key, (s, v) in waits.items():
            self.waited[eng][key] = v
        o.waits = list(waits.values())
        self.streams[eng].append(o)
        for r in reads:
            self.readers.setdefault(r, []).append(o)
        for wk in writes:
            self.lastw[wk] = o
            self.readers[wk] = []
        self.nops += 1
        return o

    def barrier(self, engines=None):
        tot = {}
        for e in self.COMPUTE:
            if self.ecnt[e] > 0:
                tot[id(self.esem[e])] = (self.esem[e], self.ecnt[e])
        for q in self.dsem:
            for i in range(self.NDMA):
                if self.dcnt[q][i] > 0:
                    tot[id(self.dsem[q][i])] = (self.dsem[q][i], self.dcnt[q][i])
        for eng in (engines or list(self.h)):
            waits = []
            for key, (s, v) in tot.items():
                if self.waited[eng].get(key, 0) < v:
                    if eng in self.COMPUTE and s is self.esem[eng]:
                        continue
                    waits.append((s, v))
                    self.waited[eng][key] = v
            if waits:
                o = _Op()
                o.eng, o.fn, o.dma, o.sem, o.val = eng, None, False, None, 0
                o.waits = waits
                self.streams[eng].append(o)
        self.lastw = {}
        self.readers = {}

    def emit(self, block):
        def mk(ename):
            def body(e):
                for o in self.streams[ename]:
                    for (s, v) in o.waits:
                        e.wait_ge(s, v)
                    if o.fn is None:
                        continue
                    o.fn(e).then_inc(o.sem, 16 if o.dma else 1)
            return body
        block.sync(mk("sp"))
        block.scalar(mk("act"))
        block.vector(mk("dve"))
        block.gpsimd(mk("pool"))
        block.tensor(mk("pe"))


def _f32(a):
    return np.ascontiguousarray(np.asarray(a, dtype=np.float32))


MON = {"x": (128, 64, 65, SEQ), "c": (4, 2, 3, CTX)}


def host_constants():
    c = {}
    c["ident"] = _f32(np.eye(128))
    k2 = np.arange(128)[:, None]
    n2 = np.arange(128)[None, :]
    G = np.exp(2j * np.pi * k2 * n2 / 128.0)
    gt = np.zeros((128, 2, 2, 128))
    for hf in range(2):
        sl = slice(hf * 64, hf * 64 + 64)
        gt[:, hf, 0, 0:64] = G.real[:, sl]
        gt[:, hf, 0, 64:128] = G.imag[:, sl]
        gt[:, hf, 1, 0:64] = -G.imag[:, sl]
        gt[:, hf, 1, 64:128] = G.real[:, sl]
    c["gtab"] = _f32(gt)
    deltas = np.abs(np.linspace(math.log(1e-2) / 0.3, math.log(1e-2) / 1.5, 512))
    for s, (N1, NZ, NK, L) in MON.items():
        N = 128 * N1
        n1 = np.arange(NZ)[:, None]
        k1 = np.arange(NK)[None, :]
        ang = 2 * np.pi * n1 * k1 / N1
        c["f1tab_" + s] = _f32(np.concatenate([np.cos(ang), -np.sin(ang)], axis=1))
        f2 = np.zeros((NK, 128, 4, 128))
        nn = np.arange(128)[:, None]
        kk = np.arange(128)[None, :]
        for a in range(NK):
            M = np.exp(-2j * np.pi * nn * (a + N1 * kk) / N)
            f2[a, :, 0] = M.real
            f2[a, :, 1] = M.imag
            f2[a, :, 2] = -M.imag
            f2[a, :, 3] = -M.real
        c["f2tab_" + s] = _f32(f2.reshape(NK, 128, 512))
        ck = np.full(NK, 2.0)
        ck[0] = 1.0
        ck[NK - 1] = 1.0
        i2 = np.zeros((NK, 128, 2, NZ))
        kq = np.arange(NK)[:, None, None]
        nq = np.arange(128)[None, :, None]
        mq = np.arange(NZ)[None, None, :]
        R = (ck[:, None, None] / N) * np.exp(2j * np.pi * kq * (128 * mq + nq) / N)
        i2[:, :, 0, :] = R.real
        i2[:, :, 1, :] = -R.imag
        c["i2tab_" + s] = _f32(i2)
        t = np.arange(L) / (L - 1.0)
        wv = (2.0 * np.pi / L) * np.arange(L)
        f = np.linspace(1e-4, 15.0, 16)
        z = np.concatenate([t[:, None], np.cos(f[None, :] * wv[:, None]), -np.sin(f[None, :] * wv[:, None])], axis=1)
        c["zT_" + s] = _f32(z.T)
        c["d1_" + s] = _f32(np.exp(-(128.0 * np.arange(NZ)[:, None] / (L - 1.0)) * deltas[None, :]))
        c["d2_" + s] = _f32(np.exp(-(np.arange(128)[:, None] / (L - 1.0)) * deltas[None, :]))
    sel = np.zeros((128, 64, 128))
    for gl in range(8):
        for jj in range(8):
            for hi in range(16):
                sel[gl * 16 + hi, gl * 8 + jj, jj * 16 + hi] = 1.0
    c["sel"] = _f32(sel)
    c["selT"] = _f32(sel.transpose(2, 1, 0))
    J = np.zeros((128, 128))
    for p in range(64):
        J[p, 64 + p] = 1.0
        J[64 + p, p] = 1.0
    c["jswap"] = _f32(J)
    msk = np.zeros((128, 2, 2, 256))
    idp = np.zeros((128, 2, 256))
    for jh in range(2):
        for jj in range(8):
            j = jh * 8 + jj
            for hi in range(16):
                for t in range(16):
                    if t >= j:
                        msk[jj * 16 + hi, 0, jh, t * 16:(t + 1) * 16] = 1.0
                    if t <= j:
                        msk[jj * 16 + hi, 1, jh, t * 16:(t + 1) * 16] = 1.0
                idp[jj * 16 + hi, jh, j * 16 + hi] = 1.0
    c["s5msk"] = _f32(msk)
    c["s5idp"] = _f32(idp)
    sg = np.zeros((128, 4))
    sg[:64, 0], sg[64:, 0] = -1.0, 1.0
    sg[:64, 1], sg[64:, 1] = 1.0, -1.0
    sg[:64, 2], sg[64:, 2] = math.pi / 2, 0.0
    sg[:64, 3], sg[64:, 3] = 0.0, math.pi / 2
    c["s5sg"] = _f32(sg)
    return c


def layout_weights(inp):
    w = {}
    g = lambda k: np.asarray(inp[k], dtype=np.float32)
    w["w_ada"] = _f32(g("w_ada"))
    w["b_adaT"] = _f32(g("b_ada").reshape(DEPTH, 48, 128).transpose(0, 2, 1))
    w["ngT"] = _f32(g("norm_g").reshape(DEPTH, 4, 8, 128).transpose(0, 3, 1, 2))
    w["w_in"] = _f32(g("w_in"))
    w["w_out"] = _f32(g("w_out"))
    w["w_up"] = _f32(g("ffn_w_up"))
    w["w_down"] = _f32(g("ffn_w_down"))
    w["fcwT"] = _f32(g("ffn_conv_w").reshape(DEPTH, 9, NFF, 128).transpose(0, 3, 1, 2))
    w["fcbT"] = _f32(g("ffn_conv_b").reshape(DEPTH, NFF, 128).transpose(0, 2, 1))
    w["hswT"] = _f32(g("hy_short_w").reshape(DEPTH, 3, 12, 128).transpose(0, 3, 1, 2))
    w["hsbT"] = _f32(g("hy_short_b").reshape(DEPTH, 12, 128).transpose(0, 2, 1))
    w["hbiasT"] = _f32(g("hy_bias").reshape(DEPTH, 4, 128).transpose(0, 2, 1))
    w["f_win"] = _f32(g("filt_w_in"))
    w["f_whid"] = _f32(g("filt_w_hid").transpose(0, 2, 1, 3))
    w["f_b"] = _f32(np.concatenate([g("filt_b_in")[:, :, None], g("filt_b_hid").transpose(0, 2, 1)], axis=2))
    w["f_freq"] = _f32(g("filt_freq")[:, :, None])
    w["f_wout"] = _f32(g("filt_w_out"))
    dup = lambda a: np.concatenate([a, a], axis=1)
    lam = np.stack([g("s5_lam_re").reshape(DEPTH, 64, 64).transpose(0, 2, 1),
                    g("s5_lam_im").reshape(DEPTH, 64, 64).transpose(0, 2, 1)], axis=2)
    w["s5lam"] = _f32(dup(lam))
    w["s5ls"] = _f32(np.broadcast_to(g("s5_log_step").reshape(DEPTH, 1, 64), (DEPTH, 128, 64)))
    bb = np.stack([g("s5_b_re").reshape(DEPTH, 64, 64, 16).transpose(0, 2, 1, 3),
                   g("s5_b_im").reshape(DEPTH, 64, 64, 16).transpose(0, 2, 1, 3)], axis=2)
    w["s5b"] = _f32(dup(bb))
    cc = np.stack([g("s5_c_re").reshape(DEPTH, 64, 16, 64).transpose(0, 3, 1, 2),
                   g("s5_c_im").reshape(DEPTH, 64, 16, 64).transpose(0, 3, 1, 2)], axis=2)
    w["s5c"] = _f32(dup(cc))
    dd = g("s5_d").reshape(DEPTH, 32, 16).transpose(0, 2, 1)
    w["s5dT"] = _f32(np.tile(dd, (1, 8, 1)))
    w["s5bglu"] = _f32(g("s5_b_glu").reshape(DEPTH, 4, 128).transpose(0, 2, 1))
    w["w_glu"] = _f32(g("s5_w_glu"))
    return w


class Seq:
    pass


def build_program(depth_run=DEPTH, feed_y=False, dbg=False):
    nc = bass.Bass("TRN2", target_bir_lowering=False)
    dram_in = {}

    def din(name, shape, dt=F32):
        dram_in[name] = nc.dram_tensor(name, list(shape), dt, kind="ExternalInput").ap()
        return dram_in[name]

    def dscr(name, shape, dt):
        if dbg and name in dbg:
            return nc.dram_tensor(name, list(shape), dt, kind="ExternalOutput").ap()
        return nc.dram_tensor(name, list(shape), dt, kind="Internal").ap()

    x_d = din("x", [SEQ, D])
    ctx_d = din("ctx", [CTX, D])
    cT_d = din("cT", [128, 8, 2])
    w_ada_d = din("w_ada", [DEPTH, D, 6 * D])
    b_adaT_d = din("b_adaT", [DEPTH, 128, 48])
    ngT_d = din("ngT", [DEPTH, 128, 4, 8])
    w_in_d = din("w_in", [DEPTH, D, 2 * D])
    w_out_d = din("w_out", [DEPTH, D, D])
    w_up_d = din("w_up", [DEPTH, D, 2 * DFF])
    w_down_d = din("w_down", [DEPTH, DFF, D])
    fcwT_d = din("fcwT", [DEPTH, 128, 9, NFF])
    fcbT_d = din("fcbT", [DEPTH, 128, NFF])
    ident_d = din("ident", [128, 128])
    hswT_d = din("hswT", [DEPTH, 128, 3, 12])
    hsbT_d = din("hsbT", [DEPTH, 128, 12])
    hbiasT_d = din("hbiasT", [DEPTH, 128, 4])
    f_win_d = din("f_win", [DEPTH, 33, 64])
    f_whid_d = din("f_whid", [DEPTH, 64, 2, 64])
    f_b_d = din("f_b", [DEPTH, 64, 3])
    f_freq_d = din("f_freq", [DEPTH, 64, 1])
    f_wout_d = din("f_wout", [DEPTH, 64, 1024])
    gtab_d = din("gtab", [128, 2, 2, 128])
    s5lam_d = din("s5lam", [DEPTH, 128, 2, 64])
    s5ls_d = din("s5ls", [DEPTH, 128, 64])
    s5b_d = din("s5b", [DEPTH, 128, 2, 64, 16])
    s5c_d = din("s5c", [DEPTH, 128, 2, 64, 16])
    s5dT_d = din("s5dT", [DEPTH, 128, 32])
    s5bglu_d = din("s5bglu", [DEPTH, 128, 4])
    w_glu_d = din("w_glu", [DEPTH, 512, 512])
    sel_d = din("sel", [128, 64, 128])
    selT_d = din("selT", [128, 64, 128])
    jswap_d = din("jswap", [128, 128])
    s5msk_d = din("s5msk", [128, 2, 2, 256])
    s5idp_d = din("s5idp", [128, 2, 256])
    s5sg_d = din("s5sg", [128, 4])
    mon_d = {}
    for s_, (N1_, NZ_, NK_, L_) in MON.items():
        mon_d[s_] = dict(f1=din("f1tab_" + s_, [NZ_, 2 * NK_]), f2=din("f2tab_" + s_, [NK_, 128, 512]),
                         i2=din("i2tab_" + s_, [NK_, 128, 2, NZ_]), z=din("zT_" + s_, [33, L_]),
                         d1=din("d1_" + s_, [NZ_, 512]), d2=din("d2_" + s_, [128, 512]))
    if feed_y:
        fy_x = din("fy_x", [D, SEQ])
        fy_c = din("fy_c", [D, CTX])

    y_d = nc.dram_tensor("y", [SEQ, D], F32, kind="ExternalOutput").ap()
    dbg_out = {}

    cres_d = dscr("cres", [CTX, D], F32)
    xmix_d = dscr("xmix", [SEQ, D], F32)
    cmix_d = dscr("cmix", [CTX, D], F32)
    gv_d = dscr("gvec", [2, 2, D], F32)
    wupb_d = dscr("wupb", [D, 2 * DFF], BF16)
    pT = {"x": dscr("pT_x", [2 * D, SEQ + 2 * PAD], BF16), "c": dscr("pT_c", [2 * D, CTX + 2 * PAD], BF16)}
    yT = {"x": dscr("yT_x", [D, SEQ], BF16), "c": dscr("yT_c", [D, CTX], BF16)}
    s5tab_d = dscr("s5tab", [32, 128, 1536], BF16)
    lamt_d = dscr("lamt", [64, 128, 1152], BF16)
    dgff_d = dscr("dgff", [NFF, 128, 1152], BF16)
    f2b_d = {s_: dscr("f2b_" + s_, [MON[s_][2], 128, 512], BF16) for s_ in MON}
    i2b_d = {s_: dscr("i2b_" + s_, [MON[s_][2], 128, 2 * MON[s_][1]], BF16) for s_ in MON}
    uT = {"x": dscr("uT_x", [512, SEQ], BF16), "c": dscr("uT_c", [512, CTX], BF16)}
    x0T = {"x": dscr("x0T_x", [512, SEQ], BF16), "c": dscr("x0T_c", [512, CTX], BF16)}
    yhT = {"x": dscr("yhT_x", [512, SEQ], BF16), "c": dscr("yhT_c", [512, CTX], BF16)}

    with ExitStack() as st:
        P = Prog(nc, st)

        uniq = [0]

        def sb(stack, name, shape, dt):
            uniq[0] += 1
            return stack.enter_context(nc.sbuf_tensor("%s_%d" % (name, uniq[0]), list(shape), dt))

        ps = [st.enter_context(nc.psum_tensor("ps%d" % i, [128, 512], F32)) for i in range(8)]
        psc = [0]

        def bank():
            i = psc[0]
            psc[0] = (i + 1) % 8
            return ps[i], ("ps", i)

        def dma(q, out, in_, reads=(), writes=(), slow=False):
            if slow:
                P.op(q, lambda e: e.dma_start(out=out, in_=in_, allow_slow_non_contiguous=True), reads, writes, dma=True)
            else:
                P.op(q, lambda e: e.dma_start(out=out, in_=in_), reads, writes, dma=True)

        def mm(out, lhsT, rhs, start, stop, reads, writes):
            P.op("pe", lambda e: e.matmul(out, lhsT=lhsT, rhs=rhs, start=start, stop=stop), reads, writes)

        def act(out, in_, func, reads, writes, scale=None, bias=None, accum=None):
            kw = {}
            if scale is not None:
                kw["scale"] = scale
            if bias is not None:
                kw["bias"] = bias
            if accum is not None:
                kw["accum_out"] = accum
            P.op("act", lambda e: e.activation(out=out, in_=in_, func=func, **kw), reads, writes)

        def tt(eng, out, in0, in1, op, reads, writes):
            P.op(eng, lambda e: e.tensor_tensor(out=out, in0=in0, in1=in1, op=op), reads, writes)

        def ts(eng, out, in0, s1, s2, op0, op1, reads, writes):
            if op1 is None:
                P.op(eng, lambda e: e.tensor_scalar(out=out, in0=in0, scalar1=s1, scalar2=None, op0=op0), reads, writes)
            else:
                P.op(eng, lambda e: e.tensor_scalar(out=out, in0=in0, scalar1=s1, scalar2=s2, op0=op0, op1=op1), reads, writes)

        def stt(out, in0, scalar, in1, op0, op1, reads, writes):
            P.op("dve", lambda e: e.scalar_tensor_tensor(out=out, in0=in0, scalar=scalar, in1=in1, op0=op0, op1=op1), reads, writes)

        def cp(eng, out, in_, reads, writes):
            if eng == "act":
                act(out, in_, AF.Copy, reads, writes)
            else:
                P.op(eng, lambda e: e.tensor_copy(out=out, in_=in_), reads, writes)

        def memset(eng, ap, val, writes):
            P.op(eng, lambda e: e.memset(ap, val), (), writes)

        identf = sb(st, "identf", [128, 128], F32)
        identb = sb(st, "identb", [128, 128], BF16)
        cond = sb(st, "cond", [128, 8, 2], F32)
        modT = sb(st, "modT", [128, 48, 2], F32)
        ngt = sb(st, "ngt", [128, 4, 8], F32)
        vec = {}
        for s in ("x", "c"):
            for nm in ("gs1", "sh1", "gs3", "sh3", "ga2", "ga4"):
                vec[(s, nm)] = sb(st, "v_%s_%s" % (s, nm), [128, 8], F32)
        G2 = {s: sb(st, "G2" + s, [128, D], F32) for s in ("x", "c")}
        G4 = {s: sb(st, "G4" + s, [128, D], F32) for s in ("x", "c")}
        SI = {"x": 0, "c": 1}
        epsb = sb(st, "epsb", [128, 1], F32)

        dma("sp", identf[:], ident_d[:, :], writes=["identf"])
        cp("dve", identb[:], identf[:], ["identf"], ["identb"])
        memset("dve", epsb[:], EPS, ["epsb"])
        dma("sp", cond[:], cT_d[:, :, :], writes=["cond"])
        act(cond[:], cond[:], AF.Silu, ["cond"], ["cond"])
        with ExitStack() as ph:
            zt = sb(ph, "zt", [128, 16, PAD], BF16)
            memset("dve", zt[:], 0.0, ["zt"])
            for s, L in (("x", SEQ), ("c", CTX)):
                v = pT[s].rearrange("(m p) t -> p m t", p=128)
                dma("sp", v[:, :, 0:PAD], zt[:], reads=["zt"])
                dma("sp", v[:, :, PAD + L:PAD + L + PAD], zt[:], reads=["zt"])
            P.barrier()

        with ExitStack() as ph:
            stg = [sb(ph, "tcs%d" % i, [128, 8, 512], BF16) for i in range(2)]
            cnt = 0
            for s_ in MON:
                NK_ = MON[s_][2]
                NZ_ = MON[s_][1]
                for k0 in range(0, NK_, 8):
                    nk = min(8, NK_ - k0)
                    sl = cnt % 2
                    cnt += 1
                    dma("pool", stg[sl][:, 0:nk, :], mon_d[s_]["f2"][k0:k0 + nk].rearrange("k p n -> p k n"), writes=[("tcs", sl)])
                    dma("sp", f2b_d[s_][k0:k0 + nk].rearrange("k p n -> p k n"), stg[sl][:, 0:nk, :], reads=[("tcs", sl)])
                w_ = 2 * NZ_
                for n0 in range(0, 128, 32):
                    sl = cnt % 2
                    cnt += 1
                    v = stg[sl][0:NK_, :, :].rearrange("p a b -> p (a b)")[:, 0:32 * w_]
                    dma("pool", v, mon_d[s_]["i2"][:, n0:n0 + 32].rearrange("k n r z -> k (n r z)"), writes=[("tcs", sl)])
                    dma("sp", i2b_d[s_][:, n0:n0 + 32, :].rearrange("k n w -> k (n w)"), v, reads=[("tcs", sl)])
            P.barrier()

        seqs = {}
        for s, L, src in (("c", CTX, ctx_d), ("x", SEQ, x_d)):
            q = Seq()
            q.name, q.L, q.src = s, L, src
            q.res = y_d if s == "x" else cres_d
            q.mix = xmix_d if s == "x" else cmix_d
            seqs[s] = q

        def phase_mod(l):
            with ExitStack() as ph:
                wa = [sb(ph, "wa%d" % i, [128, 8, 512], F32) for i in range(6)]
                bad = sb(ph, "bad", [128, 48], F32)
                tmp = sb(ph, "modtmp", [128, 8], F32)
                dma("sp", bad[:], b_adaT_d[l], writes=["bad"])
                dma("sp", ngt[:], ngT_d[l], writes=["ngt"])
                wv = w_ada_d[l].rearrange("(k p) n -> p k n", p=128)
                bk, bkey = bank()
                for cb in range(12):
                    slot = cb % 6
                    dma(("sp", "pool", "act")[cb % 3], wa[slot][:], wv[:, :, cb * 512:(cb + 1) * 512], writes=[("wa", slot)])
                    for mi in range(4):
                        m = cb * 4 + mi
                        for k in range(8):
                            mm(bk[:, 2 * m:2 * m + 2], wa[slot][:, k, mi * 128:(mi + 1) * 128], cond[:, k, :],
                               k == 0, k == 7, [("wa", slot), "cond"], [bkey])
                tt("dve", modT[:], bk[:, 0:96].rearrange("p (m s) -> p m s", s=2),
                   bad[:].unsqueeze(2).to_broadcast([128, 48, 2]), ALU.add, [bkey, "bad"], ["modT"])
                for s in ("x", "c"):
                    si = SI[s]
                    md = lambda i: modT[:, i * 8:(i + 1) * 8, si]
                    stt(vec[(s, "gs1")][:], md(1), 1.0, ngt[:, 0, :], ALU.add, ALU.mult, ["modT", "ngt"], [("v", s, "gs1")])
                    cp("dve", vec[(s, "sh1")][:], md(0), ["modT"], [("v", s, "sh1")])
                    stt(vec[(s, "gs3")][:], md(4), 1.0, ngt[:, 2, :], ALU.add, ALU.mult, ["modT", "ngt"], [("v", s, "gs3")])
                    cp("dve", vec[(s, "sh3")][:], md(3), ["modT"], [("v", s, "sh3")])
                    tt("dve", vec[(s, "ga2")][:], md(2), ngt[:, 1, :], ALU.mult, ["modT", "ngt"], [("v", s, "ga2")])
                    tt("dve", vec[(s, "ga4")][:], md(5), ngt[:, 3, :], ALU.mult, ["modT", "ngt"], [("v", s, "ga4")])
                    for j, nm in enumerate(("ga2", "ga4")):
                        dma("sp", gv_d[si, j].rearrange("(k p) -> p k", p=128), vec[(s, nm)][:],
                            reads=[("v", s, nm)], writes=[("gv", si, j)], slow=True)
                    dma("sp", G2[s][:], gv_d[si, 0:1, :].to_broadcast([128, D]), reads=[("gv", si, 0)], writes=[("G2", s)])
                    dma("sp", G4[s][:], gv_d[si, 1:2, :].to_broadcast([128, D]), reads=[("gv", si, 1)], writes=[("G4", s)])
                P.barrier()

        def norm_rows(xt_ap, na, ss, rs, junk, keys_r, key_ss, key_rs):
            for a in range(na):
                act(junk[:], xt_ap[:, a, :], AF.Square, keys_r, [key_ss], accum=ss[:, a:a + 1])
            act(rs[:, 0:na], ss[:, 0:na], AF.Sqrt, [key_ss], [key_rs], scale=1.0 / D, bias=epsb[:])
            P.op("dve", lambda e: e.reciprocal(out=rs[:, 0:na], in_=rs[:, 0:na]), [key_rs], [key_rs])

        def phase_in(l):
            with ExitStack() as ph:
                win = sb(ph, "win", [128, 8, 2 * D], BF16)
                wv = w_in_d[l].rearrange("(k p) n -> p k n", p=128)
                for k in range(8):
                    dma("pool", win[:, k, :], wv[:, k, :], writes=[("win", k)])
                xts = [sb(ph, "xt%d" % i, [128, 4, D], F32) for i in range(2)]
                xss = [sb(ph, "xs%d" % i, [128, 4, D], BF16) for i in range(2)]
                xnT = [sb(ph, "xnT%d" % i, [128, 8, 512], BF16) for i in range(2)]
                pout = [sb(ph, "pout%d" % i, [128, 16, 512], BF16) for i in range(2)]
                junk = sb(ph, "junk", [128, D], BF16)
                ssq = [sb(ph, "ssq%d" % i, [128, 4], F32) for i in range(2)]
                rsd = [sb(ph, "rsd%d" % i, [128, 4], F32) for i in range(2)]
                cnt = 0
                for s in ("c", "x"):
                    q = seqs[s]
                    src = q.src if l == 0 else q.res
                    TB = min(512, q.L)
                    for t0 in range(0, q.L, TB):
                        nt = TB
                        na = nt // 128
                        sl = cnt % 2
                        cnt += 1
                        xt, xs = xts[sl], xss[sl]
                        dma("sp", xt[:, 0:na, :], src[t0:t0 + nt, :].rearrange("(a p) f -> p a f", p=128), writes=[("xt", sl)])
                        norm_rows(xt, na, ssq[sl], rsd[sl], junk, [("xt", sl)], ("ss", sl), ("rs", sl))
                        for a in range(na):
                            if a % 2 == 0:
                                ts("dve", xs[:, a, :], xt[:, a, :], rsd[sl][:, a:a + 1], None, ALU.mult, None,
                                   [("xt", sl), ("rs", sl)], [("xs", sl, a)])
                            else:
                                act(xs[:, a, :], xt[:, a, :], AF.Copy, [("xt", sl), ("rs", sl)], [("xs", sl, a)], scale=rsd[sl][:, a:a + 1])
                        for k in range(8):
                            bk, bkey = bank()
                            for a in range(na):
                                mm(bk[:, a * 128:(a + 1) * 128], xs[:, a, k * 128:(k + 1) * 128], identb[:], True, True,
                                   [("xs", sl, a), "identb"], [bkey])
                            act(xnT[sl][:, k, 0:nt], bk[:, 0:nt], AF.Identity, [bkey, ("v", s, "gs1"), ("v", s, "sh1")], [("xnT", sl, k)],
                                scale=vec[(s, "gs1")][:, k:k + 1], bias=vec[(s, "sh1")][:, k:k + 1])
                        for m in range(16):
                            bk, bkey = bank()
                            for k in range(8):
                                mm(bk[:, 0:nt], win[:, k, m * 128:(m + 1) * 128], xnT[sl][:, k, 0:nt], k == 0, k == 7,
                                   [("win", k), ("xnT", sl, k)], [bkey])
                            cp("act" if m % 2 == 0 else "dve", pout[sl][:, m, 0:nt], bk[:, 0:nt], [bkey], [("pout", sl)])
                        dma("sp", pT[s].rearrange("(m p) t -> p m t", p=128)[:, :, PAD + t0:PAD + t0 + nt], pout[sl][:, :, 0:nt],
                            reads=[("pout", sl)])
                P.barrier()

        def resid_epilogue(bk0, bk1, k0, k1, s, Gt, Gkey, xres_ap, xres_key, out_ap, out_key, scr):
            ss2, rs1, junk2, tmp = scr
            act(junk2[:, 0:512], bk0[:], AF.Square, [k0], ["ss2"], accum=ss2[:, 0:1])
            act(junk2[:, 512:1024], bk1[:], AF.Square, [k1], ["ss2"], accum=ss2[:, 1:2])
            tt("dve", rs1[:], ss2[:, 0:1], ss2[:, 1:2], ALU.add, ["ss2"], ["rs1"])
            act(rs1[:], rs1[:], AF.Sqrt, ["rs1"], ["rs1"], scale=1.0 / D, bias=epsb[:])
            P.op("dve", lambda e: e.reciprocal(out=rs1[:], in_=rs1[:]), ["rs1"], ["rs1"])
            stt(tmp[:, 0:512], bk0[:], rs1[:, 0:1], Gt[:, 0:512], ALU.mult, ALU.mult, [k0, "rs1", Gkey], ["etmp0"])
            stt(tmp[:, 512:1024], bk1[:], rs1[:, 0:1], Gt[:, 512:1024], ALU.mult, ALU.mult, [k1, "rs1", Gkey], ["etmp1"])
            tt("pool", out_ap, tmp[:], xres_ap, ALU.add, ["etmp0", "etmp1", xres_key], [out_key])

        def phase_out(l):
            with ExitStack() as ph:
                wo = sb(ph, "wo", [128, 8, D], BF16)
                wv = w_out_d[l].rearrange("(k p) n -> p k n", p=128)
                for k in range(8):
                    dma("pool", wo[:, k, :], wv[:, k, :], writes=[("wo", k)])
                yts = [sb(ph, "yt%d" % i, [128, 8, 512], BF16) for i in range(2)]
                xts = [sb(ph, "xt%d" % i, [128, 4, D], F32) for i in range(2)]
                xos = [sb(ph, "xo%d" % i, [128, 4, D], F32) for i in range(2)]
                scr = (sb(ph, "ss2", [128, 2], F32), sb(ph, "rs1", [128, 1], F32), sb(ph, "junk2", [128, D], BF16),
                       sb(ph, "etmp", [128, D], F32))
                cnt = 0
                for s in (("c", "x") if l < DEPTH - 1 else ("x",)):
                    q = seqs[s]
                    src = q.src if l == 0 else q.res
                    TB = min(512, q.L)
                    for t0 in range(0, q.L, TB):
                        nt = TB
                        na = nt // 128
                        sl = cnt % 2
                        cnt += 1
                        dma("sp", yts[sl][:, :, 0:nt], yT[s].rearrange("(k p) t -> p k t", p=128)[:, :, t0:t0 + nt], writes=[("yt", sl)])
                        dma("sp", xts[sl][:, 0:na, :], src[t0:t0 + nt, :].rearrange("(a p) f -> p a f", p=128), writes=[("xt", sl)])
                        for a in range(na):
                            b0, k0 = bank()
                            b1, k1 = bank()
                            for k in range(8):
                                mm(b0[:], yts[sl][:, k, a * 128:(a + 1) * 128], wo[:, k, 0:512], k == 0, k == 7, [("yt", sl), ("wo", k)], [k0])
                                mm(b1[:], yts[sl][:, k, a * 128:(a + 1) * 128], wo[:, k, 512:1024], k == 0, k == 7, [("yt", sl), ("wo", k)], [k1])
                            resid_epilogue(b0, b1, k0, k1, s, G2[s], ("G2", s), xts[sl][:, a, :], ("xt", sl), xos[sl][:, a, :], ("xo", sl, a), scr)
                        dma("sp", q.mix[t0:t0 + nt, :].rearrange("(a p) f -> p a f", p=128), xos[sl][:, 0:na, :],
                            reads=[("xo", sl, a) for a in range(na)])
                P.barrier()

        def phase_ffn(l):
            with ExitStack() as ph:
                wd = sb(ph, "wd", [128, NFF, D], BF16)
                wv = w_down_d[l].rearrange("(k p) n -> p k n", p=128)
                for k in range(NFF):
                    dma("pool", wd[:, k, :], wv[:, k, :], writes=[("wd", k)])
                stg = [sb(ph, "stg%d" % i, [128, 2 * DFF], BF16) for i in range(2)]
                for k in range(8):
                    dma("pool", stg[k % 2][:], w_up_d[l, k * 128:(k + 1) * 128, :], writes=[("stg", k % 2)])
                    dma("sp", wupb_d[k * 128:(k + 1) * 128, :], stg[k % 2][:], reads=[("stg", k % 2)])
                P.barrier()
                wupv = wupb_d.rearrange("(k p) n -> p k n", p=128)
                cw = sb(ph, "cw", [128, 9, NFF], F32)
                cb = sb(ph, "cb", [128, NFF], F32)
                dma("sp", cw[:], fcwT_d[l], writes=["cw"])
                dma("sp", cb[:], fcbT_d[l], writes=["cb"])
                with ExitStack() as p3:
                    dst_ = [sb(p3, "dgst%d" % i, [128, 2, 1152], BF16) for i in range(2)]
                    for m0 in range(0, NFF, 2):
                        sl = (m0 // 2) % 2
                        for mi in range(2):
                            for tap in range(9):
                                act(dst_[sl][:, mi, tap * 128:(tap + 1) * 128], identf[:], AF.Copy, ["identf", "cw"], [("dgst", sl)],
                                    scale=cw[:, tap, m0 + mi:m0 + mi + 1])
                        dma("sp", dgff_d[m0:m0 + 2].rearrange("m p n -> p m n"), dst_[sl][:], reads=[("dgst", sl)])
                    P.barrier()
                XE = 1152
                xnT = sb(ph, "fxnT", [128, 8, XE], BF16)
                hT = sb(ph, "hT", [128, NFF, 1024], BF16)
                xt1 = [sb(ph, "fxt%d" % i, [128, D], F32) for i in range(2)]
                xs1 = [sb(ph, "fxs%d" % i, [128, D], BF16) for i in range(2)]
                ss1 = [sb(ph, "fss%d" % i, [128, 1], F32) for i in range(2)]
                rs1b = [sb(ph, "frs%d" % i, [128, 1], F32) for i in range(2)]
                junk = sb(ph, "fjunk", [128, D], BF16)
                wg = [sb(ph, "wg%d" % i, [128, 8, 128], BF16) for i in range(2)]
                wvv = [sb(ph, "wv%d" % i, [128, 8, 128], BF16) for i in range(2)]
                dg = [sb(ph, "dg%d" % i, [128, 9, 128], BF16) for i in range(2)]
                gbuf = [sb(ph, "gbuf%d" % i, [128, 18, 64], BF16) for i in range(2)]
                gel = [sb(ph, "gel%d" % i, [128, 512], F32) for i in range(2)]
                xo = [sb(ph, "fxo%d" % i, [128, D], F32) for i in range(2)]
                scr = (sb(ph, "ss2", [128, 2], F32), sb(ph, "rs1", [128, 1], F32), sb(ph, "junk2", [128, D], BF16),
                       sb(ph, "etmp", [128, D], F32))
                tcnt = 0
                mcnt = 0
                for s in (("c", "x") if l < DEPTH - 1 else ("x",)):
                    q = seqs[s]
                    L = q.L
                    if s == "x":
                        ncols, BR = 64, 16
                    else:
                        ncols, BR = 256, 1
                    NT = BR * ncols
                    vert = s == "x"
                    for t0 in range(0, L, NT):
                        top = vert and t0 > 0
                        bot = vert and t0 + NT < L
                        e0 = t0 - (64 if top else 0)
                        e1 = t0 + NT + (64 if bot else 0)
                        tiles = []
                        tt0 = e0
                        while tt0 < e1:
                            n = min(128, e1 - tt0)
                            tiles.append((tt0, n))
                            tt0 += n
                        for (ta, n) in tiles:
                            sl = tcnt % 2
                            tcnt += 1
                            dma("sp", xt1[sl][0:n, :], q.mix[ta:ta + n, :], writes=[("fxt", sl)])
                            act(junk[0:n, :], xt1[sl][0:n, :], AF.Square, [("fxt", sl)], [("fss", sl)], accum=ss1[sl][0:n, :])
                            act(rs1b[sl][0:n, :], ss1[sl][0:n, :], AF.Sqrt, [("fss", sl)], [("frs", sl)], scale=1.0 / D, bias=epsb[0:n, :])
                            P.op("dve", (lambda r, n: lambda e: e.reciprocal(out=r[0:n, :], in_=r[0:n, :]))(rs1b[sl], n), [("frs", sl)], [("frs", sl)])
                            ts("dve", xs1[sl][0:n, :], xt1[sl][0:n, :], rs1b[sl][0:n, 0:1], None, ALU.mult, None, [("fxt", sl), ("frs", sl)], [("fxs", sl)])
                            c0 = ta - e0
                            for kk in range(2):
                                bk, bkey = bank()
                                for k4 in range(4):
                                    k = kk * 4 + k4
                                    mm(bk[:, k4 * 128:k4 * 128 + n], xs1[sl][0:n, k * 128:(k + 1) * 128], identb[0:n, 0:n], True, True,
                                       [("fxs", sl), "identb"], [bkey])
                                for k4 in range(4):
                                    k = kk * 4 + k4
                                    act(xnT[:, k, c0:c0 + n], bk[:, k4 * 128:k4 * 128 + n], AF.Identity,
                                        [bkey, ("v", s, "gs3"), ("v", s, "sh3")], [("fxnT", k)],
                                        scale=vec[(s, "gs3")][:, k:k + 1], bias=vec[(s, "sh3")][:, k:k + 1])
                        ne = e1 - e0
                        goff = 0 if top else 1
                        nrows_e = ne // ncols if vert else 1
                        for m in range(NFF):
                            sl = mcnt % 2
                            mcnt += 1
                            dma("sp", wg[sl][:], wupv[:, :, m * 128:(m + 1) * 128], writes=[("wg", sl)])
                            dma("sp", wvv[sl][:], wupv[:, :, DFF + m * 128:DFF + (m + 1) * 128], writes=[("wv", sl)])
                            dma("sp", dg[sl][:].rearrange("p t k -> p (t k)"), dgff_d[m], writes=[("dg", sl)])
                            gb = gbuf[sl]
                            gview = gb[:].rearrange("p r c -> p (r c)")
                            if vert:
                                if not top:
                                    memset("pool", gb[:, 0, :], 0.0, [("gbuf", sl)])
                                if not bot:
                                    memset("pool", gb[:, 17, :], 0.0, [("gbuf", sl)])
                            o = 0
                            while o < ne:
                                n = min(512, ne - o)
                                bk, bkey = bank()
                                for k in range(8):
                                    mm(bk[:, 0:n], wg[sl][:, k, :], xnT[:, k, o:o + n], k == 0, k == 7, [("wg", sl), ("fxnT", k)], [bkey])
                                gdst = gview[:, goff * 64 + o:goff * 64 + o + n] if vert else gview[:, o:o + n]
                                cp("act", gdst, bk[:, 0:n], [bkey], [("gbuf", sl)])
                                o += n
                            cen0 = t0 - e0
                            for sbk in range(0, NT, 512):
                                nsub = min(512, NT - sbk)
                                bv, bvkey = bank()
                                for k in range(8):
                                    mm(bv[:, 0:nsub], wvv[sl][:, k, :], xnT[:, k, cen0 + sbk:cen0 + sbk + nsub], k == 0, k == 7,
                                       [("wv", sl), ("fxnT", k)], [bvkey])
                                bc, bckey = bank()
                                if vert:
                                    r0 = 1 + sbk // 64
                                    nr = nsub // 64
                                    bc3 = bc[:, 0:nsub].rearrange("p (r c) -> p r c", c=64)
                                    first = True
                                    order = [4, 1, 7, 3, 5, 0, 2, 6, 8]
                                    for ti, tap in enumerate(order):
                                        dy, dx = tap // 3 - 1, tap % 3 - 1
                                        if dx == 0:
                                            rhs = gb[:, r0 + dy:r0 + dy + nr, :]
                                            out = bc3
                                        elif dx == -1:
                                            rhs = gb[:, r0 + dy:r0 + dy + nr, 0:63]
                                            out = bc3[:, :, 1:64]
                                        else:
                                            rhs = gb[:, r0 + dy:r0 + dy + nr, 1:64]
                                            out = bc3[:, :, 0:63]
                                        mm(out, dg[sl][:, tap, :], rhs, first, ti == 8, [("dg", sl), ("gbuf", sl)], [bckey])
                                        first = False
                                else:
                                    for ti, tap in enumerate([4, 3, 5]):
                                        dx = tap % 3 - 1
                                        if dx == 0:
                                            rhs, out = gview[:, 0:nsub], bc[:, 0:nsub]
                                        elif dx == -1:
                                            rhs, out = gview[:, 0:nsub - 1], bc[:, 1:nsub]
                                        else:
                                            rhs, out = gview[:, 1:nsub], bc[:, 0:nsub - 1]
                                        mm(out, dg[sl][:, tap, :], rhs, ti == 0, ti == 2, [("dg", sl), ("gbuf", sl)], [bckey])
                                gsl = (mcnt + sbk // 512) % 2
                                act(gel[gsl][:, 0:nsub], bc[:, 0:nsub], AF.Gelu_apprx_tanh, [bckey, "cb"], [("gel", gsl)], bias=cb[:, m:m + 1])
                                tt("dve", hT[:, m, sbk:sbk + nsub], bv[:, 0:nsub], gel[gsl][:, 0:nsub], ALU.mult, [bvkey, ("gel", gsl)], [("hT", m)])
                        for a in range(NT // 128):
                            sl = a % 2
                            b0, k0 = bank()
                            b1, k1 = bank()
                            for k in range(NFF):
                                mm(b0[:], hT[:, k, a * 128:(a + 1) * 128], wd[:, k, 0:512], k == 0, k == NFF - 1, [("hT", k), ("wd", k)], [k0])
                                mm(b1[:], hT[:, k, a * 128:(a + 1) * 128], wd[:, k, 512:1024], k == 0, k == NFF - 1, [("hT", k), ("wd", k)], [k1])
                            ta = t0 + a * 128
                            dma("sp", xt1[sl][:], q.mix[ta:ta + 128, :], writes=[("fxt", sl)])
                            resid_epilogue(b0, b1, k0, k1, s, G4[s], ("G4", s), xt1[sl][:], ("fxt", sl), xo[sl][:], ("fxo", sl), scr)
                            dma("sp", q.res[ta:ta + 128, :], xo[sl][:], reads=[("fxo", sl)])
                P.barrier()


        def sin_rr(out_ap, arg, tmpk, n_part, keys_arg, key_out, key_tmp):
            ts("dve", tmpk, arg, 1.0 / TWO_PI, MAGIC, ALU.mult, ALU.add, keys_arg, [key_tmp])
            ts("dve", tmpk, tmpk, MAGIC, -TWO_PI, ALU.subtract, ALU.mult, [key_tmp], [key_tmp])
            tt("dve", tmpk, arg, tmpk, ALU.add, keys_arg + [key_tmp], [key_tmp])
            ts("dve", tmpk, tmpk, 3.1415925, -3.1415925, ALU.min, ALU.max, [key_tmp], [key_tmp])
            act(out_ap, tmpk, AF.Sin, [key_tmp], [key_out])

        def phase_hyprep(l):
            with ExitStack() as ph:
                hw = sb(ph, "hw", [128, 3, 12], F32)
                hb = sb(ph, "hb", [128, 12], F32)
                dma("sp", hw[:], hswT_d[l], writes=["hw"])
                dma("sp", hb[:], hsbT_d[l], writes=["hb"])
                dgs = sb(ph, "hdg", [128, 36, 128], BF16)
                for ch in range(12):
                    for d in range(3):
                        act(dgs[:, ch * 3 + d, :], identf[:], AF.Copy, ["identf", "hw"], ["hdg"], scale=hw[:, d, ch:ch + 1])
                pin = [sb(ph, "pin%d" % i, [128, 3, 514], BF16) for i in range(2)]
                vB = [sb(ph, "vB%d" % i, [128, 512], F32) for i in range(2)]
                uo = [sb(ph, "uo%d" % i, [128, 512], BF16) for i in range(2)]
                xo = [sb(ph, "x0o%d" % i, [128, 512], BF16) for i in range(2)]
                cnt = 0
                for s in (("c", "x") if l < DEPTH - 1 else ("x",)):
                    L = seqs[s].L
                    TB = min(512, L)
                    for cc in range(4):
                        for t0 in range(0, L, TB):
                            sl = cnt % 2
                            cnt += 1
                            for j in range(3):
                                r0 = j * 512 + cc * 128
                                dma("sp" if j != 1 else "pool", pin[sl][:, j, 0:TB + 2], pT[s][r0:r0 + 128, PAD + t0 - 1:PAD + t0 + TB + 1], writes=[("pin", sl, j)])
                            bks = []
                            for j in range(3):
                                bk, bkey = bank()
                                ch = j * 4 + cc
                                for d in range(3):
                                    mm(bk[:, 0:TB], dgs[:, ch * 3 + d, :], pin[sl][:, j, d:d + TB], d == 0, d == 2, ["hdg", ("pin", sl, j)], [bkey])
                                bks.append((bk, bkey))
                            act(vB[sl][:, 0:TB], bks[2][0][:, 0:TB], AF.Identity, [bks[2][1], "hb"], [("vB", sl)], bias=hb[:, 8 + cc:9 + cc])
                            act(xo[sl][:, 0:TB], bks[0][0][:, 0:TB], AF.Identity, [bks[0][1], "hb"], [("x0o", sl)], bias=hb[:, cc:cc + 1])
                            stt(uo[sl][:, 0:TB], bks[1][0][:, 0:TB], hb[:, 4 + cc:5 + cc], vB[sl][:, 0:TB], ALU.add, ALU.mult,
                                [bks[1][1], "hb", ("vB", sl)], [("uo", sl)])
                            dma("sp", uT[s][cc * 128:(cc + 1) * 128, t0:t0 + TB], uo[sl][:, 0:TB], reads=[("uo", sl)])
                            dma("sp", x0T[s][cc * 128:(cc + 1) * 128, t0:t0 + TB], xo[sl][:, 0:TB], reads=[("x0o", sl)])
                P.barrier()

        def phase_hyena(l, s):
            N1, NZ, NK, L = MON[s]
            md = mon_d[s]
            NK2 = 2 * NK
            with ExitStack() as ph:
                hidT = sb(ph, "hidT", [64, L], BF16)
                woutb = sb(ph, "woutb", [64, 1024], BF16)
                d1 = sb(ph, "d1", [NZ, 512], F32)
                d2 = sb(ph, "d2", [128, 512], F32)
                f1t = sb(ph, "f1t", [NZ, NK2], BF16)
                gtb = sb(ph, "gtb", [128, 2, 2, 128], BF16)
                hbias = sb(ph, "hbias", [128, 4], F32)
                dma("pool", woutb[:], f_wout_d[l], writes=["woutb"])
                dma("sp", d1[:], md["d1"][:, :], writes=["d1"])
                dma("sp", d2[:], md["d2"][:, :], writes=["d2"])
                dma("pool", f1t[:], md["f1"][:, :], writes=["f1t"])
                dma("pool", gtb[:], gtab_d[:, :, :, :], writes=["gtb"])
                with ExitStack() as p2:
                    zt = sb(p2, "zt", [33, L], F32)
                    fwin = sb(p2, "fwin", [33, 64], F32)
                    fwh = sb(p2, "fwh", [64, 2, 64], F32)
                    fb = sb(p2, "fb", [64, 3], F32)
                    ffr = sb(p2, "ffr", [64, 1], F32)
                    frb = sb(p2, "frb", [64, 3], F32)
                    dma("sp", zt[:], md["z"][:, :], writes=["zt"])
                    dma("sp", fwin[:], f_win_d[l], writes=["fwin"])
                    dma("sp", fwh[:], f_whid_d[l], writes=["fwh"])
                    dma("sp", fb[:], f_b_d[l], writes=["fb"])
                    dma("sp", ffr[:], f_freq_d[l], writes=["ffr"])
                    ts("dve", frb[:], fb[:], ffr[:, 0:1], None, ALU.mult, None, ["fb", "ffr"], ["frb"])
                    args = [sb(p2, "farg%d" % i, [64, 512], F32) for i in range(2)]
                    tmpk = [sb(p2, "ftmp%d" % i, [64, 512], F32) for i in range(2)]
                    hts = [sb(p2, "fh%d" % i, [64, 512], F32) for i in range(4)]
                    TB = min(512, L)
                    cnt = 0
                    for t0 in range(0, L, TB):
                        prev = None
                        for li in range(3):
                            sl = cnt % 2
                            cnt += 1
                            bk, bkey = bank()
                            if li == 0:
                                mm(bk[0:64, 0:TB], fwin[:], zt[:, t0:t0 + TB], True, True, ["fwin", "zt"], [bkey])
                            else:
                                mm(bk[0:64, 0:TB], fwh[:, li - 1, :], prev[0][:, 0:TB], True, True, ["fwh", prev[1]], [bkey])
                            act(args[sl][:, 0:TB], bk[0:64, 0:TB], AF.Identity, [bkey, "ffr", "frb"], [("farg", sl)],
                                scale=ffr[:, 0:1], bias=frb[:, li:li + 1])
                            if li < 2:
                                hsl = (t0 // TB * 2 + li) % 4
                                sin_rr(hts[hsl][:, 0:TB], args[sl][:, 0:TB], tmpk[sl][:, 0:TB], 64, [("farg", sl)], ("fh", hsl), ("ftmp", sl))
                                prev = (hts[hsl], ("fh", hsl))
                            else:
                                sin_rr(hidT[:, t0:t0 + TB], args[sl][:, 0:TB], tmpk[sl][:, 0:TB], 64, [("farg", sl)], "hidT", ("ftmp", sl))
                    P.barrier()
                dma("sp", hbias[:], hbiasT_d[l], writes=["hbias"])
                Xraw = sb(ph, "monX", [NZ, 128 * 128], BF16)
                X = Xraw[:, :].rearrange("p (c n) -> p c n", n=128)
                Xf = Xraw[:, :].rearrange("p (n c) -> p n c", c=128)
                A = sb(ph, "monA", [128, 128, 2, NK], BF16)
                Kf = sb(ph, "monK", [128, 2, NK, 128], BF16)
                BB = sb(ph, "monB", [128, 2 * 65 * 128], BF16)
                Ab = BB[:, 0:2 * NK * 128].rearrange("p (c r k) -> p c r k", r=2, k=NK, c=128)
                B0 = BB[0:NK, 0:2 * 64 * 128].rearrange("p (c r n) -> p c r n", r=2, n=64, c=128)
                f2r = [sb(ph, "f2r%d" % i, [128, 4, 128], BF16) for i in range(8)]
                i2r = [sb(ph, "i2r%d" % i, [NK, 4, 2, NZ], BF16) for i in range(4)]
                tm = [sb(ph, "fmt%d" % i, [128, 512], F32) for i in range(4)]
                KB = 4
                nkb = (NK + KB - 1) // KB
                Akeys = [("A", kb) for kb in range(nkb)]
                f2c = [0]
                i2c = [0]
                nb1 = max(1, min(128, 512 // NK2))

                def f1(dst, dstkeys, scale_d2, cg):
                    c = 0
                    i = 0
                    while c < 128:
                        nb = min(nb1, 128 - c)
                        bk, bkey = bank()
                        for q in range(nb):
                            lhs = Xf[0:NZ, :, c + q] if scale_d2 else X[0:NZ, c + q, :]
                            mm(bk[:, q * NK2:(q + 1) * NK2], lhs, f1t[0:NZ, :], True, True, ["X", "f1t"], [bkey])
                        out = dst[:, c:c + nb, :, :].rearrange("p c r k -> p c (r k)")
                        src = bk[:, 0:nb * NK2].rearrange("p (i x) -> p i x", x=NK2)
                        if scale_d2:
                            tt("dve", out, src, d2[:, cg * 128 + c:cg * 128 + c + nb].unsqueeze(2).to_broadcast([128, nb, NK2]), ALU.mult,
                               [bkey, "d2"], dstkeys)
                        else:
                            cp("act" if i % 2 == 0 else "dve", out, src, [bkey], dstkeys)
                        c += nb
                        i += 1

                def load_f2(k1):
                    sl = f2c[0] % 8
                    f2c[0] += 1
                    dma("sp", f2r[sl][:].rearrange("p v k -> p (v k)"), f2b_d[s][k1], writes=[("f2r", sl)])
                    return f2r[sl], ("f2r", sl)

                for cg in range(4):
                    for dr in range(2):
                        for n20 in range(0, 128, 4):
                            bk, bkey = bank()
                            for q in range(4):
                                n2 = n20 + q
                                mm(bk[0:NZ, q * 128:(q + 1) * 128], hidT[:, n2:L:128], woutb[:, dr * 512 + cg * 128:dr * 512 + (cg + 1) * 128],
                                   True, True, ["hidT", "woutb"], [bkey])
                            tt("dve", Xf[0:NZ, n20:n20 + 4, :],
                               bk[0:NZ, 0:512].rearrange("p (q c) -> p q c", c=128),
                               d1[0:NZ, cg * 128:(cg + 1) * 128].unsqueeze(1).to_broadcast([NZ, 4, 128]), ALU.mult, [bkey, "d1"], ["X"])
                        if dr == 1:
                            memset("dve", Xf[0:1, 0:1, :], 0.0, ["X"])
                        if dr == 0:
                            f1(A, Akeys, True, cg)
                        else:
                            f1(Ab, ["B0"], True, cg)
                    for kb in range(nkb):
                        k1s = list(range(kb * KB, min(NK, (kb + 1) * KB)))
                        br, brk = bank()
                        bi, bik = bank()
                        for q, k1 in enumerate(k1s):
                            f2, f2k = load_f2(k1)
                            cs = slice(q * 128, (q + 1) * 128)
                            mm(br[:, cs], f2[:, 0, :], A[:, :, 0, k1], True, False, [f2k, ("A", kb)], [brk])
                            mm(br[:, cs], f2[:, 2, :], A[:, :, 1, k1], False, False, [f2k, ("A", kb)], [brk])
                            mm(br[:, cs], f2[:, 0, :], Ab[:, :, 0, k1], False, False, [f2k, "B0"], [brk])
                            mm(br[:, cs], f2[:, 2, :], Ab[:, :, 1, k1], False, True, [f2k, "B0"], [brk])
                            mm(bi[:, cs], f2[:, 1, :], A[:, :, 0, k1], True, False, [f2k, ("A", kb)], [bik])
                            mm(bi[:, cs], f2[:, 0, :], A[:, :, 1, k1], False, False, [f2k, ("A", kb)], [bik])
                            mm(bi[:, cs], f2[:, 2, :], Ab[:, :, 0, k1], False, False, [f2k, "B0"], [bik])
                            mm(bi[:, cs], f2[:, 3, :], Ab[:, :, 1, k1], False, True, [f2k, "B0"], [bik])
                        nn = len(k1s) * 128
                        cp("act", Kf[:, 0, k1s[0]:k1s[-1] + 1, :].rearrange("p k c -> p (k c)"), br[:, 0:nn], [brk], [("Kf", kb)])
                        cp("act", Kf[:, 1, k1s[0]:k1s[-1] + 1, :].rearrange("p k c -> p (k c)"), bi[:, 0:nn], [bik], [("Kf", kb)])
                    dma("sp", X[0:NZ, :, :], uT[s][cg * 128:(cg + 1) * 128, :].rearrange("c (a b) -> a c b", b=128), writes=["X"])
                    f1(A, Akeys, False, cg)
                    for kb in range(nkb):
                        k1s = list(range(kb * KB, min(NK, (kb + 1) * KB)))
                        br, brk = bank()
                        bi, bik = bank()
                        for q, k1 in enumerate(k1s):
                            f2, f2k = load_f2(k1)
                            cs = slice(q * 128, (q + 1) * 128)
                            mm(br[:, cs], f2[:, 0, :], A[:, :, 0, k1], True, False, [f2k, ("A", kb)], [brk])
                            mm(br[:, cs], f2[:, 2, :], A[:, :, 1, k1], False, True, [f2k, ("A", kb)], [brk])
                            mm(bi[:, cs], f2[:, 1, :], A[:, :, 0, k1], True, False, [f2k, ("A", kb)], [bik])
                            mm(bi[:, cs], f2[:, 0, :], A[:, :, 1, k1], False, True, [f2k, ("A", kb)], [bik])
                        nn = len(k1s) * 128
                        kr = Kf[:, 0, k1s[0]:k1s[-1] + 1, :].rearrange("p k c -> p (k c)")
                        ki = Kf[:, 1, k1s[0]:k1s[-1] + 1, :].rearrange("p k c -> p (k c)")
                        tt("dve", tm[0][:, 0:nn], br[:, 0:nn], kr, ALU.mult, [brk, ("Kf", kb)], [("tm", 0)])
                        tt("dve", tm[1][:, 0:nn], bi[:, 0:nn], ki, ALU.mult, [bik, ("Kf", kb)], [("tm", 1)])
                        tt("dve", tm[2][:, 0:nn], br[:, 0:nn], ki, ALU.mult, [brk, ("Kf", kb)], [("tm", 2)])
                        tt("dve", tm[3][:, 0:nn], bi[:, 0:nn], kr, ALU.mult, [bik, ("Kf", kb)], [("tm", 3)])
                        v3 = lambda t: t[:, 0:nn].rearrange("p (k c) -> p k c", c=128)
                        tt("pool", A[:, :, 0, k1s[0]:k1s[-1] + 1].rearrange("p c k -> p k c"), v3(tm[0]), v3(tm[1]), ALU.subtract,
                           [("tm", 0), ("tm", 1)], [("A", kb)])
                        tt("pool", A[:, :, 1, k1s[0]:k1s[-1] + 1].rearrange("p c k -> p k c"), v3(tm[2]), v3(tm[3]), ALU.add,
                           [("tm", 2), ("tm", 3)], [("A", kb)])
                    for hf in range(2):
                        for c0 in range(0, 128, 4):
                            bk, bkey = bank()
                            for q in range(4):
                                cs = slice(q * 128, (q + 1) * 128)
                                mm(bk[0:NK, cs], A[:, c0 + q, 0, 0:NK], gtb[:, hf, 0, :], True, False, Akeys + ["gtb"], [bkey])
                                mm(bk[0:NK, cs], A[:, c0 + q, 1, 0:NK], gtb[:, hf, 1, :], False, True, Akeys + ["gtb"], [bkey])
                            cp("act" if (c0 // 4) % 2 == 0 else "dve", B0[:, c0:c0 + 4, :, :].rearrange("p c r n -> p (c r n)"), bk[0:NK, 0:512], [bkey], ["B0"])
                        for n20 in range(0, 64, 4):
                            sl = i2c[0] % 4
                            i2c[0] += 1
                            ng = hf * 64 + n20
                            dma("sp", i2r[sl][:].rearrange("k q r z -> k q (r z)"), i2b_d[s][:, ng:ng + 4, :], writes=[("i2r", sl)])
                            bk, bkey = bank()
                            for q in range(4):
                                cs = slice(q * 128, (q + 1) * 128)
                                mm(bk[0:NZ, cs], i2r[sl][:, q, 0, :], B0[:, :, 0, n20 + q], True, False, [("i2r", sl), "B0"], [bkey])
                                mm(bk[0:NZ, cs], i2r[sl][:, q, 1, :], B0[:, :, 1, n20 + q], False, True, [("i2r", sl), "B0"], [bkey])
                            cp("act" if (n20 // 4) % 2 == 0 else "dve", X[0:NZ, :, ng:ng + 4].rearrange("p c q -> p q c"),
                               bk[0:NZ, 0:512].rearrange("p (q c) -> p q c", c=128), [bkey], ["X"])
                    dma("sp", yhT[s][cg * 128:(cg + 1) * 128, :].rearrange("c (a b) -> a c b", b=128), X[0:NZ, :, :], reads=["X"])
                P.barrier()
            with ExitStack() as ph:
                hbias = sb(ph, "hbias2", [128, 4], F32)
                dma("sp", hbias[:], hbiasT_d[l], writes=["hbias"])
                TB = min(2048, L)
                ty = [sb(ph, "ey%d" % i, [128, TB], BF16) for i in range(2)]
                tu = [sb(ph, "eu%d" % i, [128, TB], BF16) for i in range(2)]
                tx = [sb(ph, "ex%d" % i, [128, TB], BF16) for i in range(2)]
                t1 = [sb(ph, "et%d" % i, [128, TB], F32) for i in range(2)]
                to = [sb(ph, "eo%d" % i, [128, TB], BF16) for i in range(2)]
                cnt = 0
                for cc in range(4):
                    for t0 in range(0, L, TB):
                        sl = cnt % 2
                        cnt += 1
                        rs_ = slice(cc * 128, (cc + 1) * 128)
                        dma("sp", ty[sl][:], yhT[s][rs_, t0:t0 + TB], writes=[("ey", sl)])
                        dma("sp", tu[sl][:], uT[s][rs_, t0:t0 + TB], writes=[("eu", sl)])
                        dma("pool", tx[sl][:], x0T[s][rs_, t0:t0 + TB], writes=[("ex", sl)])
                        stt(t1[sl][:], tu[sl][:], hbias[:, cc:cc + 1], ty[sl][:], ALU.mult, ALU.add, [("eu", sl), ("ey", sl), "hbias"], [("et", sl)])
                        tt("pool", to[sl][:], t1[sl][:], tx[sl][:], ALU.mult, [("et", sl), ("ex", sl)], [("eo", sl)])
                        dma("sp", yT[s][rs_, t0:t0 + TB], to[sl][:], reads=[("eo", sl)])
                P.barrier()

        lamP = sb(st, "lamP", [128, 9, 2, 64], F32)
        st0 = sb(st, "st0", [128, 64], BF16)
        jswapf = sb(st, "jswapf", [128, 128], F32)
        sg = sb(st, "sg", [128, 4], F32)
        dma("sp", jswapf[:], jswap_d[:, :], writes=["jswapf"])
        dma("sp", sg[:], s5sg_d[:, :], writes=["sg"])
        SLOT_M = {0: [15 - i for i in range(16)] + [-i for i in range(16)] + [i for i in range(16)] + [i + 1 for i in range(16)],
                  1: [i for i in range(16)] + [-i for i in range(16)] + [16 - i for i in range(16)] + [0] * 16}

        def phase_s5tab(l):
            with ExitStack() as ph:
                lam = sb(ph, "lam", [128, 2, 64], F32)
                ls = sb(ph, "ls", [128, 64], F32)
                Bd = sb(ph, "Bd", [128, 2, 64, 16], F32)
                Cd = sb(ph, "Cd", [128, 2, 64, 16], F32)
                dT = sb(ph, "dT", [128, 32], F32)
                msk = sb(ph, "msk", [128, 2, 2, 256], F32)
                idp = sb(ph, "idp", [128, 2, 256], F32)
                dma("sp", lam[:], s5lam_d[l], writes=["lam"])
                dma("sp", ls[:], s5ls_d[l], writes=["ls"])
                dma("sp", Bd[:], s5b_d[l], writes=["Bd"])
                dma("sp", Cd[:], s5c_d[l], writes=["Cd"])
                dma("sp", dT[:], s5dT_d[l], writes=["dT"])
                dma("sp", msk[:], s5msk_d[:, :, :, :], writes=["msk"])
                dma("sp", idp[:], s5idp_d[:, :, :], writes=["idp"])
                xr = sb(ph, "xr", [128, 64], F32)
                xi = sb(ph, "xi", [128, 64], F32)
                act(ls[:], ls[:], AF.Exp, ["ls"], ["ls"])
                tt("dve", xr[:], lam[:, 0, :], ls[:], ALU.mult, ["lam", "ls"], ["xr"])
                tt("dve", xi[:], lam[:, 1, :], ls[:], ALU.mult, ["lam", "ls"], ["xi"])
                E1 = sb(ph, "E1", [128, 2, 64, 32], F32)
                E2 = sb(ph, "E2", [128, 2, 64, 32], F32)
                p2 = ExitStack()
                MAG = sb(p2, "MAG", [128, 2, 64, 32], F32)
                TK = sb(p2, "TK", [128, 2, 64, 32], F32)
                for d in range(2):
                    for sl_, m in enumerate(SLOT_M[d]):
                        us = slice(d * 32, d * 32 + 32)
                        act(MAG[:, d, sl_, :], xr[:, us], AF.Exp, ["xr"], ["MAG"], scale=float(m))
                        ts("dve", E1[:, d, sl_, :], xi[:, us], float(m), sg[:, 2:3], ALU.mult, ALU.add, ["xi", "sg"], ["E1"])
                        ts("dve", E2[:, d, sl_, :], xi[:, us], float(m), sg[:, 3:4], ALU.mult, ALU.add, ["xi", "sg"], ["E2"])
                fl = lambda t: t[:].rearrange("p d s u -> p (d s u)")
                sin_rr(fl(E1), fl(E1), fl(TK), 128, ["E1"], "E1", "TK")
                sin_rr(fl(E2), fl(E2), fl(TK), 128, ["E2"], "E2", "TK")
                tt("dve", fl(E1), fl(E1), fl(MAG), ALU.mult, ["E1", "MAG"], ["E1"])
                tt("pool", fl(E2), fl(E2), fl(MAG), ALU.mult, ["E2", "MAG"], ["E2"])
                P.barrier()
                p2.close()
                a1r = sb(ph, "a1r", [128, 64], F32)
                a1i = sb(ph, "a1i", [128, 64], F32)
                Lr = sb(ph, "Lr", [128, 64], F32)
                Li = sb(ph, "Li", [128, 64], F32)
                for d in range(2):
                    us = slice(d * 32, d * 32 + 32)
                    s1 = 33 if d == 0 else 1
                    s16 = 63 if d == 0 else 32
                    for (dst, slot) in ((a1r, s1), (Lr, s16)):
                        cp("dve", dst[0:64, us], E1[0:64, d, slot, :], ["E1"], [dst.name])
                        cp("dve", dst[64:128, us], E2[64:128, d, slot, :], ["E2"], [dst.name])
                    for (dst, slot) in ((a1i, s1), (Li, s16)):
                        cp("dve", dst[0:64, us], E2[0:64, d, slot, :], ["E2"], [dst.name])
                        cp("dve", dst[64:128, us], E1[64:128, d, slot, :], ["E1"], [dst.name])
                t1 = sb(ph, "sq1", [128, 64], F32)
                t2 = sb(ph, "sq2", [128, 64], F32)
                for k in range(9):
                    cp("dve", lamP[:, k, 0, :], Lr[:], [Lr.name], ["lamP"])
                    ts("dve", lamP[:, k, 1, :], Li[:], sg[:, 1:2], None, ALU.mult, None, [Li.name, "sg"], ["lamP"])
                    if k < 8:
                        tt("dve", t1[:], Lr[:], Lr[:], ALU.mult, [Lr.name], ["sq1"])
                        tt("dve", t2[:], Li[:], Li[:], ALU.mult, [Li.name], ["sq2"])
                        tt("dve", Li[:], Lr[:], Li[:], ALU.mult, [Lr.name, Li.name], [Li.name])
                        ts("dve", Li[:], Li[:], 2.0, None, ALU.mult, None, [Li.name], [Li.name])
                        tt("dve", Lr[:], t1[:], t2[:], ALU.subtract, ["sq1", "sq2"], [Lr.name])
                with ExitStack() as p3:
                    lst = [sb(p3, "lst%d" % i, [128, 4, 1152], BF16) for i in range(2)]
                    lt1 = [sb(p3, "lt1_%d" % i, [128, 128], F32) for i in range(4)]
                    c4 = 0
                    for u0 in range(0, 64, 4):
                        sl = (u0 // 4) % 2
                        for ui in range(4):
                            u = u0 + ui
                            for k in range(9):
                                ti = c4 % 4
                                c4 += 1
                                act(lt1[ti][:], identf[:], AF.Copy, ["identf", "lamP"], [("lt1", ti)], scale=lamP[:, k, 0, u:u + 1])
                                stt(lst[sl][:, ui, k * 128:(k + 1) * 128], jswapf[:], lamP[:, k, 1, u:u + 1], lt1[ti][:], ALU.mult, ALU.add,
                                    ["jswapf", "lamP", ("lt1", ti)], [("lst", sl)])
                        dma("sp", lamt_d[u0:u0 + 4].rearrange("u p n -> p u n"), lst[sl][:], reads=[("lst", sl)])
                qr = sb(ph, "qr", [128, 64], F32)
                qi = sb(ph, "qi", [128, 64], F32)
                den = sb(ph, "den", [128, 64], F32)
                ts("dve", a1r[:], a1r[:], -1.0, None, ALU.add, None, [a1r.name], [a1r.name])
                tt("dve", t1[:], lam[:, 0, :], lam[:, 0, :], ALU.mult, ["lam"], ["sq1"])
                tt("dve", t2[:], lam[:, 1, :], lam[:, 1, :], ALU.mult, ["lam"], ["sq2"])
                tt("dve", den[:], t1[:], t2[:], ALU.add, ["sq1", "sq2"], ["den"])
                P.op("dve", lambda e: e.reciprocal(out=den[:], in_=den[:]), ["den"], ["den"])
                tt("dve", t1[:], a1r[:], lam[:, 0, :], ALU.mult, [a1r.name, "lam"], ["sq1"])
                tt("dve", t2[:], a1i[:], lam[:, 1, :], ALU.mult, [a1i.name, "lam"], ["sq2"])
                tt("dve", qr[:], t1[:], t2[:], ALU.add, ["sq1", "sq2"], ["qr"])
                tt("dve", qr[:], qr[:], den[:], ALU.mult, ["qr", "den"], ["qr"])
                tt("dve", t1[:], a1i[:], lam[:, 0, :], ALU.mult, [a1i.name, "lam"], ["sq1"])
                tt("dve", t2[:], a1r[:], lam[:, 1, :], ALU.mult, [a1r.name, "lam"], ["sq2"])
                tt("dve", qi[:], t1[:], t2[:], ALU.subtract, ["sq1", "sq2"], ["qi"])
                tt("dve", qi[:], qi[:], den[:], ALU.mult, ["qi", "den"], ["qi"])
                Za = sb(ph, "Za", [128, 64, 16], F32)
                Zb = sb(ph, "Zb", [128, 64, 16], F32)
                tb1 = sb(ph, "tb1", [128, 64, 16], F32)
                qrb = qr[:].unsqueeze(2).to_broadcast([128, 64, 16])
                qib = qi[:].unsqueeze(2).to_broadcast([128, 64, 16])
                tt("dve", Za[:], Bd[:, 0], qrb, ALU.mult, ["Bd", "qr"], ["Za"])
                tt("dve", tb1[:], Bd[:, 1], qib, ALU.mult, ["Bd", "qi"], ["tb1"])
                tt("dve", Za[:], Za[:], tb1[:], ALU.subtract, ["Za", "tb1"], ["Za"])
                tt("dve", Zb[:], Bd[:, 0], qib, ALU.mult, ["Bd", "qi"], ["Zb"])
                tt("dve", tb1[:], Bd[:, 1], qrb, ALU.mult, ["Bd", "qr"], ["tb1"])
                tt("dve", Zb[:], Zb[:], tb1[:], ALU.add, ["Zb", "tb1"], ["Zb"])
                ts("dve", Zb[:], Zb[:], sg[:, 0:1], None, ALU.mult, None, ["Zb", "sg"], ["Zb"])
                ts("dve", Cd[:, 0], Cd[:, 0], sg[:, 1:2], None, ALU.mult, None, ["Cd"], ["Cd"])
                ts("dve", Cd[:, 1], Cd[:, 1], -1.0, None, ALU.mult, None, ["Cd"], ["Cd"])
                prod = [sb(ph, "prod%d" % i, [128, 8, 256], BF16) for i in range(7)]
                pt = [sb(ph, "pt%d" % i, [128, 8, 16, 16], F32) for i in range(4)]
                stage = sb(ph, "stage", [128, 8, 1536], BF16)
                wt = [sb(ph, "wkt%d" % i, [128, 256], F32) for i in range(4)]
                pcnt = [0]

                def product(dst, d, blk, g0, Z1, Z2, zkey):
                    i = pcnt[0] % 2
                    pcnt[0] += 1
                    eng = "dve" if i == 0 else "pool"
                    e1 = E1[:, d, blk * 16:(blk + 1) * 16, g0:g0 + 8].rearrange("p m g -> p g m").unsqueeze(3).to_broadcast([128, 8, 16, 16])
                    e2 = E2[:, d, blk * 16:(blk + 1) * 16, g0:g0 + 8].rearrange("p m g -> p g m").unsqueeze(3).to_broadcast([128, 8, 16, 16])
                    u0 = d * 32 + g0
                    z1 = Z1[:, u0:u0 + 8, :].unsqueeze(2).to_broadcast([128, 8, 16, 16])
                    z2 = Z2[:, u0:u0 + 8, :].unsqueeze(2).to_broadcast([128, 8, 16, 16])
                    ta, tb = pt[2 * i], pt[2 * i + 1]
                    zk = ["Za", "Zb"] if zkey == "Za" else [zkey]
                    tt(eng, ta[:], e1, z1, ALU.mult, ["E1"] + zk, [("pt", 2 * i)])
                    tt(eng, tb[:], e2, z2, ALU.mult, ["E2"] + zk, [("pt", 2 * i + 1)])
                    tt(eng, dst[:].rearrange("p g (m h) -> p g m h", h=16), ta[:], tb[:], ALU.add, [("pt", 2 * i), ("pt", 2 * i + 1)], [dst.name])

                for gb in range(4):
                    g0 = gb * 8
                    XQ0, KL0, KR0, WS0, XQ1, KR1, WS1 = prod
                    product(XQ0, 0, 0, g0, Za, Zb, "Za")
                    product(KL0, 0, 1, g0, Za, Zb, "Za")
                    product(KR0, 0, 2, g0, Cd[:, 0], Cd[:, 1], "Cd")
                    product(WS0, 0, 3, g0, Cd[:, 0], Cd[:, 1], "Cd")
                    product(XQ1, 1, 0, g0, Za, Zb, "Za")
                    product(KR1, 1, 1, g0, Cd[:, 0], Cd[:, 1], "Cd")
                    product(WS1, 1, 2, g0, Cd[:, 0], Cd[:, 1], "Cd")
                    for gl in range(8):
                        g = g0 + gl
                        for d, XQ in ((0, XQ0), (1, XQ1)):
                            bk, bkey = bank()
                            for jh in range(2):
                                mm(bk[:, jh * 128:(jh + 1) * 128], XQ[:, gl, jh * 128:(jh + 1) * 128], identb[:], True, True, [XQ.name, "identb"], [bkey])
                            cp("act", stage[:, gl, d * 256:(d + 1) * 256], bk[:, 0:256], [bkey], ["stage"])
                        cp("pool", stage[:, gl, 512:768], WS0[:, gl, :], [WS0.name], ["stage"])
                        cp("pool", stage[:, gl, 768:1024], WS1[:, gl, :], [WS1.name], ["stage"])
                        for jh in range(2):
                            bf, bfk = bank()
                            bb_, bbk = bank()
                            mm(bf[:, 0:256], KL0[:, gl, jh * 128:(jh + 1) * 128], KR0[:, gl, :], True, True, [KL0.name, KR0.name], [bfk])
                            mm(bb_[:, 0:256], XQ1[:, gl, jh * 128:(jh + 1) * 128], KR1[:, gl, :], True, True, [XQ1.name, KR1.name], [bbk])
                            tt("dve", wt[0][:], bf[:, 0:256], msk[:, 0, jh, :], ALU.mult, [bfk, "msk"], ["wkt0"])
                            tt("dve", wt[1][:], bb_[:, 0:256], msk[:, 1, jh, :], ALU.mult, [bbk, "msk"], ["wkt1"])
                            stt(wt[2][:], idp[:, jh, :], dT[:, g:g + 1], wt[0][:], ALU.mult, ALU.add, ["idp", "dT", "wkt0"], ["wkt2"])
                            tt("pool", stage[:, gl, 1024 + jh * 256:1024 + (jh + 1) * 256], wt[2][:], wt[1][:], ALU.add, ["wkt2", "wkt1"], ["stage"])
                    dma("sp", s5tab_d[g0:g0 + 8].rearrange("g p n -> p g n"), stage[:], reads=["stage"])
                P.barrier()

        def phase_s5(l, s):
            L = seqs[s].L
            NCH = L // 16
            nsteps = int(round(math.log2(NCH)))
            with ExitStack() as ph:
                selb = sb(ph, "selb", [128, 64, 128], BF16)
                selTb = sb(ph, "selTb", [128, 64, 128], BF16)
                dma("pool", selb[:], sel_d[:, :, :], writes=["selb"])
                dma("pool", selTb[:], selT_d[:, :, :], writes=["selTb"])
                uTb = [sb(ph, "uTb%d" % i, [128, L], BF16) for i in range(2)]
                tabr = [sb(ph, "tabr%d" % i, [128, 1536], BF16) for i in range(2)]
                Ug = [sb(ph, "Ug%d" % i, [128, 2, NCH], BF16) for i in range(2)]
                Sb = [sb(ph, "Sb%d" % i, [128, NCH], BF16) for i in range(8)]
                Sext = [[sb(ph, "Sext%d_%d" % (d, i), [128, NCH + 2], BF16) for i in range(2)] for d in range(2)]
                LamT = [sb(ph, "LamT%d" % i, [128, 9, 128], BF16) for i in range(4)]
                Yg = sb(ph, "Yg", [128, 8, 2, NCH], BF16)
                ysT = sb(ph, "ysT", [128, 4, L], BF16)
                zcol = sb(ph, "zcol", [128, 1], BF16)
                memset("dve", zcol[:], 0.0, ["zcol"])
                gcnt = 0
                ucnt = 0
                for cbk in range(4):
                    ub = uTb[cbk % 2]
                    ubk = ("uTb", cbk % 2)
                    dma("sp", ub[:], pT[s][1536 + cbk * 128:1536 + (cbk + 1) * 128, PAD:PAD + L], writes=[ubk])
                    for gp in range(0, 8, 2):
                        chains = []
                        ginfo = []
                        for gl in (gp, gp + 1):
                            g = cbk * 8 + gl
                            gs_ = gl % 2
                            tb = tabr[gs_]
                            tbk = ("tabr", gs_)
                            dma("sp", tb[:], s5tab_d[g], writes=[tbk])
                            ug = Ug[gs_]
                            ugk = ("Ug", gs_)
                            for jh in range(2):
                                bk, bkey = bank()
                                for jj in range(8):
                                    j = jh * 8 + jj
                                    mm(bk[:, 0:NCH], selb[:, gl * 8 + jj, :], ub[:, j:L:16], jj == 0, jj == 7, ["selb", ubk], [bkey])
                                cp("act" if jh == 0 else "dve", ug[:, jh, :], bk[:, 0:NCH], [bkey], [ugk])
                            ginfo.append((gl, g, gs_, tb, tbk, ug, ugk))
                            for d in range(2):
                                u = d * 32 + g
                                ci = gs_ * 2 + d
                                lt = LamT[ci]
                                ltk = ("LamT", ci)
                                dma("sp", lt[:].rearrange("p k q -> p (k q)"), lamt_d[u], writes=[ltk])
                                bk, bkey = bank()
                                init = s == "x"
                                mm(bk[:, 0:NCH], tb[:, d * 256:d * 256 + 128], ug[:, 0, :], True, False, [tbk, ugk], [bkey])
                                mm(bk[:, 0:NCH], tb[:, d * 256 + 128:d * 256 + 256], ug[:, 1, :], False, not init, [tbk, ugk], [bkey])
                                if init:
                                    col = 0 if d == 0 else NCH - 1
                                    mm(bk[:, col:col + 1], lt[:, 0, :], st0[:, u:u + 1], False, True, [ltk, "st0"], [bkey])
                                cur, curk = Sb[ci * 2], ("Sb", ci * 2)
                                cp("act" if d == 0 else "dve", cur[:], bk[:, 0:NCH], [bkey], [curk])
                                chains.append(dict(d=d, u=u, ci=ci, lt=lt, ltk=ltk, cur=cur, curk=curk, par=0, gs=gs_, init=init))
                        for k in range(nsteps):
                            sh = 1 << k
                            for ch in chains:
                                d, cur, curk, lt, ltk = ch["d"], ch["cur"], ch["curk"], ch["lt"], ch["ltk"]
                                bk, bkey = bank()
                                mm(bk[:, 0:NCH], identb[:], cur[:, 0:NCH], True, False, ["identb", curk], [bkey])
                                if d == 0:
                                    mm(bk[:, sh:NCH], lt[:, k, :], cur[:, 0:NCH - sh], False, True, [ltk, curk], [bkey])
                                else:
                                    mm(bk[:, 0:NCH - sh], lt[:, k, :], cur[:, sh:NCH], False, True, [ltk, curk], [bkey])
                                eng = "act" if (k + ch["ci"]) % 2 == 0 else "dve"
                                if k == nsteps - 1:
                                    off = 1 if d == 0 else 0
                                    sxt, sxtk = Sext[d][ch["gs"]], ("Sext", d, ch["gs"])
                                    cp(eng, sxt[:, off:off + NCH], bk[:, 0:NCH], [bkey], [sxtk])
                                else:
                                    ch["par"] = 1 - ch["par"]
                                    ni = ch["ci"] * 2 + ch["par"]
                                    nxt, nxtk = Sb[ni], ("Sb", ni)
                                    cp(eng, nxt[:], bk[:, 0:NCH], [bkey], [nxtk])
                                    ch["cur"], ch["curk"] = nxt, nxtk
                        for ch in chains:
                            d, u = ch["d"], ch["u"]
                            sxt, sxtk = Sext[d][ch["gs"]], ("Sext", d, ch["gs"])
                            ecol = 0 if d == 0 else NCH
                            if ch["init"]:
                                cp("dve", sxt[:, ecol:ecol + 1], st0[:, u:u + 1], ["st0"], [sxtk])
                            else:
                                cp("dve", sxt[:, ecol:ecol + 1], zcol[:], ["zcol"], [sxtk])
                                fcol = NCH if d == 0 else 0
                                cp("dve", st0[:, u:u + 1], sxt[:, fcol:fcol + 1], [sxtk], ["st0"])
                        for (gl, g, gs_, tb, tbk, ug, ugk) in ginfo:
                            sx = Sext[0][gs_], Sext[1][gs_]
                            sxk = ("Sext", 0, gs_), ("Sext", 1, gs_)
                            for th in range(2):
                                bk, bkey = bank()
                                cs = slice(th * 128, (th + 1) * 128)
                                mm(bk[:, 0:NCH], tb[:, 1024:1280][:, cs], ug[:, 0, :], True, False, [tbk, ugk], [bkey])
                                mm(bk[:, 0:NCH], tb[:, 1280:1536][:, cs], ug[:, 1, :], False, False, [tbk, ugk], [bkey])
                                mm(bk[:, 0:NCH], tb[:, 512:768][:, cs], sx[0][:, 0:NCH], False, False, [tbk, sxk[0]], [bkey])
                                mm(bk[:, 0:NCH], tb[:, 768:1024][:, cs], sx[1][:, 1:NCH + 1], False, True, [tbk, sxk[1]], [bkey])
                                act(Yg[:, gl, th, :], bk[:, 0:NCH], AF.Gelu_apprx_tanh, [bkey], [("Yg", gl)])
                    for j in range(16):
                        jh, jj = divmod(j, 8)
                        bk, bkey = bank()
                        for gl in range(8):
                            mm(bk[:, 0:NCH], selTb[:, gl * 8 + jj, :], Yg[:, gl, jh, :], gl == 0, gl == 7, ["selTb", ("Yg", gl)], [bkey])
                        cp("act" if j % 2 == 0 else "dve", ysT[:, cbk, j:L:16], bk[:, 0:NCH], [bkey], [("ysT", cbk)])
                wgl = sb(ph, "wgl", [128, 4, 512], BF16)
                bgl = sb(ph, "bgl", [128, 4], F32)
                dma("pool", wgl[:], w_glu_d[l].rearrange("(k p) n -> p k n", p=128), writes=["wgl"])
                dma("sp", bgl[:], s5bglu_d[l], writes=["bgl"])
                TB = min(512, L)
                sig = [sb(ph, "sig%d" % i, [128, TB], BF16) for i in range(2)]
                go = [sb(ph, "go%d" % i, [128, TB], BF16) for i in range(2)]
                cnt = 0
                for t0 in range(0, L, TB):
                    for mo in range(4):
                        sl = cnt % 2
                        cnt += 1
                        bk, bkey = bank()
                        for k in range(4):
                            mm(bk[:, 0:TB], wgl[:, k, mo * 128:(mo + 1) * 128], ysT[:, k, t0:t0 + TB], k == 0, k == 3, ["wgl", ("ysT", k)], [bkey])
                        act(sig[sl][:], bk[:, 0:TB], AF.Sigmoid, [bkey, "bgl"], [("sig", sl)], bias=bgl[:, mo:mo + 1])
                        tt("dve" if mo % 2 == 0 else "pool", go[sl][:], sig[sl][:], ysT[:, mo, t0:t0 + TB], ALU.mult, [("sig", sl), ("ysT", mo)], [("go", sl)])
                        dma("sp", yT[s][512 + mo * 128:512 + (mo + 1) * 128, t0:t0 + TB], go[sl][:], reads=[("go", sl)])
                P.barrier()

        def phase_feed_y():
            with ExitStack() as ph:
                k0, k1_ = feed_y
                nk = k1_ - k0
                t = sb(ph, "fyt", [128, nk, 2048], F32)
                tb = sb(ph, "fytb", [128, nk, 2048], BF16)
                for s, srcd in (("c", fy_c), ("x", fy_x)):
                    L = seqs[s].L
                    TB = min(2048, L)
                    for t0 in range(0, L, TB):
                        dma("sp", t[:, :, 0:TB], srcd.rearrange("(k p) t -> p k t", p=128)[:, k0:k1_, t0:t0 + TB], writes=["fyt"])
                        cp("dve", tb[:, :, 0:TB], t[:, :, 0:TB], ["fyt"], ["fytb"])
                        dma("sp", yT[s].rearrange("(k p) t -> p k t", p=128)[:, k0:k1_, t0:t0 + TB], tb[:, :, 0:TB], reads=["fytb"])
                P.barrier()

        for l in range(depth_run):
            phase_mod(l)
            phase_in(l)
            if not feed_y or feed_y[1] < 8:
                phase_s5tab(l)
                phase_s5(l, "c")
                phase_s5(l, "x")
            if not feed_y or feed_y[0] > 0:
                phase_hyprep(l)
                if l < DEPTH - 1:
                    phase_hyena(l, "c")
                phase_hyena(l, "x")
            if feed_y:
                phase_feed_y()
            phase_out(l)
            phase_ffn(l)

        P.barrier()
        print("program ops:", P.nops, {e: len(v) for e, v in P.streams.items()})
        with nc.Block() as block:
            P.emit(block)
    return nc, dram_in


_CACHE = {}


def kernel(**inputs):
    x = np.asarray(inputs["x"], dtype=np.float32)
    ctx = np.asarray(inputs["ctx"], dtype=np.float32)
    c = np.asarray(inputs["c"], dtype=np.float32)
    c_ctx = np.asarray(inputs["c_ctx"], dtype=np.float32)
    B = x.shape[0]
    if "nc" not in _CACHE:
        _CACHE["nc"] = build_program()
    nc, _ = _CACHE["nc"]
    shared = layout_weights(inputs)
    shared.update(host_constants())
    in_maps = []
    for core in range(8):
        b = core % B
        m = dict(shared)
        m["x"] = _f32(x[b])
        m["ctx"] = _f32(ctx[b])
        cT = np.stack([c[b].reshape(8, 128).T, c_ctx.reshape(8, 128).T], axis=-1)
        m["cT"] = _f32(cT)
        in_maps.append(m)
    res = run_bass_kernel_spmd(nc, in_maps, core_ids=list(range(8)))
    out = np.stack([np.asarray(res.results[b]["y"], dtype=np.float32) for b in range(B)], axis=0)
    return out
```

```python
import math
import numpy as np
from contextlib import ExitStack
import concourse.bass as bass
import concourse.mybir as mybir
from concourse.bass_utils import run_bass_kernel_spmd

F32 = mybir.dt.float32
BF16 = mybir.dt.bfloat16
AF = mybir.ActivationFunctionType
ALU = mybir.AluOpType

D = 1024
SEQ = 8192
CTX = 256
DEPTH = 4
DFF = 2816
NFF = DFF // 128
PAD = 8
EPS = 1e-6
MAGIC = 12582912.0
TWO_PI = 2.0 * math.pi


class _Op:
    __slots__ = ("eng", "fn", "waits", "sem", "val", "dma")


class Prog:
    COMPUTE = ("pe", "act", "dve", "pool")
    NDMA = 8

    def __init__(self, nc, stack):
        self.nc = nc
        self.h = {"pe": nc.tensor, "act": nc.scalar, "dve": nc.vector, "pool": nc.gpsimd, "sp": nc.sync}
        self.streams = {e: [] for e in self.h}
        self.esem = {e: stack.enter_context(nc.semaphore("s_" + e)) for e in self.COMPUTE}
        self.ecnt = {e: 0 for e in self.COMPUTE}
        self.dsem, self.dcnt, self.drr = {}, {}, {}
        for q in ("sp", "pool", "act"):
            self.dsem[q] = [stack.enter_context(nc.semaphore("d_%s%d" % (q, i))) for i in range(self.NDMA)]
            self.dcnt[q] = [0] * self.NDMA
            self.drr[q] = 0
        self.lastw = {}
        self.readers = {}
        self.waited = {e: {} for e in self.h}
        self.nops = 0

    def _dep(self, eng, op, waits):
        if op is None:
            return
        if (not op.dma) and op.eng == eng and eng == "pe":
            return
        key = id(op.sem)
        if self.waited[eng].get(key, 0) >= op.val:
            return
        cur = waits.get(key)
        if cur is None or cur[1] < op.val:
            waits[key] = (op.sem, op.val)

    def op(self, eng, fn, reads=(), writes=(), dma=False):
        o = _Op()
        o.eng, o.fn, o.dma = eng, fn, dma
        waits = {}
        for r in reads:
            self._dep(eng, self.lastw.get(r), waits)
        for wk in writes:
            self._dep(eng, self.lastw.get(wk), waits)
            for rd in self.readers.get(wk, ()):
                self._dep(eng, rd, waits)
        if dma:
            i = self.drr[eng]
            self.drr[eng] = (i + 1) % self.NDMA
            sem = self.dsem[eng][i]
            prev = self.dcnt[eng][i]
            if prev > 0 and self.waited[eng].get(id(sem), 0) < prev:
                cur = waits.get(id(sem))
                if cur is None or cur[1] < prev:
                    waits[id(sem)] = (sem, prev)
            self.dcnt[eng][i] = prev + 16
            o.sem, o.val = sem, prev + 16
        else:
            self.ecnt[eng] += 1
            o.sem, o.val = self.esem[eng], self.ecnt[eng]
        for key, (s, v) in waits.items():
            self.waited[eng][key] = v
        o.waits = list(waits.values())
        self.streams[eng].append(o)
        for r in reads:
            self.readers.setdefault(r, []).append(o)
        for wk in writes:
            self.lastw[wk] = o
            self.readers[wk] = []
        self.nops += 1
        return o

    def barrier(self, engines=None):
        tot = {}
        for e in self.COMPUTE:
            if self.ecnt[e] > 0:
                tot[id(self.esem[e])] = (self.esem[e], self.ecnt[e])
        for q in self.dsem:
            for i in range(self.NDMA):
                if self.dcnt[q][i] > 0:
                    tot[id(self.dsem[q][i])] = (self.dsem[q][i], self.dcnt[q][i])
        for eng in (engines or list(self.h)):
            waits = []
            for key, (s, v) in tot.items():
                if self.waited[eng].get(key, 0) < v:
                    if eng in self.COMPUTE and s is self.esem[eng]:
                        continue
                    waits.append((s, v))
                    self.waited[eng][key] = v
            if waits:
                o = _Op()
                o.eng, o.fn, o.dma, o.sem, o.val = eng, None, False, None, 0
                o.waits = waits
                self.streams[eng].append(o)
        self.lastw = {}
        self.readers = {}

    def emit(self, block):
        def mk(ename):
            def body(e):
                for o in self.streams[ename]:
                    for (s, v) in o.waits:
                        e.wait_ge(s, v)
                    if o.fn is None:
                        continue
                    o.fn(e).then_inc(o.sem, 16 if o.dma else 1)
            return body
        block.sync(mk("sp"))
        block.scalar(mk("act"))
        block.vector(mk("dve"))
        block.gpsimd(mk("pool"))
        block.tensor(mk("pe"))


def _f32(a):
    return np.ascontiguousarray(np.asarray(a, dtype=np.float32))


MON = {"x": (128, 64, 65, SEQ), "c": (4, 2, 3, CTX)}


def host_constants():
    c = {}
    c["ident"] = _f32(np.eye(128))
    k2 = np.arange(128)[:, None]
    n2 = np.arange(128)[None, :]
    G = np.exp(2j * np.pi * k2 * n2 / 128.0)
    gt = np.zeros((128, 2, 2, 128))
    for hf in range(2):
        sl = slice(hf * 64, hf * 64 + 64)
        gt[:, hf, 0, 0:64] = G.real[:, sl]
        gt[:, hf, 0, 64:128] = G.imag[:, sl]
        gt[:, hf, 1, 0:64] = -G.imag[:, sl]
        gt[:, hf, 1, 64:128] = G.real[:, sl]
    c["gtab"] = _f32(gt)
    deltas = np.abs(np.linspace(math.log(1e-2) / 0.3, math.log(1e-2) / 1.5, 512))
    for s, (N1, NZ, NK, L) in MON.items():
        N = 128 * N1
        n1 = np.arange(NZ)[:, None]
        k1 = np.arange(NK)[None, :]
        ang = 2 * np.pi * n1 * k1 / N1
        c["f1tab_" + s] = _f32(np.concatenate([np.cos(ang), -np.sin(ang)], axis=1))
        f2 = np.zeros((NK, 128, 4, 128))
        nn = np.arange(128)[:, None]
        kk = np.arange(128)[None, :]
        for a in range(NK):
            M = np.exp(-2j * np.pi * nn * (a + N1 * kk) / N)
            f2[a, :, 0] = M.real
            f2[a, :, 1] = M.imag
            f2[a, :, 2] = -M.imag
            f2[a, :, 3] = -M.real
        c["f2tab_" + s] = _f32(f2.reshape(NK, 128, 512))
        ck = np.full(NK, 2.0)
        ck[0] = 1.0
        ck[NK - 1] = 1.0
        i2 = np.zeros((NK, 128, 2, NZ))
        kq = np.arange(NK)[:, None, None]
        nq = np.arange(128)[None, :, None]
        mq = np.arange(NZ)[None, None, :]
        R = (ck[:, None, None] / N) * np.exp(2j * np.pi * kq * (128 * mq + nq) / N)
        i2[:, :, 0, :] = R.real
        i2[:, :, 1, :] = -R.imag
        c["i2tab_" + s] = _f32(i2)
        t = np.arange(L) / (L - 1.0)
        wv = (2.0 * np.pi / L) * np.arange(L)
        f = np.linspace(1e-4, 15.0, 16)
        z = np.concatenate([t[:, None], np.cos(f[None, :] * wv[:, None]), -np.sin(f[None, :] * wv[:, None])], axis=1)
        c["zT_" + s] = _f32(z.T)
        c["d1_" + s] = _f32(np.exp(-(128.0 * np.arange(NZ)[:, None] / (L - 1.0)) * deltas[None, :]))
        c["d2_" + s] = _f32(np.exp(-(np.arange(128)[:, None] / (L - 1.0)) * deltas[None, :]))
    sel = np.zeros((128, 64, 128))
    for gl in range(8):
        for jj in range(8):
            for hi in range(16):
                sel[gl * 16 + hi, gl * 8 + jj, jj * 16 + hi] = 1.0
    c["sel"] = _f32(sel)
    c["selT"] = _f32(sel.transpose(2, 1, 0))
    J = np.zeros((128, 128))
    for p in range(64):
        J[p, 64 + p] = 1.0
        J[64 + p, p] = 1.0
    c["jswap"] = _f32(J)
    msk = np.zeros((128, 2, 2, 256))
    idp = np.zeros((128, 2, 256))
    for jh in range(2):
        for jj in range(8):
            j = jh * 8 + jj
            for hi in range(16):
                for t in range(16):
                    if t >= j:
                        msk[jj * 16 + hi, 0, jh, t * 16:(t + 1) * 16] = 1.0
                    if t <= j:
                        msk[jj * 16 + hi, 1, jh, t * 16:(t + 1) * 16] = 1.0
                idp[jj * 16 + hi, jh, j * 16 + hi] = 1.0
    c["s5msk"] = _f32(msk)
    c["s5idp"] = _f32(idp)
    sg = np.zeros((128, 4))
    sg[:64, 0], sg[64:, 0] = -1.0, 1.0
    sg[:64, 1], sg[64:, 1] = 1.0, -1.0
    sg[:64, 2], sg[64:, 2] = math.pi / 2, 0.0
    sg[:64, 3], sg[64:, 3] = 0.0, math.pi / 2
    c["s5sg"] = _f32(sg)
    return c


def layout_weights(inp):
    w = {}
    g = lambda k: np.asarray(inp[k], dtype=np.float32)
    w["w_ada"] = _f32(g("w_ada"))
    w["b_adaT"] = _f32(g("b_ada").reshape(DEPTH, 48, 128).transpose(0, 2, 1))
    w["ngT"] = _f32(g("norm_g").reshape(DEPTH, 4, 8, 128).transpose(0, 3, 1, 2))
    w["w_in"] = _f32(g("w_in"))
    w["w_out"] = _f32(g("w_out"))
    w["w_up"] = _f32(g("ffn_w_up"))
    w["w_down"] = _f32(g("ffn_w_down"))
    w["fcwT"] = _f32(g("ffn_conv_w").reshape(DEPTH, 9, NFF, 128).transpose(0, 3, 1, 2))
    w["fcbT"] = _f32(g("ffn_conv_b").reshape(DEPTH, NFF, 128).transpose(0, 2, 1))
    w["hswT"] = _f32(g("hy_short_w").reshape(DEPTH, 3, 12, 128).transpose(0, 3, 1, 2))
    w["hsbT"] = _f32(g("hy_short_b").reshape(DEPTH, 12, 128).transpose(0, 2, 1))
    w["hbiasT"] = _f32(g("hy_bias").reshape(DEPTH, 4, 128).transpose(0, 2, 1))
    w["f_win"] = _f32(g("filt_w_in"))
    w["f_whid"] = _f32(g("filt_w_hid").transpose(0, 2, 1, 3))
    w["f_b"] = _f32(np.concatenate([g("filt_b_in")[:, :, None], g("filt_b_hid").transpose(0, 2, 1)], axis=2))
    w["f_freq"] = _f32(g("filt_freq")[:, :, None])
    w["f_wout"] = _f32(g("filt_w_out"))
    dup = lambda a: np.concatenate([a, a], axis=1)
    lam = np.stack([g("s5_lam_re").reshape(DEPTH, 64, 64).transpose(0, 2, 1),
                    g("s5_lam_im").reshape(DEPTH, 64, 64).transpose(0, 2, 1)], axis=2)
    w["s5lam"] = _f32(dup(lam))
    w["s5ls"] = _f32(np.broadcast_to(g("s5_log_step").reshape(DEPTH, 1, 64), (DEPTH, 128, 64)))
    bb = np.stack([g("s5_b_re").reshape(DEPTH, 64, 64, 16).transpose(0, 2, 1, 3),
                   g("s5_b_im").reshape(DEPTH, 64, 64, 16).transpose(0, 2, 1, 3)], axis=2)
    w["s5b"] = _f32(dup(bb))
    cc = np.stack([g("s5_c_re").reshape(DEPTH, 64, 16, 64).transpose(0, 3, 1, 2),
                   g("s5_c_im").reshape(DEPTH, 64, 16, 64).transpose(0, 3, 1, 2)], axis=2)
    w["s5c"] = _f32(dup(cc))
    dd = g("s5_d").reshape(DEPTH, 32, 16).transpose(0, 2, 1)
    w["s5dT"] = _f32(np.tile(dd, (1, 8, 1)))
    w["s5bglu"] = _f32(g("s5_b_glu").reshape(DEPTH, 4, 128).transpose(0, 2, 1))
    w["w_glu"] = _f32(g("s5_w_glu"))
    return w


class Seq:
    pass


def build_program(depth_run=DEPTH, feed_y=False, dbg=False):
    nc = bass.Bass("TRN2", target_bir_lowering=False)
    dram_in = {}

    def din(name, shape, dt=F32):
        dram_in[name] = nc.dram_tensor(name, list(shape), dt, kind="ExternalInput").ap()
        return dram_in[name]

    def dscr(name, shape, dt):
        if dbg and name in dbg:
            return nc.dram_tensor(name, list(shape), dt, kind="ExternalOutput").ap()
        return nc.dram_tensor(name, list(shape), dt, kind="Internal").ap()

    x_d = din("x", [SEQ, D])
    ctx_d = din("ctx", [CTX, D])
    cT_d = din("cT", [128, 8, 2])
    w_ada_d = din("w_ada", [DEPTH, D, 6 * D])
    b_adaT_d = din("b_adaT", [DEPTH, 128, 48])
    ngT_d = din("ngT", [DEPTH, 128, 4, 8])
    w_in_d = din("w_in", [DEPTH, D, 2 * D])
    w_out_d = din("w_out", [DEPTH, D, D])
    w_up_d = din("w_up", [DEPTH, D, 2 * DFF])
    w_down_d = din("w_down", [DEPTH, DFF, D])
    fcwT_d = din("fcwT", [DEPTH, 128, 9, NFF])
    fcbT_d = din("fcbT", [DEPTH, 128, NFF])
    ident_d = din("ident", [128, 128])
    hswT_d = din("hswT", [DEPTH, 128, 3, 12])
    hsbT_d = din("hsbT", [DEPTH, 128, 12])
    hbiasT_d = din("hbiasT", [DEPTH, 128, 4])
    f_win_d = din("f_win", [DEPTH, 33, 64])
    f_whid_d = din("f_whid", [DEPTH, 64, 2, 64])
    f_b_d = din("f_b", [DEPTH, 64, 3])
    f_freq_d = din("f_freq", [DEPTH, 64, 1])
    f_wout_d = din("f_wout", [DEPTH, 64, 1024])
    gtab_d = din("gtab", [128, 2, 2, 128])
    s5lam_d = din("s5lam", [DEPTH, 128, 2, 64])
    s5ls_d = din("s5ls", [DEPTH, 128, 64])
    s5b_d = din("s5b", [DEPTH, 128, 2, 64, 16])
    s5c_d = din("s5c", [DEPTH, 128, 2, 64, 16])
    s5dT_d = din("s5dT", [DEPTH, 128, 32])
    s5bglu_d = din("s5bglu", [DEPTH, 128, 4])
    w_glu_d = din("w_glu", [DEPTH, 512, 512])
    sel_d = din("sel", [128, 64, 128])
    selT_d = din("selT", [128, 64, 128])
    jswap_d = din("jswap", [128, 128])
    s5msk_d = din("s5msk", [128, 2, 2, 256])
    s5idp_d = din("s5idp", [128, 2, 256])
    s5sg_d = din("s5sg", [128, 4])
    mon_d = {}
    for s_, (N1_, NZ_, NK_, L_) in MON.items():
        mon_d[s_] = dict(f1=din("f1tab_" + s_, [NZ_, 2 * NK_]), f2=din("f2tab_" + s_, [NK_, 128, 512]),
                         i2=din("i2tab_" + s_, [NK_, 128, 2, NZ_]), z=din("zT_" + s_, [33, L_]),
                         d1=din("d1_" + s_, [NZ_, 512]), d2=din("d2_" + s_, [128, 512]))
    if feed_y:
        fy_x = din("fy_x", [D, SEQ])
        fy_c = din("fy_c", [D, CTX])

    y_d = nc.dram_tensor("y", [SEQ, D], F32, kind="ExternalOutput").ap()
    dbg_out = {}

    cres_d = dscr("cres", [CTX, D], F32)
    xmix_d = dscr("xmix", [SEQ, D], F32)
    cmix_d = dscr("cmix", [CTX, D], F32)
    gv_d = dscr("gvec", [2, 2, D], F32)
    wupb_d = dscr("wupb", [D, 2 * DFF], BF16)
    pT = {"x": dscr("pT_x", [2 * D, SEQ + 2 * PAD], BF16), "c": dscr("pT_c", [2 * D, CTX + 2 * PAD], BF16)}
    yT = {"x": dscr("yT_x", [D, SEQ], BF16), "c": dscr("yT_c", [D, CTX], BF16)}
    s5tab_d = dscr("s5tab", [32, 128, 1536], BF16)
    lamt_d = dscr("lamt", [64, 128, 1152], BF16)
    dgff_d = dscr("dgff", [NFF, 128, 1152], BF16)
    f2b_d = {s_: dscr("f2b_" + s_, [MON[s_][2], 128, 512], BF16) for s_ in MON}
    i2b_d = {s_: dscr("i2b_" + s_, [MON[s_][2], 128, 2 * MON[s_][1]], BF16) for s_ in MON}
    uT = {"x": dscr("uT_x", [512, SEQ], BF16), "c": dscr("uT_c", [512, CTX], BF16)}
    x0T = {"x": dscr("x0T_x", [512, SEQ], BF16), "c": dscr("x0T_c", [512, CTX], BF16)}
    yhT = {"x": dscr("yhT_x", [512, SEQ], BF16), "c": dscr("yhT_c", [512, CTX], BF16)}

    with ExitStack() as st:
        P = Prog(nc, st)

        uniq = [0]

        def sb(stack, name, shape, dt):
            uniq[0] += 1
            return stack.enter_context(nc.sbuf_tensor("%s_%d" % (name, uniq[0]), list(shape), dt))

        ps = [st.enter_context(nc.psum_tensor("ps%d" % i, [128, 512], F32)) for i in range(8)]
        psc = [0]

        def bank():
            i = psc[0]
            psc[0] = (i + 1) % 8
            return ps[i], ("ps", i)

        def dma(q, out, in_, reads=(), writes=(), slow=False):
            if slow:
                P.op(q, lambda e: e.dma_start(out=out, in_=in_, allow_slow_non_contiguous=True), reads, writes, dma=True)
            else:
                P.op(q, lambda e: e.dma_start(out=out, in_=in_), reads, writes, dma=True)

        def mm(out, lhsT, rhs, start, stop, reads, writes):
            P.op("pe", lambda e: e.matmul(out, lhsT=lhsT, rhs=rhs, start=start, stop=stop), reads, writes)

        def act(out, in_, func, reads, writes, scale=None, bias=None, accum=None):
            kw = {}
            if scale is not None:
                kw["scale"] = scale
            if bias is not None:
                kw["bias"] = bias
            if accum is not None:
                kw["accum_out"] = accum
            P.op("act", lambda e: e.activation(out=out, in_=in_, func=func, **kw), reads, writes)

        def tt(eng, out, in0, in1, op, reads, writes):
            P.op(eng, lambda e: e.tensor_tensor(out=out, in0=in0, in1=in1, op=op), reads, writes)

        def ts(eng, out, in0, s1, s2, op0, op1, reads, writes):
            if op1 is None:
                P.op(eng, lambda e: e.tensor_scalar(out=out, in0=in0, scalar1=s1, scalar2=None, op0=op0), reads, writes)
            else:
                P.op(eng, lambda e: e.tensor_scalar(out=out, in0=in0, scalar1=s1, scalar2=s2, op0=op0, op1=op1), reads, writes)

        def stt(out, in0, scalar, in1, op0, op1, reads, writes):
            P.op("dve", lambda e: e.scalar_tensor_tensor(out=out, in0=in0, scalar=scalar, in1=in1, op0=op0, op1=op1), reads, writes)

        def cp(eng, out, in_, reads, writes):
            if eng == "act":
                act(out, in_, AF.Copy, reads, writes)
            else:
                P.op(eng, lambda e: e.tensor_copy(out=out, in_=in_), reads, writes)

        def memset(eng, ap, val, writes):
            P.op(eng, lambda e: e.memset(ap, val), (), writes)

        identf = sb(st, "identf", [128, 128], F32)
        identb = sb(st, "identb", [128, 128], BF16)
        cond = sb(st, "cond", [128, 8, 2], F32)
        modT = sb(st, "modT", [128, 48, 2], F32)
        ngt = sb(st, "ngt", [128, 4, 8], F32)
        vec = {}
        for s in ("x", "c"):
            for nm in ("gs1", "sh1", "gs3", "sh3", "ga2", "ga4"):
                vec[(s, nm)] = sb(st, "v_%s_%s" % (s, nm), [128, 8], F32)
        G2 = {s: sb(st, "G2" + s, [128, D], F32) for s in ("x", "c")}
        G4 = {s: sb(st, "G4" + s, [128, D], F32) for s in ("x", "c")}
        SI = {"x": 0, "c": 1}
        epsb = sb(st, "epsb", [128, 1], F32)

        dma("sp", identf[:], ident_d[:, :], writes=["identf"])
        cp("dve", identb[:], identf[:], ["identf"], ["identb"])
        memset("dve", epsb[:], EPS, ["epsb"])
        dma("sp", cond[:], cT_d[:, :, :], writes=["cond"])
        act(cond[:], cond[:], AF.Silu, ["cond"], ["cond"])
        with ExitStack() as ph:
            zt = sb(ph, "zt", [128, 16, PAD], BF16)
            memset("dve", zt[:], 0.0, ["zt"])
            for s, L in (("x", SEQ), ("c", CTX)):
                v = pT[s].rearrange("(m p) t -> p m t", p=128)
                dma("sp", v[:, :, 0:PAD], zt[:], reads=["zt"])
                dma("sp", v[:, :, PAD + L:PAD + L + PAD], zt[:], reads=["zt"])
            P.barrier()

        with ExitStack() as ph:
            stg = [sb(ph, "tcs%d" % i, [128, 8, 512], BF16) for i in range(2)]
            cnt = 0
            for s_ in MON:
                NK_ = MON[s_][2]
                NZ_ = MON[s_][1]
                for k0 in range(0, NK_, 8):
                    nk = min(8, NK_ - k0)
                    sl = cnt % 2
                    cnt += 1
                    dma("pool", stg[sl][:, 0:nk, :], mon_d[s_]["f2"][k0:k0 + nk].rearrange("k p n -> p k n"), writes=[("tcs", sl)])
                    dma("sp", f2b_d[s_][k0:k0 + nk].rearrange("k p n -> p k n"), stg[sl][:, 0:nk, :], reads=[("tcs", sl)])
                w_ = 2 * NZ_
                for n0 in range(0, 128, 32):
                    sl = cnt % 2
                    cnt += 1
                    v = stg[sl][0:NK_, :, :].rearrange("p a b -> p (a b)")[:, 0:32 * w_]
                    dma("pool", v, mon_d[s_]["i2"][:, n0:n0 + 32].rearrange("k n r z -> k (n r z)"), writes=[("tcs", sl)])
                    dma("sp", i2b_d[s_][:, n0:n0 + 32, :].rearrange("k n w -> k (n w)"), v, reads=[("tcs", sl)])
            P.barrier()

        seqs = {}
        for s, L, src in (("c", CTX, ctx_d), ("x", SEQ, x_d)):
            q = Seq()
            q.name, q.L, q.src = s, L, src
            q.res = y_d if s == "x" else cres_d
            q.mix = xmix_d if s == "x" else cmix_d
            seqs[s] = q

        def phase_mod(l):
            with ExitStack() as ph:
                wa = [sb(ph, "wa%d" % i, [128, 8, 512], F32) for i in range(6)]
                bad = sb(ph, "bad", [128, 48], F32)
                tmp = sb(ph, "modtmp", [128, 8], F32)
                dma("sp", bad[:], b_adaT_d[l], writes=["bad"])
                dma("sp", ngt[:], ngT_d[l], writes=["ngt"])
                wv = w_ada_d[l].rearrange("(k p) n -> p k n", p=128)
                bk, bkey = bank()
                for cb in range(12):
                    slot = cb % 6
                    dma(("sp", "pool", "act")[cb % 3], wa[slot][:], wv[:, :, cb * 512:(cb + 1) * 512], writes=[("wa", slot)])
                    for mi in range(4):
                        m = cb * 4 + mi
                        for k in range(8):
                            mm(bk[:, 2 * m:2 * m + 2], wa[slot][:, k, mi * 128:(mi + 1) * 128], cond[:, k, :],
                               k == 0, k == 7, [("wa", slot), "cond"], [bkey])
                tt("dve", modT[:], bk[:, 0:96].rearrange("p (m s) -> p m s", s=2),
                   bad[:].unsqueeze(2).to_broadcast([128, 48, 2]), ALU.add, [bkey, "bad"], ["modT"])
                for s in ("x", "c"):
                    si = SI[s]
                    md = lambda i: modT[:, i * 8:(i + 1) * 8, si]
                    stt(vec[(s, "gs1")][:], md(1), 1.0, ngt[:, 0, :], ALU.add, ALU.mult, ["modT", "ngt"], [("v", s, "gs1")])
                    cp("dve", vec[(s, "sh1")][:], md(0), ["modT"], [("v", s, "sh1")])
                    stt(vec[(s, "gs3")][:], md(4), 1.0, ngt[:, 2, :], ALU.add, ALU.mult, ["modT", "ngt"], [("v", s, "gs3")])
                    cp("dve", vec[(s, "sh3")][:], md(3), ["modT"], [("v", s, "sh3")])
                    tt("dve", vec[(s, "ga2")][:], md(2), ngt[:, 1, :], ALU.mult, ["modT", "ngt"], [("v", s, "ga2")])
                    tt("dve", vec[(s, "ga4")][:], md(5), ngt[:, 3, :], ALU.mult, ["modT", "ngt"], [("v", s, "ga4")])
                    for j, nm in enumerate(("ga2", "ga4")):
                        dma("sp", gv_d[si, j].rearrange("(k p) -> p k", p=128), vec[(s, nm)][:],
                            reads=[("v", s, nm)], writes=[("gv", si, j)], slow=True)
                    dma("sp", G2[s][:], gv_d[si, 0:1, :].to_broadcast([128, D]), reads=[("gv", si, 0)], writes=[("G2", s)])
                    dma("sp", G4[s][:], gv_d[si, 1:2, :].to_broadcast([128, D]), reads=[("gv", si, 1)], writes=[("G4", s)])
                P.barrier()

        def norm_rows(xt_ap, na, ss, rs, junk, keys_r, key_ss, key_rs):
            for a in range(na):
                act(junk[:], xt_ap[:, a, :], AF.Square, keys_r, [key_ss], accum=ss[:, a:a + 1])
            act(rs[:, 0:na], ss[:, 0:na], AF.Sqrt, [key_ss], [key_rs], scale=1.0 / D, bias=epsb[:])
            P.op("dve", lambda e: e.reciprocal(out=rs[:, 0:na], in_=rs[:, 0:na]), [key_rs], [key_rs])

        def phase_in(l):
            with ExitStack() as ph:
                win = sb(ph, "win", [128, 8, 2 * D], BF16)
                wv = w_in_d[l].rearrange("(k p) n -> p k n", p=128)
                for k in range(8):
                    dma("pool", win[:, k, :], wv[:, k, :], writes=[("win", k)])
                xts = [sb(ph, "xt%d" % i, [128, 4, D], F32) for i in range(2)]
                xss = [sb(ph, "xs%d" % i, [128, 4, D], BF16) for i in range(2)]
                xnT = [sb(ph, "xnT%d" % i, [128, 8, 512], BF16) for i in range(2)]
                pout = [sb(ph, "pout%d" % i, [128, 16, 512], BF16) for i in range(2)]
                junk = sb(ph, "junk", [128, D], BF16)
                ssq = [sb(ph, "ssq%d" % i, [128, 4], F32) for i in range(2)]
                rsd = [sb(ph, "rsd%d" % i, [128, 4], F32) for i in range(2)]
                cnt = 0
                for s in ("c", "x"):
                    q = seqs[s]
                    src = q.src if l == 0 else q.res
                    TB = min(512, q.L)
                    for t0 in range(0, q.L, TB):
                        nt = TB
                        na = nt // 128
                        sl = cnt % 2
                        cnt += 1
                        xt, xs = xts[sl], xss[sl]
                        dma("sp", xt[:, 0:na, :], src[t0:t0 + nt, :].rearrange("(a p) f -> p a f", p=128), writes=[("xt", sl)])
                        norm_rows(xt, na, ssq[sl], rsd[sl], junk, [("xt", sl)], ("ss", sl), ("rs", sl))
                        for a in range(na):
                            if a % 2 == 0:
                                ts("dve", xs[:, a, :], xt[:, a, :], rsd[sl][:, a:a + 1], None, ALU.mult, None,
                                   [("xt", sl), ("rs", sl)], [("xs", sl, a)])
                            else:
                                act(xs[:, a, :], xt[:, a, :], AF.Copy, [("xt", sl), ("rs", sl)], [("xs", sl, a)], scale=rsd[sl][:, a:a + 1])
                        for k in range(8):
                            bk, bkey = bank()
                            for a in range(na):
                                mm(bk[:, a * 128:(a + 1) * 128], xs[:, a, k * 128:(k + 1) * 128], identb[:], True, True,
                                   [("xs", sl, a), "identb"], [bkey])
                            act(xnT[sl][:, k, 0:nt], bk[:, 0:nt], AF.Identity, [bkey, ("v", s, "gs1"), ("v", s, "sh1")], [("xnT", sl, k)],
                                scale=vec[(s, "gs1")][:, k:k + 1], bias=vec[(s, "sh1")][:, k:k + 1])
                        for m in range(16):
                            bk, bkey = bank()
                            for k in range(8):
                                mm(bk[:, 0:nt], win[:, k, m * 128:(m + 1) * 128], xnT[sl][:, k, 0:nt], k == 0, k == 7,
                                   [("win", k), ("xnT", sl, k)], [bkey])
                            cp("act" if m % 2 == 0 else "dve", pout[sl][:, m, 0:nt], bk[:, 0:nt], [bkey], [("pout", sl)])
                        dma("sp", pT[s].rearrange("(m p) t -> p m t", p=128)[:, :, PAD + t0:PAD + t0 + nt], pout[sl][:, :, 0:nt],
                            reads=[("pout", sl)])
                P.barrier()

        def resid_epilogue(bk0, bk1, k0, k1, s, Gt, Gkey, xres_ap, xres_key, out_ap, out_key, scr):
            ss2, rs1, junk2, tmp = scr
            act(junk2[:, 0:512], bk0[:], AF.Square, [k0], ["ss2"], accum=ss2[:, 0:1])
            act(junk2[:, 512:1024], bk1[:], AF.Square, [k1], ["ss2"], accum=ss2[:, 1:2])
            tt("dve", rs1[:], ss2[:, 0:1], ss2[:, 1:2], ALU.add, ["ss2"], ["rs1"])
            act(rs1[:], rs1[:], AF.Sqrt, ["rs1"], ["rs1"], scale=1.0 / D, bias=epsb[:])
            P.op("dve", lambda e: e.reciprocal(out=rs1[:], in_=rs1[:]), ["rs1"], ["rs1"])
            stt(tmp[:, 0:512], bk0[:], rs1[:, 0:1], Gt[:, 0:512], ALU.mult, ALU.mult, [k0, "rs1", Gkey], ["etmp0"])
            stt(tmp[:, 512:1024], bk1[:], rs1[:, 0:1], Gt[:, 512:1024], ALU.mult, ALU.mult, [k1, "rs1", Gkey], ["etmp1"])
            tt("pool", out_ap, tmp[:], xres_ap, ALU.add, ["etmp0", "etmp1", xres_key], [out_key])

        def phase_out(l):
            with ExitStack() as ph:
                wo = sb(ph, "wo", [128, 8, D], BF16)
                wv = w_out_d[l].rearrange("(k p) n -> p k n", p=128)
                for k in range(8):
                    dma("pool", wo[:, k, :], wv[:, k, :], writes=[("wo", k)])
                yts = [sb(ph, "yt%d" % i, [128, 8, 512], BF16) for i in range(2)]
                xts = [sb(ph, "xt%d" % i, [128, 4, D], F32) for i in range(2)]
                xos = [sb(ph, "xo%d" % i, [128, 4, D], F32) for i in range(2)]
                scr = (sb(ph, "ss2", [128, 2], F32), sb(ph, "rs1", [128, 1], F32), sb(ph, "junk2", [128, D], BF16),
                       sb(ph, "etmp", [128, D], F32))
                cnt = 0
                for s in (("c", "x") if l < DEPTH - 1 else ("x",)):
                    q = seqs[s]
                    src = q.src if l == 0 else q.res
                    TB = min(512, q.L)
                    for t0 in range(0, q.L, TB):
                        nt = TB
                        na = nt // 128
                        sl = cnt % 2
                        cnt += 1
                        dma("sp", yts[sl][:, :, 0:nt], yT[s].rearrange("(k p) t -> p k t", p=128)[:, :, t0:t0 + nt], writes=[("yt", sl)])
                        dma("sp", xts[sl][:, 0:na, :], src[t0:t0 + nt, :].rearrange("(a p) f -> p a f", p=128), writes=[("xt", sl)])
                        for a in range(na):
                            b0, k0 = bank()
                            b1, k1 = bank()
                            for k in range(8):
                                mm(b0[:], yts[sl][:, k, a * 128:(a + 1) * 128], wo[:, k, 0:512], k == 0, k == 7, [("yt", sl), ("wo", k)], [k0])
                                mm(b1[:], yts[sl][:, k, a * 128:(a + 1) * 128], wo[:, k, 512:1024], k == 0, k == 7, [("yt", sl), ("wo", k)], [k1])
                            resid_epilogue(b0, b1, k0, k1, s, G2[s], ("G2", s), xts[sl][:, a, :], ("xt", sl), xos[sl][:, a, :], ("xo", sl, a), scr)
                        dma("sp", q.mix[t0:t0 + nt, :].rearrange("(a p) f -> p a f", p=128), xos[sl][:, 0:na, :],
                            reads=[("xo", sl, a) for a in range(na)])
                P.barrier()

        def phase_ffn(l):
            with ExitStack() as ph:
                wd = sb(ph, "wd", [128, NFF, D], BF16)
                wv = w_down_d[l].rearrange("(k p) n -> p k n", p=128)
                for k in range(NFF):
                    dma("pool", wd[:, k, :], wv[:, k, :], writes=[("wd", k)])
                stg = [sb(ph, "stg%d" % i, [128, 2 * DFF], BF16) for i in range(2)]
                for k in range(8):
                    dma("pool", stg[k % 2][:], w_up_d[l, k * 128:(k + 1) * 128, :], writes=[("stg", k % 2)])
                    dma("sp", wupb_d[k * 128:(k + 1) * 128, :], stg[k % 2][:], reads=[("stg", k % 2)])
                P.barrier()
                wupv = wupb_d.rearrange("(k p) n -> p k n", p=128)
                cw = sb(ph, "cw", [128, 9, NFF], F32)
                cb = sb(ph, "cb", [128, NFF], F32)
                dma("sp", cw[:], fcwT_d[l], writes=["cw"])
                dma("sp", cb[:], fcbT_d[l], writes=["cb"])
                with ExitStack() as p3:
                    dst_ = [sb(p3, "dgst%d" % i, [128, 2, 1152], BF16) for i in range(2)]
                    for m0 in range(0, NFF, 2):
                        sl = (m0 // 2) % 2
                        for mi in range(2):
                            for tap in range(9):
                                act(dst_[sl][:, mi, tap * 128:(tap + 1) * 128], identf[:], AF.Copy, ["identf", "cw"], [("dgst", sl)],
                                    scale=cw[:, tap, m0 + mi:m0 + mi + 1])
                        dma("sp", dgff_d[m0:m0 + 2].rearrange("m p n -> p m n"), dst_[sl][:], reads=[("dgst", sl)])
                    P.barrier()
                XE = 1152
                xnT = sb(ph, "fxnT", [128, 8, XE], BF16)
                hT = sb(ph, "hT", [128, NFF, 1024], BF16)
                xt1 = [sb(ph, "fxt%d" % i, [128, D], F32) for i in range(2)]
                xs1 = [sb(ph, "fxs%d" % i, [128, D], BF16) for i in range(2)]
                ss1 = [sb(ph, "fss%d" % i, [128, 1], F32) for i in range(2)]
                rs1b = [sb(ph, "frs%d" % i, [128, 1], F32) for i in range(2)]
                junk = sb(ph, "fjunk", [128, D], BF16)
                wg = [sb(ph, "wg%d" % i, [128, 8, 128], BF16) for i in range(2)]
                wvv = [sb(ph, "wv%d" % i, [128, 8, 128], BF16) for i in range(2)]
                dg = [sb(ph, "dg%d" % i, [128, 9, 128], BF16) for i in range(2)]
                gbuf = [sb(ph, "gbuf%d" % i, [128, 18, 64], BF16) for i in range(2)]
                gel = [sb(ph, "gel%d" % i, [128, 512], F32) for i in range(2)]
                xo = [sb(ph, "fxo%d" % i, [128, D], F32) for i in range(2)]
                scr = (sb(ph, "ss2", [128, 2], F32), sb(ph, "rs1", [128, 1], F32), sb(ph, "junk2", [128, D], BF16),
                       sb(ph, "etmp", [128, D], F32))
                tcnt = 0
                mcnt = 0
                for s in (("c", "x") if l < DEPTH - 1 else ("x",)):
                    q = seqs[s]
                    L = q.L
                    if s == "x":
                        ncols, BR = 64, 16
                    else:
                        ncols, BR = 256, 1
                    NT = BR * ncols
                    vert = s == "x"
                    for t0 in range(0, L, NT):
                        top = vert and t0 > 0
                        bot = vert and t0 + NT < L
                        e0 = t0 - (64 if top else 0)
                        e1 = t0 + NT + (64 if bot else 0)
                        tiles = []
                        tt0 = e0
                        while tt0 < e1:
                            n = min(128, e1 - tt0)
                            tiles.append((tt0, n))
                            tt0 += n
                        for (ta, n) in tiles:
                            sl = tcnt % 2
                            tcnt += 1
                            dma("sp", xt1[sl][0:n, :], q.mix[ta:ta + n, :], writes=[("fxt", sl)])
                            act(junk[0:n, :], xt1[sl][0:n, :], AF.Square, [("fxt", sl)], [("fss", sl)], accum=ss1[sl][0:n, :])
                            act(rs1b[sl][0:n, :], ss1[sl][0:n, :], AF.Sqrt, [("fss", sl)], [("frs", sl)], scale=1.0 / D, bias=epsb[0:n, :])
                            P.op("dve", (lambda r, n: lambda e: e.reciprocal(out=r[0:n, :], in_=r[0:n, :]))(rs1b[sl], n), [("frs", sl)], [("frs", sl)])
                            ts("dve", xs1[sl][0:n, :], xt1[sl][0:n, :], rs1b[sl][0:n, 0:1], None, ALU.mult, None, [("fxt", sl), ("frs", sl)], [("fxs", sl)])
                            c0 = ta - e0
                            for kk in range(2):
                                bk, bkey = bank()
                                for k4 in range(4):
                                    k = kk * 4 + k4
                                    mm(bk[:, k4 * 128:k4 * 128 + n], xs1[sl][0:n, k * 128:(k + 1) * 128], identb[0:n, 0:n], True, True,
                                       [("fxs", sl), "identb"], [bkey])
                                for k4 in range(4):
                                    k = kk * 4 + k4
                                    act(xnT[:, k, c0:c0 + n], bk[:, k4 * 128:k4 * 128 + n], AF.Identity,
                                        [bkey, ("v", s, "gs3"), ("v", s, "sh3")], [("fxnT", k)],
                                        scale=vec[(s, "gs3")][:, k:k + 1], bias=vec[(s, "sh3")][:, k:k + 1])
                        ne = e1 - e0
                        goff = 0 if top else 1
                        nrows_e = ne // ncols if vert else 1
                        for m in range(NFF):
                            sl = mcnt % 2
                            mcnt += 1
                            dma("sp", wg[sl][:], wupv[:, :, m * 128:(m + 1) * 128], writes=[("wg", sl)])
                            dma("sp", wvv[sl][:], wupv[:, :, DFF + m * 128:DFF + (m + 1) * 128], writes=[("wv", sl)])
                            dma("sp", dg[sl][:].rearrange("p t k -> p (t k)"), dgff_d[m], writes=[("dg", sl)])
                            gb = gbuf[sl]
                            gview = gb[:].rearrange("p r c -> p (r c)")
                            if vert:
                                if not top:
                                    memset("pool", gb[:, 0, :], 0.0, [("gbuf", sl)])
                                if not bot:
                                    memset("pool", gb[:, 17, :], 0.0, [("gbuf", sl)])
                            o = 0
                            while o < ne:
                                n = min(512, ne - o)
                                bk, bkey = bank()
                                for k in range(8):
                                    mm(bk[:, 0:n], wg[sl][:, k, :], xnT[:, k, o:o + n], k == 0, k == 7, [("wg", sl), ("fxnT", k)], [bkey])
                                gdst = gview[:, goff * 64 + o:goff * 64 + o + n] if vert else gview[:, o:o + n]
                                cp("act", gdst, bk[:, 0:n], [bkey], [("gbuf", sl)])
                                o += n
                            cen0 = t0 - e0
                            for sbk in range(0, NT, 512):
                                nsub = min(512, NT - sbk)
                                bv, bvkey = bank()
                                for k in range(8):
                                    mm(bv[:, 0:nsub], wvv[sl][:, k, :], xnT[:, k, cen0 + sbk:cen0 + sbk + nsub], k == 0, k == 7,
                                       [("wv", sl), ("fxnT", k)], [bvkey])
                                bc, bckey = bank()
                                if vert:
                                    r0 = 1 + sbk // 64
                                    nr = nsub // 64
                                    bc3 = bc[:, 0:nsub].rearrange("p (r c) -> p r c", c=64)
                                    first = True
                                    order = [4, 1, 7, 3, 5, 0, 2, 6, 8]
                                    for ti, tap in enumerate(order):
                                        dy, dx = tap // 3 - 1, tap % 3 - 1
                                        if dx == 0:
                                            rhs = gb[:, r0 + dy:r0 + dy + nr, :]
                                            out = bc3
                                        elif dx == -1:
                                            rhs = gb[:, r0 + dy:r0 + dy + nr, 0:63]
                                            out = bc3[:, :, 1:64]
                                        else:
                                            rhs = gb[:, r0 + dy:r0 + dy + nr, 1:64]
                                            out = bc3[:, :, 0:63]
                                        mm(out, dg[sl][:, tap, :], rhs, first, ti == 8, [("dg", sl), ("gbuf", sl)], [bckey])
                                        first = False
                                else:
                                    for ti, tap in enumerate([4, 3, 5]):
                                        dx = tap % 3 - 1
                                        if dx == 0:
                                            rhs, out = gview[:, 0:nsub], bc[:, 0:nsub]
                                        elif dx == -1:
                                            rhs, out = gview[:, 0:nsub - 1], bc[:, 1:nsub]
                                        else:
                                            rhs, out = gview[:, 1:nsub], bc[:, 0:nsub - 1]
                                        mm(out, dg[sl][:, tap, :], rhs, ti == 0, ti == 2, [("dg", sl), ("gbuf", sl)], [bckey])
                                gsl = (mcnt + sbk // 512) % 2
                                act(gel[gsl][:, 0:nsub], bc[:, 0:nsub], AF.Gelu_apprx_tanh, [bckey, "cb"], [("gel", gsl)], bias=cb[:, m:m + 1])
                                tt("dve", hT[:, m, sbk:sbk + nsub], bv[:, 0:nsub], gel[gsl][:, 0:nsub], ALU.mult, [bvkey, ("gel", gsl)], [("hT", m)])
                        for a in range(NT // 128):
                            sl = a % 2
                            b0, k0 = bank()
                            b1, k1 = bank()
                            for k in range(NFF):
                                mm(b0[:], hT[:, k, a * 128:(a + 1) * 128], wd[:, k, 0:512], k == 0, k == NFF - 1, [("hT", k), ("wd", k)], [k0])
                                mm(b1[:], hT[:, k, a * 128:(a + 1) * 128], wd[:, k, 512:1024], k == 0, k == NFF - 1, [("hT", k), ("wd", k)], [k1])
                            ta = t0 + a * 128
                            dma("sp", xt1[sl][:], q.mix[ta:ta + 128, :], writes=[("fxt", sl)])
                            resid_epilogue(b0, b1, k0, k1, s, G4[s], ("G4", s), xt1[sl][:], ("fxt", sl), xo[sl][:], ("fxo", sl), scr)
                            dma("sp", q.res[ta:ta + 128, :], xo[sl][:], reads=[("fxo", sl)])
                P.barrier()


        def sin_rr(out_ap, arg, tmpk, n_part, keys_arg, key_out, key_tmp):
            ts("dve", tmpk, arg, 1.0 / TWO_PI, MAGIC, ALU.mult, ALU.add, keys_arg, [key_tmp])
            ts("dve", tmpk, tmpk, MAGIC, -TWO_PI, ALU.subtract, ALU.mult, [key_tmp], [key_tmp])
            tt("dve", tmpk, arg, tmpk, ALU.add, keys_arg + [key_tmp], [key_tmp])
            ts("dve", tmpk, tmpk, 3.1415925, -3.1415925, ALU.min, ALU.max, [key_tmp], [key_tmp])
            act(out_ap, tmpk, AF.Sin, [key_tmp], [key_out])

        def phase_hyprep(l):
            with ExitStack() as ph:
                hw = sb(ph, "hw", [128, 3, 12], F32)
                hb = sb(ph, "hb", [128, 12], F32)
                dma("sp", hw[:], hswT_d[l], writes=["hw"])
                dma("sp", hb[:], hsbT_d[l], writes=["hb"])
                dgs = sb(ph, "hdg", [128, 36, 128], BF16)
                for ch in range(12):
                    for d in range(3):
                        act(dgs[:, ch * 3 + d, :], identf[:], AF.Copy, ["identf", "hw"], ["hdg"], scale=hw[:, d, ch:ch + 1])
                pin = [sb(ph, "pin%d" % i, [128, 3, 514], BF16) for i in range(2)]
                vB = [sb(ph, "vB%d" % i, [128, 512], F32) for i in range(2)]
                uo = [sb(ph, "uo%d" % i, [128, 512], BF16) for i in range(2)]
                xo = [sb(ph, "x0o%d" % i, [128, 512], BF16) for i in range(2)]
                cnt = 0
                for s in (("c", "x") if l < DEPTH - 1 else ("x",)):
                    L = seqs[s].L
                    TB = min(512, L)
                    for cc in range(4):
                        for t0 in range(0, L, TB):
                            sl = cnt % 2
                            cnt += 1
                            for j in range(3):
                                r0 = j * 512 + cc * 128
                                dma("sp" if j != 1 else "pool", pin[sl][:, j, 0:TB + 2], pT[s][r0:r0 + 128, PAD + t0 - 1:PAD + t0 + TB + 1], writes=[("pin", sl, j)])
                            bks = []
                            for j in range(3):
                                bk, bkey = bank()
                                ch = j * 4 + cc
                                for d in range(3):
                                    mm(bk[:, 0:TB], dgs[:, ch * 3 + d, :], pin[sl][:, j, d:d + TB], d == 0, d == 2, ["hdg", ("pin", sl, j)], [bkey])
                                bks.append((bk, bkey))
                            act(vB[sl][:, 0:TB], bks[2][0][:, 0:TB], AF.Identity, [bks[2][1], "hb"], [("vB", sl)], bias=hb[:, 8 + cc:9 + cc])
                            act(xo[sl][:, 0:TB], bks[0][0][:, 0:TB], AF.Identity, [bks[0][1], "hb"], [("x0o", sl)], bias=hb[:, cc:cc + 1])
                            stt(uo[sl][:, 0:TB], bks[1][0][:, 0:TB], hb[:, 4 + cc:5 + cc], vB[sl][:, 0:TB], ALU.add, ALU.mult,
                                [bks[1][1], "hb", ("vB", sl)], [("uo", sl)])
                            dma("sp", uT[s][cc * 128:(cc + 1) * 128, t0:t0 + TB], uo[sl][:, 0:TB], reads=[("uo", sl)])
                            dma("sp", x0T[s][cc * 128:(cc + 1) * 128, t0:t0 + TB], xo[sl][:, 0:TB], reads=[("x0o", sl)])
                P.barrier()

        def phase_hyena(l, s):
            N1, NZ, NK, L = MON[s]
            md = mon_d[s]
            NK2 = 2 * NK
            with ExitStack() as ph:
                hidT = sb(ph, "hidT", [64, L], BF16)
                woutb = sb(ph, "woutb", [64, 1024], BF16)
                d1 = sb(ph, "d1", [NZ, 512], F32)
                d2 = sb(ph, "d2", [128, 512], F32)
                f1t = sb(ph, "f1t", [NZ, NK2], BF16)
                gtb = sb(ph, "gtb", [128, 2, 2, 128], BF16)
                hbias = sb(ph, "hbias", [128, 4], F32)
                dma("pool", woutb[:], f_wout_d[l], writes=["woutb"])
                dma("sp", d1[:], md["d1"][:, :], writes=["d1"])
                dma("sp", d2[:], md["d2"][:, :], writes=["d2"])
                dma("pool", f1t[:], md["f1"][:, :], writes=["f1t"])
                dma("pool", gtb[:], gtab_d[:, :, :, :], writes=["gtb"])
                with ExitStack() as p2:
                    zt = sb(p2, "zt", [33, L], F32)
                    fwin = sb(p2, "fwin", [33, 64], F32)
                    fwh = sb(p2, "fwh", [64, 2, 64], F32)
                    fb = sb(p2, "fb", [64, 3], F32)
                    ffr = sb(p2, "ffr", [64, 1], F32)
                    frb = sb(p2, "frb", [64, 3], F32)
                    dma("sp", zt[:], md["z"][:, :], writes=["zt"])
                    dma("sp", fwin[:], f_win_d[l], writes=["fwin"])
                    dma("sp", fwh[:], f_whid_d[l], writes=["fwh"])
                    dma("sp", fb[:], f_b_d[l], writes=["fb"])
                    dma("sp", ffr[:], f_freq_d[l], writes=["ffr"])
                    ts("dve", frb[:], fb[:], ffr[:, 0:1], None, ALU.mult, None, ["fb", "ffr"], ["frb"])
                    args = [sb(p2, "farg%d" % i, [64, 512], F32) for i in range(2)]
                    tmpk = [sb(p2, "ftmp%d" % i, [64, 512], F32) for i in range(2)]
                    hts = [sb(p2, "fh%d" % i, [64, 512], F32) for i in range(4)]
                    TB = min(512, L)
                    cnt = 0
                    for t0 in range(0, L, TB):
                        prev = None
                        for li in range(3):
                            sl = cnt % 2
                            cnt += 1
                            bk, bkey = bank()
                            if li == 0:
                                mm(bk[0:64, 0:TB], fwin[:], zt[:, t0:t0 + TB], True, True, ["fwin", "zt"], [bkey])
                            else:
                                mm(bk[0:64, 0:TB], fwh[:, li - 1, :], prev[0][:, 0:TB], True, True, ["fwh", prev[1]], [bkey])
                            act(args[sl][:, 0:TB], bk[0:64, 0:TB], AF.Identity, [bkey, "ffr", "frb"], [("farg", sl)],
                                scale=ffr[:, 0:1], bias=frb[:, li:li + 1])
                            if li < 2:
                                hsl = (t0 // TB * 2 + li) % 4
                                sin_rr(hts[hsl][:, 0:TB], args[sl][:, 0:TB], tmpk[sl][:, 0:TB], 64, [("farg", sl)], ("fh", hsl), ("ftmp", sl))
                                prev = (hts[hsl], ("fh", hsl))
                            else:
                                sin_rr(hidT[:, t0:t0 + TB], args[sl][:, 0:TB], tmpk[sl][:, 0:TB], 64, [("farg", sl)], "hidT", ("ftmp", sl))
                    P.barrier()
                dma("sp", hbias[:], hbiasT_d[l], writes=["hbias"])
                Xraw = sb(ph, "monX", [NZ, 128 * 128], BF16)
                X = Xraw[:, :].rearrange("p (c n) -> p c n", n=128)
                Xf = Xraw[:, :].rearrange("p (n c) -> p n c", c=128)
                A = sb(ph, "monA", [128, 128, 2, NK], BF16)
                Kf = sb(ph, "monK", [128, 2, NK, 128], BF16)
                BB = sb(ph, "monB", [128, 2 * 65 * 128], BF16)
                Ab = BB[:, 0:2 * NK * 128].rearrange("p (c r k) -> p c r k", r=2, k=NK, c=128)
                B0 = BB[0:NK, 0:2 * 64 * 128].rearrange("p (c r n) -> p c r n", r=2, n=64, c=128)
                f2r = [sb(ph, "f2r%d" % i, [128, 4, 128], BF16) for i in range(8)]
                i2r = [sb(ph, "i2r%d" % i, [NK, 4, 2, NZ], BF16) for i in range(4)]
                tm = [sb(ph, "fmt%d" % i, [128, 512], F32) for i in range(4)]
                KB = 4
                nkb = (NK + KB - 1) // KB
                Akeys = [("A", kb) for kb in range(nkb)]
                f2c = [0]
                i2c = [0]
                nb1 = max(1, min(128, 512 // NK2))

                def f1(dst, dstkeys, scale_d2, cg):
                    c = 0
                    i = 0
                    while c < 128:
                        nb = min(nb1, 128 - c)
                        bk, bkey = bank()
                        for q in range(nb):
                            lhs = Xf[0:NZ, :, c + q] if scale_d2 else X[0:NZ, c + q, :]
                            mm(bk[:, q * NK2:(q + 1) * NK2], lhs, f1t[0:NZ, :], True, True, ["X", "f1t"], [bkey])
                        out = dst[:, c:c + nb, :, :].rearrange("p c r k -> p c (r k)")
                        src = bk[:, 0:nb * NK2].rearrange("p (i x) -> p i x", x=NK2)
                        if scale_d2:
                            tt("dve", out, src, d2[:, cg * 128 + c:cg * 128 + c + nb].unsqueeze(2).to_broadcast([128, nb, NK2]), ALU.mult,
                               [bkey, "d2"], dstkeys)
                        else:
                            cp("act" if i % 2 == 0 else "dve", out, src, [bkey], dstkeys)
                        c += nb
                        i += 1

                def load_f2(k1):
                    sl = f2c[0] % 8
                    f2c[0] += 1
                    dma("sp", f2r[sl][:].rearrange("p v k -> p (v k)"), f2b_d[s][k1], writes=[("f2r", sl)])
                    return f2r[sl], ("f2r", sl)

                for cg in range(4):
                    for dr in range(2):
                        for n20 in range(0, 128, 4):
                            bk, bkey = bank()
                            for q in range(4):
                                n2 = n20 + q
                                mm(bk[0:NZ, q * 128:(q + 1) * 128], hidT[:, n2:L:128], woutb[:, dr * 512 + cg * 128:dr * 512 + (cg + 1) * 128],
                                   True, True, ["hidT", "woutb"], [bkey])
                            tt("dve", Xf[0:NZ, n20:n20 + 4, :],
                               bk[0:NZ, 0:512].rearrange("p (q c) -> p q c", c=128),
                               d1[0:NZ, cg * 128:(cg + 1) * 128].unsqueeze(1).to_broadcast([NZ, 4, 128]), ALU.mult, [bkey, "d1"], ["X"])
                        if dr == 1:
                            memset("dve", Xf[0:1, 0:1, :], 0.0, ["X"])
                        if dr == 0:
                            f1(A, Akeys, True, cg)
                        else:
                            f1(Ab, ["B0"], True, cg)
                    for kb in range(nkb):
                        k1s = list(range(kb * KB, min(NK, (kb + 1) * KB)))
                        br, brk = bank()
                        bi, bik = bank()
                        for q, k1 in enumerate(k1s):
                            f2, f2k = load_f2(k1)
                            cs = slice(q * 128, (q + 1) * 128)
                            mm(br[:, cs], f2[:, 0, :], A[:, :, 0, k1], True, False, [f2k, ("A", kb)], [brk])
                            mm(br[:, cs], f2[:, 2, :], A[:, :, 1, k1], False, False, [f2k, ("A", kb)], [brk])
                            mm(br[:, cs], f2[:, 0, :], Ab[:, :, 0, k1], False, False, [f2k, "B0"], [brk])
                            mm(br[:, cs], f2[:, 2, :], Ab[:, :, 1, k1], False, True, [f2k, "B0"], [brk])
                            mm(bi[:, cs], f2[:, 1, :], A[:, :, 0, k1], True, False, [f2k, ("A", kb)], [bik])
                            mm(bi[:, cs], f2[:, 0, :], A[:, :, 1, k1], False, False, [f2k, ("A", kb)], [bik])
                            mm(bi[:, cs], f2[:, 2, :], Ab[:, :, 0, k1], False, False, [f2k, "B0"], [bik])
                            mm(bi[:, cs], f2[:, 3, :], Ab[:, :, 1, k1], False, True, [f2k, "B0"], [bik])
                        nn = len(k1s) * 128
                        cp("act", Kf[:, 0, k1s[0]:k1s[-1] + 1, :].rearrange("p k c -> p (k c)"), br[:, 0:nn], [brk], [("Kf", kb)])
                        cp("act", Kf[:, 1, k1s[0]:k1s[-1] + 1, :].rearrange("p k c -> p (k c)"), bi[:, 0:nn], [bik], [("Kf", kb)])
                    dma("sp", X[0:NZ, :, :], uT[s][cg * 128:(cg + 1) * 128, :].rearrange("c (a b) -> a c b", b=128), writes=["X"])
                    f1(A, Akeys, False, cg)
                    for kb in range(nkb):
                        k1s = list(range(kb * KB, min(NK, (kb + 1) * KB)))
                        br, brk = bank()
                        bi, bik = bank()
                        for q, k1 in enumerate(k1s):
                            f2, f2k = load_f2(k1)
                            cs = slice(q * 128, (q + 1) * 128)
                            mm(br[:, cs], f2[:, 0, :], A[:, :, 0, k1], True, False, [f2k, ("A", kb)], [brk])
                            mm(br[:, cs], f2[:, 2, :], A[:, :, 1, k1], False, True, [f2k, ("A", kb)], [brk])
                            mm(bi[:, cs], f2[:, 1, :], A[:, :, 0, k1], True, False, [f2k, ("A", kb)], [bik])
                            mm(bi[:, cs], f2[:, 0, :], A[:, :, 1, k1], False, True, [f2k, ("A", kb)], [bik])
                        nn = len(k1s) * 128
                        kr = Kf[:, 0, k1s[0]:k1s[-1] + 1, :].rearrange("p k c -> p (k c)")
                        ki = Kf[:, 1, k1s[0]:k1s[-1] + 1, :].rearrange("p k c -> p (k c)")
                        tt("dve", tm[0][:, 0:nn], br[:, 0:nn], kr, ALU.mult, [brk, ("Kf", kb)], [("tm", 0)])
                        tt("dve", tm[1][:, 0:nn], bi[:, 0:nn], ki, ALU.mult, [bik, ("Kf", kb)], [("tm", 1)])
                        tt("dve", tm[2][:, 0:nn], br[:, 0:nn], ki, ALU.mult, [brk, ("Kf", kb)], [("tm", 2)])
                        tt("dve", tm[3][:, 0:nn], bi[:, 0:nn], kr, ALU.mult, [bik, ("Kf", kb)], [("tm", 3)])
                        v3 = lambda t: t[:, 0:nn].rearrange("p (k c) -> p k c", c=128)
                        tt("pool", A[:, :, 0, k1s[0]:k1s[-1] + 1].rearrange("p c k -> p k c"), v3(tm[0]), v3(tm[1]), ALU.subtract,
                           [("tm", 0), ("tm", 1)], [("A", kb)])
                        tt("pool", A[:, :, 1, k1s[0]:k1s[-1] + 1].rearrange("p c k -> p k c"), v3(tm[2]), v3(tm[3]), ALU.add,
                           [("tm", 2), ("tm", 3)], [("A", kb)])
                    for hf in range(2):
                        for c0 in range(0, 128, 4):
                            bk, bkey = bank()
                            for q in range(4):
                                cs = slice(q * 128, (q + 1) * 128)
                                mm(bk[0:NK, cs], A[:, c0 + q, 0, 0:NK], gtb[:, hf, 0, :], True, False, Akeys + ["gtb"], [bkey])
                                mm(bk[0:NK, cs], A[:, c0 + q, 1, 0:NK], gtb[:, hf, 1, :], False, True, Akeys + ["gtb"], [bkey])
                            cp("act" if (c0 // 4) % 2 == 0 else "dve", B0[:, c0:c0 + 4, :, :].rearrange("p c r n -> p (c r n)"), bk[0:NK, 0:512], [bkey], ["B0"])
                        for n20 in range(0, 64, 4):
                            sl = i2c[0] % 4
                            i2c[0] += 1
                            ng = hf * 64 + n20
                            dma("sp", i2r[sl][:].rearrange("k q r z -> k q (r z)"), i2b_d[s][:, ng:ng + 4, :], writes=[("i2r", sl)])
                            bk, bkey = bank()
                            for q in range(4):
                                cs = slice(q * 128, (q + 1) * 128)
                                mm(bk[0:NZ, cs], i2r[sl][:, q, 0, :], B0[:, :, 0, n20 + q], True, False, [("i2r", sl), "B0"], [bkey])
                                mm(bk[0:NZ, cs], i2r[sl][:, q, 1, :], B0[:, :, 1, n20 + q], False, True, [("i2r", sl), "B0"], [bkey])
                            cp("act" if (n20 // 4) % 2 == 0 else "dve", X[0:NZ, :, ng:ng + 4].rearrange("p c q -> p q c"),
                               bk[0:NZ, 0:512].rearrange("p (q c) -> p q c", c=128), [bkey], ["X"])
                    dma("sp", yhT[s][cg * 128:(cg + 1) * 128, :].rearrange("c (a b) -> a c b", b=128), X[0:NZ, :, :], reads=["X"])
                P.barrier()
            with ExitStack() as ph:
                hbias = sb(ph, "hbias2", [128, 4], F32)
                dma("sp", hbias[:], hbiasT_d[l], writes=["hbias"])
                TB = min(2048, L)
                ty = [sb(ph, "ey%d" % i, [128, TB], BF16) for i in range(2)]
                tu = [sb(ph, "eu%d" % i, [128, TB], BF16) for i in range(2)]
                tx = [sb(ph, "ex%d" % i, [128, TB], BF16) for i in range(2)]
                t1 = [sb(ph, "et%d" % i, [128, TB], F32) for i in range(2)]
                to = [sb(ph, "eo%d" % i, [128, TB], BF16) for i in range(2)]
                cnt = 0
                for cc in range(4):
                    for t0 in range(0, L, TB):
                        sl = cnt % 2
                        cnt += 1
                        rs_ = slice(cc * 128, (cc + 1) * 128)
                        dma("sp", ty[sl][:], yhT[s][rs_, t0:t0 + TB], writes=[("ey", sl)])
                        dma("sp", tu[sl][:], uT[s][rs_, t0:t0 + TB], writes=[("eu", sl)])
                        dma("pool", tx[sl][:], x0T[s][rs_, t0:t0 + TB], writes=[("ex", sl)])
                        stt(t1[sl][:], tu[sl][:], hbias[:, cc:cc + 1], ty[sl][:], ALU.mult, ALU.add, [("eu", sl), ("ey", sl), "hbias"], [("et", sl)])
                        tt("pool", to[sl][:], t1[sl][:], tx[sl][:], ALU.mult, [("et", sl), ("ex", sl)], [("eo", sl)])
                        dma("sp", yT[s][rs_, t0:t0 + TB], to[sl][:], reads=[("eo", sl)])
                P.barrier()

        lamP = sb(st, "lamP", [128, 9, 2, 64], F32)
        st0 = sb(st, "st0", [128, 64], BF16)
        jswapf = sb(st, "jswapf", [128, 128], F32)
        sg = sb(st, "sg", [128, 4], F32)
        dma("sp", jswapf[:], jswap_d[:, :], writes=["jswapf"])
        dma("sp", sg[:], s5sg_d[:, :], writes=["sg"])
        SLOT_M = {0: [15 - i for i in range(16)] + [-i for i in range(16)] + [i for i in range(16)] + [i + 1 for i in range(16)],
                  1: [i for i in range(16)] + [-i for i in range(16)] + [16 - i for i in range(16)] + [0] * 16}

        def phase_s5tab(l):
            with ExitStack() as ph:
                lam = sb(ph, "lam", [128, 2, 64], F32)
                ls = sb(ph, "ls", [128, 64], F32)
                Bd = sb(ph, "Bd", [128, 2, 64, 16], F32)
                Cd = sb(ph, "Cd", [128, 2, 64, 16], F32)
                dT = sb(ph, "dT", [128, 32], F32)
                msk = sb(ph, "msk", [128, 2, 2, 256], F32)
                idp = sb(ph, "idp", [128, 2, 256], F32)
                dma("sp", lam[:], s5lam_d[l], writes=["lam"])
                dma("sp", ls[:], s5ls_d[l], writes=["ls"])
                dma("sp", Bd[:], s5b_d[l], writes=["Bd"])
                dma("sp", Cd[:], s5c_d[l], writes=["Cd"])
                dma("sp", dT[:], s5dT_d[l], writes=["dT"])
                dma("sp", msk[:], s5msk_d[:, :, :, :], writes=["msk"])
                dma("sp", idp[:], s5idp_d[:, :, :], writes=["idp"])
                xr = sb(ph, "xr", [128, 64], F32)
                xi = sb(ph, "xi", [128, 64], F32)
                act(ls[:], ls[:], AF.Exp, ["ls"], ["ls"])
                tt("dve", xr[:], lam[:, 0, :], ls[:], ALU.mult, ["lam", "ls"], ["xr"])
                tt("dve", xi[:], lam[:, 1, :], ls[:], ALU.mult, ["lam", "ls"], ["xi"])
                E1 = sb(ph, "E1", [128, 2, 64, 32], F32)
                E2 = sb(ph, "E2", [128, 2, 64, 32], F32)
                p2 = ExitStack()
                MAG = sb(p2, "MAG", [128, 2, 64, 32], F32)
                TK = sb(p2, "TK", [128, 2, 64, 32], F32)
                for d in range(2):
                    for sl_, m in enumerate(SLOT_M[d]):
                        us = slice(d * 32, d * 32 + 32)
                        act(MAG[:, d, sl_, :], xr[:, us], AF.Exp, ["xr"], ["MAG"], scale=float(m))
                        ts("dve", E1[:, d, sl_, :], xi[:, us], float(m), sg[:, 2:3], ALU.mult, ALU.add, ["xi", "sg"], ["E1"])
                        ts("dve", E2[:, d, sl_, :], xi[:, us], float(m), sg[:, 3:4], ALU.mult, ALU.add, ["xi", "sg"], ["E2"])
                fl = lambda t: t[:].rearrange("p d s u -> p (d s u)")
                sin_rr(fl(E1), fl(E1), fl(TK), 128, ["E1"], "E1", "TK")
                sin_rr(fl(E2), fl(E2), fl(TK), 128, ["E2"], "E2", "TK")
                tt("dve", fl(E1), fl(E1), fl(MAG), ALU.mult, ["E1", "MAG"], ["E1"])
                tt("pool", fl(E2), fl(E2), fl(MAG), ALU.mult, ["E2", "MAG"], ["E2"])
                P.barrier()
                p2.close()
                a1r = sb(ph, "a1r", [128, 64], F32)
                a1i = sb(ph, "a1i", [128, 64], F32)
                Lr = sb(ph, "Lr", [128, 64], F32)
                Li = sb(ph, "Li", [128, 64], F32)
                for d in range(2):
                    us = slice(d * 32, d * 32 + 32)
                    s1 = 33 if d == 0 else 1
                    s16 = 63 if d == 0 else 32
                    for (dst, slot) in ((a1r, s1), (Lr, s16)):
                        cp("dve", dst[0:64, us], E1[0:64, d, slot, :], ["E1"], [dst.name])
                        cp("dve", dst[64:128, us], E2[64:128, d, slot, :], ["E2"], [dst.name])
                    for (dst, slot) in ((a1i, s1), (Li, s16)):
                        cp("dve", dst[0:64, us], E2[0:64, d, slot, :], ["E2"], [dst.name])
                        cp("dve", dst[64:128, us], E1[64:128, d, slot, :], ["E1"], [dst.name])
                t1 = sb(ph, "sq1", [128, 64], F32)
                t2 = sb(ph, "sq2", [128, 64], F32)
                for k in range(9):
                    cp("dve", lamP[:, k, 0, :], Lr[:], [Lr.name], ["lamP"])
                    ts("dve", lamP[:, k, 1, :], Li[:], sg[:, 1:2], None, ALU.mult, None, [Li.name, "sg"], ["lamP"])
                    if k < 8:
                        tt("dve", t1[:], Lr[:], Lr[:], ALU.mult, [Lr.name], ["sq1"])
                        tt("dve", t2[:], Li[:], Li[:], ALU.mult, [Li.name], ["sq2"])
                        tt("dve", Li[:], Lr[:], Li[:], ALU.mult, [Lr.name, Li.name], [Li.name])
                        ts("dve", Li[:], Li[:], 2.0, None, ALU.mult, None, [Li.name], [Li.name])
                        tt("dve", Lr[:], t1[:], t2[:], ALU.subtract, ["sq1", "sq2"], [Lr.name])
                with ExitStack() as p3:
                    lst = [sb(p3, "lst%d" % i, [128, 4, 1152], BF16) for i in range(2)]
                    lt1 = [sb(p3, "lt1_%d" % i, [128, 128], F32) for i in range(4)]
                    c4 = 0
                    for u0 in range(0, 64, 4):
                        sl = (u0 // 4) % 2
                        for ui in range(4):
                            u = u0 + ui
                            for k in range(9):
                                ti = c4 % 4
                                c4 += 1
                                act(lt1[ti][:], identf[:], AF.Copy, ["identf", "lamP"], [("lt1", ti)], scale=lamP[:, k, 0, u:u + 1])
                                stt(lst[sl][:, ui, k * 128:(k + 1) * 128], jswapf[:], lamP[:, k, 1, u:u + 1], lt1[ti][:], ALU.mult, ALU.add,
                                    ["jswapf", "lamP", ("lt1", ti)], [("lst", sl)])
                        dma("sp", lamt_d[u0:u0 + 4].rearrange("u p n -> p u n"), lst[sl][:], reads=[("lst", sl)])
                qr = sb(ph, "qr", [128, 64], F32)
                qi = sb(ph, "qi", [128, 64], F32)
                den = sb(ph, "den", [128, 64], F32)
                ts("dve", a1r[:], a1r[:], -1.0, None, ALU.add, None, [a1r.name], [a1r.name])
                tt("dve", t1[:], lam[:, 0, :], lam[:, 0, :], ALU.mult, ["lam"], ["sq1"])
                tt("dve", t2[:], lam[:, 1, :], lam[:, 1, :], ALU.mult, ["lam"], ["sq2"])
                tt("dve", den[:], t1[:], t2[:], ALU.add, ["sq1", "sq2"], ["den"])
                P.op("dve", lambda e: e.reciprocal(out=den[:], in_=den[:]), ["den"], ["den"])
                tt("dve", t1[:], a1r[:], lam[:, 0, :], ALU.mult, [a1r.name, "lam"], ["sq1"])
                tt("dve", t2[:], a1i[:], lam[:, 1, :], ALU.mult, [a1i.name, "lam"], ["sq2"])
                tt("dve", qr[:], t1[:], t2[:], ALU.add, ["sq1", "sq2"], ["qr"])
                tt("dve", qr[:], qr[:], den[:], ALU.mult, ["qr", "den"], ["qr"])
                tt("dve", t1[:], a1i[:], lam[:, 0, :], ALU.mult, [a1i.name, "lam"], ["sq1"])
                tt("dve", t2[:], a1r[:], lam[:, 1, :], ALU.mult, [a1r.name, "lam"], ["sq2"])
                tt("dve", qi[:], t1[:], t2[:], ALU.subtract, ["sq1", "sq2"], ["qi"])
                tt("dve", qi[:], qi[:], den[:], ALU.mult, ["qi", "den"], ["qi"])
                Za = sb(ph, "Za", [128, 64, 16], F32)
                Zb = sb(ph, "Zb", [128, 64, 16], F32)
                tb1 = sb(ph, "tb1", [128, 64, 16], F32)
                qrb = qr[:].unsqueeze(2).to_broadcast([128, 64, 16])
                qib = qi[:].unsqueeze(2).to_broadcast([128, 64, 16])
                tt("dve", Za[:], Bd[:, 0], qrb, ALU.mult, ["Bd", "qr"], ["Za"])
                tt("dve", tb1[:], Bd[:, 1], qib, ALU.mult, ["Bd", "qi"], ["tb1"])
                tt("dve", Za[:], Za[:], tb1[:], ALU.subtract, ["Za", "tb1"], ["Za"])
                tt("dve", Zb[:], Bd[:, 0], qib, ALU.mult, ["Bd", "qi"], ["Zb"])
                tt("dve", tb1[:], Bd[:, 1], qrb, ALU.mult, ["Bd", "qr"], ["tb1"])
                tt("dve", Zb[:], Zb[:], tb1[:], ALU.add, ["Zb", "tb1"], ["Zb"])
                ts("dve", Zb[:], Zb[:], sg[:, 0:1], None, ALU.mult, None, ["Zb", "sg"], ["Zb"])
                ts("dve", Cd[:, 0], Cd[:, 0], sg[:, 1:2], None, ALU.mult, None, ["Cd"], ["Cd"])
                ts("dve", Cd[:, 1], Cd[:, 1], -1.0, None, ALU.mult, None, ["Cd"], ["Cd"])
                prod = [sb(ph, "prod%d" % i, [128, 8, 256], BF16) for i in range(7)]
                pt = [sb(ph, "pt%d" % i, [128, 8, 16, 16], F32) for i in range(4)]
                stage = sb(ph, "stage", [128, 8, 1536], BF16)
                wt = [sb(ph, "wkt%d" % i, [128, 256], F32) for i in range(4)]
                pcnt = [0]

                def product(dst, d, blk, g0, Z1, Z2, zkey):
                    i = pcnt[0] % 2
                    pcnt[0] += 1
                    eng = "dve" if i == 0 else "pool"
                    e1 = E1[:, d, blk * 16:(blk + 1) * 16, g0:g0 + 8].rearrange("p m g -> p g m").unsqueeze(3).to_broadcast([128, 8, 16, 16])
                    e2 = E2[:, d, blk * 16:(blk + 1) * 16, g0:g0 + 8].rearrange("p m g -> p g m").unsqueeze(3).to_broadcast([128, 8, 16, 16])
                    u0 = d * 32 + g0
                    z1 = Z1[:, u0:u0 + 8, :].unsqueeze(2).to_broadcast([128, 8, 16, 16])
                    z2 = Z2[:, u0:u0 + 8, :].unsqueeze(2).to_broadcast([128, 8, 16, 16])
                    ta, tb = pt[2 * i], pt[2 * i + 1]
                    zk = ["Za", "Zb"] if zkey == "Za" else [zkey]
                    tt(eng, ta[:], e1, z1, ALU.mult, ["E1"] + zk, [("pt", 2 * i)])
                    tt(eng, tb[:], e2, z2, ALU.mult, ["E2"] + zk, [("pt", 2 * i + 1)])
                    tt(eng, dst[:].rearrange("p g (m h) -> p g m h", h=16), ta[:], tb[:], ALU.add, [("pt", 2 * i), ("pt", 2 * i + 1)], [dst.name])

                for gb in range(4):
                    g0 = gb * 8
                    XQ0, KL0, KR0, WS0, XQ1, KR1, WS1 = prod
                    product(XQ0, 0, 0, g0, Za, Zb, "Za")
                    product(KL0, 0, 1, g0, Za, Zb, "Za")
                    product(KR0, 0, 2, g0, Cd[:, 0], Cd[:, 1], "Cd")
                    product(WS0, 0, 3, g0, Cd[:, 0], Cd[:, 1], "Cd")
                    product(XQ1, 1, 0, g0, Za, Zb, "Za")
                    product(KR1, 1, 1, g0, Cd[:, 0], Cd[:, 1], "Cd")
                    product(WS1, 1, 2, g0, Cd[:, 0], Cd[:, 1], "Cd")
                    for gl in range(8):
                        g = g0 + gl
                        for d, XQ in ((0, XQ0), (1, XQ1)):
                            bk, bkey = bank()
                            for jh in range(2):
                                mm(bk[:, jh * 128:(jh + 1) * 128], XQ[:, gl, jh * 128:(jh + 1) * 128], identb[:], True, True, [XQ.name, "identb"], [bkey])
                            cp("act", stage[:, gl, d * 256:(d + 1) * 256], bk[:, 0:256], [bkey], ["stage"])
                        cp("pool", stage[:, gl, 512:768], WS0[:, gl, :], [WS0.name], ["stage"])
                        cp("pool", stage[:, gl, 768:1024], WS1[:, gl, :], [WS1.name], ["stage"])
                        for jh in range(2):
                            bf, bfk = bank()
                            bb_, bbk = bank()
                            mm(bf[:, 0:256], KL0[:, gl, jh * 128:(jh + 1) * 128], KR0[:, gl, :], True, True, [KL0.name, KR0.name], [bfk])
                            mm(bb_[:, 0:256], XQ1[:, gl, jh * 128:(jh + 1) * 128], KR1[:, gl, :], True, True, [XQ1.name, KR1.name], [bbk])
                            tt("dve", wt[0][:], bf[:, 0:256], msk[:, 0, jh, :], ALU.mult, [bfk, "msk"], ["wkt0"])
                            tt("dve", wt[1][:], bb_[:, 0:256], msk[:, 1, jh, :], ALU.mult, [bbk, "msk"], ["wkt1"])
                            stt(wt[2][:], idp[:, jh, :], dT[:, g:g + 1], wt[0][:], ALU.mult, ALU.add, ["idp", "dT", "wkt0"], ["wkt2"])
                            tt("pool", stage[:, gl, 1024 + jh * 256:1024 + (jh + 1) * 256], wt[2][:], wt[1][:], ALU.add, ["wkt2", "wkt1"], ["stage"])
                    dma("sp", s5tab_d[g0:g0 + 8].rearrange("g p n -> p g n"), stage[:], reads=["stage"])
                P.barrier()

        def phase_s5(l, s):
            L = seqs[s].L
            NCH = L // 16
            nsteps = int(round(math.log2(NCH)))
            with ExitStack() as ph:
                selb = sb(ph, "selb", [128, 64, 128], BF16)
                selTb = sb(ph, "selTb", [128, 64, 128], BF16)
                dma("pool", selb[:], sel_d[:, :, :], writes=["selb"])
                dma("pool", selTb[:], selT_d[:, :, :], writes=["selTb"])
                uTb = [sb(ph, "uTb%d" % i, [128, L], BF16) for i in range(2)]
                tabr = [sb(ph, "tabr%d" % i, [128, 1536], BF16) for i in range(2)]
                Ug = [sb(ph, "Ug%d" % i, [128, 2, NCH], BF16) for i in range(2)]
                Sb = [sb(ph, "Sb%d" % i, [128, NCH], BF16) for i in range(8)]
                Sext = [[sb(ph, "Sext%d_%d" % (d, i), [128, NCH + 2], BF16) for i in range(2)] for d in range(2)]
                LamT = [sb(ph, "LamT%d" % i, [128, 9, 128], BF16) for i in range(4)]
                Yg = sb(ph, "Yg", [128, 8, 2, NCH], BF16)
                ysT = sb(ph, "ysT", [128, 4, L], BF16)
                zcol = sb(ph, "zcol", [128, 1], BF16)
                memset("dve", zcol[:], 0.0, ["zcol"])
                gcnt = 0
                ucnt = 0
                for cbk in range(4):
                    ub = uTb[cbk % 2]
                    ubk = ("uTb", cbk % 2)
                    dma("sp", ub[:], pT[s][1536 + cbk * 128:1536 + (cbk + 1) * 128, PAD:PAD + L], writes=[ubk])
                    for gp in range(0, 8, 2):
                        chains = []
                        ginfo = []
                        for gl in (gp, gp + 1):
                            g = cbk * 8 + gl
                            gs_ = gl % 2
                            tb = tabr[gs_]
                            tbk = ("tabr", gs_)
                            dma("sp", tb[:], s5tab_d[g], writes=[tbk])
                            ug = Ug[gs_]
                            ugk = ("Ug", gs_)
                            for jh in range(2):
                                bk, bkey = bank()
                                for jj in range(8):
                                    j = jh * 8 + jj
                                    mm(bk[:, 0:NCH], selb[:, gl * 8 + jj, :], ub[:, j:L:16], jj == 0, jj == 7, ["selb", ubk], [bkey])
                                cp("act" if jh == 0 else "dve", ug[:, jh, :], bk[:, 0:NCH], [bkey], [ugk])
                            ginfo.append((gl, g, gs_, tb, tbk, ug, ugk))
                            for d in range(2):
                                u = d * 32 + g
                                ci = gs_ * 2 + d
                                lt = LamT[ci]
                                ltk = ("LamT", ci)
                                dma("sp", lt[:].rearrange("p k q -> p (k q)"), lamt_d[u], writes=[ltk])
                                bk, bkey = bank()
                                init = s == "x"
                                mm(bk[:, 0:NCH], tb[:, d * 256:d * 256 + 128], ug[:, 0, :], True, False, [tbk, ugk], [bkey])
                                mm(bk[:, 0:NCH], tb[:, d * 256 + 128:d * 256 + 256], ug[:, 1, :], False, not init, [tbk, ugk], [bkey])
                                if init:
                                    col = 0 if d == 0 else NCH - 1
                                    mm(bk[:, col:col + 1], lt[:, 0, :], st0[:, u:u + 1], False, True, [ltk, "st0"], [bkey])
                                cur, curk = Sb[ci * 2], ("Sb", ci * 2)
                                cp("act" if d == 0 else "dve", cur[:], bk[:, 0:NCH], [bkey], [curk])
                                chains.append(dict(d=d, u=u, ci=ci, lt=lt, ltk=ltk, cur=cur, curk=curk, par=0, gs=gs_, init=init))
                        for k in range(nsteps):
                            sh = 1 << k
                            for ch in chains:
                                d, cur, curk, lt, ltk = ch["d"], ch["cur"], ch["curk"], ch["lt"], ch["ltk"]
                                bk, bkey = bank()
                                mm(bk[:, 0:NCH], identb[:], cur[:, 0:NCH], True, False, ["identb", curk], [bkey])
                                if d == 0:
                                    mm(bk[:, sh:NCH], lt[:, k, :], cur[:, 0:NCH - sh], False, True, [ltk, curk], [bkey])
                                else:
                                    mm(bk[:, 0:NCH - sh], lt[:, k, :], cur[:, sh:NCH], False, True, [ltk, curk], [bkey])
                                eng = "act" if (k + ch["ci"]) % 2 == 0 else "dve"
                                if k == nsteps - 1:
                                    off = 1 if d == 0 else 0
                                    sxt, sxtk = Sext[d][ch["gs"]], ("Sext", d, ch["gs"])
                                    cp(eng, sxt[:, off:off + NCH], bk[:, 0:NCH], [bkey], [sxtk])
                                else:
                                    ch["par"] = 1 - ch["par"]
                                    ni = ch["ci"] * 2 + ch["par"]
                                    nxt, nxtk = Sb[ni], ("Sb", ni)
                                    cp(eng, nxt[:], bk[:, 0:NCH], [bkey], [nxtk])
                                    ch["cur"], ch["curk"] = nxt, nxtk
                        for ch in chains:
                            d, u = ch["d"], ch["u"]
                            sxt, sxtk = Sext[d][ch["gs"]], ("Sext", d, ch["gs"])
                            ecol = 0 if d == 0 else NCH
                            if ch["init"]:
                                cp("dve", sxt[:, ecol:ecol + 1], st0[:, u:u + 1], ["st0"], [sxtk])
                            else:
                                cp("dve", sxt[:, ecol:ecol + 1], zcol[:], ["zcol"], [sxtk])
                                fcol = NCH if d == 0 else 0
                                cp("dve", st0[:, u:u + 1], sxt[:, fcol:fcol + 1], [sxtk], ["st0"])
                        for (gl, g, gs_, tb, tbk, ug, ugk) in ginfo:
                            sx = Sext[0][gs_], Sext[1][gs_]
                            sxk = ("Sext", 0, gs_), ("Sext", 1, gs_)
                            for th in range(2):
                                bk, bkey = bank()
                                cs = slice(th * 128, (th + 1) * 128)
                                mm(bk[:, 0:NCH], tb[:, 1024:1280][:, cs], ug[:, 0, :], True, False, [tbk, ugk], [bkey])
                                mm(bk[:, 0:NCH], tb[:, 1280:1536][:, cs], ug[:, 1, :], False, False, [tbk, ugk], [bkey])
                                mm(bk[:, 0:NCH], tb[:, 512:768][:, cs], sx[0][:, 0:NCH], False, False, [tbk, sxk[0]], [bkey])
                                mm(bk[:, 0:NCH], tb[:, 768:1024][:, cs], sx[1][:, 1:NCH + 1], False, True, [tbk, sxk[1]], [bkey])
                                act(Yg[:, gl, th, :], bk[:, 0:NCH], AF.Gelu_apprx_tanh, [bkey], [("Yg", gl)])
                    for j in range(16):
                        jh, jj = divmod(j, 8)
                        bk, bkey = bank()
                        for gl in range(8):
                            mm(bk[:, 0:NCH], selTb[:, gl * 8 + jj, :], Yg[:, gl, jh, :], gl == 0, gl == 7, ["selTb", ("Yg", gl)], [bkey])
                        cp("act" if j % 2 == 0 else "dve", ysT[:, cbk, j:L:16], bk[:, 0:NCH], [bkey], [("ysT", cbk)])
                wgl = sb(ph, "wgl", [128, 4, 512], BF16)
                bgl = sb(ph, "bgl", [128, 4], F32)
                dma("pool", wgl[:], w_glu_d[l].rearrange("(k p) n -> p k n", p=128), writes=["wgl"])
                dma("sp", bgl[:], s5bglu_d[l], writes=["bgl"])
                TB = min(512, L)
                sig = [sb(ph, "sig%d" % i, [128, TB], BF16) for i in range(2)]
                go = [sb(ph, "go%d" % i, [128, TB], BF16) for i in range(2)]
                cnt = 0
                for t0 in range(0, L, TB):
                    for mo in range(4):
                        sl = cnt % 2
                        cnt += 1
                        bk, bkey = bank()
                        for k in range(4):
                            mm(bk[:, 0:TB], wgl[:, k, mo * 128:(mo + 1) * 128], ysT[:, k, t0:t0 + TB], k == 0, k == 3, ["wgl", ("ysT", k)], [bkey])
                        act(sig[sl][:], bk[:, 0:TB], AF.Sigmoid, [bkey, "bgl"], [("sig", sl)], bias=bgl[:, mo:mo + 1])
                        tt("dve" if mo % 2 == 0 else "pool", go[sl][:], sig[sl][:], ysT[:, mo, t0:t0 + TB], ALU.mult, [("sig", sl), ("ysT", mo)], [("go", sl)])
                        dma("sp", yT[s][512 + mo * 128:512 + (mo + 1) * 128, t0:t0 + TB], go[sl][:], reads=[("go", sl)])
                P.barrier()

        def phase_feed_y():
            with ExitStack() as ph:
                k0, k1_ = feed_y
                nk = k1_ - k0
                t = sb(ph, "fyt", [128, nk, 2048], F32)
                tb = sb(ph, "fytb", [128, nk, 2048], BF16)
                for s, srcd in (("c", fy_c), ("x", fy_x)):
                    L = seqs[s].L
                    TB = min(2048, L)
                    for t0 in range(0, L, TB):
                        dma("sp", t[:, :, 0:TB], srcd.rearrange("(k p) t -> p k t", p=128)[:, k0:k1_, t0:t0 + TB], writes=["fyt"])
                        cp("dve", tb[:, :, 0:TB], t[:, :, 0:TB], ["fyt"], ["fytb"])
                        dma("sp", yT[s].rearrange("(k p) t -> p k t", p=128)[:, k0:k1_, t0:t0 + TB], tb[:, :, 0:TB], reads=["fytb"])
                P.barrier()

        for l in range(depth_run):
            phase_mod(l)
            phase_in(l)
            if not feed_y or feed_y[1] < 8:
                phase_s5tab(l)
                phase_s5(l, "c")
                phase_s5(l, "x")
            if not feed_y or feed_y[0] > 0:
                phase_hyprep(l)
                if l < DEPTH - 1:
                    phase_hyena(l, "c")
                phase_hyena(l, "x")
            if feed_y:
                phase_feed_y()
            phase_out(l)
            phase_ffn(l)

        P.barrier()
        print("program ops:", P.nops, {e: len(v) for e, v in P.streams.items()})
        with nc.Block() as block:
            P.emit(block)
    return nc, dram_in


_CACHE = {}


def kernel(**inputs):
    x = np.asarray(inputs["x"], dtype=np.float32)
    ctx = np.asarray(inputs["ctx"], dtype=np.float32)
    c = np.asarray(inputs["c"], dtype=np.float32)
    c_ctx = np.asarray(inputs["c_ctx"], dtype=np.float32)
    B = x.shape[0]
    if "nc" not in _CACHE:
        _CACHE["nc"] = build_program()
    nc, _ = _CACHE["nc"]
    weights = layout_weights(inputs)
    consts = host_constants()
    shared = dict(weights)
    shared.update(consts)
    idle = {k: np.zeros_like(v) for k, v in weights.items()}
    for k in ("s5lam", "s5ls"):
        idle[k] = weights[k]
    idle.update(consts)
    idle["x"] = np.zeros((SEQ, D), np.float32)
    idle["ctx"] = np.zeros((CTX, D), np.float32)
    idle["cT"] = np.zeros((128, 8, 2), np.float32)
    active = [0, 2, 4, 6][:B]
    in_maps = []
    for core in range(8):
        if core in active:
            b = active.index(core)
            m = dict(shared)
            m["x"] = _f32(x[b])
            m["ctx"] = _f32(ctx[b])
            cT = np.stack([c[b].reshape(8, 128).T, c_ctx.reshape(8, 128).T], axis=-1)
            m["cT"] = _f32(cT)
        else:
            m = idle
        in_maps.append(m)
    res = run_bass_kernel_spmd(nc, in_maps, core_ids=list(range(8)))
    out = np.stack([np.asarray(res.results[active[b]]["y"], dtype=np.float32) for b in range(B)], axis=0)
    return out
```
